# Optimizing a Trainium2 kernel written in Bass

```python
import math
import jax, jax.numpy as jnp
from jax import lax
import numpy as np

D_MODEL = 4096
BATCH = 2
SEQ = 8192
DEPTH = 1

SSM_EXPAND = 2
D_SSM = SSM_EXPAND * D_MODEL
SSM_HEADDIM = 64
SSM_HEADS = D_SSM // SSM_HEADDIM
SSM_GROUPS = 8
SSM_STATE = 128
CONV_WIDTH = 4
CHUNK = 128
D_CONV = D_SSM + 2 * SSM_GROUPS * SSM_STATE

MLA_HEADS = 32
QK_NOPE = 128
QK_ROPE = 64
V_HEAD = 128
Q_LORA = 1024
KV_LORA = 512
D_ATTN = MLA_HEADS * V_HEAD
ROPE_THETA = 10000.0
Q_BLOCK = 128

EPS = 1e-6

IN_SIZES = (D_SSM, D_CONV, SSM_HEADS, Q_LORA, KV_LORA + QK_ROPE, D_ATTN, D_MODEL, D_MODEL)
D_IN = D_SSM + D_CONV + SSM_HEADS + Q_LORA + KV_LORA + QK_ROPE + D_ATTN + 2 * D_MODEL

kernel_name = "hybrid_ssd_mla_gated_merge"


def rmsnorm(x, w):
    xf = x.astype(jnp.float32)
    xf = xf * lax.rsqrt(jnp.mean(xf * xf, axis=-1, keepdims=True) + EPS)
    return xf.astype(x.dtype) * w


def causal_depthwise_conv(u, w, b):
    L = u.shape[1]
    up = jnp.pad(u, ((0, 0), (CONV_WIDTH - 1, 0), (0, 0)))
    out = b
    for k in range(CONV_WIDTH):
        out = out + up[:, k:k + L] * w[k]
    return out


def rope(x, pos):
    half = x.shape[-1] // 2
    inv_freq = ROPE_THETA ** (-jnp.arange(0, half, dtype=jnp.float32) / half)
    ang = pos.astype(jnp.float32)[:, None] * inv_freq[None, :]
    cos = jnp.cos(ang)[None, :, None, :]
    sin = jnp.sin(ang)[None, :, None, :]
    xf = x.astype(jnp.float32)
    x1, x2 = xf[..., :half], xf[..., half:]
    out = jnp.concatenate([x1 * cos - x2 * sin, x2 * cos + x1 * sin], axis=-1)
    return out.astype(x.dtype)


def ssd_chunked(x, dt, A, Bm, Cm):
    b, L = x.shape[:2]
    c = L // CHUNK
    r = SSM_HEADS // SSM_GROUPS
    G, N, P = SSM_GROUPS, SSM_STATE, SSM_HEADDIM
    xdt = (x.astype(jnp.float32) * dt[..., None]).reshape(b, c, CHUNK, G, r, P)
    a = (dt * A).reshape(b, c, CHUNK, G, r)
    Bc = Bm.astype(jnp.float32).reshape(b, c, CHUNK, G, N)
    Cc = Cm.astype(jnp.float32).reshape(b, c, CHUNK, G, N)
    a_cs = jnp.cumsum(a, axis=2)
    seg = a_cs[:, :, :, None] - a_cs[:, :, None]
    causal = jnp.tril(jnp.ones((CHUNK, CHUNK), dtype=bool))[None, None, :, :, None, None]
    decay = jnp.exp(jnp.where(causal, seg, -jnp.inf))
    cb = jnp.einsum("bclgn,bcsgn->bclsg", Cc, Bc)
    scores = cb[..., None] * decay
    y_diag = jnp.einsum("bclsgr,bcsgrp->bclgrp", scores, xdt)
    decay_states = jnp.exp(a_cs[:, :, -1:] - a_cs)
    states = jnp.einsum("bclgn,bclgrp->bcgrpn", Bc, xdt * decay_states[..., None])
    chunk_decay = jnp.exp(a_cs[:, :, -1])

    def step(h, inp):
        s, d = inp
        return h * d[..., None, None] + s, h

    h0 = jnp.zeros((b, G, r, P, N), jnp.float32)
    _, prev = lax.scan(step, h0, (jnp.moveaxis(states, 1, 0), jnp.moveaxis(chunk_decay, 1, 0)))
    prev = jnp.moveaxis(prev, 0, 1)
    y_off = jnp.einsum("bclgn,bcgrpn->bclgrp", Cc, prev) * jnp.exp(a_cs)[..., None]
    return (y_diag + y_off).reshape(b, L, SSM_HEADS, P)


def gated_group_rmsnorm(y, z, w):
    b, L, _ = y.shape
    g = (y.astype(jnp.float32) * jax.nn.silu(z.astype(jnp.float32))).reshape(b, L, SSM_GROUPS, D_SSM // SSM_GROUPS)
    g = g * lax.rsqrt(jnp.mean(g * g, axis=-1, keepdims=True) + EPS)
    return g.reshape(b, L, D_SSM).astype(y.dtype) * w


def mla(cq_raw, kv_raw, q_norm_w, w_uq, kv_norm_w, w_ukv, pos):
    b, L = cq_raw.shape[:2]
    cq = rmsnorm(cq_raw, q_norm_w)
    q = (cq @ w_uq).reshape(b, L, MLA_HEADS, QK_NOPE + QK_ROPE)
    q_nope = q[..., :QK_NOPE]
    q_rope = rope(q[..., QK_NOPE:], pos)
    ckv = rmsnorm(kv_raw[..., :KV_LORA], kv_norm_w)
    k_rope = rope(kv_raw[..., KV_LORA:][:, :, None, :], pos)[:, :, 0]
    kv = (ckv @ w_ukv).reshape(b, L, MLA_HEADS, QK_NOPE + V_HEAD)
    k_nope, v = kv[..., :QK_NOPE], kv[..., QK_NOPE:]
    scale = 1.0 / math.sqrt(QK_NOPE + QK_ROPE)
    nb = L // Q_BLOCK
    qn = q_nope.reshape(b, nb, Q_BLOCK, MLA_HEADS, QK_NOPE).swapaxes(0, 1)
    qr = q_rope.reshape(b, nb, Q_BLOCK, MLA_HEADS, QK_ROPE).swapaxes(0, 1)
    kpos = pos

    def block(args):
        qn_i, qr_i, i = args
        s = jnp.einsum("bqhd,bkhd->bhqk", qn_i, k_nope) + jnp.einsum("bqhr,bkr->bhqk", qr_i, k_rope)
        s = s.astype(jnp.float32) * scale
        qpos = lax.dynamic_slice_in_dim(pos, i * Q_BLOCK, Q_BLOCK)
        mask = kpos[None, :] <= qpos[:, None]
        s = jnp.where(mask[None, None], s, -jnp.inf)
        p = jax.nn.softmax(s, axis=-1).astype(v.dtype)
        return jnp.einsum("bhqk,bkhd->bqhd", p, v)

    o = lax.map(block, (qn, qr, jnp.arange(nb)))
    return o.swapaxes(0, 1).reshape(b, L, D_ATTN)


def setup_inputs(seed: int = 0) -> dict:
    key = jax.random.key(seed)
    ks = jax.random.split(key, 20)
    f32 = jnp.float32

    def lin(k, fan_in, fan_out):
        return jax.random.normal(k, (DEPTH, fan_in, fan_out), f32) * fan_in ** -0.5

    def gain(k, n):
        return 1.0 + 0.02 * jax.random.normal(k, (DEPTH, n), f32)

    dt0 = jnp.exp(jax.random.uniform(ks[5], (DEPTH, SSM_HEADS), f32, math.log(1e-3), math.log(1e-1)))
    dt_bias = dt0 + jnp.log(-jnp.expm1(-dt0))
    a_log = jnp.log(jax.random.uniform(ks[6], (DEPTH, SSM_HEADS), f32, 1.0, 16.0))
    return {
        "x": jax.random.normal(ks[0], (BATCH, SEQ, D_MODEL), f32),
        "norm_in_w": gain(ks[1], D_MODEL),
        "w_in": lin(ks[2], D_MODEL, D_IN),
        "conv_w": jax.random.normal(ks[3], (DEPTH, CONV_WIDTH, D_CONV), f32) * CONV_WIDTH ** -0.5,
        "conv_b": 0.01 * jax.random.normal(ks[4], (DEPTH, D_CONV), f32),
        "dt_bias": dt_bias,
        "a_log": a_log,
        "d_skip": 1.0 + 0.1 * jax.random.normal(ks[7], (DEPTH, SSM_HEADS), f32),
        "ssm_norm_w": gain(ks[8], D_SSM),
        "q_norm_w": gain(ks[9], Q_LORA),
        "w_uq": lin(ks[10], Q_LORA, MLA_HEADS * (QK_NOPE + QK_ROPE)),
        "kv_norm_w": gain(ks[11], KV_LORA),
        "w_ukv": lin(ks[12], KV_LORA, MLA_HEADS * (QK_NOPE + V_HEAD)),
        "w_branch_ssm": lin(ks[13], D_SSM, D_MODEL),
        "w_branch_attn": lin(ks[14], D_ATTN, D_MODEL),
        "w_out": lin(ks[15], D_MODEL, D_MODEL),
        "norm_final_w": 1.0 + 0.02 * jax.random.normal(ks[16], (D_MODEL,), f32),
    }


def reference(x, norm_in_w, w_in, conv_w, conv_b, dt_bias, a_log, d_skip, ssm_norm_w,
              q_norm_w, w_uq, kv_norm_w, w_ukv, w_branch_ssm, w_branch_attn, w_out, norm_final_w):
    b, L, _ = x.shape
    pos = jnp.arange(L, dtype=jnp.int32)
    splits = tuple(int(s) for s in np.cumsum(IN_SIZES)[:-1])
    h = x
    for layer in range(DEPTH):
        u = rmsnorm(h, norm_in_w[layer])
        proj = u @ w_in[layer]
        z, xbc, dt_raw, cq_raw, kv_raw, g_attn, gate_ssm, gate_attn = jnp.split(proj, splits, axis=-1)

        xbc = jax.nn.silu(causal_depthwise_conv(xbc, conv_w[layer], conv_b[layer]))
        xs = xbc[..., :D_SSM]
        Bm = xbc[..., D_SSM:D_SSM + SSM_GROUPS * SSM_STATE].reshape(b, L, SSM_GROUPS, SSM_STATE)
        Cm = xbc[..., D_SSM + SSM_GROUPS * SSM_STATE:].reshape(b, L, SSM_GROUPS, SSM_STATE)
        dt = jax.nn.softplus(dt_raw.astype(jnp.float32) + dt_bias[layer].astype(jnp.float32))
        A = -jnp.exp(a_log[layer].astype(jnp.float32))
        xh = xs.reshape(b, L, SSM_HEADS, SSM_HEADDIM)
        y = ssd_chunked(xh, dt, A, Bm, Cm) + xh.astype(jnp.float32) * d_skip[layer].astype(jnp.float32)[:, None]
        y_ssm = gated_group_rmsnorm(y.astype(x.dtype).reshape(b, L, D_SSM), z, ssm_norm_w[layer])

        o = mla(cq_raw, kv_raw, q_norm_w[layer], w_uq[layer], kv_norm_w[layer], w_ukv[layer], pos)
        y_attn = o * jax.nn.silu(g_attn)

        merged = (jax.nn.sigmoid(gate_ssm) * (y_ssm @ w_branch_ssm[layer])
                  + jax.nn.sigmoid(gate_attn) * (y_attn @ w_branch_attn[layer]))
        h = h + merged @ w_out[layer]
    return rmsnorm(h, norm_final_w)
```

```python
import os
import math
import numpy as np
import ml_dtypes
import concourse.bass as bass
import concourse.mybir as mybir
from concourse.bass_utils import run_bass_kernel_spmd

F32 = mybir.dt.float32
BF16 = mybir.dt.bfloat16
AF = mybir.ActivationFunctionType
ALU = mybir.AluOpType
EPS = 1e-6

D_MODEL = 4096
D_SSM = 8192
D_CONV = 10240
SEQ = 8192


class Buf:
    def __init__(self, name, ap, accum=False):
        self.name = name
        self.ap = ap
        self.ws = {}
        self.rs = {}
        self.accum = accum
        self.dsem = None
        self.dcount = 0

    def __getitem__(self, idx):
        return self.ap[idx]


class Prog:
    ENGS = ("pe", "act", "dve", "pool", "sp")

    def __init__(self, nc):
        self.nc = nc
        self.streams = {e: [] for e in self.ENGS}
        self.cnt = {e: 0 for e in self.ENGS}
        self.sems = {}
        self.semval = {}
        self._ctx = []
        self._semctx = []
        self.pending = {e: [] for e in self.ENGS}
        self.nsem = 0
        self.ninstr = 0
        for e in ("pe", "act", "dve", "pool"):
            self._mksem(e)

    def _mksem(self, key):
        self.nsem += 1
        g = self.nc.semaphore("sem%d" % self.nsem)
        h = g.__enter__()
        self._semctx.append(g)
        self.sems[key] = h
        self.semval[key] = 0
        return h

    def sb(self, name, shape, dt):
        g = self.nc.sbuf_tensor("sb_" + name, list(shape), dt)
        t = g.__enter__()
        self._ctx.append(g)
        return Buf(name, t)

    def ps(self, name, shape, dt=F32):
        g = self.nc.psum_tensor(name, list(shape), dt)
        t = g.__enter__()
        self._ctx.append(g)
        return Buf(name, t)

    def dram(self, name, shape, dt, kind="Internal"):
        t = self.nc.dram_tensor(name, list(shape), dt, kind=kind)
        return Buf(name, t, accum=True)

    def _deps(self, eng, reads, writes):
        need = {}

        def add(k, v):
            if k == "pe" and eng == "pe":
                return
            if need.get(k, 0) < v:
                need[k] = v
        for b in reads:
            for k, v in b.ws.items():
                add(k, v)
        for b in writes:
            if b.accum:
                continue
            for k, v in b.ws.items():
                add(k, v)
            for k, v in b.rs.items():
                add(k, v)
        return need

    @staticmethod
    def _rec(tok, reads, writes):
        k, v = tok
        for b in reads:
            b.rs[k] = v
        for b in writes:
            if b.accum:
                b.ws[k] = v
            else:
                b.ws = {k: v}
                b.rs = {}

    def op(self, eng, fn, reads=(), writes=(), signal=True):
        need = self._deps(eng, reads, writes)
        waits = [(self.sems[k], v) for k, v in need.items()]
        self.ninstr += 1
        if signal:
            self.cnt[eng] += 1
            self.semval[eng] = self.cnt[eng]
            tok = (eng, self.cnt[eng])
            pr = [b for (b, kind) in self.pending[eng] if kind == "r"]
            pw = [b for (b, kind) in self.pending[eng] if kind == "w"]
            self.pending[eng] = []
            self._rec(tok, list(reads) + pr, list(writes) + pw)
        else:
            for b in reads:
                self.pending[eng].append((b, "r"))
            for b in writes:
                self.pending[eng].append((b, "w"))
        sem = self.sems[eng]

        def thunk(e, waits=waits, fn=fn, signal=signal, sem=sem):
            for (s, v) in waits:
                e.wait_ge(s, v)
            ins = fn(e)
            if signal:
                ins.then_inc(sem, 1)
        self.streams[eng].append(thunk)

    def dma(self, q, out_ap, in_ap, reads=(), writes=(), sembuf=None):
        need = self._deps(q, reads, writes)
        waits = [(self.sems[k], v) for k, v in need.items()]
        self.ninstr += 1
        sbf = sembuf
        if sbf is None:
            cands = [b for b in list(writes) + list(reads) if not b.accum]
            sbf = cands[0] if cands else (list(writes) + list(reads))[0]
        if sbf.dsem is None:
            sbf.dsem = ("d", self.nsem + 1)
            self._mksem(sbf.dsem)
        sbf.dcount += 16
        self.semval[sbf.dsem] = sbf.dcount
        tok = (sbf.dsem, sbf.dcount)
        self._rec(tok, reads, writes)
        sem = self.sems[sbf.dsem]

        def thunk(e, waits=waits, sem=sem, out_ap=out_ap, in_ap=in_ap):
            for (s, v) in waits:
                e.wait_ge(s, v)
            e.dma_start(out=out_ap, in_=in_ap).then_inc(sem, 16)
        self.streams[q].append(thunk)

    def barrier(self):
        for e in self.ENGS:
            assert not self.pending[e]
        waits = [(self.sems[k], v) for k, v in self.semval.items() if v > 0]
        for eng in self.ENGS:
            def thunk(e, waits=waits):
                for (s, v) in waits:
                    e.wait_ge(s, v)
            self.streams[eng].append(thunk)

    def emit(self):
        nc = self.nc
        for e in self.ENGS:
            assert not self.pending[e], "unsignalled pending ops on %s" % e
        with nc.Block() as block:
            @block.tensor
            def _(e):
                for t in self.streams["pe"]:
                    t(e)

            @block.scalar
            def _(e):
                for t in self.streams["act"]:
                    t(e)

            @block.vector
            def _(e):
                for t in self.streams["dve"]:
                    t(e)

            @block.gpsimd
            def _(e):
                for t in self.streams["pool"]:
                    t(e)

            @block.sync
            def _(e):
                for t in self.streams["sp"]:
                    t(e)

    def close(self):
        for g in reversed(self._ctx):
            g.__exit__(None, None, None)
        for g in reversed(self._semctx):
            g.__exit__(None, None, None)

    def free_from(self, mark):
        while len(self._ctx) > mark:
            g = self._ctx.pop()
            g.__exit__(None, None, None)


class Rot:
    def __init__(self, items):
        self.items = items
        self.i = 0

    def next(self):
        b = self.items[self.i % len(self.items)]
        self.i += 1
        return b


class K:
    def __init__(self, L, debug=(), phases="AMBCGD"):
        self.L = L
        self.phases = phases
        self.debug = set(debug)
        self.nc = nc = bass.Bass("TRN2", target_bir_lowering=False)
        nc.cache_partition_id()
        self.P = Prog(nc)
        self.inp = {}
        self.out_bufs = []
        self.NB = L // 1024
        self.LS = L // 4

    def ein(self, name, shape, dt=F32):
        need = {"w_gates": "D", "w_bs": "D", "w_ba": "D", "w_out": "D", "xres": "D", "xTs": "D", "nw_fin": "D",
                "tokoff": "D", "w_in": "A", "xT": "A"}
        if name in need and need[name] not in self.phases:
            return None
        t = self.nc.dram_tensor(name, list(shape), dt, kind="ExternalInput")
        b = Buf(name, t)
        self.inp[name] = b
        return b

    def scratch(self, name, shape, dt):
        kind = "ExternalOutput" if name in self.debug else "Internal"
        b = self.P.dram(name, shape, dt, kind=kind)
        if kind == "ExternalOutput":
            self.out_bufs.append(b)
        return b

    def declare(self):
        L, NB, LS = self.L, self.NB, self.LS
        e = self.ein
        self.xT = e("xT", [NB, 128, 32, 1024])
        self.xTs = e("xTs", [128, 32, LS])
        self.xres = e("xres", [LS, 4096])
        self.w_in = e("w_in", [15, 128, 32, 512])
        self.w_gates = e("w_gates", [16, 128, 32, 512])
        self.w_bs = e("w_bs", [16, 128, 32, 512])
        self.w_ba = e("w_ba", [8, 128, 32, 512])
        self.w_out = e("w_out", [8, 128, 32, 512])
        self.w_uq = e("w_uq", [128, 8, 2048])
        self.w_ukv = e("w_ukv", [128, 4, 2048])
        self.convp = e("convp", [128, 20, 5])
        self.nw_in = e("nw_in", [128, 32])
        self.nw_q = e("nw_q", [128, 8])
        self.nw_kv = e("nw_kv", [128, 4])
        self.dtb = e("dtb", [128, 32])
        self.alog = e("alog", [128, 32])
        self.dskip = e("dskip", [128, 32])
        self.nw_ssm = e("nw_ssm", [128, 2048])
        self.nw_fin = e("nw_fin", [128, 4096])
        self.ropec = e("ropec", [64, L])
        self.ropes = e("ropes", [64, L])
        self.c_ident = e("c_ident", [128, 128], BF16)
        self.c_ones = e("c_ones", [128, 128], BF16)
        self.c_onesf = e("c_onesf", [128, 128])
        self.c_trile = e("c_trile", [128, 128])
        self.c_trigt = e("c_trigt", [128, 128])
        self.c_maskb = e("c_maskb", [128, 128], BF16)
        s = self.scratch
        self.SZ = s("SZ", [L, 2048], BF16)
        self.XC = s("XC", [2560, L], BF16)
        self.DT = s("DT", [L, 32], F32)
        self.CQ = s("CQ", [1024, L], BF16)
        self.CKV = s("CKV", [512, L], BF16)
        self.RR = s("RR", [128, L], F32)
        self.GA = s("GA", [1024, L], BF16)
        self.QN = s("QN", [8, 128, L], BF16)
        self.QR = s("QR", [8, 64, L], BF16)
        self.KN = s("KN", [8, 128, L], BF16)
        self.KR = s("KR", [64, L], BF16)
        self.V = s("V", [L, 1024], BF16)
        self.YC = s("YC", [24, 4, 128, LS], BF16)
        self.YG = s("YG", [24, 4, 4, 128, LS], BF16)
        self.YQ = s("YQ", [24, 4, 128, LS], BF16)
        self.OUT = self.P.dram("out", [LS, 4096], F32, kind="ExternalOutput")
        self.out_bufs.append(self.OUT)

    def yc_store(self, jt0, nj, t0, ntok, src_fn, srcbuf):
        LS = self.LS
        t = t0
        while t < t0 + ntok:
            tq, tl = t // LS, t % LS
            n = min(LS - tl, t0 + ntok - t)
            dst = self.YC.ap[jt0:jt0 + nj, tq, :, tl:tl + n].rearrange("j p t -> p j t")
            self.P.dma("sp", dst, src_fn(t - t0, t - t0 + n), reads=[srcbuf], writes=[self.YC])
            t += n

    def load_consts(self):
        P = self.P
        self.ident = P.sb("ident", [128, 128], BF16)
        self.ones = P.sb("ones", [128, 128], BF16)
        for sbuf, src in ((self.ident, self.c_ident), (self.ones, self.c_ones)):
            P.dma("sp", sbuf[:], src[:], reads=[src], writes=[sbuf])
        self.psb = [P.ps("psb%d" % i, [128, 512]) for i in range(8)]
        self.psr = Rot(self.psb)

    def make_uT(self, src_fn, nsub, uT, xs, sq, rt, rs, win, ps_rot):
        P = self.P
        ones = self.ones
        epsb = self.epsb
        for s in range(nsub):
            x = xs[s % len(xs)]
            P.dma("sp", x[:], src_fn(s)[0], reads=[src_fn(s)[1]], writes=[x])
            P.op("act", lambda e, x=x: e.activation(out=sq[:], in_=x[:], func=AF.Square),
                 reads=[x], writes=[sq])
            ps = ps_rot.next()
            for kc in range(32):
                P.op("pe", lambda e, kc=kc, ps=ps: e.matmul(ps[:, 0:128], lhsT=ones[:], rhs=sq[:, kc, :],
                                                          start=(kc == 0), stop=(kc == 31)),
                     reads=[ones, sq], writes=[ps], signal=(kc == 31))
            P.op("act", lambda e, ps=ps: e.activation(out=rt[:], in_=ps[:, 0:128], func=AF.Sqrt,
                                                      scale=1.0 / D_MODEL, bias=epsb[:, 0:1]),
                 reads=[ps, epsb], writes=[rt])
            P.op("dve", lambda e: e.reciprocal(out=rs[:], in_=rt[:]), reads=[rt], writes=[rs])
            P.op("dve", lambda e, x=x: e.tensor_tensor(out=x[:], in0=x[:],
                                                       in1=win[:].unsqueeze(2).to_broadcast([128, 32, 128]),
                                                       op=ALU.mult),
                 reads=[x, win], writes=[x])
            P.op("dve", lambda e, x=x, s=s: e.tensor_tensor(out=uT[:, :, s * 128:(s + 1) * 128], in0=x[:],
                                                            in1=rs[:].unsqueeze(1).to_broadcast([128, 32, 128]),
                                                            op=ALU.mult),
                 reads=[x, rs], writes=[uT])

    def phase_A(self):
        P, L, NB = self.P, self.L, self.NB
        mark = len(P._ctx)
        uT = P.sb("uT", [128, 32, 1024], BF16)
        xs = [P.sb("xs%d" % i, [128, 32, 128], F32) for i in range(2)]
        sq = P.sb("sq", [128, 32, 128], BF16)
        rt = P.sb("rt", [128, 128], F32)
        rs = P.sb("rs", [128, 128], F32)
        win = P.sb("win", [128, 32], F32)
        self.epsb = P.sb("epsb", [128, 1], F32)
        P.op("pool", lambda e, t=self.epsb: e.memset(t[:], EPS), writes=[self.epsb])
        P.dma("sp", win[:], self.nw_in[:], reads=[self.nw_in], writes=[win])
        wts = Rot([P.sb("wt%d" % i, [128, 32, 512], BF16) for i in range(2)])
        stg = Rot([P.sb("stg%d" % i, [128, 512], BF16) for i in range(4)])
        stf = Rot([P.sb("stf%d" % i, [128, 512], F32) for i in range(2)])
        raws = Rot([P.sb("raw%d" % i, [128, 515], F32) for i in range(2)])
        accs = Rot([P.sb("acc%d" % i, [128, 512], F32) for i in range(2)])
        halo = P.sb("halo", [128, 20, 3], F32)
        cp = P.sb("cp", [128, 20, 5], F32)
        dtb = P.sb("dtb", [128, 32], F32)
        dtx = P.sb("dtx", [128, 8, 32], F32)
        dte = P.sb("dte", [128, 8, 32], F32)
        dtt = P.sb("dtt", [128, 8, 32], F32)
        P.dma("sp", cp[:], self.convp[:], reads=[self.convp], writes=[cp])
        P.dma("sp", dtb[:], self.dtb[:], reads=[self.dtb], writes=[dtb])
        psr = self.psr

        for tb in range(NB):
            t0 = tb * 1024
            self.make_uT(lambda s, tb=tb: (self.xT[tb, :, :, s * 128:(s + 1) * 128], self.xT),
                         8, uT, xs, sq, rt, rs, win, psr)
            for t in range(15):
                wt = wts.next()
                P.dma("pool", wt[:], self.w_in[t], reads=[self.w_in], writes=[wt])
                if t < 4:
                    for tt in range(8):
                        ps = psr.next()
                        for kc in range(32):
                            P.op("pe", lambda e, kc=kc, ps=ps, tt=tt, wt=wt: e.matmul(
                                ps[:], lhsT=uT[:, kc, tt * 128:(tt + 1) * 128], rhs=wt[:, kc, :],
                                start=(kc == 0), stop=(kc == 31)),
                                reads=[uT, wt], writes=[ps], signal=(kc == 31))
                        st = stg.next()
                        P.op("act", lambda e, ps=ps, st=st: e.activation(out=st[:], in_=ps[:], func=AF.Silu),
                             reads=[ps], writes=[st])
                        P.dma("sp", self.SZ[t0 + tt * 128:t0 + (tt + 1) * 128, t * 512:(t + 1) * 512], st[:],
                              reads=[st], writes=[self.SZ])
                    continue
                ncs = 1 if t == 14 else 4
                for cs in range(ncs):
                    for th in range(2):
                        ps = psr.next()
                        for kc in range(32):
                            P.op("pe", lambda e, kc=kc, ps=ps, cs=cs, th=th, wt=wt: e.matmul(
                                ps[:], lhsT=wt[:, kc, cs * 128:(cs + 1) * 128], rhs=uT[:, kc, th * 512:(th + 1) * 512],
                                start=(kc == 0), stop=(kc == 31)),
                                reads=[uT, wt], writes=[ps], signal=(kc == 31))
                        tok = slice(t0 + th * 512, t0 + (th + 1) * 512)
                        if 4 <= t <= 8:
                            idx = (t - 4) * 4 + cs
                            raw = raws.next()
                            acc = accs.next()
                            P.op("dve", lambda e, raw=raw, ps=ps: e.tensor_copy(out=raw[:, 3:515], in_=ps[:]),
                                 reads=[ps], writes=[raw])
                            if tb == 0 and th == 0:
                                P.op("pool", lambda e, raw=raw: e.memset(raw[:, 0:3], 0.0), writes=[raw])
                            else:
                                P.op("pool", lambda e, raw=raw, idx=idx: e.tensor_copy(out=raw[:, 0:3], in_=halo[:, idx, :]),
                                     reads=[halo], writes=[raw])
                            P.op("pool", lambda e, raw=raw, idx=idx: e.tensor_copy(out=halo[:, idx, :], in_=raw[:, 512:515]),
                                 reads=[raw], writes=[halo])
                            P.op("dve", lambda e, raw=raw, acc=acc, idx=idx: e.tensor_scalar(
                                out=acc[:], in0=raw[:, 3:515], scalar1=cp[:, idx, 3:4], scalar2=cp[:, idx, 4:5],
                                op0=ALU.mult, op1=ALU.add), reads=[raw, cp], writes=[acc])
                            for k in range(3):
                                P.op("dve", lambda e, raw=raw, acc=acc, idx=idx, k=k: e.scalar_tensor_tensor(
                                    out=acc[:], in0=raw[:, k:k + 512], scalar=cp[:, idx, k:k + 1], in1=acc[:],
                                    op0=ALU.mult, op1=ALU.add), reads=[raw, cp, acc], writes=[acc])
                            st = stg.next()
                            P.op("act", lambda e, acc=acc, st=st: e.activation(out=st[:], in_=acc[:], func=AF.Silu),
                                 reads=[acc], writes=[st])
                            P.dma("sp", self.XC[idx * 128:(idx + 1) * 128, tok], st[:], reads=[st], writes=[self.XC])
                        elif t in (9, 10, 11, 12, 13):
                            st = stg.next()
                            fn = AF.Silu if t >= 12 else AF.Copy
                            P.op("act", lambda e, ps=ps, st=st, fn=fn: e.activation(out=st[:], in_=ps[:], func=fn),
                                 reads=[ps], writes=[st])
                            if t <= 10:
                                r0 = ((t - 9) * 4 + cs) * 128
                                dst, dbuf = self.CQ[r0:r0 + 128, tok], self.CQ
                            elif t == 11:
                                dst, dbuf = self.CKV[cs * 128:(cs + 1) * 128, tok], self.CKV
                            else:
                                r0 = ((t - 12) * 4 + cs) * 128
                                dst, dbuf = self.GA[r0:r0 + 128, tok], self.GA
                            P.dma("sp", dst, st[:], reads=[st], writes=[dbuf])
                        else:
                            st = stf.next()
                            P.op("act", lambda e, ps=ps, st=st: e.activation(out=st[:], in_=ps[:], func=AF.Copy),
                                 reads=[ps], writes=[st])
                            P.dma("sp", self.RR[:, tok], st[:], reads=[st], writes=[self.RR])
                if t == 14:
                    ps = psr.next()
                    for tt in range(8):
                        for kc in range(32):
                            P.op("pe", lambda e, kc=kc, ps=ps, tt=tt, wt=wt: e.matmul(
                                ps[:, tt * 32:(tt + 1) * 32], lhsT=uT[:, kc, tt * 128:(tt + 1) * 128],
                                rhs=wt[:, kc, 128:160], start=(kc == 0), stop=(kc == 31)),
                                reads=[uT, wt], writes=[ps], signal=(kc == 31 and tt == 7))
                    P.op("dve", lambda e, ps=ps: e.tensor_tensor(
                        out=dtx[:], in0=ps[:, 0:256].rearrange("p (a b) -> p a b", a=8),
                        in1=dtb[:].unsqueeze(1).to_broadcast([128, 8, 32]), op=ALU.add),
                        reads=[ps, dtb], writes=[dtx])
                    P.op("act", lambda e: e.activation(out=dte[:], in_=dtx[:], func=AF.Exp), reads=[dtx], writes=[dte])
                    P.op("act", lambda e: e.activation(out=dtt[:], in_=dte[:], func=AF.Ln, bias=1.0, scale=1.0),
                         reads=[dte], writes=[dtt])
                    P.dma("sp", self.DT[t0:t0 + 1024, :].rearrange("(a p) h -> p a h", p=128), dtt[:],
                          reads=[dtt], writes=[self.DT])
        P.barrier()
        P.free_from(mark)


    def phase_A2(self):
        P, L = self.P, self.L
        mark = len(P._ctx)
        psr = self.psr
        wq = P.sb("wq", [128, 8, 2048], BF16)
        wkv = P.sb("wkv", [128, 4, 2048], BF16)
        P.dma("pool", wq[:], self.w_uq[:], reads=[self.w_uq], writes=[wq])
        P.dma("pool", wkv[:], self.w_ukv[:], reads=[self.w_ukv], writes=[wkv])
        nwq = P.sb("nwq", [128, 8], F32)
        nwkv = P.sb("nwkv", [128, 4], F32)
        P.dma("sp", nwq[:], self.nw_q[:], reads=[self.nw_q], writes=[nwq])
        P.dma("sp", nwkv[:], self.nw_kv[:], reads=[self.nw_kv], writes=[nwkv])
        epsb = P.sb("epsb2", [128, 1], F32)
        P.op("pool", lambda e: e.memset(epsb[:], EPS), writes=[epsb])
        cqs = Rot([P.sb("cq%d" % i, [128, 8, 512], BF16) for i in range(2)])
        ckvs = Rot([P.sb("ckv%d" % i, [128, 4, 512], BF16) for i in range(2)])
        rras = Rot([P.sb("rra%d" % i, [64, 512], F32) for i in range(2)])
        rrbs = Rot([P.sb("rrb%d" % i, [64, 512], F32) for i in range(2)])
        coss = Rot([P.sb("cos%d" % i, [64, 512], F32) for i in range(2)])
        sins = Rot([P.sb("sin%d" % i, [64, 512], F32) for i in range(2)])
        sqb = P.sb("sqb", [128, 8, 512], BF16)
        rt = P.sb("rt2", [128, 512], F32)
        rq = P.sb("rq", [128, 512], F32)
        rkv = P.sb("rkv", [128, 512], F32)
        cqw = P.sb("cqw", [128, 8, 512], BF16)
        ckvw = P.sb("ckvw", [128, 4, 512], BF16)
        stg = Rot([P.sb("stg2_%d" % i, [128, 512], BF16) for i in range(4)])
        tA = Rot([P.sb("tA%d" % i, [64, 512], F32) for i in range(2)])
        tB = Rot([P.sb("tB%d" % i, [64, 512], F32) for i in range(2)])
        ones = self.ones

        def rope(srcA, srcB, bufsA, bufsB, cos, sin, dst, dbuf):
            a, b = tA.next(), tB.next()
            P.op("dve", lambda e: e.tensor_tensor(out=a[:], in0=srcA, in1=cos[:], op=ALU.mult),
                 reads=bufsA + [cos], writes=[a])
            P.op("dve", lambda e: e.tensor_tensor(out=b[:], in0=srcB, in1=sin[:], op=ALU.mult),
                 reads=bufsB + [sin], writes=[b])
            st = stg.next()
            P.op("pool", lambda e: e.tensor_tensor(out=st[0:64, :], in0=a[:], in1=b[:], op=ALU.add),
                 reads=[a, b], writes=[st])
            P.dma("sp", dst, st[0:64, :], reads=[st], writes=[dbuf])

        for tk in range(L // 512):
            tok = slice(tk * 512, (tk + 1) * 512)
            cq, ckv, rra, rrb, cos, sin = cqs.next(), ckvs.next(), rras.next(), rrbs.next(), coss.next(), sins.next()
            P.dma("sp", cq[:], self.CQ[:, tok].rearrange("(j p) t -> p j t", p=128), reads=[self.CQ], writes=[cq])
            P.dma("sp", ckv[:], self.CKV[:, tok].rearrange("(j p) t -> p j t", p=128), reads=[self.CKV], writes=[ckv])
            P.dma("sp", rra[:], self.RR[0:64, tok], reads=[self.RR], writes=[rra])
            P.dma("sp", rrb[:], self.RR[64:128, tok], reads=[self.RR], writes=[rrb])
            P.dma("sp", cos[:], self.ropec[:, tok], reads=[self.ropec], writes=[cos])
            P.dma("sp", sin[:], self.ropes[:, tok], reads=[self.ropes], writes=[sin])
            for (src, nk, nw, rr_, dst, dim) in ((cq, 8, nwq, rq, cqw, 1024), (ckv, 4, nwkv, rkv, ckvw, 512)):
                P.op("act", lambda e, src=src, nk=nk: e.activation(out=sqb[:, 0:nk, :], in_=src[:], func=AF.Square),
                     reads=[src], writes=[sqb])
                ps = psr.next()
                for kc in range(nk):
                    P.op("pe", lambda e, kc=kc, ps=ps, nk=nk: e.matmul(ps[:], lhsT=ones[:], rhs=sqb[:, kc, :],
                                                                    start=(kc == 0), stop=(kc == nk - 1)),
                         reads=[ones, sqb], writes=[ps], signal=(kc == nk - 1))
                P.op("act", lambda e, ps=ps, dim=dim: e.activation(out=rt[:], in_=ps[:], func=AF.Sqrt,
                                                                   scale=1.0 / dim, bias=epsb[:, 0:1]),
                     reads=[ps, epsb], writes=[rt])
                P.op("dve", lambda e, rr_=rr_: e.reciprocal(out=rr_[:], in_=rt[:]), reads=[rt], writes=[rr_])
                for kc in range(nk):
                    P.op("dve", lambda e, kc=kc, src=src, nw=nw, rr_=rr_, dst=dst: e.scalar_tensor_tensor(
                        out=dst[:, kc, :], in0=src[:, kc, :], scalar=nw[:, kc:kc + 1], in1=rr_[:],
                        op0=ALU.mult, op1=ALU.mult), reads=[src, nw, rr_], writes=[dst])
            for hl in range(8):
                ps = psr.next()
                for kc in range(8):
                    P.op("pe", lambda e, kc=kc, ps=ps, hl=hl: e.matmul(
                        ps[:], lhsT=wq[:, kc, hl * 256:hl * 256 + 128], rhs=cqw[:, kc, :],
                        start=(kc == 0), stop=(kc == 7)), reads=[wq, cqw], writes=[ps], signal=(kc == 7))
                st = stg.next()
                P.op("act", lambda e, ps=ps, st=st: e.activation(out=st[:], in_=ps[:], func=AF.Copy), reads=[ps], writes=[st])
                P.dma("sp", self.QN[hl, :, tok], st[:], reads=[st], writes=[self.QN])
                psA, psB = psr.next(), psr.next()
                for (pp, off) in ((psA, 128), (psB, 192)):
                    for kc in range(8):
                        P.op("pe", lambda e, kc=kc, pp=pp, hl=hl, off=off: e.matmul(
                            pp[0:64, :], lhsT=wq[:, kc, hl * 256 + off:hl * 256 + off + 64], rhs=cqw[:, kc, :],
                            start=(kc == 0), stop=(kc == 7)), reads=[wq, cqw], writes=[pp], signal=(kc == 7))
                rope(psA[0:64, :], psB[0:64, :], [psA], [psB], cos, sin, self.QR[hl, :, tok], self.QR)
                ps = psr.next()
                for kc in range(4):
                    P.op("pe", lambda e, kc=kc, ps=ps, hl=hl: e.matmul(
                        ps[:], lhsT=wkv[:, kc, hl * 128:(hl + 1) * 128], rhs=ckvw[:, kc, :],
                        start=(kc == 0), stop=(kc == 3)), reads=[wkv, ckvw], writes=[ps], signal=(kc == 3))
                st = stg.next()
                P.op("act", lambda e, ps=ps, st=st: e.activation(out=st[:], in_=ps[:], func=AF.Copy), reads=[ps], writes=[st])
                P.dma("sp", self.KN[hl, :, tok], st[:], reads=[st], writes=[self.KN])
            for tt in range(4):
                for hf in range(2):
                    ps = psr.next()
                    for kc in range(4):
                        P.op("pe", lambda e, kc=kc, ps=ps, tt=tt, hf=hf: e.matmul(
                            ps[:], lhsT=ckvw[:, kc, tt * 128:(tt + 1) * 128],
                            rhs=wkv[:, kc, 1024 + hf * 512:1024 + (hf + 1) * 512],
                            start=(kc == 0), stop=(kc == 3)), reads=[wkv, ckvw], writes=[ps], signal=(kc == 3))
                    st = stg.next()
                    P.op("act", lambda e, ps=ps, st=st: e.activation(out=st[:], in_=ps[:], func=AF.Copy), reads=[ps], writes=[st])
                    r0 = tk * 512 + tt * 128
                    P.dma("sp", self.V[r0:r0 + 128, hf * 512:(hf + 1) * 512], st[:], reads=[st], writes=[self.V])
            rope(rra[:], rrb[:], [rra], [rrb], cos, sin, self.KR[:, tok], self.KR)
        P.barrier()
        P.free_from(mark)

    def phase_C(self):
        P, L = self.P, self.L
        mark = len(P._ctx)
        psr = self.psr
        NKT = L // 128
        scale = 1.0 / math.sqrt(192.0)
        ones = self.ones
        maskb = P.sb("maskb", [128, 128], BF16)
        P.dma("sp", maskb[:], self.c_maskb[:], reads=[self.c_maskb], writes=[maskb])
        kr = P.sb("kr", [64, L], BF16)
        P.dma("sp", kr[:], self.KR[:, :], reads=[self.KR], writes=[kr])
        qns = Rot([P.sb("qn%d" % i, [128, L], BF16) for i in range(2)])
        qrs = Rot([P.sb("qr%d" % i, [64, L], BF16) for i in range(2)])
        kns = Rot([P.sb("kn%d" % i, [128, L], BF16) for i in range(2)])
        vs = Rot([P.sb("v%d" % i, [128, NKT, 128], BF16) for i in range(2)])
        pts = Rot([P.sb("pt%d" % i, [128, 512], BF16) for i in range(3)])
        gas = Rot([P.sb("ga%d" % i, [128, 512], BF16) for i in range(2)])
        rinv = P.sb("rinv", [128, 512], F32)
        o1 = P.sb("o1", [128, 512], F32)
        stg = Rot([P.sb("stg3_%d" % i, [128, 512], BF16) for i in range(2)])
        acc_r = Rot(self.psb[0:4])
        s_r = Rot(self.psb[4:8])
        for hl in range(8):
            qn, qr, kn, v = qns.next(), qrs.next(), kns.next(), vs.next()
            P.dma("sp", qn[:], self.QN[hl], reads=[self.QN], writes=[qn])
            P.dma("sp", qr[:], self.QR[hl], reads=[self.QR], writes=[qr])
            P.dma("sp", kn[:], self.KN[hl], reads=[self.KN], writes=[kn])
            P.dma("sp", v[:], self.V[:, hl * 128:(hl + 1) * 128].rearrange("(a p) d -> p a d", p=128),
                  reads=[self.V], writes=[v])
            for qi in range(L // 512):
                ga = gas.next()
                P.dma("sp", ga[:], self.GA[hl * 128:(hl + 1) * 128, qi * 512:(qi + 1) * 512], reads=[self.GA], writes=[ga])
                O, Rs = acc_r.next(), acc_r.next()
                nkt = 4 * qi + 4
                for kt in range(nkt):
                    d = kt - 4 * qi
                    q0 = max(d, 0) * 128
                    Sb = s_r.next()
                    qs = slice(qi * 512 + q0, (qi + 1) * 512)
                    ks = slice(kt * 128, (kt + 1) * 128)
                    P.op("pe", lambda e, Sb=Sb, q0=q0, qs=qs, ks=ks, kn=kn, qn=qn: e.matmul(
                        Sb[:, q0:512], lhsT=kn[:, ks], rhs=qn[:, qs], start=True, stop=False),
                        reads=[kn, qn], writes=[Sb], signal=False)
                    P.op("pe", lambda e, Sb=Sb, q0=q0, qs=qs, ks=ks, qr=qr: e.matmul(
                        Sb[:, q0:512], lhsT=kr[:, ks], rhs=qr[:, qs], start=False, stop=True),
                        reads=[kr, qr], writes=[Sb])
                    pt = pts.next()
                    P.op("act", lambda e, Sb=Sb, pt=pt, q0=q0: e.activation(out=pt[:, q0:512], in_=Sb[:, q0:512],
                                                                         func=AF.Exp, scale=scale),
                         reads=[Sb], writes=[pt])
                    if d >= 0:
                        P.op("dve", lambda e, pt=pt, q0=q0: e.tensor_tensor(out=pt[:, q0:q0 + 128], in0=pt[:, q0:q0 + 128],
                                                                           in1=maskb[:], op=ALU.mult),
                             reads=[pt, maskb], writes=[pt])
                    P.op("pe", lambda e, O=O, pt=pt, q0=q0, kt=kt, nkt=nkt, v=v: e.matmul(
                        O[:, q0:512], lhsT=v[:, kt, :], rhs=pt[:, q0:512], start=(kt == 0), stop=(kt == nkt - 1)),
                        reads=[v, pt], writes=[O], signal=False)
                    P.op("pe", lambda e, Rs=Rs, pt=pt, q0=q0, kt=kt, nkt=nkt: e.matmul(
                        Rs[:, q0:512], lhsT=ones[:], rhs=pt[:, q0:512], start=(kt == 0), stop=(kt == nkt - 1)),
                        reads=[ones, pt], writes=[Rs])
                P.op("dve", lambda e, Rs=Rs: e.reciprocal(out=rinv[:], in_=Rs[:]), reads=[Rs], writes=[rinv])
                P.op("dve", lambda e, O=O: e.tensor_tensor(out=o1[:], in0=O[:], in1=rinv[:], op=ALU.mult),
                     reads=[O, rinv], writes=[o1])
                st = stg.next()
                P.op("pool", lambda e, st=st, ga=ga: e.tensor_tensor(out=st[:], in0=o1[:], in1=ga[:], op=ALU.mult),
                     reads=[o1, ga], writes=[st])
                self.yc_store(16 + hl, 1, qi * 512, 512, lambda lo, hi, st=st: st[:, lo:hi].unsqueeze(1), st)
        P.barrier()
        P.free_from(mark)


    def phase_B(self):
        P, L = self.P, self.L
        mark = len(P._ctx)
        psr = self.psr
        NCH = L // 128
        ident, ones = self.ident, self.ones

        def cload(name, src, shape, dt):
            b = P.sb(name, shape, dt)
            P.dma("sp", b[:], src[:], reads=[src], writes=[b])
            return b
        trile = cload("trile", self.c_trile, [128, 128], F32)
        trigt = cload("trigt", self.c_trigt, [128, 128], F32)
        onesf = cload("onesf", self.c_onesf, [128, 128], F32)
        maskb = cload("maskbB", self.c_maskb, [128, 128], BF16)
        Abc = cload("Abc", self.alog, [128, 32], F32)
        dsk = cload("dsk", self.dskip, [128, 32], F32)
        nws = cload("nws", self.nw_ssm, [128, 2048], F32)
        P.op("act", lambda e: e.activation(out=Abc[:], in_=Abc[:], func=AF.Exp), reads=[Abc], writes=[Abc])
        P.op("dve", lambda e: e.tensor_scalar(out=Abc[:], in0=Abc[:], scalar1=-1.0, scalar2=None, op0=ALU.mult),
             reads=[Abc], writes=[Abc])
        epsb = P.sb("epsbB", [128, 1], F32)
        P.op("pool", lambda e: e.memset(epsb[:], EPS), writes=[epsb])
        h = P.sb("h", [128, 2048], F32)
        hb = P.sb("hb", [128, 2048], BF16)
        P.op("pool", lambda e: e.memset(h[:], 0.0), writes=[h])
        P.op("pool", lambda e: e.memset(hb[:], 0.0), writes=[hb])
        xT4s = Rot([P.sb("xT4_%d" % i, [128, 20, 512], BF16) for i in range(2)])
        dt4s = Rot([P.sb("dt4_%d" % i, [128, 4, 32], F32) for i in range(2)])
        szs = Rot([P.sb("sz%d" % i, [128, 2048], BF16) for i in range(2)])
        xtm = P.sb("xtm", [128, 2048], BF16)
        btm = P.sb("btm", [128, 256], BF16)
        sm = {n: P.sb("sm_" + n, [128, 32], F32) for n in ("a", "acs", "ea", "cd", "dsd", "ds", "w2")}
        R = P.sb("R", [128, 32, 128], F32)
        E = P.sb("E", [128, 32, 128], BF16)
        cbm = P.sb("cbm", [128, 2, 128], BF16)
        S = P.sb("S", [128, 32, 128], BF16)
        xdt = P.sb("xdt", [128, 2048], BF16)
        xdd = P.sb("xdd", [128, 2048], BF16)
        xD = P.sb("xD", [128, 2048], BF16)
        y = P.sb("y", [128, 2048], F32)
        junk = P.sb("junk", [128, 1024], BF16)
        ssq = P.sb("ssq", [128, 2], F32)
        rtn = P.sb("rtn", [128, 2], F32)
        rsn = P.sb("rsn", [128, 2], F32)
        ytm = P.sb("ytm", [128, 2048], BF16)
        yT4 = P.sb("yT4", [128, 16, 512], BF16)

        def pbf(bank):
            return bank.ap.bitcast(BF16)

        xT4 = dt4 = None
        for c in range(NCH):
            cc = c % 4
            csl = slice(cc * 128, (cc + 1) * 128)
            if cc == 0:
                xT4, dt4 = xT4s.next(), dt4s.next()
                tok = slice(c * 128, c * 128 + 512)
                P.dma("sp", xT4[:], self.XC[:, tok].rearrange("(j p) t -> p j t", p=128), reads=[self.XC], writes=[xT4])
                P.dma("sp", dt4[:], self.DT[tok, :].rearrange("(a p) h -> p a h", p=128), reads=[self.DT], writes=[dt4])
            sz = szs.next()
            P.dma("sp", sz[:], self.SZ[c * 128:(c + 1) * 128, :], reads=[self.SZ], writes=[sz])
            dtc = dt4[:, cc, :]
            for i in range(2):
                bank = psr.next()
                pb = pbf(bank)
                for jj in range(8):
                    j = i * 8 + jj
                    P.op("pe", lambda e, pb=pb, jj=jj, j=j, xT4=xT4, csl=csl: e.transpose(
                        pb[:, jj * 128:(jj + 1) * 128], xT4[:, j, csl], ident[:]),
                        reads=[xT4, ident], writes=[bank], signal=(jj == 7))
                P.op("act", lambda e, pb=pb, i=i: e.activation(out=xtm[:, i * 1024:(i + 1) * 1024], in_=pb[:, 0:1024], func=AF.Copy),
                     reads=[bank], writes=[xtm])
            bank = psr.next()
            pb = pbf(bank)
            for g2 in range(2):
                P.op("pe", lambda e, pb=pb, g2=g2, xT4=xT4, csl=csl: e.transpose(
                    pb[:, g2 * 128:(g2 + 1) * 128], xT4[:, 16 + g2, csl], ident[:]),
                    reads=[xT4, ident], writes=[bank], signal=(g2 == 1))
            P.op("act", lambda e, pb=pb: e.activation(out=btm[:], in_=pb[:, 0:256], func=AF.Copy), reads=[bank], writes=[btm])
            a, acs, ea, cd, dsd, ds_, w2 = (sm[n] for n in ("a", "acs", "ea", "cd", "dsd", "ds", "w2"))
            P.op("dve", lambda e, dtc=dtc: e.tensor_tensor(out=a[:], in0=dtc, in1=Abc[:], op=ALU.mult),
                 reads=[dt4, Abc], writes=[a])
            bank = psr.next()
            P.op("pe", lambda e, bank=bank: e.matmul(bank[:, 0:32], lhsT=trile[:], rhs=a[:], start=True, stop=True),
                 reads=[trile, a], writes=[bank], signal=False)
            P.op("pe", lambda e, bank=bank: e.matmul(bank[:, 32:64], lhsT=onesf[:], rhs=a[:], start=True, stop=True),
                 reads=[onesf, a], writes=[bank])
            P.op("act", lambda e, bank=bank: e.activation(out=acs[:], in_=bank[:, 0:32], func=AF.Copy), reads=[bank], writes=[acs])
            P.op("act", lambda e, bank=bank: e.activation(out=ea[:], in_=bank[:, 0:32], func=AF.Exp), reads=[bank], writes=[ea])
            P.op("act", lambda e, bank=bank: e.activation(out=cd[:], in_=bank[:, 32:64], func=AF.Exp), reads=[bank], writes=[cd])
            P.op("dve", lambda e, bank=bank: e.tensor_tensor(out=dsd[:], in0=bank[:, 32:64], in1=acs[:], op=ALU.subtract),
                 reads=[bank, acs], writes=[dsd])
            P.op("act", lambda e: e.activation(out=ds_[:], in_=dsd[:], func=AF.Exp), reads=[dsd], writes=[ds_])
            P.op("dve", lambda e, dtc=dtc: e.tensor_tensor(out=w2[:], in0=dtc, in1=ds_[:], op=ALU.mult),
                 reads=[dt4, ds_], writes=[w2])
            P.op("dve", lambda e: e.tensor_tensor(out=R[:], in0=a[:].unsqueeze(2).to_broadcast([128, 32, 128]),
                                                  in1=trile[:].unsqueeze(1).to_broadcast([128, 32, 128]), op=ALU.mult),
                 reads=[a, trile], writes=[R])
            for j in range(8):
                bank = psr.next()
                P.op("pe", lambda e, bank=bank, j=j: e.matmul(
                    bank[:], lhsT=trigt[:], rhs=R[:, 4 * j:4 * j + 4, :].rearrange("p a b -> p (a b)"), start=True, stop=True),
                    reads=[trigt, R], writes=[bank])
                P.op("act", lambda e, bank=bank, j=j: e.activation(
                    out=E[:, 4 * j:4 * j + 4, :].rearrange("p a b -> p (a b)"), in_=bank[:], func=AF.Exp),
                    reads=[bank], writes=[E])
            bank = psr.next()
            for g2 in range(2):
                P.op("pe", lambda e, bank=bank, g2=g2, xT4=xT4, csl=csl: e.matmul(
                    bank[:, g2 * 128:(g2 + 1) * 128], lhsT=xT4[:, 16 + g2, csl], rhs=xT4[:, 18 + g2, csl], start=True, stop=True),
                    reads=[xT4], writes=[bank], signal=(g2 == 1))
            P.op("dve", lambda e, bank=bank: e.tensor_tensor(
                out=cbm[:], in0=bank[:, 0:256].rearrange("p (g l) -> p g l", g=2),
                in1=maskb[:].unsqueeze(1).to_broadcast([128, 2, 128]), op=ALU.mult),
                reads=[bank, maskb], writes=[cbm])
            P.op("dve", lambda e: e.tensor_tensor(
                out=S[:].rearrange("p (g r) l -> p g r l", g=2), in0=E[:].rearrange("p (g r) l -> p g r l", g=2),
                in1=cbm[:].unsqueeze(2).to_broadcast([128, 2, 16, 128]), op=ALU.mult),
                reads=[E, cbm], writes=[S])
            x3 = xtm[:].rearrange("p (h d) -> p h d", d=64)
            P.op("dve", lambda e, dtc=dtc, x3=x3: e.tensor_tensor(
                out=xdt[:].rearrange("p (h d) -> p h d", d=64), in0=x3,
                in1=dtc.unsqueeze(2).to_broadcast([128, 32, 64]), op=ALU.mult), reads=[xtm, dt4], writes=[xdt])
            P.op("dve", lambda e, x3=x3: e.tensor_tensor(
                out=xdd[:].rearrange("p (h d) -> p h d", d=64), in0=x3,
                in1=w2[:].unsqueeze(2).to_broadcast([128, 32, 64]), op=ALU.mult), reads=[xtm, w2], writes=[xdd])
            P.op("pool", lambda e, x3=x3: e.tensor_tensor(
                out=xD[:].rearrange("p (h d) -> p h d", d=64), in0=x3,
                in1=dsk[:].unsqueeze(2).to_broadcast([128, 32, 64]), op=ALU.mult), reads=[xtm, dsk], writes=[xD])
            for q in range(4):
                g2 = q // 2
                yo = psr.next()
                P.op("pe", lambda e, yo=yo, q=q, g2=g2, xT4=xT4, csl=csl: e.matmul(
                    yo[:], lhsT=xT4[:, 18 + g2, csl], rhs=hb[:, q * 512:(q + 1) * 512], start=True, stop=True),
                    reads=[xT4, hb], writes=[yo])
                yd = psr.next()
                P.op("pe", lambda e, yd=yd, q=q: e.matmul(yd[:], lhsT=ident[:], rhs=xD[:, q * 512:(q + 1) * 512],
                                                         start=True, stop=False),
                     reads=[ident, xD], writes=[yd], signal=False)
                for hh in range(8):
                    hd = q * 8 + hh
                    P.op("pe", lambda e, yd=yd, hh=hh, hd=hd: e.matmul(
                        yd[:, hh * 64:(hh + 1) * 64], lhsT=S[:, hd, :], rhs=xdt[:, hd * 64:(hd + 1) * 64],
                        start=False, stop=(hh == 7)), reads=[S, xdt], writes=[yd], signal=(hh == 7))
                ysl = y[:, q * 512:(q + 1) * 512]
                P.op("dve", lambda e, yo=yo, q=q, ysl=ysl: e.tensor_tensor(
                    out=ysl.rearrange("p (h d) -> p h d", d=64), in0=yo[:].rearrange("p (h d) -> p h d", d=64),
                    in1=ea[:, q * 8:(q + 1) * 8].unsqueeze(2).to_broadcast([128, 8, 64]), op=ALU.mult),
                    reads=[yo, ea], writes=[y])
                P.op("dve", lambda e, yd=yd, ysl=ysl: e.tensor_tensor(out=ysl, in0=yd[:], in1=ysl, op=ALU.add),
                     reads=[yd, y], writes=[y])
            P.op("dve", lambda e, sz=sz: e.tensor_tensor(out=y[:], in0=y[:], in1=sz[:], op=ALU.mult), reads=[y, sz], writes=[y])
            for g2 in range(2):
                P.op("act", lambda e, g2=g2: e.activation(out=junk[:], in_=y[:, g2 * 1024:(g2 + 1) * 1024], func=AF.Square,
                                                          accum_out=ssq[:, g2:g2 + 1]), reads=[y], writes=[junk, ssq])
            P.op("act", lambda e: e.activation(out=rtn[:], in_=ssq[:], func=AF.Sqrt, scale=1.0 / 1024, bias=epsb[:, 0:1]),
                 reads=[ssq, epsb], writes=[rtn])
            P.op("dve", lambda e: e.reciprocal(out=rsn[:], in_=rtn[:]), reads=[rtn], writes=[rsn])
            for g2 in range(2):
                gs = slice(g2 * 1024, (g2 + 1) * 1024)
                P.op("dve", lambda e, g2=g2, gs=gs: e.scalar_tensor_tensor(
                    out=ytm[:, gs], in0=y[:, gs], scalar=rsn[:, g2:g2 + 1], in1=nws[:, gs], op0=ALU.mult, op1=ALU.mult),
                    reads=[y, rsn, nws], writes=[ytm])
            for i in range(2):
                bank = psr.next()
                pb = pbf(bank)
                for jj in range(8):
                    j = i * 8 + jj
                    P.op("pe", lambda e, pb=pb, jj=jj, j=j: e.transpose(
                        pb[:, jj * 128:(jj + 1) * 128], ytm[:, j * 128:(j + 1) * 128], ident[:]),
                        reads=[ytm, ident], writes=[bank], signal=(jj == 7))
                P.op("act", lambda e, pb=pb, i=i, csl=csl: e.activation(
                    out=yT4[:, i * 8:(i + 1) * 8, csl], in_=pb[:, 0:1024].rearrange("p (j t) -> p j t", j=8), func=AF.Copy),
                    reads=[bank], writes=[yT4])
            if cc == 3:
                self.yc_store(0, 16, (c - 3) * 128, 512, lambda lo, hi: yT4[:, :, lo:hi], yT4)
            P.op("dve", lambda e: e.tensor_tensor(
                out=h[:].rearrange("p (h d) -> p h d", d=64), in0=h[:].rearrange("p (h d) -> p h d", d=64),
                in1=cd[:].unsqueeze(2).to_broadcast([128, 32, 64]), op=ALU.mult), reads=[h, cd], writes=[h])
            for q in range(4):
                g2 = q // 2
                st = psr.next()
                P.op("pe", lambda e, st=st, q=q, g2=g2: e.matmul(
                    st[:], lhsT=btm[:, g2 * 128:(g2 + 1) * 128], rhs=xdd[:, q * 512:(q + 1) * 512], start=True, stop=True),
                    reads=[btm, xdd], writes=[st])
                P.op("dve", lambda e, st=st, q=q: e.tensor_tensor(
                    out=h[:, q * 512:(q + 1) * 512], in0=st[:], in1=h[:, q * 512:(q + 1) * 512], op=ALU.add),
                    reads=[st, h], writes=[h])
            P.op("act", lambda e: e.activation(out=hb[:], in_=h[:], func=AF.Copy), reads=[h], writes=[hb])
        P.barrier()
        P.free_from(mark)

    def phase_G(self):
        P = self.P
        markG = len(P._ctx)
        P.barrier()
        sem = P._mksem("cc")
        YC, YG = self.YC, self.YG

        def thunk(e):
            n = 0
            for jt in range(24):
                for tq in range(4):
                    e.collective_compute("AllGather", ALU.bypass, replica_groups=[[0, 1, 2, 3], [4, 5, 6, 7]],
                                         ins=[YC.ap[jt, tq]], outs=[YG.ap[jt, tq].rearrange("r p t -> (r p) t")]).then_inc(sem)
                    n += 1
            e.wait_ge(sem, n)
        P.streams["pool"].append(thunk)
        P.semval["cc"] = 96
        P.barrier()
        for nm, srcb in (("YCd", YC), ("YGd", YG)):
            if nm in self.debug:
                flat = srcb.ap[:].flatten_outer_dims()
                nrow = flat.shape[0]
                d = P.dram(nm, [nrow, self.LS], BF16, kind="ExternalOutput")
                self.out_bufs.append(d)
                tmp = P.sb("dbg" + nm, [128, self.LS], BF16)
                for r in range(nrow // 128):
                    P.dma("sp", tmp[:], flat[r * 128:(r + 1) * 128, :], reads=[srcb], writes=[tmp])
                    P.dma("sp", d.ap[r * 128:(r + 1) * 128, :], tmp[:], reads=[tmp], writes=[d])
                P.barrier()
        P.free_from(markG)

    def phase_D(self):
        P, L, LS = self.P, self.L, self.LS
        mark = len(P._ctx)
        psr = self.psr
        TB = min(256, LS)
        NT = TB // 128
        uT = P.sb("uTD", [128, 32, TB], BF16)
        xs = [P.sb("xsD%d" % i, [128, 32, 128], F32) for i in range(1)]
        sq = P.sb("sqD", [128, 32, 128], BF16)
        rt = P.sb("rtD", [128, 128], F32)
        rs = P.sb("rsD", [128, 128], F32)
        win = P.sb("winD", [128, 32], F32)
        self.epsb = P.sb("epsbD", [128, 1], F32)
        P.op("pool", lambda e, t=self.epsb: e.memset(t[:], EPS), writes=[self.epsb])
        P.dma("sp", win[:], self.nw_in[:], reads=[self.nw_in], writes=[win])
        wts = Rot([P.sb("wtD%d" % i, [128, 32, 512], BF16) for i in range(2)])
        sg = [P.sb("sg%d" % i, [128, 32, TB], BF16) for i in range(2)]
        yT = P.sb("yTD", [128, 64, TB], BF16)
        tmpf = Rot([P.sb("tmpf%d" % i, [128, TB], F32) for i in range(2)])
        hrow = P.sb("hrow", [128, 4096], F32)
        xr = Rot([P.sb("xr%d" % i, [128, 512], F32) for i in range(2)])
        nwf = P.sb("nwf", [128, 4096], F32)
        P.dma("sp", nwf[:], self.nw_fin[:], reads=[self.nw_fin], writes=[nwf])
        ssq = P.sb("ssqD", [128, 1], F32)
        rtn = P.sb("rtnD", [128, 1], F32)
        rsn = P.sb("rsnD", [128, 1], F32)
        YG = self.YG
        tokv = {}
        YQ = self.YQ
        if YQ.dsem is None:
            YQ.dsem = ("d", P.nsem + 1)
            P._mksem(YQ.dsem)
        qsem = P.sems[YQ.dsem]
        for j0 in range(0, 24, 2):
            YQ.dcount += 16
            P.semval[YQ.dsem] = YQ.dcount
            P._rec((YQ.dsem, YQ.dcount), [YG], [YQ])
            P.ninstr += 1

            def qthunk(e, j0=j0):
                if "v" not in tokv:
                    tokv["v"] = e.snap(e.partition_id() % 4, min_val=0, max_val=3)
                src = YG.ap[j0:j0 + 2, bass.ds(tokv["v"], 1), :, :, :].rearrange("j o r p t -> j (o r p t)")
                dst = YQ.ap[j0:j0 + 2].rearrange("j r p t -> j (r p t)")
                e.dma_start(out=dst, in_=src).then_inc(qsem, 16)
            P.streams["sp"].append(qthunk)

        def yg_load(dst_ap, row0, nj, bk):
            def thunk_fn(e):
                gidx = e.partition_id() % 4
                src = YG.ap[row0:row0 + nj * 128, bass.ds(gidx * LS + bk * TB, TB)].rearrange("(j p) t -> p j t", p=128)
                return src
            return thunk_fn

        for bk in range(LS // TB):
            self.make_uT(lambda s, bk=bk: (self.xTs[:, :, bk * TB + s * 128:bk * TB + (s + 1) * 128], self.xTs),
                         NT, uT, xs, sq, rt, rs, win, psr)
            for m in range(2):
                for ct in range(8):
                    wt = wts.next()
                    P.dma("pool", wt[:], self.w_gates[m * 8 + ct], reads=[self.w_gates], writes=[wt])
                    for cs in range(4):
                        ps = psr.next()
                        for kc in range(32):
                            P.op("pe", lambda e, kc=kc, ps=ps, cs=cs, wt=wt: e.matmul(
                                ps[:, 0:TB], lhsT=wt[:, kc, cs * 128:(cs + 1) * 128], rhs=uT[:, kc, :],
                                start=(kc == 0), stop=(kc == 31)), reads=[uT, wt], writes=[ps], signal=(kc == 31))
                        P.op("act", lambda e, ps=ps, m=m, ct=ct, cs=cs: e.activation(
                            out=sg[m][:, ct * 4 + cs, :], in_=ps[:, 0:TB], func=AF.Sigmoid), reads=[ps], writes=[sg[m]])
            for m in range(2):
                nkh = 2 if m == 0 else 1
                for r in range(4):
                    row0 = 0 if m == 0 else 16
                    nj = 16 if m == 0 else 8
                    P.dma("sp", yT[:, r * nj:(r + 1) * nj, :],
                          self.YQ.ap[row0:row0 + nj, r, :, bk * TB:(bk + 1) * TB].rearrange("j p t -> p j t"),
                          reads=[self.YQ], writes=[yT])
                for ct in range(8):
                    w_list = []
                    for kh in range(nkh):
                        wt = wts.next()
                        srcw = self.w_bs[kh * 8 + ct] if m == 0 else self.w_ba[ct]
                        sbuf_ = self.w_bs if m == 0 else self.w_ba
                        w_list.append((wt, srcw, sbuf_))
                    pss = [psr.next() for _ in range(4)]
                    for kh, (wt, srcw, sbuf_) in enumerate(w_list):
                        P.dma("pool", wt[:], srcw, reads=[sbuf_], writes=[wt])
                        for cs in range(4):
                            ps = pss[cs]
                            for kc in range(32):
                                first = (kh == 0 and kc == 0)
                                last = (kh == nkh - 1 and kc == 31)
                                P.op("pe", lambda e, kc=kc, ps=ps, cs=cs, wt=wt, kh=kh, first=first, last=last: e.matmul(
                                    ps[:, 0:TB], lhsT=wt[:, kc, cs * 128:(cs + 1) * 128], rhs=yT[:, kh * 32 + kc, :],
                                    start=first, stop=last), reads=[yT, wt], writes=[ps], signal=(kc == 31))
                    for cs in range(4):
                        ps = pss[cs]
                        ci = ct * 4 + cs
                        if m == 0:
                            P.op("dve", lambda e, ps=ps, ci=ci: e.tensor_tensor(out=sg[0][:, ci, :], in0=ps[:, 0:TB],
                                                                               in1=sg[0][:, ci, :], op=ALU.mult),
                                 reads=[ps, sg[0]], writes=[sg[0]])
                        else:
                            tf = tmpf.next()
                            P.op("dve", lambda e, ps=ps, ci=ci, tf=tf: e.tensor_tensor(out=tf[:], in0=ps[:, 0:TB],
                                                                                     in1=sg[1][:, ci, :], op=ALU.mult),
                                 reads=[ps, sg[1]], writes=[tf])
                            P.op("pool", lambda e, ci=ci, tf=tf: e.tensor_tensor(out=sg[1][:, ci, :], in0=tf[:],
                                                                                in1=sg[0][:, ci, :], op=ALU.add),
                                 reads=[tf, sg[0], sg[1]], writes=[sg[1]])
            mT = sg[1]
            for tt in range(NT):
                r0 = bk * TB + tt * 128
                for ct in range(8):
                    xrow = xr.next()
                    P.dma("sp", xrow[:], self.xres[r0:r0 + 128, ct * 512:(ct + 1) * 512], reads=[self.xres], writes=[xrow])
                    wt = wts.next()
                    P.dma("pool", wt[:], self.w_out[ct], reads=[self.w_out], writes=[wt])
                    ps = psr.next()
                    for kc in range(32):
                        P.op("pe", lambda e, kc=kc, ps=ps, tt=tt, wt=wt: e.matmul(
                            ps[:], lhsT=mT[:, kc, tt * 128:(tt + 1) * 128], rhs=wt[:, kc, :],
                            start=(kc == 0), stop=(kc == 31)), reads=[mT, wt], writes=[ps], signal=(kc == 31))
                    P.op("dve", lambda e, ps=ps, ct=ct, xrow=xrow: e.tensor_tensor(
                        out=hrow[:, ct * 512:(ct + 1) * 512], in0=ps[:], in1=xrow[:], op=ALU.add),
                        reads=[ps, xrow], writes=[hrow])
                P.op("act", lambda e: e.activation(out=xs[0][:].rearrange("p a b -> p (a b)"), in_=hrow[:], func=AF.Square,
                                                   accum_out=ssq[:]), reads=[hrow], writes=[xs[0], ssq])
                P.op("act", lambda e, epsb=self.epsb: e.activation(out=rtn[:], in_=ssq[:], func=AF.Sqrt, scale=1.0 / 4096,
                                                                   bias=epsb[:, 0:1]),
                     reads=[ssq, self.epsb], writes=[rtn])
                P.op("dve", lambda e: e.reciprocal(out=rsn[:], in_=rtn[:]), reads=[rtn], writes=[rsn])
                P.op("dve", lambda e: e.scalar_tensor_tensor(out=hrow[:], in0=hrow[:], scalar=rsn[:, 0:1], in1=nwf[:],
                                                             op0=ALU.mult, op1=ALU.mult), reads=[hrow, rsn, nwf], writes=[hrow])
                P.dma("sp", self.OUT[r0:r0 + 128, :], hrow[:], reads=[hrow], writes=[self.OUT])
        P.barrier()
        P.free_from(mark)

    def finish(self):
        P = self.P
        P.barrier()
        P.emit()
        P.close()
        return self.nc


def build_program(L, phases="A", debug=()):
    k = K(L, debug, phases)
    k.declare()
    k.load_consts()
    if "A" in phases:
        k.phase_A()
    if "M" in phases:
        k.phase_A2()
    if "B" in phases:
        k.phase_B()
    if "C" in phases:
        k.phase_C()
    if "G" in phases:
        k.phase_G()
    if "D" in phases:
        k.phase_D()
    return k


def _tile_w(w, cols=None):
    if cols is not None:
        wz = np.zeros((w.shape[0], len(cols)), np.float32)
        m = cols >= 0
        wz[:, m] = w[:, cols[m]]
        w = wz
    Kd, N = w.shape
    return np.ascontiguousarray(w.reshape(Kd // 128, 128, N // 512, 512).transpose(2, 1, 0, 3))


def _in_cols(g):
    c = []
    c += list(range(g * 2048, (g + 1) * 2048))
    c += list(range(8192 + g * 2048, 8192 + (g + 1) * 2048))
    c += list(range(16384 + g * 256, 16384 + (g + 1) * 256))
    c += list(range(17408 + g * 256, 17408 + (g + 1) * 256))
    c += list(range(18560, 18560 + 1024))
    c += list(range(19584, 19584 + 512))
    c += list(range(20160 + g * 1024, 20160 + (g + 1) * 1024))
    rope = list(range(20096, 20160))
    c += rope + rope[32:] + rope[:32]
    c += list(range(18432 + g * 32, 18432 + (g + 1) * 32))
    c += [-1] * (15 * 512 - len(c))
    return np.array(c, np.int64)


def prepare_inputs(inp, L):
    f = lambda a: np.asarray(a, np.float32)
    x = f(inp["x"])[:, :L]
    NB, LS = L // 1024, L // 4
    w_in = f(inp["w_in"])[0]
    conv_w, conv_b = f(inp["conv_w"])[0], f(inp["conv_b"])[0]
    w_uq, w_ukv = f(inp["w_uq"])[0], f(inp["w_ukv"])[0]
    shared = {}
    shared["w_gates"] = _tile_w(w_in[:, 24256:24256 + 8192])
    wbs = f(inp["w_branch_ssm"])[0]
    shared["w_bs"] = np.concatenate([_tile_w(wbs[:4096]), _tile_w(wbs[4096:])], 0)
    shared["w_ba"] = _tile_w(f(inp["w_branch_attn"])[0])
    shared["w_out"] = _tile_w(f(inp["w_out"])[0])
    shared["nw_in"] = np.ascontiguousarray(f(inp["norm_in_w"])[0].reshape(32, 128).T)
    shared["nw_q"] = np.ascontiguousarray(f(inp["q_norm_w"])[0].reshape(8, 128).T)
    shared["nw_kv"] = np.ascontiguousarray(f(inp["kv_norm_w"])[0].reshape(4, 128).T)
    shared["nw_fin"] = np.ascontiguousarray(np.broadcast_to(f(inp["norm_final_w"])[None, :], (128, 4096)))
    half = 32
    inv_freq = (np.float32(10000.0) ** (-(np.arange(0, half, dtype=np.float32) / np.float32(half)))).astype(np.float32)
    ang = (np.arange(L, dtype=np.float32)[None, :] * inv_freq[:, None]).astype(np.float32)
    cos, sin = np.cos(ang).astype(np.float32), np.sin(ang).astype(np.float32)
    shared["ropec"] = np.concatenate([cos, cos], 0)
    shared["ropes"] = np.concatenate([-sin, sin], 0)
    shared["c_ident"] = np.eye(128, dtype=np.float32).astype(ml_dtypes.bfloat16)
    shared["c_ones"] = np.ones((128, 128), ml_dtypes.bfloat16)
    shared["c_onesf"] = np.ones((128, 128), np.float32)
    k = np.arange(128)
    shared["c_trile"] = (k[:, None] <= k[None, :]).astype(np.float32)
    shared["c_trigt"] = (k[:, None] > k[None, :]).astype(np.float32)
    shared["c_maskb"] = (k[:, None] <= k[None, :]).astype(np.float32).astype(ml_dtypes.bfloat16)
    per_g = []
    for g in range(4):
        d = {}
        d["w_in"] = _tile_w(w_in, _in_cols(g))
        cols = []
        for hl in range(8):
            H = 8 * g + hl
            base = H * 192
            rope = list(range(base + 128, base + 192))
            cols += list(range(base, base + 128)) + rope + rope[32:] + rope[:32]
        d["w_uq"] = np.ascontiguousarray(w_uq[:, cols].reshape(8, 128, 2048).transpose(1, 0, 2))
        cols = []
        for hl in range(8):
            H = 8 * g + hl
            cols += list(range(H * 256, H * 256 + 128))
        for hl in range(8):
            H = 8 * g + hl
            cols += list(range(H * 256 + 128, H * 256 + 256))
        d["w_ukv"] = np.ascontiguousarray(w_ukv[:, cols].reshape(4, 128, 2048).transpose(1, 0, 2))
        ch = np.concatenate([np.arange(g * 2048, (g + 1) * 2048),
                             8192 + g * 256 + np.arange(256), 8192 + 1024 + g * 256 + np.arange(256)])
        cpar = np.concatenate([conv_w[:, ch], conv_b[None, ch]], 0)
        d["convp"] = np.ascontiguousarray(cpar.reshape(5, 20, 128).transpose(2, 1, 0))
        hs = slice(g * 32, (g + 1) * 32)
        bc = lambda v: np.ascontiguousarray(np.broadcast_to(v[None, :], (128, v.shape[0])))
        d["dtb"] = bc(f(inp["dt_bias"])[0, hs])
        d["alog"] = bc(f(inp["a_log"])[0, hs])
        d["dskip"] = bc(f(inp["d_skip"])[0, hs])
        d["nw_ssm"] = bc(f(inp["ssm_norm_w"])[0, g * 2048:(g + 1) * 2048])
        per_g.append(d)
    maps = []
    for c in range(8):
        b, g = c // 4, c % 4
        m = dict(shared)
        m.update(per_g[g])
        xb = x[b]
        m["xT"] = np.ascontiguousarray(xb.reshape(NB, 1024, 32, 128).transpose(0, 3, 2, 1))
        xsl = xb[g * LS:(g + 1) * LS]
        m["xTs"] = np.ascontiguousarray(xsl.reshape(LS, 32, 128).transpose(2, 1, 0))
        m["xres"] = np.ascontiguousarray(xsl)
        maps.append(m)
    return maps


_CACHE = {}


def run(inputs, L=SEQ, phases="AMBCGD", debug=()):
    import time
    t0 = time.time()
    key = (L, phases, tuple(debug))
    if key not in _CACHE:
        k = build_program(L, phases, debug)
        nc = k.finish()
        _CACHE[key] = (nc, set(k.inp.keys()), k.P.ninstr)
    nc, names, ninstr = _CACHE[key]
    t1 = time.time()
    maps = prepare_inputs(inputs, L)
    maps = [{n: m[n] for n in names} for m in maps]
    t2 = time.time()
    res = run_bass_kernel_spmd(nc, maps, core_ids=list(range(8)))
    if os.environ.get("KDBG"):
        print("ninstr %d build %.1fs prep %.1fs run %.1fs" % (ninstr, t1 - t0, t2 - t1, time.time() - t2), flush=True)
    return res.results


def kernel(**inputs):
    res = run(inputs)
    LS = SEQ // 4
    out = np.empty((2, SEQ, D_MODEL), np.float32)
    for c in range(8):
        b, g = c // 4, c % 4
        out[b, g * LS:(g + 1) * LS] = res[c]["out"]
    return out
```

```python
import os
import math
import numpy as np
import ml_dtypes
import concourse.bass as bass
import concourse.mybir as mybir
from concourse.bass_utils import run_bass_kernel_spmd

F32 = mybir.dt.float32
BF16 = mybir.dt.bfloat16
AF = mybir.ActivationFunctionType
ALU = mybir.AluOpType
EPS = 1e-6

D_MODEL = 4096
D_SSM = 8192
D_CONV = 10240
SEQ = 8192


class Buf:
    def __init__(self, name, ap, accum=False):
        self.name = name
        self.ap = ap
        self.ws = {}
        self.rs = {}
        self.accum = accum
        self.dsem = None
        self.dcount = 0

    def __getitem__(self, idx):
        return self.ap[idx]


class Prog:
    ENGS = ("pe", "act", "dve", "pool", "sp")

    def __init__(self, nc):
        self.nc = nc
        self.streams = {e: [] for e in self.ENGS}
        self.cnt = {e: 0 for e in self.ENGS}
        self.sems = {}
        self.semval = {}
        self._ctx = []
        self._semctx = []
        self.pending = {e: [] for e in self.ENGS}
        self.waited = {e: {} for e in self.ENGS}
        self.nsem = 0
        self.ninstr = 0
        for e in ("pe", "act", "dve", "pool"):
            self._mksem(e)

    def _mksem(self, key):
        self.nsem += 1
        g = self.nc.semaphore("sem%d" % self.nsem)
        h = g.__enter__()
        self._semctx.append(g)
        self.sems[key] = h
        self.semval[key] = 0
        return h

    def sb(self, name, shape, dt):
        g = self.nc.sbuf_tensor("sb_" + name, list(shape), dt)
        t = g.__enter__()
        self._ctx.append(g)
        return Buf(name, t)

    def ps(self, name, shape, dt=F32):
        g = self.nc.psum_tensor(name, list(shape), dt)
        t = g.__enter__()
        self._ctx.append(g)
        return Buf(name, t)

    def dram(self, name, shape, dt, kind="Internal"):
        t = self.nc.dram_tensor(name, list(shape), dt, kind=kind)
        return Buf(name, t, accum=True)

    def _deps(self, eng, reads, writes):
        need = {}

        def add(k, v):
            if k == "pe" and eng == "pe":
                return
            if need.get(k, 0) < v:
                need[k] = v
        for b in reads:
            for k, v in b.ws.items():
                add(k, v)
        for b in writes:
            if b.accum:
                continue
            for k, v in b.ws.items():
                add(k, v)
            for k, v in b.rs.items():
                add(k, v)
        w = self.waited[eng]
        need = {k: v for k, v in need.items() if w.get(k, 0) < v}
        for k, v in need.items():
            w[k] = v
        return need

    @staticmethod
    def _rec(tok, reads, writes):
        k, v = tok
        for b in reads:
            b.rs[k] = v
        for b in writes:
            if b.accum:
                b.ws[k] = v
            else:
                b.ws = {k: v}
                b.rs = {}

    def op(self, eng, fn, reads=(), writes=(), signal=True):
        need = self._deps(eng, reads, writes)
        waits = [(self.sems[k], v) for k, v in need.items()]
        self.ninstr += 1
        if signal:
            self.cnt[eng] += 1
            self.semval[eng] = self.cnt[eng]
            tok = (eng, self.cnt[eng])
            pr = [b for (b, kind) in self.pending[eng] if kind == "r"]
            pw = [b for (b, kind) in self.pending[eng] if kind == "w"]
            self.pending[eng] = []
            self._rec(tok, list(reads) + pr, list(writes) + pw)
        else:
            for b in reads:
                self.pending[eng].append((b, "r"))
            for b in writes:
                self.pending[eng].append((b, "w"))
        sem = self.sems[eng]

        def thunk(e, waits=waits, fn=fn, signal=signal, sem=sem):
            for (s, v) in waits:
                e.wait_ge(s, v)
            ins = fn(e)
            if signal:
                ins.then_inc(sem, 1)
        self.streams[eng].append(thunk)

    def dma(self, q, out_ap, in_ap, reads=(), writes=(), sembuf=None):
        need = self._deps(q, reads, writes)
        waits = [(self.sems[k], v) for k, v in need.items()]
        self.ninstr += 1
        sbf = sembuf
        if sbf is None:
            cands = [b for b in list(writes) + list(reads) if not b.accum]
            sbf = cands[0] if cands else (list(writes) + list(reads))[0]
        if sbf.dsem is None:
            sbf.dsem = ("d", self.nsem + 1)
            self._mksem(sbf.dsem)
        sbf.dcount += 16
        self.semval[sbf.dsem] = sbf.dcount
        tok = (sbf.dsem, sbf.dcount)
        self._rec(tok, reads, writes)
        sem = self.sems[sbf.dsem]

        def thunk(e, waits=waits, sem=sem, out_ap=out_ap, in_ap=in_ap):
            for (s, v) in waits:
                e.wait_ge(s, v)
            e.dma_start(out=out_ap, in_=in_ap).then_inc(sem, 16)
        self.streams[q].append(thunk)

    def barrier(self):
        for e in self.ENGS:
            assert not self.pending[e]
        for eng in self.ENGS:
            w = self.waited[eng]
            waits = [(self.sems[k], v) for k, v in self.semval.items() if v > 0 and w.get(k, 0) < v]
            for k, v in self.semval.items():
                if v > 0:
                    w[k] = max(w.get(k, 0), v)

            def thunk(e, waits=waits):
                for (s, v) in waits:
                    e.wait_ge(s, v)
            self.streams[eng].append(thunk)

    def emit(self):
        nc = self.nc
        for e in self.ENGS:
            assert not self.pending[e], "unsignalled pending ops on %s" % e
        with nc.Block() as block:
            @block.tensor
            def _(e):
                for t in self.streams["pe"]:
                    t(e)

            @block.scalar
            def _(e):
                for t in self.streams["act"]:
                    t(e)

            @block.vector
            def _(e):
                for t in self.streams["dve"]:
                    t(e)

            @block.gpsimd
            def _(e):
                for t in self.streams["pool"]:
                    t(e)

            @block.sync
            def _(e):
                for t in self.streams["sp"]:
                    t(e)

    def close(self):
        for g in reversed(self._ctx):
            g.__exit__(None, None, None)
        for g in reversed(self._semctx):
            g.__exit__(None, None, None)

    def free_from(self, mark):
        while len(self._ctx) > mark:
            g = self._ctx.pop()
            g.__exit__(None, None, None)


class Rot:
    def __init__(self, items):
        self.items = items
        self.i = 0

    def next(self):
        b = self.items[self.i % len(self.items)]
        self.i += 1
        return b


class K:
    def __init__(self, L, debug=(), phases="AMBCGD"):
        self.L = L
        self.phases = phases
        self.debug = set(debug)
        self.nc = nc = bass.Bass("TRN2", target_bir_lowering=False)
        nc.cache_partition_id()
        self.P = Prog(nc)
        self.inp = {}
        self.out_bufs = []
        self.NB = L // 1024
        self.LS = L // 4

    def ein(self, name, shape, dt=F32):
        need = {"w_gates": "D", "w_bs": "D", "w_ba": "D", "w_out": "D", "xres": "D", "xTs": "D", "nw_fin": "D",
                "tokoff": "D", "w_in": "A", "xT": "A"}
        if name in need and need[name] not in self.phases:
            return None
        t = self.nc.dram_tensor(name, list(shape), dt, kind="ExternalInput")
        b = Buf(name, t)
        self.inp[name] = b
        return b

    def scratch(self, name, shape, dt):
        kind = "ExternalOutput" if name in self.debug else "Internal"
        b = self.P.dram(name, shape, dt, kind=kind)
        if kind == "ExternalOutput":
            self.out_bufs.append(b)
        return b

    def declare(self):
        L, NB, LS = self.L, self.NB, self.LS
        e = self.ein
        self.xT = e("xT", [NB, 128, 32, 1024])
        self.xTs = e("xTs", [128, 32, LS])
        self.xres = e("xres", [LS, 4096])
        self.w_in = e("w_in", [15, 128, 32, 512])
        self.w_gates = e("w_gates", [16, 128, 32, 512])
        self.w_bs = e("w_bs", [16, 128, 32, 512])
        self.w_ba = e("w_ba", [8, 128, 32, 512])
        self.w_out = e("w_out", [8, 128, 32, 512])
        self.w_uq = e("w_uq", [128, 8, 2048])
        self.w_ukv = e("w_ukv", [128, 4, 2048])
        self.convp = e("convp", [128, 20, 5])
        self.nw_in = e("nw_in", [128, 32])
        self.nw_q = e("nw_q", [128, 8])
        self.nw_kv = e("nw_kv", [128, 4])
        self.dtb = e("dtb", [128, 32])
        self.alog = e("alog", [128, 32])
        self.dskip = e("dskip", [128, 32])
        self.nw_ssm = e("nw_ssm", [128, 2048])
        self.nw_fin = e("nw_fin", [128, 4096])
        self.ropec = e("ropec", [64, L])
        self.ropes = e("ropes", [64, L])
        self.c_ident = e("c_ident", [128, 128], BF16)
        self.c_ones = e("c_ones", [128, 128], BF16)
        self.c_onesf = e("c_onesf", [128, 128])
        self.c_trile = e("c_trile", [128, 128])
        self.c_trigt = e("c_trigt", [128, 128])
        self.c_maskb = e("c_maskb", [128, 128], BF16)
        s = self.scratch
        self.SZ = s("SZ", [L, 2048], BF16)
        self.XC = s("XC", [2560, L], BF16)
        self.DT = s("DT", [L, 32], F32)
        self.CQ = s("CQ", [1024, L], BF16)
        self.CKV = s("CKV", [512, L], BF16)
        self.RR = s("RR", [128, L], F32)
        self.GA = s("GA", [1024, L], BF16)
        self.QN = s("QN", [8, 128, L], BF16)
        self.QR = s("QR", [8, 64, L], BF16)
        self.KN = s("KN", [8, 128, L], BF16)
        self.KR = s("KR", [64, L], BF16)
        self.V = s("V", [L, 1024], BF16)
        self.YC = s("YC", [24, 4, 128, LS], BF16)
        self.YG = s("YG", [24, 4, 4, 128, LS], BF16)
        self.YQ = s("YQ", [24, 4, 128, LS], BF16)
        self.OUT = self.P.dram("out", [LS, 4096], F32, kind="ExternalOutput")
        self.out_bufs.append(self.OUT)

    def yc_store(self, jt0, nj, t0, ntok, src_fn, srcbuf):
        LS = self.LS
        t = t0
        while t < t0 + ntok:
            tq, tl = t // LS, t % LS
            n = min(LS - tl, t0 + ntok - t)
            dst = self.YC.ap[jt0:jt0 + nj, tq, :, tl:tl + n].rearrange("j p t -> p j t")
            self.P.dma("sp", dst, src_fn(t - t0, t - t0 + n), reads=[srcbuf], writes=[self.YC])
            t += n

    def load_consts(self):
        P = self.P
        self.ident = P.sb("ident", [128, 128], BF16)
        self.ones = P.sb("ones", [128, 128], BF16)
        for sbuf, src in ((self.ident, self.c_ident), (self.ones, self.c_ones)):
            P.dma("sp", sbuf[:], src[:], reads=[src], writes=[sbuf])
        self.psb = [P.ps("psb%d" % i, [128, 512]) for i in range(8)]
        self.psr = Rot(self.psb)

    def make_uT(self, src_fn, nsub, uT, xs, sq, rt, rs, win, ps_rot):
        P = self.P
        ones = self.ones
        epsb = self.epsb
        for s in range(nsub):
            x = xs[s % len(xs)]
            P.dma("sp", x[:], src_fn(s)[0], reads=[src_fn(s)[1]], writes=[x])
            P.op("act", lambda e, x=x: e.activation(out=sq[:], in_=x[:], func=AF.Square),
                 reads=[x], writes=[sq])
            ps = ps_rot.next()
            for kc in range(32):
                P.op("pe", lambda e, kc=kc, ps=ps: e.matmul(ps[:, 0:128], lhsT=ones[:], rhs=sq[:, kc, :],
                                                          start=(kc == 0), stop=(kc == 31)),
                     reads=[ones, sq], writes=[ps], signal=(kc == 31))
            P.op("act", lambda e, ps=ps: e.activation(out=rt[:], in_=ps[:, 0:128], func=AF.Sqrt,
                                                      scale=1.0 / D_MODEL, bias=epsb[:, 0:1]),
                 reads=[ps, epsb], writes=[rt])
            P.op("dve", lambda e: e.reciprocal(out=rs[:], in_=rt[:]), reads=[rt], writes=[rs])
            P.op("dve", lambda e, x=x: e.tensor_tensor(out=x[:], in0=x[:],
                                                       in1=win[:].unsqueeze(2).to_broadcast([128, 32, 128]),
                                                       op=ALU.mult),
                 reads=[x, win], writes=[x])
            P.op("dve", lambda e, x=x, s=s: e.tensor_tensor(out=uT[:, :, s * 128:(s + 1) * 128], in0=x[:],
                                                            in1=rs[:].unsqueeze(1).to_broadcast([128, 32, 128]),
                                                            op=ALU.mult),
                 reads=[x, rs], writes=[uT])

    def phase_A(self):
        P, L, NB = self.P, self.L, self.NB
        mark = len(P._ctx)
        uT = P.sb("uT", [128, 32, 1024], BF16)
        xs = [P.sb("xs%d" % i, [128, 32, 128], F32) for i in range(2)]
        sq = P.sb("sq", [128, 32, 128], BF16)
        rt = P.sb("rt", [128, 128], F32)
        rs = P.sb("rs", [128, 128], F32)
        win = P.sb("win", [128, 32], F32)
        self.epsb = P.sb("epsb", [128, 1], F32)
        P.op("pool", lambda e, t=self.epsb: e.memset(t[:], EPS), writes=[self.epsb])
        P.dma("sp", win[:], self.nw_in[:], reads=[self.nw_in], writes=[win])
        wts = Rot([P.sb("wt%d" % i, [128, 32, 512], BF16) for i in range(2)])
        stg = Rot([P.sb("stg%d" % i, [128, 512], BF16) for i in range(4)])
        stf = Rot([P.sb("stf%d" % i, [128, 512], F32) for i in range(2)])
        raws = Rot([P.sb("raw%d" % i, [128, 515], F32) for i in range(2)])
        accs = Rot([P.sb("acc%d" % i, [128, 512], F32) for i in range(2)])
        halo = P.sb("halo", [128, 20, 3], F32)
        cp = P.sb("cp", [128, 20, 5], F32)
        dtb = P.sb("dtb", [128, 32], F32)
        dtx = P.sb("dtx", [128, 8, 32], F32)
        dte = P.sb("dte", [128, 8, 32], F32)
        dtt = P.sb("dtt", [128, 8, 32], F32)
        P.dma("sp", cp[:], self.convp[:], reads=[self.convp], writes=[cp])
        P.dma("sp", dtb[:], self.dtb[:], reads=[self.dtb], writes=[dtb])
        psr = self.psr

        for tb in range(NB):
            t0 = tb * 1024
            self.make_uT(lambda s, tb=tb: (self.xT[tb, :, :, s * 128:(s + 1) * 128], self.xT),
                         8, uT, xs, sq, rt, rs, win, psr)
            for t in range(15):
                wt = wts.next()
                P.dma("pool", wt[:], self.w_in[t], reads=[self.w_in], writes=[wt])
                if t < 4:
                    for tt in range(8):
                        ps = psr.next()
                        for kc in range(32):
                            P.op("pe", lambda e, kc=kc, ps=ps, tt=tt, wt=wt: e.matmul(
                                ps[:], lhsT=uT[:, kc, tt * 128:(tt + 1) * 128], rhs=wt[:, kc, :],
                                start=(kc == 0), stop=(kc == 31)),
                                reads=[uT, wt], writes=[ps], signal=(kc == 31))
                        st = stg.next()
                        P.op("act", lambda e, ps=ps, st=st: e.activation(out=st[:], in_=ps[:], func=AF.Silu),
                             reads=[ps], writes=[st])
                        P.dma("sp", self.SZ[t0 + tt * 128:t0 + (tt + 1) * 128, t * 512:(t + 1) * 512], st[:],
                              reads=[st], writes=[self.SZ])
                    continue
                ncs = 1 if t == 14 else 4
                for cs in range(ncs):
                    for th in range(2):
                        ps = psr.next()
                        for kc in range(32):
                            P.op("pe", lambda e, kc=kc, ps=ps, cs=cs, th=th, wt=wt: e.matmul(
                                ps[:], lhsT=wt[:, kc, cs * 128:(cs + 1) * 128], rhs=uT[:, kc, th * 512:(th + 1) * 512],
                                start=(kc == 0), stop=(kc == 31)),
                                reads=[uT, wt], writes=[ps], signal=(kc == 31))
                        tok = slice(t0 + th * 512, t0 + (th + 1) * 512)
                        if 4 <= t <= 8:
                            idx = (t - 4) * 4 + cs
                            raw = raws.next()
                            acc = accs.next()
                            P.op("dve", lambda e, raw=raw, ps=ps: e.tensor_copy(out=raw[:, 3:515], in_=ps[:]),
                                 reads=[ps], writes=[raw])
                            if tb == 0 and th == 0:
                                P.op("pool", lambda e, raw=raw: e.memset(raw[:, 0:3], 0.0), writes=[raw])
                            else:
                                P.op("pool", lambda e, raw=raw, idx=idx: e.tensor_copy(out=raw[:, 0:3], in_=halo[:, idx, :]),
                                     reads=[halo], writes=[raw])
                            P.op("pool", lambda e, raw=raw, idx=idx: e.tensor_copy(out=halo[:, idx, :], in_=raw[:, 512:515]),
                                 reads=[raw], writes=[halo])
                            P.op("dve", lambda e, raw=raw, acc=acc, idx=idx: e.tensor_scalar(
                                out=acc[:], in0=raw[:, 3:515], scalar1=cp[:, idx, 3:4], scalar2=cp[:, idx, 4:5],
                                op0=ALU.mult, op1=ALU.add), reads=[raw, cp], writes=[acc])
                            for k in range(3):
                                P.op("dve", lambda e, raw=raw, acc=acc, idx=idx, k=k: e.scalar_tensor_tensor(
                                    out=acc[:], in0=raw[:, k:k + 512], scalar=cp[:, idx, k:k + 1], in1=acc[:],
                                    op0=ALU.mult, op1=ALU.add), reads=[raw, cp, acc], writes=[acc])
                            st = stg.next()
                            P.op("act", lambda e, acc=acc, st=st: e.activation(out=st[:], in_=acc[:], func=AF.Silu),
                                 reads=[acc], writes=[st])
                            P.dma("sp", self.XC[idx * 128:(idx + 1) * 128, tok], st[:], reads=[st], writes=[self.XC])
                        elif t in (9, 10, 11, 12, 13):
                            st = stg.next()
                            fn = AF.Silu if t >= 12 else AF.Copy
                            P.op("act", lambda e, ps=ps, st=st, fn=fn: e.activation(out=st[:], in_=ps[:], func=fn),
                                 reads=[ps], writes=[st])
                            if t <= 10:
                                r0 = ((t - 9) * 4 + cs) * 128
                                dst, dbuf = self.CQ[r0:r0 + 128, tok], self.CQ
                            elif t == 11:
                                dst, dbuf = self.CKV[cs * 128:(cs + 1) * 128, tok], self.CKV
                            else:
                                r0 = ((t - 12) * 4 + cs) * 128
                                dst, dbuf = self.GA[r0:r0 + 128, tok], self.GA
                            P.dma("sp", dst, st[:], reads=[st], writes=[dbuf])
                        else:
                            st = stf.next()
                            P.op("act", lambda e, ps=ps, st=st: e.activation(out=st[:], in_=ps[:], func=AF.Copy),
                                 reads=[ps], writes=[st])
                            P.dma("sp", self.RR[:, tok], st[:], reads=[st], writes=[self.RR])
                if t == 14:
                    ps = psr.next()
                    for tt in range(8):
                        for kc in range(32):
                            P.op("pe", lambda e, kc=kc, ps=ps, tt=tt, wt=wt: e.matmul(
                                ps[:, tt * 32:(tt + 1) * 32], lhsT=uT[:, kc, tt * 128:(tt + 1) * 128],
                                rhs=wt[:, kc, 128:160], start=(kc == 0), stop=(kc == 31)),
                                reads=[uT, wt], writes=[ps], signal=(kc == 31 and tt == 7))
                    P.op("dve", lambda e, ps=ps: e.tensor_tensor(
                        out=dtx[:], in0=ps[:, 0:256].rearrange("p (a b) -> p a b", a=8),
                        in1=dtb[:].unsqueeze(1).to_broadcast([128, 8, 32]), op=ALU.add),
                        reads=[ps, dtb], writes=[dtx])
                    P.op("act", lambda e: e.activation(out=dte[:], in_=dtx[:], func=AF.Exp), reads=[dtx], writes=[dte])
                    P.op("act", lambda e: e.activation(out=dtt[:], in_=dte[:], func=AF.Ln, bias=1.0, scale=1.0),
                         reads=[dte], writes=[dtt])
                    P.dma("sp", self.DT[t0:t0 + 1024, :].rearrange("(a p) h -> p a h", p=128), dtt[:],
                          reads=[dtt], writes=[self.DT])
        P.barrier()
        P.free_from(mark)


    def phase_A2(self):
        P, L = self.P, self.L
        mark = len(P._ctx)
        psr = self.psr
        wq = P.sb("wq", [128, 8, 2048], BF16)
        wkv = P.sb("wkv", [128, 4, 2048], BF16)
        P.dma("pool", wq[:], self.w_uq[:], reads=[self.w_uq], writes=[wq])
        P.dma("pool", wkv[:], self.w_ukv[:], reads=[self.w_ukv], writes=[wkv])
        nwq = P.sb("nwq", [128, 8], F32)
        nwkv = P.sb("nwkv", [128, 4], F32)
        P.dma("sp", nwq[:], self.nw_q[:], reads=[self.nw_q], writes=[nwq])
        P.dma("sp", nwkv[:], self.nw_kv[:], reads=[self.nw_kv], writes=[nwkv])
        epsb = P.sb("epsb2", [128, 1], F32)
        P.op("pool", lambda e: e.memset(epsb[:], EPS), writes=[epsb])
        cqs = Rot([P.sb("cq%d" % i, [128, 8, 512], BF16) for i in range(2)])
        ckvs = Rot([P.sb("ckv%d" % i, [128, 4, 512], BF16) for i in range(2)])
        rras = Rot([P.sb("rra%d" % i, [64, 512], F32) for i in range(2)])
        rrbs = Rot([P.sb("rrb%d" % i, [64, 512], F32) for i in range(2)])
        coss = Rot([P.sb("cos%d" % i, [64, 512], F32) for i in range(2)])
        sins = Rot([P.sb("sin%d" % i, [64, 512], F32) for i in range(2)])
        sqb = P.sb("sqb", [128, 8, 512], BF16)
        rt = P.sb("rt2", [128, 512], F32)
        rq = P.sb("rq", [128, 512], F32)
        rkv = P.sb("rkv", [128, 512], F32)
        cqw = P.sb("cqw", [128, 8, 512], BF16)
        ckvw = P.sb("ckvw", [128, 4, 512], BF16)
        stg = Rot([P.sb("stg2_%d" % i, [128, 512], BF16) for i in range(4)])
        tA = Rot([P.sb("tA%d" % i, [64, 512], F32) for i in range(2)])
        tB = Rot([P.sb("tB%d" % i, [64, 512], F32) for i in range(2)])
        ones = self.ones

        def rope(srcA, srcB, bufsA, bufsB, cos, sin, dst, dbuf):
            a, b = tA.next(), tB.next()
            P.op("dve", lambda e: e.tensor_tensor(out=a[:], in0=srcA, in1=cos[:], op=ALU.mult),
                 reads=bufsA + [cos], writes=[a])
            P.op("dve", lambda e: e.tensor_tensor(out=b[:], in0=srcB, in1=sin[:], op=ALU.mult),
                 reads=bufsB + [sin], writes=[b])
            st = stg.next()
            P.op("pool", lambda e: e.tensor_tensor(out=st[0:64, :], in0=a[:], in1=b[:], op=ALU.add),
                 reads=[a, b], writes=[st])
            P.dma("sp", dst, st[0:64, :], reads=[st], writes=[dbuf])

        for tk in range(L // 512):
            tok = slice(tk * 512, (tk + 1) * 512)
            cq, ckv, rra, rrb, cos, sin = cqs.next(), ckvs.next(), rras.next(), rrbs.next(), coss.next(), sins.next()
            P.dma("sp", cq[:], self.CQ[:, tok].rearrange("(j p) t -> p j t", p=128), reads=[self.CQ], writes=[cq])
            P.dma("sp", ckv[:], self.CKV[:, tok].rearrange("(j p) t -> p j t", p=128), reads=[self.CKV], writes=[ckv])
            P.dma("sp", rra[:], self.RR[0:64, tok], reads=[self.RR], writes=[rra])
            P.dma("sp", rrb[:], self.RR[64:128, tok], reads=[self.RR], writes=[rrb])
            P.dma("sp", cos[:], self.ropec[:, tok], reads=[self.ropec], writes=[cos])
            P.dma("sp", sin[:], self.ropes[:, tok], reads=[self.ropes], writes=[sin])
            for (src, nk, nw, rr_, dst, dim) in ((cq, 8, nwq, rq, cqw, 1024), (ckv, 4, nwkv, rkv, ckvw, 512)):
                P.op("act", lambda e, src=src, nk=nk: e.activation(out=sqb[:, 0:nk, :], in_=src[:], func=AF.Square),
                     reads=[src], writes=[sqb])
                ps = psr.next()
                for kc in range(nk):
                    P.op("pe", lambda e, kc=kc, ps=ps, nk=nk: e.matmul(ps[:], lhsT=ones[:], rhs=sqb[:, kc, :],
                                                                    start=(kc == 0), stop=(kc == nk - 1)),
                         reads=[ones, sqb], writes=[ps], signal=(kc == nk - 1))
                P.op("act", lambda e, ps=ps, dim=dim: e.activation(out=rt[:], in_=ps[:], func=AF.Sqrt,
                                                                   scale=1.0 / dim, bias=epsb[:, 0:1]),
                     reads=[ps, epsb], writes=[rt])
                P.op("dve", lambda e, rr_=rr_: e.reciprocal(out=rr_[:], in_=rt[:]), reads=[rt], writes=[rr_])
                for kc in range(nk):
                    P.op("dve", lambda e, kc=kc, src=src, nw=nw, rr_=rr_, dst=dst: e.scalar_tensor_tensor(
                        out=dst[:, kc, :], in0=src[:, kc, :], scalar=nw[:, kc:kc + 1], in1=rr_[:],
                        op0=ALU.mult, op1=ALU.mult), reads=[src, nw, rr_], writes=[dst])
            for hl in range(8):
                ps = psr.next()
                for kc in range(8):
                    P.op("pe", lambda e, kc=kc, ps=ps, hl=hl: e.matmul(
                        ps[:], lhsT=wq[:, kc, hl * 256:hl * 256 + 128], rhs=cqw[:, kc, :],
                        start=(kc == 0), stop=(kc == 7)), reads=[wq, cqw], writes=[ps], signal=(kc == 7))
                st = stg.next()
                P.op("act", lambda e, ps=ps, st=st: e.activation(out=st[:], in_=ps[:], func=AF.Copy), reads=[ps], writes=[st])
                P.dma("sp", self.QN[hl, :, tok], st[:], reads=[st], writes=[self.QN])
                psA, psB = psr.next(), psr.next()
                for (pp, off) in ((psA, 128), (psB, 192)):
                    for kc in range(8):
                        P.op("pe", lambda e, kc=kc, pp=pp, hl=hl, off=off: e.matmul(
                            pp[0:64, :], lhsT=wq[:, kc, hl * 256 + off:hl * 256 + off + 64], rhs=cqw[:, kc, :],
                            start=(kc == 0), stop=(kc == 7)), reads=[wq, cqw], writes=[pp], signal=(kc == 7))
                rope(psA[0:64, :], psB[0:64, :], [psA], [psB], cos, sin, self.QR[hl, :, tok], self.QR)
                ps = psr.next()
                for kc in range(4):
                    P.op("pe", lambda e, kc=kc, ps=ps, hl=hl: e.matmul(
                        ps[:], lhsT=wkv[:, kc, hl * 128:(hl + 1) * 128], rhs=ckvw[:, kc, :],
                        start=(kc == 0), stop=(kc == 3)), reads=[wkv, ckvw], writes=[ps], signal=(kc == 3))
                st = stg.next()
                P.op("act", lambda e, ps=ps, st=st: e.activation(out=st[:], in_=ps[:], func=AF.Copy), reads=[ps], writes=[st])
                P.dma("sp", self.KN[hl, :, tok], st[:], reads=[st], writes=[self.KN])
            for tt in range(4):
                for hf in range(2):
                    ps = psr.next()
                    for kc in range(4):
                        P.op("pe", lambda e, kc=kc, ps=ps, tt=tt, hf=hf: e.matmul(
                            ps[:], lhsT=ckvw[:, kc, tt * 128:(tt + 1) * 128],
                            rhs=wkv[:, kc, 1024 + hf * 512:1024 + (hf + 1) * 512],
                            start=(kc == 0), stop=(kc == 3)), reads=[wkv, ckvw], writes=[ps], signal=(kc == 3))
                    st = stg.next()
                    P.op("act", lambda e, ps=ps, st=st: e.activation(out=st[:], in_=ps[:], func=AF.Copy), reads=[ps], writes=[st])
                    r0 = tk * 512 + tt * 128
                    P.dma("sp", self.V[r0:r0 + 128, hf * 512:(hf + 1) * 512], st[:], reads=[st], writes=[self.V])
            rope(rra[:], rrb[:], [rra], [rrb], cos, sin, self.KR[:, tok], self.KR)
        P.barrier()
        P.free_from(mark)

    def phase_C(self):
        P, L = self.P, self.L
        mark = len(P._ctx)
        psr = self.psr
        NKT = L // 128
        scale = 1.0 / math.sqrt(192.0)
        ones = self.ones
        maskb = P.sb("maskb", [128, 128], BF16)
        P.dma("sp", maskb[:], self.c_maskb[:], reads=[self.c_maskb], writes=[maskb])
        kr = P.sb("kr", [64, L], BF16)
        P.dma("sp", kr[:], self.KR[:, :], reads=[self.KR], writes=[kr])
        qns = Rot([P.sb("qn%d" % i, [128, L], BF16) for i in range(2)])
        qrs = Rot([P.sb("qr%d" % i, [64, L], BF16) for i in range(2)])
        kns = Rot([P.sb("kn%d" % i, [128, L], BF16) for i in range(2)])
        vs = Rot([P.sb("v%d" % i, [128, NKT, 128], BF16) for i in range(2)])
        pts = Rot([P.sb("pt%d" % i, [128, 512], BF16) for i in range(3)])
        gas = Rot([P.sb("ga%d" % i, [128, 512], BF16) for i in range(2)])
        rinv = P.sb("rinv", [128, 512], F32)
        o1 = P.sb("o1", [128, 512], F32)
        stg = Rot([P.sb("stg3_%d" % i, [128, 512], BF16) for i in range(2)])
        acc_r = Rot(self.psb[0:4])
        s_r = Rot(self.psb[4:8])
        for hl in range(8):
            qn, qr, kn, v = qns.next(), qrs.next(), kns.next(), vs.next()
            P.dma("sp", qn[:], self.QN[hl], reads=[self.QN], writes=[qn])
            P.dma("sp", qr[:], self.QR[hl], reads=[self.QR], writes=[qr])
            P.dma("sp", kn[:], self.KN[hl], reads=[self.KN], writes=[kn])
            P.dma("sp", v[:], self.V[:, hl * 128:(hl + 1) * 128].rearrange("(a p) d -> p a d", p=128),
                  reads=[self.V], writes=[v])
            for qi in range(L // 512):
                ga = gas.next()
                P.dma("sp", ga[:], self.GA[hl * 128:(hl + 1) * 128, qi * 512:(qi + 1) * 512], reads=[self.GA], writes=[ga])
                O, Rs = acc_r.next(), acc_r.next()
                nkt = 4 * qi + 4
                for kt in range(nkt):
                    d = kt - 4 * qi
                    q0 = max(d, 0) * 128
                    Sb = s_r.next()
                    qs = slice(qi * 512 + q0, (qi + 1) * 512)
                    ks = slice(kt * 128, (kt + 1) * 128)
                    P.op("pe", lambda e, Sb=Sb, q0=q0, qs=qs, ks=ks, kn=kn, qn=qn: e.matmul(
                        Sb[:, q0:512], lhsT=kn[:, ks], rhs=qn[:, qs], start=True, stop=False),
                        reads=[kn, qn], writes=[Sb], signal=False)
                    P.op("pe", lambda e, Sb=Sb, q0=q0, qs=qs, ks=ks, qr=qr: e.matmul(
                        Sb[:, q0:512], lhsT=kr[:, ks], rhs=qr[:, qs], start=False, stop=True),
                        reads=[kr, qr], writes=[Sb])
                    pt = pts.next()
                    P.op("act", lambda e, Sb=Sb, pt=pt, q0=q0: e.activation(out=pt[:, q0:512], in_=Sb[:, q0:512],
                                                                         func=AF.Exp, scale=scale),
                         reads=[Sb], writes=[pt])
                    if d >= 0:
                        P.op("dve", lambda e, pt=pt, q0=q0: e.tensor_tensor(out=pt[:, q0:q0 + 128], in0=pt[:, q0:q0 + 128],
                                                                           in1=maskb[:], op=ALU.mult),
                             reads=[pt, maskb], writes=[pt])
                    P.op("pe", lambda e, O=O, pt=pt, q0=q0, kt=kt, nkt=nkt, v=v: e.matmul(
                        O[:, q0:512], lhsT=v[:, kt, :], rhs=pt[:, q0:512], start=(kt == 0), stop=(kt == nkt - 1)),
                        reads=[v, pt], writes=[O], signal=False)
                    P.op("pe", lambda e, Rs=Rs, pt=pt, q0=q0, kt=kt, nkt=nkt: e.matmul(
                        Rs[:, q0:512], lhsT=ones[:], rhs=pt[:, q0:512], start=(kt == 0), stop=(kt == nkt - 1)),
                        reads=[ones, pt], writes=[Rs])
                P.op("dve", lambda e, Rs=Rs: e.reciprocal(out=rinv[:], in_=Rs[:]), reads=[Rs], writes=[rinv])
                P.op("dve", lambda e, O=O: e.tensor_tensor(out=o1[:], in0=O[:], in1=rinv[:], op=ALU.mult),
                     reads=[O, rinv], writes=[o1])
                st = stg.next()
                P.op("pool", lambda e, st=st, ga=ga: e.tensor_tensor(out=st[:], in0=o1[:], in1=ga[:], op=ALU.mult),
                     reads=[o1, ga], writes=[st])
                self.yc_store(16 + hl, 1, qi * 512, 512, lambda lo, hi, st=st: st[:, lo:hi].unsqueeze(1), st)
        P.barrier()
        P.free_from(mark)


    def phase_B(self):
        P, L = self.P, self.L
        mark = len(P._ctx)
        psr = self.psr
        NCH = L // 128
        ident, ones = self.ident, self.ones

        def cload(name, src, shape, dt):
            b = P.sb(name, shape, dt)
            P.dma("sp", b[:], src[:], reads=[src], writes=[b])
            return b
        trile = cload("trile", self.c_trile, [128, 128], F32)
        trigt = cload("trigt", self.c_trigt, [128, 128], F32)
        onesf = cload("onesf", self.c_onesf, [128, 128], F32)
        maskb = cload("maskbB", self.c_maskb, [128, 128], BF16)
        Abc = cload("Abc", self.alog, [128, 32], F32)
        dsk = cload("dsk", self.dskip, [128, 32], F32)
        nws = cload("nws", self.nw_ssm, [128, 2048], F32)
        P.op("act", lambda e: e.activation(out=Abc[:], in_=Abc[:], func=AF.Exp), reads=[Abc], writes=[Abc])
        P.op("dve", lambda e: e.tensor_scalar(out=Abc[:], in0=Abc[:], scalar1=-1.0, scalar2=None, op0=ALU.mult),
             reads=[Abc], writes=[Abc])
        epsb = P.sb("epsbB", [128, 1], F32)
        P.op("pool", lambda e: e.memset(epsb[:], EPS), writes=[epsb])
        h = P.sb("h", [128, 2048], F32)
        hb = P.sb("hb", [128, 2048], BF16)
        P.op("pool", lambda e: e.memset(h[:], 0.0), writes=[h])
        P.op("pool", lambda e: e.memset(hb[:], 0.0), writes=[hb])
        xT4s = Rot([P.sb("xT4_%d" % i, [128, 20, 512], BF16) for i in range(2)])
        dt4s = Rot([P.sb("dt4_%d" % i, [128, 4, 32], F32) for i in range(2)])
        szs = Rot([P.sb("sz%d" % i, [128, 2048], BF16) for i in range(2)])
        xtm = P.sb("xtm", [128, 2048], BF16)
        btm = P.sb("btm", [128, 256], BF16)
        sm = {n: P.sb("sm_" + n, [128, 32], F32) for n in ("a", "acs", "ea", "cd", "dsd", "ds", "w2")}
        R = P.sb("R", [128, 32, 128], F32)
        E = P.sb("E", [128, 32, 128], BF16)
        cbm = P.sb("cbm", [128, 2, 128], BF16)
        S = P.sb("S", [128, 32, 128], BF16)
        xdt = P.sb("xdt", [128, 2048], BF16)
        xdd = P.sb("xdd", [128, 2048], BF16)
        xD = P.sb("xD", [128, 2048], BF16)
        y = P.sb("y", [128, 2048], F32)
        junk = P.sb("junk", [128, 1024], BF16)
        ssq = P.sb("ssq", [128, 2], F32)
        rtn = P.sb("rtn", [128, 2], F32)
        rsn = P.sb("rsn", [128, 2], F32)
        ytm = P.sb("ytm", [128, 2048], BF16)
        yT4 = P.sb("yT4", [128, 16, 512], BF16)

        def pbf(bank):
            return bank.ap.bitcast(BF16)

        xT4 = dt4 = None
        for c in range(NCH):
            cc = c % 4
            csl = slice(cc * 128, (cc + 1) * 128)
            if cc == 0:
                xT4, dt4 = xT4s.next(), dt4s.next()
                tok = slice(c * 128, c * 128 + 512)
                P.dma("sp", xT4[:], self.XC[:, tok].rearrange("(j p) t -> p j t", p=128), reads=[self.XC], writes=[xT4])
                P.dma("sp", dt4[:], self.DT[tok, :].rearrange("(a p) h -> p a h", p=128), reads=[self.DT], writes=[dt4])
            sz = szs.next()
            P.dma("sp", sz[:], self.SZ[c * 128:(c + 1) * 128, :], reads=[self.SZ], writes=[sz])
            dtc = dt4[:, cc, :]
            for i in range(2):
                bank = psr.next()
                pb = pbf(bank)
                for jj in range(8):
                    j = i * 8 + jj
                    P.op("pe", lambda e, pb=pb, jj=jj, j=j, xT4=xT4, csl=csl: e.transpose(
                        pb[:, jj * 128:(jj + 1) * 128], xT4[:, j, csl], ident[:]),
                        reads=[xT4, ident], writes=[bank], signal=(jj == 7))
                P.op("act", lambda e, pb=pb, i=i: e.activation(out=xtm[:, i * 1024:(i + 1) * 1024], in_=pb[:, 0:1024], func=AF.Copy),
                     reads=[bank], writes=[xtm])
            bank = psr.next()
            pb = pbf(bank)
            for g2 in range(2):
                P.op("pe", lambda e, pb=pb, g2=g2, xT4=xT4, csl=csl: e.transpose(
                    pb[:, g2 * 128:(g2 + 1) * 128], xT4[:, 16 + g2, csl], ident[:]),
                    reads=[xT4, ident], writes=[bank], signal=(g2 == 1))
            P.op("act", lambda e, pb=pb: e.activation(out=btm[:], in_=pb[:, 0:256], func=AF.Copy), reads=[bank], writes=[btm])
            a, acs, ea, cd, dsd, ds_, w2 = (sm[n] for n in ("a", "acs", "ea", "cd", "dsd", "ds", "w2"))
            P.op("dve", lambda e, dtc=dtc: e.tensor_tensor(out=a[:], in0=dtc, in1=Abc[:], op=ALU.mult),
                 reads=[dt4, Abc], writes=[a])
            bank = psr.next()
            P.op("pe", lambda e, bank=bank: e.matmul(bank[:, 0:32], lhsT=trile[:], rhs=a[:], start=True, stop=True),
                 reads=[trile, a], writes=[bank], signal=False)
            P.op("pe", lambda e, bank=bank: e.matmul(bank[:, 32:64], lhsT=onesf[:], rhs=a[:], start=True, stop=True),
                 reads=[onesf, a], writes=[bank])
            P.op("act", lambda e, bank=bank: e.activation(out=acs[:], in_=bank[:, 0:32], func=AF.Copy), reads=[bank], writes=[acs])
            P.op("act", lambda e, bank=bank: e.activation(out=ea[:], in_=bank[:, 0:32], func=AF.Exp), reads=[bank], writes=[ea])
            P.op("act", lambda e, bank=bank: e.activation(out=cd[:], in_=bank[:, 32:64], func=AF.Exp), reads=[bank], writes=[cd])
            P.op("dve", lambda e, bank=bank: e.tensor_tensor(out=dsd[:], in0=bank[:, 32:64], in1=acs[:], op=ALU.subtract),
                 reads=[bank, acs], writes=[dsd])
            P.op("act", lambda e: e.activation(out=ds_[:], in_=dsd[:], func=AF.Exp), reads=[dsd], writes=[ds_])
            P.op("dve", lambda e, dtc=dtc: e.tensor_tensor(out=w2[:], in0=dtc, in1=ds_[:], op=ALU.mult),
                 reads=[dt4, ds_], writes=[w2])
            P.op("dve", lambda e: e.tensor_tensor(out=R[:], in0=a[:].unsqueeze(2).to_broadcast([128, 32, 128]),
                                                  in1=trile[:].unsqueeze(1).to_broadcast([128, 32, 128]), op=ALU.mult),
                 reads=[a, trile], writes=[R])
            for j in range(8):
                bank = psr.next()
                P.op("pe", lambda e, bank=bank, j=j: e.matmul(
                    bank[:], lhsT=trigt[:], rhs=R[:, 4 * j:4 * j + 4, :].rearrange("p a b -> p (a b)"), start=True, stop=True),
                    reads=[trigt, R], writes=[bank])
                P.op("act", lambda e, bank=bank, j=j: e.activation(
                    out=E[:, 4 * j:4 * j + 4, :].rearrange("p a b -> p (a b)"), in_=bank[:], func=AF.Exp),
                    reads=[bank], writes=[E])
            bank = psr.next()
            for g2 in range(2):
                P.op("pe", lambda e, bank=bank, g2=g2, xT4=xT4, csl=csl: e.matmul(
                    bank[:, g2 * 128:(g2 + 1) * 128], lhsT=xT4[:, 16 + g2, csl], rhs=xT4[:, 18 + g2, csl], start=True, stop=True),
                    reads=[xT4], writes=[bank], signal=(g2 == 1))
            P.op("dve", lambda e, bank=bank: e.tensor_tensor(
                out=cbm[:], in0=bank[:, 0:256].rearrange("p (g l) -> p g l", g=2),
                in1=maskb[:].unsqueeze(1).to_broadcast([128, 2, 128]), op=ALU.mult),
                reads=[bank, maskb], writes=[cbm])
            P.op("dve", lambda e: e.tensor_tensor(
                out=S[:].rearrange("p (g r) l -> p g r l", g=2), in0=E[:].rearrange("p (g r) l -> p g r l", g=2),
                in1=cbm[:].unsqueeze(2).to_broadcast([128, 2, 16, 128]), op=ALU.mult),
                reads=[E, cbm], writes=[S])
            x3 = xtm[:].rearrange("p (h d) -> p h d", d=64)
            P.op("dve", lambda e, dtc=dtc, x3=x3: e.tensor_tensor(
                out=xdt[:].rearrange("p (h d) -> p h d", d=64), in0=x3,
                in1=dtc.unsqueeze(2).to_broadcast([128, 32, 64]), op=ALU.mult), reads=[xtm, dt4], writes=[xdt])
            P.op("dve", lambda e, x3=x3: e.tensor_tensor(
                out=xdd[:].rearrange("p (h d) -> p h d", d=64), in0=x3,
                in1=w2[:].unsqueeze(2).to_broadcast([128, 32, 64]), op=ALU.mult), reads=[xtm, w2], writes=[xdd])
            P.op("pool", lambda e, x3=x3: e.tensor_tensor(
                out=xD[:].rearrange("p (h d) -> p h d", d=64), in0=x3,
                in1=dsk[:].unsqueeze(2).to_broadcast([128, 32, 64]), op=ALU.mult), reads=[xtm, dsk], writes=[xD])
            for q in range(4):
                g2 = q // 2
                yo = psr.next()
                P.op("pe", lambda e, yo=yo, q=q, g2=g2, xT4=xT4, csl=csl: e.matmul(
                    yo[:], lhsT=xT4[:, 18 + g2, csl], rhs=hb[:, q * 512:(q + 1) * 512], start=True, stop=True),
                    reads=[xT4, hb], writes=[yo])
                yd = psr.next()
                P.op("pe", lambda e, yd=yd, q=q: e.matmul(yd[:], lhsT=ident[:], rhs=xD[:, q * 512:(q + 1) * 512],
                                                         start=True, stop=False),
                     reads=[ident, xD], writes=[yd], signal=False)
                for hh in range(8):
                    hd = q * 8 + hh
                    P.op("pe", lambda e, yd=yd, hh=hh, hd=hd: e.matmul(
                        yd[:, hh * 64:(hh + 1) * 64], lhsT=S[:, hd, :], rhs=xdt[:, hd * 64:(hd + 1) * 64],
                        start=False, stop=(hh == 7)), reads=[S, xdt], writes=[yd], signal=(hh == 7))
                ysl = y[:, q * 512:(q + 1) * 512]
                P.op("dve", lambda e, yo=yo, q=q, ysl=ysl: e.tensor_tensor(
                    out=ysl.rearrange("p (h d) -> p h d", d=64), in0=yo[:].rearrange("p (h d) -> p h d", d=64),
                    in1=ea[:, q * 8:(q + 1) * 8].unsqueeze(2).to_broadcast([128, 8, 64]), op=ALU.mult),
                    reads=[yo, ea], writes=[y])
                P.op("dve", lambda e, yd=yd, ysl=ysl: e.tensor_tensor(out=ysl, in0=yd[:], in1=ysl, op=ALU.add),
                     reads=[yd, y], writes=[y])
            P.op("dve", lambda e, sz=sz: e.tensor_tensor(out=y[:], in0=y[:], in1=sz[:], op=ALU.mult), reads=[y, sz], writes=[y])
            for g2 in range(2):
                P.op("act", lambda e, g2=g2: e.activation(out=junk[:], in_=y[:, g2 * 1024:(g2 + 1) * 1024], func=AF.Square,
                                                          accum_out=ssq[:, g2:g2 + 1]), reads=[y], writes=[junk, ssq])
            P.op("act", lambda e: e.activation(out=rtn[:], in_=ssq[:], func=AF.Sqrt, scale=1.0 / 1024, bias=epsb[:, 0:1]),
                 reads=[ssq, epsb], writes=[rtn])
            P.op("dve", lambda e: e.reciprocal(out=rsn[:], in_=rtn[:]), reads=[rtn], writes=[rsn])
            for g2 in range(2):
                gs = slice(g2 * 1024, (g2 + 1) * 1024)
                P.op("dve", lambda e, g2=g2, gs=gs: e.scalar_tensor_tensor(
                    out=ytm[:, gs], in0=y[:, gs], scalar=rsn[:, g2:g2 + 1], in1=nws[:, gs], op0=ALU.mult, op1=ALU.mult),
                    reads=[y, rsn, nws], writes=[ytm])
            for i in range(2):
                bank = psr.next()
                pb = pbf(bank)
                for jj in range(8):
                    j = i * 8 + jj
                    P.op("pe", lambda e, pb=pb, jj=jj, j=j: e.transpose(
                        pb[:, jj * 128:(jj + 1) * 128], ytm[:, j * 128:(j + 1) * 128], ident[:]),
                        reads=[ytm, ident], writes=[bank], signal=(jj == 7))
                P.op("act", lambda e, pb=pb, i=i, csl=csl: e.activation(
                    out=yT4[:, i * 8:(i + 1) * 8, csl], in_=pb[:, 0:1024].rearrange("p (j t) -> p j t", j=8), func=AF.Copy),
                    reads=[bank], writes=[yT4])
            if cc == 3:
                self.yc_store(0, 16, (c - 3) * 128, 512, lambda lo, hi: yT4[:, :, lo:hi], yT4)
            P.op("dve", lambda e: e.tensor_tensor(
                out=h[:].rearrange("p (h d) -> p h d", d=64), in0=h[:].rearrange("p (h d) -> p h d", d=64),
                in1=cd[:].unsqueeze(2).to_broadcast([128, 32, 64]), op=ALU.mult), reads=[h, cd], writes=[h])
            for q in range(4):
                g2 = q // 2
                st = psr.next()
                P.op("pe", lambda e, st=st, q=q, g2=g2: e.matmul(
                    st[:], lhsT=btm[:, g2 * 128:(g2 + 1) * 128], rhs=xdd[:, q * 512:(q + 1) * 512], start=True, stop=True),
                    reads=[btm, xdd], writes=[st])
                P.op("dve", lambda e, st=st, q=q: e.tensor_tensor(
                    out=h[:, q * 512:(q + 1) * 512], in0=st[:], in1=h[:, q * 512:(q + 1) * 512], op=ALU.add),
                    reads=[st, h], writes=[h])
            P.op("act", lambda e: e.activation(out=hb[:], in_=h[:], func=AF.Copy), reads=[h], writes=[hb])
        P.barrier()
        P.free_from(mark)

    def phase_G(self):
        P = self.P
        markG = len(P._ctx)
        P.barrier()
        sem = P._mksem("cc")
        YC, YG = self.YC, self.YG

        def thunk(e):
            n = 0
            for jt in range(24):
                for tq in range(4):
                    e.collective_compute("AllGather", ALU.bypass, replica_groups=[[0, 1, 2, 3], [4, 5, 6, 7]],
                                         ins=[YC.ap[jt, tq]], outs=[YG.ap[jt, tq].rearrange("r p t -> (r p) t")]).then_inc(sem)
                    n += 1
            e.wait_ge(sem, n)
        P.streams["pool"].append(thunk)
        P.semval["cc"] = 96
        P.barrier()
        for nm, srcb in (("YCd", YC), ("YGd", YG)):
            if nm in self.debug:
                flat = srcb.ap[:].flatten_outer_dims()
                nrow = flat.shape[0]
                d = P.dram(nm, [nrow, self.LS], BF16, kind="ExternalOutput")
                self.out_bufs.append(d)
                tmp = P.sb("dbg" + nm, [128, self.LS], BF16)
                for r in range(nrow // 128):
                    P.dma("sp", tmp[:], flat[r * 128:(r + 1) * 128, :], reads=[srcb], writes=[tmp])
                    P.dma("sp", d.ap[r * 128:(r + 1) * 128, :], tmp[:], reads=[tmp], writes=[d])
                P.barrier()
        P.free_from(markG)

    def phase_D(self):
        P, L, LS = self.P, self.L, self.LS
        mark = len(P._ctx)
        psr = self.psr
        TB = min(256, LS)
        NT = TB // 128
        uT = P.sb("uTD", [128, 32, TB], BF16)
        xs = [P.sb("xsD%d" % i, [128, 32, 128], F32) for i in range(1)]
        sq = P.sb("sqD", [128, 32, 128], BF16)
        rt = P.sb("rtD", [128, 128], F32)
        rs = P.sb("rsD", [128, 128], F32)
        win = P.sb("winD", [128, 32], F32)
        self.epsb = P.sb("epsbD", [128, 1], F32)
        P.op("pool", lambda e, t=self.epsb: e.memset(t[:], EPS), writes=[self.epsb])
        P.dma("sp", win[:], self.nw_in[:], reads=[self.nw_in], writes=[win])
        wts = Rot([P.sb("wtD%d" % i, [128, 32, 512], BF16) for i in range(2)])
        sg = [P.sb("sg%d" % i, [128, 32, TB], BF16) for i in range(2)]
        yT = P.sb("yTD", [128, 64, TB], BF16)
        tmpf = Rot([P.sb("tmpf%d" % i, [128, TB], F32) for i in range(2)])
        hrow = P.sb("hrow", [128, 4096], F32)
        xr = Rot([P.sb("xr%d" % i, [128, 512], F32) for i in range(2)])
        nwf = P.sb("nwf", [128, 4096], F32)
        P.dma("sp", nwf[:], self.nw_fin[:], reads=[self.nw_fin], writes=[nwf])
        ssq = P.sb("ssqD", [128, 1], F32)
        rtn = P.sb("rtnD", [128, 1], F32)
        rsn = P.sb("rsnD", [128, 1], F32)
        YG = self.YG
        tokv = {}
        YQ = self.YQ
        if YQ.dsem is None:
            YQ.dsem = ("d", P.nsem + 1)
            P._mksem(YQ.dsem)
        qsem = P.sems[YQ.dsem]
        for j0 in range(0, 24, 2):
            YQ.dcount += 16
            P.semval[YQ.dsem] = YQ.dcount
            P._rec((YQ.dsem, YQ.dcount), [YG], [YQ])
            P.ninstr += 1

            def qthunk(e, j0=j0):
                if "v" not in tokv:
                    tokv["v"] = e.snap(e.partition_id() % 4, min_val=0, max_val=3)
                src = YG.ap[j0:j0 + 2, bass.ds(tokv["v"], 1), :, :, :].rearrange("j o r p t -> j (o r p t)")
                dst = YQ.ap[j0:j0 + 2].rearrange("j r p t -> j (r p t)")
                e.dma_start(out=dst, in_=src).then_inc(qsem, 16)
            P.streams["sp"].append(qthunk)

        def yg_load(dst_ap, row0, nj, bk):
            def thunk_fn(e):
                gidx = e.partition_id() % 4
                src = YG.ap[row0:row0 + nj * 128, bass.ds(gidx * LS + bk * TB, TB)].rearrange("(j p) t -> p j t", p=128)
                return src
            return thunk_fn

        for bk in range(LS // TB):
            self.make_uT(lambda s, bk=bk: (self.xTs[:, :, bk * TB + s * 128:bk * TB + (s + 1) * 128], self.xTs),
                         NT, uT, xs, sq, rt, rs, win, psr)
            for m in range(2):
                for ct in range(8):
                    wt = wts.next()
                    P.dma("pool", wt[:], self.w_gates[m * 8 + ct], reads=[self.w_gates], writes=[wt])
                    for cs in range(4):
                        ps = psr.next()
                        for kc in range(32):
                            P.op("pe", lambda e, kc=kc, ps=ps, cs=cs, wt=wt: e.matmul(
                                ps[:, 0:TB], lhsT=wt[:, kc, cs * 128:(cs + 1) * 128], rhs=uT[:, kc, :],
                                start=(kc == 0), stop=(kc == 31)), reads=[uT, wt], writes=[ps], signal=(kc == 31))
                        P.op("act", lambda e, ps=ps, m=m, ct=ct, cs=cs: e.activation(
                            out=sg[m][:, ct * 4 + cs, :], in_=ps[:, 0:TB], func=AF.Sigmoid), reads=[ps], writes=[sg[m]])
            for m in range(2):
                nkh = 2 if m == 0 else 1
                for r in range(4):
                    row0 = 0 if m == 0 else 16
                    nj = 16 if m == 0 else 8
                    P.dma("sp", yT[:, r * nj:(r + 1) * nj, :],
                          self.YQ.ap[row0:row0 + nj, r, :, bk * TB:(bk + 1) * TB].rearrange("j p t -> p j t"),
                          reads=[self.YQ], writes=[yT])
                for ct in range(8):
                    w_list = []
                    for kh in range(nkh):
                        wt = wts.next()
                        srcw = self.w_bs[kh * 8 + ct] if m == 0 else self.w_ba[ct]
                        sbuf_ = self.w_bs if m == 0 else self.w_ba
                        w_list.append((wt, srcw, sbuf_))
                    pss = [psr.next() for _ in range(4)]
                    for kh, (wt, srcw, sbuf_) in enumerate(w_list):
                        P.dma("pool", wt[:], srcw, reads=[sbuf_], writes=[wt])
                        for cs in range(4):
                            ps = pss[cs]
                            for kc in range(32):
                                first = (kh == 0 and kc == 0)
                                last = (kh == nkh - 1 and kc == 31)
                                P.op("pe", lambda e, kc=kc, ps=ps, cs=cs, wt=wt, kh=kh, first=first, last=last: e.matmul(
                                    ps[:, 0:TB], lhsT=wt[:, kc, cs * 128:(cs + 1) * 128], rhs=yT[:, kh * 32 + kc, :],
                                    start=first, stop=last), reads=[yT, wt], writes=[ps], signal=(kc == 31))
                    for cs in range(4):
                        ps = pss[cs]
                        ci = ct * 4 + cs
                        if m == 0:
                            P.op("dve", lambda e, ps=ps, ci=ci: e.tensor_tensor(out=sg[0][:, ci, :], in0=ps[:, 0:TB],
                                                                               in1=sg[0][:, ci, :], op=ALU.mult),
                                 reads=[ps, sg[0]], writes=[sg[0]])
                        else:
                            tf = tmpf.next()
                            P.op("dve", lambda e, ps=ps, ci=ci, tf=tf: e.tensor_tensor(out=tf[:], in0=ps[:, 0:TB],
                                                                                     in1=sg[1][:, ci, :], op=ALU.mult),
                                 reads=[ps, sg[1]], writes=[tf])
                            P.op("pool", lambda e, ci=ci, tf=tf: e.tensor_tensor(out=sg[1][:, ci, :], in0=tf[:],
                                                                                in1=sg[0][:, ci, :], op=ALU.add),
                                 reads=[tf, sg[0], sg[1]], writes=[sg[1]])
            mT = sg[1]
            for tt in range(NT):
                r0 = bk * TB + tt * 128
                for ct in range(8):
                    xrow = xr.next()
                    P.dma("sp", xrow[:], self.xres[r0:r0 + 128, ct * 512:(ct + 1) * 512], reads=[self.xres], writes=[xrow])
                    wt = wts.next()
                    P.dma("pool", wt[:], self.w_out[ct], reads=[self.w_out], writes=[wt])
                    ps = psr.next()
                    for kc in range(32):
                        P.op("pe", lambda e, kc=kc, ps=ps, tt=tt, wt=wt: e.matmul(
                            ps[:], lhsT=mT[:, kc, tt * 128:(tt + 1) * 128], rhs=wt[:, kc, :],
                            start=(kc == 0), stop=(kc == 31)), reads=[mT, wt], writes=[ps], signal=(kc == 31))
                    P.op("dve", lambda e, ps=ps, ct=ct, xrow=xrow: e.tensor_tensor(
                        out=hrow[:, ct * 512:(ct + 1) * 512], in0=ps[:], in1=xrow[:], op=ALU.add),
                        reads=[ps, xrow], writes=[hrow])
                P.op("act", lambda e: e.activation(out=xs[0][:].rearrange("p a b -> p (a b)"), in_=hrow[:], func=AF.Square,
                                                   accum_out=ssq[:]), reads=[hrow], writes=[xs[0], ssq])
                P.op("act", lambda e, epsb=self.epsb: e.activation(out=rtn[:], in_=ssq[:], func=AF.Sqrt, scale=1.0 / 4096,
                                                                   bias=epsb[:, 0:1]),
                     reads=[ssq, self.epsb], writes=[rtn])
                P.op("dve", lambda e: e.reciprocal(out=rsn[:], in_=rtn[:]), reads=[rtn], writes=[rsn])
                P.op("dve", lambda e: e.scalar_tensor_tensor(out=hrow[:], in0=hrow[:], scalar=rsn[:, 0:1], in1=nwf[:],
                                                             op0=ALU.mult, op1=ALU.mult), reads=[hrow, rsn, nwf], writes=[hrow])
                P.dma("sp", self.OUT[r0:r0 + 128, :], hrow[:], reads=[hrow], writes=[self.OUT])
        P.barrier()
        P.free_from(mark)

    def finish(self):
        P = self.P
        P.barrier()
        P.emit()
        P.close()
        return self.nc


def build_program(L, phases="A", debug=()):
    k = K(L, debug, phases)
    k.declare()
    k.load_consts()
    if "A" in phases:
        k.phase_A()
    if "M" in phases:
        k.phase_A2()
    if "B" in phases:
        k.phase_B()
    if "C" in phases:
        k.phase_C()
    if "G" in phases:
        k.phase_G()
    if "D" in phases:
        k.phase_D()
    return k


def _tile_w(w, cols=None):
    if cols is not None:
        wz = np.zeros((w.shape[0], len(cols)), np.float32)
        m = cols >= 0
        wz[:, m] = w[:, cols[m]]
        w = wz
    Kd, N = w.shape
    return np.ascontiguousarray(w.reshape(Kd // 128, 128, N // 512, 512).transpose(2, 1, 0, 3))


def _in_cols(g):
    c = []
    c += list(range(g * 2048, (g + 1) * 2048))
    c += list(range(8192 + g * 2048, 8192 + (g + 1) * 2048))
    c += list(range(16384 + g * 256, 16384 + (g + 1) * 256))
    c += list(range(17408 + g * 256, 17408 + (g + 1) * 256))
    c += list(range(18560, 18560 + 1024))
    c += list(range(19584, 19584 + 512))
    c += list(range(20160 + g * 1024, 20160 + (g + 1) * 1024))
    rope = list(range(20096, 20160))
    c += rope + rope[32:] + rope[:32]
    c += list(range(18432 + g * 32, 18432 + (g + 1) * 32))
    c += [-1] * (15 * 512 - len(c))
    return np.array(c, np.int64)


def prepare_inputs(inp, L):
    f = lambda a: np.asarray(a, np.float32)
    x = f(inp["x"])[:, :L]
    NB, LS = L // 1024, L // 4
    w_in = f(inp["w_in"])[0]
    conv_w, conv_b = f(inp["conv_w"])[0], f(inp["conv_b"])[0]
    w_uq, w_ukv = f(inp["w_uq"])[0], f(inp["w_ukv"])[0]
    shared = {}
    shared["w_gates"] = _tile_w(w_in[:, 24256:24256 + 8192])
    wbs = f(inp["w_branch_ssm"])[0]
    shared["w_bs"] = np.concatenate([_tile_w(wbs[:4096]), _tile_w(wbs[4096:])], 0)
    shared["w_ba"] = _tile_w(f(inp["w_branch_attn"])[0])
    shared["w_out"] = _tile_w(f(inp["w_out"])[0])
    shared["nw_in"] = np.ascontiguousarray(f(inp["norm_in_w"])[0].reshape(32, 128).T)
    shared["nw_q"] = np.ascontiguousarray(f(inp["q_norm_w"])[0].reshape(8, 128).T)
    shared["nw_kv"] = np.ascontiguousarray(f(inp["kv_norm_w"])[0].reshape(4, 128).T)
    shared["nw_fin"] = np.ascontiguousarray(np.broadcast_to(f(inp["norm_final_w"])[None, :], (128, 4096)))
    half = 32
    inv_freq = (np.float32(10000.0) ** (-(np.arange(0, half, dtype=np.float32) / np.float32(half)))).astype(np.float32)
    ang = (np.arange(L, dtype=np.float32)[None, :] * inv_freq[:, None]).astype(np.float32)
    cos, sin = np.cos(ang).astype(np.float32), np.sin(ang).astype(np.float32)
    shared["ropec"] = np.concatenate([cos, cos], 0)
    shared["ropes"] = np.concatenate([-sin, sin], 0)
    shared["c_ident"] = np.eye(128, dtype=np.float32).astype(ml_dtypes.bfloat16)
    shared["c_ones"] = np.ones((128, 128), ml_dtypes.bfloat16)
    shared["c_onesf"] = np.ones((128, 128), np.float32)
    k = np.arange(128)
    shared["c_trile"] = (k[:, None] <= k[None, :]).astype(np.float32)
    shared["c_trigt"] = (k[:, None] > k[None, :]).astype(np.float32)
    shared["c_maskb"] = (k[:, None] <= k[None, :]).astype(np.float32).astype(ml_dtypes.bfloat16)
    per_g = []
    for g in range(4):
        d = {}
        d["w_in"] = _tile_w(w_in, _in_cols(g))
        cols = []
        for hl in range(8):
            H = 8 * g + hl
            base = H * 192
            rope = list(range(base + 128, base + 192))
            cols += list(range(base, base + 128)) + rope + rope[32:] + rope[:32]
        d["w_uq"] = np.ascontiguousarray(w_uq[:, cols].reshape(8, 128, 2048).transpose(1, 0, 2))
        cols = []
        for hl in range(8):
            H = 8 * g + hl
            cols += list(range(H * 256, H * 256 + 128))
        for hl in range(8):
            H = 8 * g + hl
            cols += list(range(H * 256 + 128, H * 256 + 256))
        d["w_ukv"] = np.ascontiguousarray(w_ukv[:, cols].reshape(4, 128, 2048).transpose(1, 0, 2))
        ch = np.concatenate([np.arange(g * 2048, (g + 1) * 2048),
                             8192 + g * 256 + np.arange(256), 8192 + 1024 + g * 256 + np.arange(256)])
        cpar = np.concatenate([conv_w[:, ch], conv_b[None, ch]], 0)
        d["convp"] = np.ascontiguousarray(cpar.reshape(5, 20, 128).transpose(2, 1, 0))
        hs = slice(g * 32, (g + 1) * 32)
        bc = lambda v: np.ascontiguousarray(np.broadcast_to(v[None, :], (128, v.shape[0])))
        d["dtb"] = bc(f(inp["dt_bias"])[0, hs])
        d["alog"] = bc(f(inp["a_log"])[0, hs])
        d["dskip"] = bc(f(inp["d_skip"])[0, hs])
        d["nw_ssm"] = bc(f(inp["ssm_norm_w"])[0, g * 2048:(g + 1) * 2048])
        per_g.append(d)
    maps = []
    for c in range(8):
        b, g = c // 4, c % 4
        m = dict(shared)
        m.update(per_g[g])
        xb = x[b]
        m["xT"] = np.ascontiguousarray(xb.reshape(NB, 1024, 32, 128).transpose(0, 3, 2, 1))
        xsl = xb[g * LS:(g + 1) * LS]
        m["xTs"] = np.ascontiguousarray(xsl.reshape(LS, 32, 128).transpose(2, 1, 0))
        m["xres"] = np.ascontiguousarray(xsl)
        maps.append(m)
    return maps


_CACHE = {}


def run(inputs, L=SEQ, phases="AMBCGD", debug=()):
    import time
    t0 = time.time()
    key = (L, phases, tuple(debug))
    if key not in _CACHE:
        k = build_program(L, phases, debug)
        nc = k.finish()
        _CACHE[key] = (nc, set(k.inp.keys()), k.P.ninstr)
    nc, names, ninstr = _CACHE[key]
    t1 = time.time()
    maps = prepare_inputs(inputs, L)
    maps = [{n: m[n] for n in names} for m in maps]
    t2 = time.time()
    res = run_bass_kernel_spmd(nc, maps, core_ids=list(range(8)))
    if os.environ.get("KDBG"):
        print("ninstr %d build %.1fs prep %.1fs run %.1fs" % (ninstr, t1 - t0, t2 - t1, time.time() - t2), flush=True)
    return res.results


def kernel(**inputs):
    res = run(inputs)
    LS = SEQ // 4
    out = np.empty((2, SEQ, D_MODEL), np.float32)
    for c in range(8):
        b, g = c // 4, c % 4
        out[b, g * LS:(g + 1) * LS] = res[c]["out"]
    return out
```

```python
import os
import math
import numpy as np
import ml_dtypes
import concourse.bass as bass
import concourse.mybir as mybir
from concourse.bass_utils import run_bass_kernel_spmd

F32 = mybir.dt.float32
BF16 = mybir.dt.bfloat16
AF = mybir.ActivationFunctionType
ALU = mybir.AluOpType
EPS = 1e-6

D_MODEL = 4096
D_SSM = 8192
D_CONV = 10240
SEQ = 8192


class Buf:
    def __init__(self, name, ap, accum=False):
        self.name = name
        self.ap = ap
        self.ws = {}
        self.rs = {}
        self.accum = accum
        self.dsem = None
        self.dcount = 0

    def __getitem__(self, idx):
        return self.ap[idx]


class Prog:
    ENGS = ("pe", "act", "dve", "pool", "sp")

    def __init__(self, nc):
        self.nc = nc
        self.streams = {e: [] for e in self.ENGS}
        self.cnt = {e: 0 for e in self.ENGS}
        self.sems = {}
        self.semval = {}
        self._ctx = []
        self._semctx = []
        self.pending = {e: [] for e in self.ENGS}
        self.waited = {e: {} for e in self.ENGS}
        self.bg_keys = set()
        self.nsem = 0
        self.ninstr = 0
        for e in ("pe", "act", "dve", "pool"):
            self._mksem(e)

    def _mksem(self, key):
        self.nsem += 1
        g = self.nc.semaphore("sem%d" % self.nsem)
        h = g.__enter__()
        self._semctx.append(g)
        self.sems[key] = h
        self.semval[key] = 0
        return h

    def sb(self, name, shape, dt):
        g = self.nc.sbuf_tensor("sb_" + name, list(shape), dt)
        t = g.__enter__()
        self._ctx.append(g)
        return Buf(name, t)

    def ps(self, name, shape, dt=F32):
        g = self.nc.psum_tensor(name, list(shape), dt)
        t = g.__enter__()
        self._ctx.append(g)
        return Buf(name, t)

    def dram(self, name, shape, dt, kind="Internal"):
        t = self.nc.dram_tensor(name, list(shape), dt, kind=kind)
        return Buf(name, t, accum=True)

    def _deps(self, eng, reads, writes):
        need = {}

        def add(k, v):
            if k == "pe" and eng == "pe":
                return
            if need.get(k, 0) < v:
                need[k] = v
        for b in reads:
            for k, v in b.ws.items():
                add(k, v)
        for b in writes:
            if b.accum:
                continue
            for k, v in b.ws.items():
                add(k, v)
            for k, v in b.rs.items():
                add(k, v)
        w = self.waited[eng]
        need = {k: v for k, v in need.items() if w.get(k, 0) < v}
        for k, v in need.items():
            w[k] = v
        return need

    @staticmethod
    def _rec(tok, reads, writes):
        k, v = tok
        for b in reads:
            b.rs[k] = v
        for b in writes:
            if b.accum:
                b.ws[k] = v
            else:
                b.ws = {k: v}
                b.rs = {}

    def op(self, eng, fn, reads=(), writes=(), signal=True):
        need = self._deps(eng, reads, writes)
        waits = [(self.sems[k], v) for k, v in need.items()]
        self.ninstr += 1
        if signal:
            self.cnt[eng] += 1
            self.semval[eng] = self.cnt[eng]
            tok = (eng, self.cnt[eng])
            pr = [b for (b, kind) in self.pending[eng] if kind == "r"]
            pw = [b for (b, kind) in self.pending[eng] if kind == "w"]
            self.pending[eng] = []
            self._rec(tok, list(reads) + pr, list(writes) + pw)
        else:
            for b in reads:
                self.pending[eng].append((b, "r"))
            for b in writes:
                self.pending[eng].append((b, "w"))
        sem = self.sems[eng]

        def thunk(e, waits=waits, fn=fn, signal=signal, sem=sem):
            for (s, v) in waits:
                e.wait_ge(s, v)
            ins = fn(e)
            if signal:
                ins.then_inc(sem, 1)
        self.streams[eng].append(thunk)

    def dma(self, q, out_ap, in_ap, reads=(), writes=(), sembuf=None):
        need = self._deps(q, reads, writes)
        waits = [(self.sems[k], v) for k, v in need.items()]
        self.ninstr += 1
        sbf = sembuf
        if sbf is None:
            cands = [b for b in list(writes) + list(reads) if not b.accum]
            sbf = cands[0] if cands else (list(writes) + list(reads))[0]
        if sbf.dsem is None:
            sbf.dsem = ("d", self.nsem + 1)
            self._mksem(sbf.dsem)
        sbf.dcount += 16
        self.semval[sbf.dsem] = sbf.dcount
        tok = (sbf.dsem, sbf.dcount)
        self._rec(tok, reads, writes)
        sem = self.sems[sbf.dsem]

        def thunk(e, waits=waits, sem=sem, out_ap=out_ap, in_ap=in_ap):
            for (s, v) in waits:
                e.wait_ge(s, v)
            e.dma_start(out=out_ap, in_=in_ap).then_inc(sem, 16)
        self.streams[q].append(thunk)

    def barrier(self, include_bg=False):
        for e in self.ENGS:
            assert not self.pending[e]
        for eng in self.ENGS:
            w = self.waited[eng]
            items = [(k, v) for k, v in self.semval.items() if v > 0 and (include_bg or k not in self.bg_keys)]
            waits = [(self.sems[k], v) for k, v in items if w.get(k, 0) < v]
            for k, v in items:
                w[k] = max(w.get(k, 0), v)

            def thunk(e, waits=waits):
                for (s, v) in waits:
                    e.wait_ge(s, v)
            self.streams[eng].append(thunk)

    def emit(self):
        nc = self.nc
        for e in self.ENGS:
            assert not self.pending[e], "unsignalled pending ops on %s" % e
        with nc.Block() as block:
            @block.tensor
            def _(e):
                for t in self.streams["pe"]:
                    t(e)

            @block.scalar
            def _(e):
                for t in self.streams["act"]:
                    t(e)

            @block.vector
            def _(e):
                for t in self.streams["dve"]:
                    t(e)

            @block.gpsimd
            def _(e):
                for t in self.streams["pool"]:
                    t(e)

            @block.sync
            def _(e):
                for t in self.streams["sp"]:
                    t(e)

    def close(self):
        for g in reversed(self._ctx):
            g.__exit__(None, None, None)
        for g in reversed(self._semctx):
            g.__exit__(None, None, None)

    def free_from(self, mark):
        while len(self._ctx) > mark:
            g = self._ctx.pop()
            g.__exit__(None, None, None)


class Rot:
    def __init__(self, items):
        self.items = items
        self.i = 0

    def next(self):
        b = self.items[self.i % len(self.items)]
        self.i += 1
        return b


class K:
    def __init__(self, L, debug=(), phases="AMBCGD"):
        self.L = L
        self.phases = phases
        self.debug = set(debug)
        self.nc = nc = bass.Bass("TRN2", target_bir_lowering=False)
        nc.cache_partition_id()
        self.P = Prog(nc)
        self.inp = {}
        self.out_bufs = []
        self.NB = L // 1024
        self.LS = L // 4
        self.early_gather = True

    def ein(self, name, shape, dt=F32):
        need = {"w_gates": "D", "w_bs": "D", "w_ba": "D", "w_out": "D", "xres": "D", "xTs": "D", "nw_fin": "D",
                "tokoff": "D", "w_in": "A", "xT": "A"}
        if name in need and need[name] not in self.phases:
            return None
        t = self.nc.dram_tensor(name, list(shape), dt, kind="ExternalInput")
        b = Buf(name, t)
        self.inp[name] = b
        return b

    def scratch(self, name, shape, dt):
        kind = "ExternalOutput" if name in self.debug else "Internal"
        b = self.P.dram(name, shape, dt, kind=kind)
        if kind == "ExternalOutput":
            self.out_bufs.append(b)
        return b

    def declare(self):
        L, NB, LS = self.L, self.NB, self.LS
        e = self.ein
        self.xT = e("xT", [NB, 128, 32, 1024])
        self.xTs = e("xTs", [128, 32, LS])
        self.xres = e("xres", [LS, 4096])
        self.w_in = e("w_in", [15, 128, 32, 512])
        self.w_gates = e("w_gates", [16, 128, 32, 512])
        self.w_bs = e("w_bs", [16, 128, 32, 512])
        self.w_ba = e("w_ba", [8, 128, 32, 512])
        self.w_out = e("w_out", [8, 128, 32, 512])
        self.w_uq = e("w_uq", [128, 8, 2048])
        self.w_ukv = e("w_ukv", [128, 4, 2048])
        self.convp = e("convp", [128, 20, 5])
        self.nw_in = e("nw_in", [128, 32])
        self.nw_q = e("nw_q", [128, 8])
        self.nw_kv = e("nw_kv", [128, 4])
        self.dtb = e("dtb", [128, 32])
        self.alog = e("alog", [128, 32])
        self.dskip = e("dskip", [128, 32])
        self.nw_ssm = e("nw_ssm", [128, 2048])
        self.nw_fin = e("nw_fin", [128, 4096])
        self.ropec = e("ropec", [64, L])
        self.ropes = e("ropes", [64, L])
        self.c_ident = e("c_ident", [128, 128], BF16)
        self.c_ones = e("c_ones", [128, 128], BF16)
        self.c_onesf = e("c_onesf", [128, 128])
        self.c_trile = e("c_trile", [128, 128])
        self.c_trigt = e("c_trigt", [128, 128])
        self.c_maskb = e("c_maskb", [128, 128], BF16)
        s = self.scratch
        self.SZ = s("SZ", [L, 2048], BF16)
        self.XC = s("XC", [2560, L], BF16)
        self.DT = s("DT", [L, 32], F32)
        self.CQ = s("CQ", [1024, L], BF16)
        self.CKV = s("CKV", [512, L], BF16)
        self.RR = s("RR", [128, L], F32)
        self.GA = s("GA", [1024, L], BF16)
        self.QN = s("QN", [8, 128, L], BF16)
        self.QR = s("QR", [8, 64, L], BF16)
        self.KN = s("KN", [8, 128, L], BF16)
        self.KR = s("KR", [64, L], BF16)
        self.V = s("V", [L, 1024], BF16)
        self.YC = s("YC", [24, 4, 128, LS], BF16)
        self.YG = s("YG", [24, 4, 4, 128, LS], BF16)
        self.YQ = s("YQ", [24, 4, 128, LS], BF16)
        self.WD = s("WD", [48, 128, 32 * 512], BF16)
        self.OUT = self.P.dram("out", [LS, 4096], F32, kind="ExternalOutput")
        self.out_bufs.append(self.OUT)

    def yc_store(self, jt0, nj, t0, ntok, src_fn, srcbuf):
        LS = self.LS
        t = t0
        while t < t0 + ntok:
            tq, tl = t // LS, t % LS
            n = min(LS - tl, t0 + ntok - t)
            dst = self.YC.ap[jt0:jt0 + nj, tq, :, tl:tl + n].rearrange("j p t -> p j t")
            self.P.dma("sp", dst, src_fn(t - t0, t - t0 + n), reads=[srcbuf], writes=[self.YC])
            t += n

    def load_consts(self):
        P = self.P
        self.ident = P.sb("ident", [128, 128], BF16)
        self.ones = P.sb("ones", [128, 128], BF16)
        for sbuf, src in ((self.ident, self.c_ident), (self.ones, self.c_ones)):
            P.dma("sp", sbuf[:], src[:], reads=[src], writes=[sbuf])
        self.psb = [P.ps("psb%d" % i, [128, 512]) for i in range(8)]
        self.psr = Rot(self.psb)

    def make_uT(self, src_fn, nsub, uT, xs, sq, rt, rs, win, ps_rot):
        P = self.P
        ones = self.ones
        epsb = self.epsb
        for s in range(nsub):
            x = xs[s % len(xs)]
            P.dma("sp", x[:], src_fn(s)[0], reads=[src_fn(s)[1]], writes=[x])
            P.op("act", lambda e, x=x: e.activation(out=sq[:], in_=x[:], func=AF.Square),
                 reads=[x], writes=[sq])
            ps = ps_rot.next()
            for kc in range(32):
                P.op("pe", lambda e, kc=kc, ps=ps: e.matmul(ps[:, 0:128], lhsT=ones[:], rhs=sq[:, kc, :],
                                                          start=(kc == 0), stop=(kc == 31)),
                     reads=[ones, sq], writes=[ps], signal=(kc == 31))
            P.op("act", lambda e, ps=ps: e.activation(out=rt[:], in_=ps[:, 0:128], func=AF.Sqrt,
                                                      scale=1.0 / D_MODEL, bias=epsb[:, 0:1]),
                 reads=[ps, epsb], writes=[rt])
            P.op("dve", lambda e: e.reciprocal(out=rs[:], in_=rt[:]), reads=[rt], writes=[rs])
            P.op("dve", lambda e, x=x: e.tensor_tensor(out=x[:], in0=x[:],
                                                       in1=win[:].unsqueeze(2).to_broadcast([128, 32, 128]),
                                                       op=ALU.mult),
                 reads=[x, win], writes=[x])
            P.op("dve", lambda e, x=x, s=s: e.tensor_tensor(out=uT[:, :, s * 128:(s + 1) * 128], in0=x[:],
                                                            in1=rs[:].unsqueeze(1).to_broadcast([128, 32, 128]),
                                                            op=ALU.mult),
                 reads=[x, rs], writes=[uT])

    def phase_A(self):
        P, L, NB = self.P, self.L, self.NB
        mark = len(P._ctx)
        uT = P.sb("uT", [128, 32, 1024], BF16)
        xs = [P.sb("xs%d" % i, [128, 32, 128], F32) for i in range(2)]
        sq = P.sb("sq", [128, 32, 128], BF16)
        rt = P.sb("rt", [128, 128], F32)
        rs = P.sb("rs", [128, 128], F32)
        win = P.sb("win", [128, 32], F32)
        self.epsb = P.sb("epsb", [128, 1], F32)
        P.op("pool", lambda e, t=self.epsb: e.memset(t[:], EPS), writes=[self.epsb])
        P.dma("sp", win[:], self.nw_in[:], reads=[self.nw_in], writes=[win])
        wts = Rot([P.sb("wt%d" % i, [128, 32, 512], BF16) for i in range(2)])
        stg = Rot([P.sb("stg%d" % i, [128, 512], BF16) for i in range(4)])
        stf = Rot([P.sb("stf%d" % i, [128, 512], F32) for i in range(2)])
        raws = Rot([P.sb("raw%d" % i, [128, 515], F32) for i in range(2)])
        accs = Rot([P.sb("acc%d" % i, [128, 512], F32) for i in range(2)])
        halo = P.sb("halo", [128, 20, 3], F32)
        cp = P.sb("cp", [128, 20, 5], F32)
        dtb = P.sb("dtb", [128, 32], F32)
        dtx = P.sb("dtx", [128, 8, 32], F32)
        dte = P.sb("dte", [128, 8, 32], F32)
        dtt = P.sb("dtt", [128, 8, 32], F32)
        P.dma("sp", cp[:], self.convp[:], reads=[self.convp], writes=[cp])
        P.dma("sp", dtb[:], self.dtb[:], reads=[self.dtb], writes=[dtb])
        psr = self.psr

        for tb in range(NB):
            t0 = tb * 1024
            self.make_uT(lambda s, tb=tb: (self.xT[tb, :, :, s * 128:(s + 1) * 128], self.xT),
                         8, uT, xs, sq, rt, rs, win, psr)
            for t in range(15):
                wt = wts.next()
                P.dma("pool", wt[:], self.w_in[t], reads=[self.w_in], writes=[wt])
                if t < 4:
                    for tt in range(8):
                        ps = psr.next()
                        for kc in range(32):
                            P.op("pe", lambda e, kc=kc, ps=ps, tt=tt, wt=wt: e.matmul(
                                ps[:], lhsT=uT[:, kc, tt * 128:(tt + 1) * 128], rhs=wt[:, kc, :],
                                start=(kc == 0), stop=(kc == 31)),
                                reads=[uT, wt], writes=[ps], signal=(kc == 31))
                        st = stg.next()
                        P.op("act", lambda e, ps=ps, st=st: e.activation(out=st[:], in_=ps[:], func=AF.Silu),
                             reads=[ps], writes=[st])
                        P.dma("sp", self.SZ[t0 + tt * 128:t0 + (tt + 1) * 128, t * 512:(t + 1) * 512], st[:],
                              reads=[st], writes=[self.SZ])
                    continue
                ncs = 1 if t == 14 else 4
                for cs in range(ncs):
                    for th in range(2):
                        ps = psr.next()
                        for kc in range(32):
                            P.op("pe", lambda e, kc=kc, ps=ps, cs=cs, th=th, wt=wt: e.matmul(
                                ps[:], lhsT=wt[:, kc, cs * 128:(cs + 1) * 128], rhs=uT[:, kc, th * 512:(th + 1) * 512],
                                start=(kc == 0), stop=(kc == 31)),
                                reads=[uT, wt], writes=[ps], signal=(kc == 31))
                        tok = slice(t0 + th * 512, t0 + (th + 1) * 512)
                        if 4 <= t <= 8:
                            idx = (t - 4) * 4 + cs
                            raw = raws.next()
                            acc = accs.next()
                            P.op("dve", lambda e, raw=raw, ps=ps: e.tensor_copy(out=raw[:, 3:515], in_=ps[:]),
                                 reads=[ps], writes=[raw])
                            if tb == 0 and th == 0:
                                P.op("pool", lambda e, raw=raw: e.memset(raw[:, 0:3], 0.0), writes=[raw])
                            else:
                                P.op("pool", lambda e, raw=raw, idx=idx: e.tensor_copy(out=raw[:, 0:3], in_=halo[:, idx, :]),
                                     reads=[halo], writes=[raw])
                            P.op("pool", lambda e, raw=raw, idx=idx: e.tensor_copy(out=halo[:, idx, :], in_=raw[:, 512:515]),
                                 reads=[raw], writes=[halo])
                            P.op("dve", lambda e, raw=raw, acc=acc, idx=idx: e.tensor_scalar(
                                out=acc[:], in0=raw[:, 3:515], scalar1=cp[:, idx, 3:4], scalar2=cp[:, idx, 4:5],
                                op0=ALU.mult, op1=ALU.add), reads=[raw, cp], writes=[acc])
                            for k in range(3):
                                P.op("dve", lambda e, raw=raw, acc=acc, idx=idx, k=k: e.scalar_tensor_tensor(
                                    out=acc[:], in0=raw[:, k:k + 512], scalar=cp[:, idx, k:k + 1], in1=acc[:],
                                    op0=ALU.mult, op1=ALU.add), reads=[raw, cp, acc], writes=[acc])
                            st = stg.next()
                            P.op("act", lambda e, acc=acc, st=st: e.activation(out=st[:], in_=acc[:], func=AF.Silu),
                                 reads=[acc], writes=[st])
                            P.dma("sp", self.XC[idx * 128:(idx + 1) * 128, tok], st[:], reads=[st], writes=[self.XC])
                        elif t in (9, 10, 11, 12, 13):
                            st = stg.next()
                            fn = AF.Silu if t >= 12 else AF.Copy
                            P.op("act", lambda e, ps=ps, st=st, fn=fn: e.activation(out=st[:], in_=ps[:], func=fn),
                                 reads=[ps], writes=[st])
                            if t <= 10:
                                r0 = ((t - 9) * 4 + cs) * 128
                                dst, dbuf = self.CQ[r0:r0 + 128, tok], self.CQ
                            elif t == 11:
                                dst, dbuf = self.CKV[cs * 128:(cs + 1) * 128, tok], self.CKV
                            else:
                                r0 = ((t - 12) * 4 + cs) * 128
                                dst, dbuf = self.GA[r0:r0 + 128, tok], self.GA
                            P.dma("sp", dst, st[:], reads=[st], writes=[dbuf])
                        else:
                            st = stf.next()
                            P.op("act", lambda e, ps=ps, st=st: e.activation(out=st[:], in_=ps[:], func=AF.Copy),
                                 reads=[ps], writes=[st])
                            P.dma("sp", self.RR[:, tok], st[:], reads=[st], writes=[self.RR])
                if t == 14:
                    ps = psr.next()
                    for tt in range(8):
                        for kc in range(32):
                            P.op("pe", lambda e, kc=kc, ps=ps, tt=tt, wt=wt: e.matmul(
                                ps[:, tt * 32:(tt + 1) * 32], lhsT=uT[:, kc, tt * 128:(tt + 1) * 128],
                                rhs=wt[:, kc, 128:160], start=(kc == 0), stop=(kc == 31)),
                                reads=[uT, wt], writes=[ps], signal=(kc == 31 and tt == 7))
                    P.op("dve", lambda e, ps=ps: e.tensor_tensor(
                        out=dtx[:], in0=ps[:, 0:256].rearrange("p (a b) -> p a b", a=8),
                        in1=dtb[:].unsqueeze(1).to_broadcast([128, 8, 32]), op=ALU.add),
                        reads=[ps, dtb], writes=[dtx])
                    P.op("act", lambda e: e.activation(out=dte[:], in_=dtx[:], func=AF.Exp), reads=[dtx], writes=[dte])
                    P.op("act", lambda e: e.activation(out=dtt[:], in_=dte[:], func=AF.Ln, bias=1.0, scale=1.0),
                         reads=[dte], writes=[dtt])
                    P.dma("sp", self.DT[t0:t0 + 1024, :].rearrange("(a p) h -> p a h", p=128), dtt[:],
                          reads=[dtt], writes=[self.DT])
        P.barrier()
        P.free_from(mark)


    def convert_weights(self):
        P = self.P
        for (src, n, base) in ((self.w_gates, 16, 0), (self.w_bs, 16, 16), (self.w_ba, 8, 32), (self.w_out, 8, 40)):
            for t in range(n):
                P.dma("pool", self.WD.ap[base + t], src.ap[t].rearrange("p k c -> p (k c)"), reads=[src], writes=[self.WD])
            P.bg_keys.add(src.dsem)

    def phase_A2(self):
        P, L = self.P, self.L
        mark = len(P._ctx)
        psr = self.psr
        wq = P.sb("wq", [128, 8, 2048], BF16)
        wkv = P.sb("wkv", [128, 4, 2048], BF16)
        P.dma("pool", wq[:], self.w_uq[:], reads=[self.w_uq], writes=[wq])
        P.dma("pool", wkv[:], self.w_ukv[:], reads=[self.w_ukv], writes=[wkv])
        nwq = P.sb("nwq", [128, 8], F32)
        nwkv = P.sb("nwkv", [128, 4], F32)
        P.dma("sp", nwq[:], self.nw_q[:], reads=[self.nw_q], writes=[nwq])
        P.dma("sp", nwkv[:], self.nw_kv[:], reads=[self.nw_kv], writes=[nwkv])
        epsb = P.sb("epsb2", [128, 1], F32)
        P.op("pool", lambda e: e.memset(epsb[:], EPS), writes=[epsb])
        cqs = Rot([P.sb("cq%d" % i, [128, 8, 512], BF16) for i in range(2)])
        ckvs = Rot([P.sb("ckv%d" % i, [128, 4, 512], BF16) for i in range(2)])
        rras = Rot([P.sb("rra%d" % i, [64, 512], F32) for i in range(2)])
        rrbs = Rot([P.sb("rrb%d" % i, [64, 512], F32) for i in range(2)])
        coss = Rot([P.sb("cos%d" % i, [64, 512], F32) for i in range(2)])
        sins = Rot([P.sb("sin%d" % i, [64, 512], F32) for i in range(2)])
        sqb = P.sb("sqb", [128, 8, 512], BF16)
        rt = P.sb("rt2", [128, 512], F32)
        rq = P.sb("rq", [128, 512], F32)
        rkv = P.sb("rkv", [128, 512], F32)
        cqw = P.sb("cqw", [128, 8, 512], BF16)
        ckvw = P.sb("ckvw", [128, 4, 512], BF16)
        stg = Rot([P.sb("stg2_%d" % i, [128, 512], BF16) for i in range(4)])
        tA = Rot([P.sb("tA%d" % i, [64, 512], F32) for i in range(2)])
        tB = Rot([P.sb("tB%d" % i, [64, 512], F32) for i in range(2)])
        ones = self.ones

        def rope(srcA, srcB, bufsA, bufsB, cos, sin, dst, dbuf):
            a, b = tA.next(), tB.next()
            P.op("dve", lambda e: e.tensor_tensor(out=a[:], in0=srcA, in1=cos[:], op=ALU.mult),
                 reads=bufsA + [cos], writes=[a])
            P.op("dve", lambda e: e.tensor_tensor(out=b[:], in0=srcB, in1=sin[:], op=ALU.mult),
                 reads=bufsB + [sin], writes=[b])
            st = stg.next()
            P.op("pool", lambda e: e.tensor_tensor(out=st[0:64, :], in0=a[:], in1=b[:], op=ALU.add),
                 reads=[a, b], writes=[st])
            P.dma("sp", dst, st[0:64, :], reads=[st], writes=[dbuf])

        for tk in range(L // 512):
            tok = slice(tk * 512, (tk + 1) * 512)
            cq, ckv, rra, rrb, cos, sin = cqs.next(), ckvs.next(), rras.next(), rrbs.next(), coss.next(), sins.next()
            P.dma("sp", cq[:], self.CQ[:, tok].rearrange("(j p) t -> p j t", p=128), reads=[self.CQ], writes=[cq])
            P.dma("sp", ckv[:], self.CKV[:, tok].rearrange("(j p) t -> p j t", p=128), reads=[self.CKV], writes=[ckv])
            P.dma("sp", rra[:], self.RR[0:64, tok], reads=[self.RR], writes=[rra])
            P.dma("sp", rrb[:], self.RR[64:128, tok], reads=[self.RR], writes=[rrb])
            P.dma("sp", cos[:], self.ropec[:, tok], reads=[self.ropec], writes=[cos])
            P.dma("sp", sin[:], self.ropes[:, tok], reads=[self.ropes], writes=[sin])
            for (src, nk, nw, rr_, dst, dim) in ((cq, 8, nwq, rq, cqw, 1024), (ckv, 4, nwkv, rkv, ckvw, 512)):
                P.op("act", lambda e, src=src, nk=nk: e.activation(out=sqb[:, 0:nk, :], in_=src[:], func=AF.Square),
                     reads=[src], writes=[sqb])
                ps = psr.next()
                for kc in range(nk):
                    P.op("pe", lambda e, kc=kc, ps=ps, nk=nk: e.matmul(ps[:], lhsT=ones[:], rhs=sqb[:, kc, :],
                                                                    start=(kc == 0), stop=(kc == nk - 1)),
                         reads=[ones, sqb], writes=[ps], signal=(kc == nk - 1))
                P.op("act", lambda e, ps=ps, dim=dim: e.activation(out=rt[:], in_=ps[:], func=AF.Sqrt,
                                                                   scale=1.0 / dim, bias=epsb[:, 0:1]),
                     reads=[ps, epsb], writes=[rt])
                P.op("dve", lambda e, rr_=rr_: e.reciprocal(out=rr_[:], in_=rt[:]), reads=[rt], writes=[rr_])
                for kc in range(nk):
                    P.op("dve", lambda e, kc=kc, src=src, nw=nw, rr_=rr_, dst=dst: e.scalar_tensor_tensor(
                        out=dst[:, kc, :], in0=src[:, kc, :], scalar=nw[:, kc:kc + 1], in1=rr_[:],
                        op0=ALU.mult, op1=ALU.mult), reads=[src, nw, rr_], writes=[dst])
            for hl in range(8):
                ps = psr.next()
                for kc in range(8):
                    P.op("pe", lambda e, kc=kc, ps=ps, hl=hl: e.matmul(
                        ps[:], lhsT=wq[:, kc, hl * 256:hl * 256 + 128], rhs=cqw[:, kc, :],
                        start=(kc == 0), stop=(kc == 7)), reads=[wq, cqw], writes=[ps], signal=(kc == 7))
                st = stg.next()
                P.op("act", lambda e, ps=ps, st=st: e.activation(out=st[:], in_=ps[:], func=AF.Copy), reads=[ps], writes=[st])
                P.dma("sp", self.QN[hl, :, tok], st[:], reads=[st], writes=[self.QN])
                psA, psB = psr.next(), psr.next()
                for (pp, off) in ((psA, 128), (psB, 192)):
                    for kc in range(8):
                        P.op("pe", lambda e, kc=kc, pp=pp, hl=hl, off=off: e.matmul(
                            pp[0:64, :], lhsT=wq[:, kc, hl * 256 + off:hl * 256 + off + 64], rhs=cqw[:, kc, :],
                            start=(kc == 0), stop=(kc == 7)), reads=[wq, cqw], writes=[pp], signal=(kc == 7))
                rope(psA[0:64, :], psB[0:64, :], [psA], [psB], cos, sin, self.QR[hl, :, tok], self.QR)
                ps = psr.next()
                for kc in range(4):
                    P.op("pe", lambda e, kc=kc, ps=ps, hl=hl: e.matmul(
                        ps[:], lhsT=wkv[:, kc, hl * 128:(hl + 1) * 128], rhs=ckvw[:, kc, :],
                        start=(kc == 0), stop=(kc == 3)), reads=[wkv, ckvw], writes=[ps], signal=(kc == 3))
                st = stg.next()
                P.op("act", lambda e, ps=ps, st=st: e.activation(out=st[:], in_=ps[:], func=AF.Copy), reads=[ps], writes=[st])
                P.dma("sp", self.KN[hl, :, tok], st[:], reads=[st], writes=[self.KN])
            for tt in range(4):
                for hf in range(2):
                    ps = psr.next()
                    for kc in range(4):
                        P.op("pe", lambda e, kc=kc, ps=ps, tt=tt, hf=hf: e.matmul(
                            ps[:], lhsT=ckvw[:, kc, tt * 128:(tt + 1) * 128],
                            rhs=wkv[:, kc, 1024 + hf * 512:1024 + (hf + 1) * 512],
                            start=(kc == 0), stop=(kc == 3)), reads=[wkv, ckvw], writes=[ps], signal=(kc == 3))
                    st = stg.next()
                    P.op("act", lambda e, ps=ps, st=st: e.activation(out=st[:], in_=ps[:], func=AF.Copy), reads=[ps], writes=[st])
                    r0 = tk * 512 + tt * 128
                    P.dma("sp", self.V[r0:r0 + 128, hf * 512:(hf + 1) * 512], st[:], reads=[st], writes=[self.V])
            rope(rra[:], rrb[:], [rra], [rrb], cos, sin, self.KR[:, tok], self.KR)
        P.barrier()
        P.free_from(mark)

    def phase_C(self):
        P, L = self.P, self.L
        mark = len(P._ctx)
        if self.early_gather and "G" in self.phases:
            self.gather(0, 16)
        if "D" in self.phases:
            self.convert_weights()
        psr = self.psr
        NKT = L // 128
        scale = 1.0 / math.sqrt(192.0)
        ones = self.ones
        maskb = P.sb("maskb", [128, 128], BF16)
        P.dma("sp", maskb[:], self.c_maskb[:], reads=[self.c_maskb], writes=[maskb])
        kr = P.sb("kr", [64, L], BF16)
        P.dma("sp", kr[:], self.KR[:, :], reads=[self.KR], writes=[kr])
        qns = Rot([P.sb("qn%d" % i, [128, L], BF16) for i in range(2)])
        qrs = Rot([P.sb("qr%d" % i, [64, L], BF16) for i in range(2)])
        kns = Rot([P.sb("kn%d" % i, [128, L], BF16) for i in range(2)])
        vs = Rot([P.sb("v%d" % i, [128, NKT, 128], BF16) for i in range(2)])
        pts = Rot([P.sb("pt%d" % i, [128, 512], BF16) for i in range(3)])
        gas = Rot([P.sb("ga%d" % i, [128, 512], BF16) for i in range(2)])
        rinv = P.sb("rinv", [128, 512], F32)
        o1 = P.sb("o1", [128, 512], F32)
        stg = Rot([P.sb("stg3_%d" % i, [128, 512], BF16) for i in range(2)])
        acc_r = Rot(self.psb[0:4])
        s_r = Rot(self.psb[4:8])
        for hl in range(8):
            qn, qr, kn, v = qns.next(), qrs.next(), kns.next(), vs.next()
            P.dma("sp", qn[:], self.QN[hl], reads=[self.QN], writes=[qn])
            P.dma("sp", qr[:], self.QR[hl], reads=[self.QR], writes=[qr])
            P.dma("sp", kn[:], self.KN[hl], reads=[self.KN], writes=[kn])
            P.dma("sp", v[:], self.V[:, hl * 128:(hl + 1) * 128].rearrange("(a p) d -> p a d", p=128),
                  reads=[self.V], writes=[v])
            for qi in range(L // 512):
                ga = gas.next()
                P.dma("sp", ga[:], self.GA[hl * 128:(hl + 1) * 128, qi * 512:(qi + 1) * 512], reads=[self.GA], writes=[ga])
                O, Rs = acc_r.next(), acc_r.next()
                nkt = 4 * qi + 4
                def emit_qk(kt, qi=qi, kn=kn, qn=qn, qr=qr):
                    d = kt - 4 * qi
                    q0 = max(d, 0) * 128
                    Sb = s_r.next()
                    qs = slice(qi * 512 + q0, (qi + 1) * 512)
                    ks = slice(kt * 128, (kt + 1) * 128)
                    P.op("pe", lambda e, Sb=Sb, q0=q0, qs=qs, ks=ks, kn=kn, qn=qn: e.matmul(
                        Sb[:, q0:512], lhsT=kn[:, ks], rhs=qn[:, qs], start=True, stop=False),
                        reads=[kn, qn], writes=[Sb], signal=False)
                    P.op("pe", lambda e, Sb=Sb, q0=q0, qs=qs, ks=ks, qr=qr: e.matmul(
                        Sb[:, q0:512], lhsT=kr[:, ks], rhs=qr[:, qs], start=False, stop=True),
                        reads=[kr, qr], writes=[Sb])
                    return Sb, d, q0
                nxt = emit_qk(0)
                for kt in range(nkt):
                    Sb, d, q0 = nxt
                    if kt + 1 < nkt:
                        nxt = emit_qk(kt + 1)
                    pt = pts.next()
                    P.op("act", lambda e, Sb=Sb, pt=pt, q0=q0: e.activation(out=pt[:, q0:512], in_=Sb[:, q0:512],
                                                                         func=AF.Exp, scale=scale),
                         reads=[Sb], writes=[pt])
                    if d >= 0:
                        P.op("dve", lambda e, pt=pt, q0=q0: e.tensor_tensor(out=pt[:, q0:q0 + 128], in0=pt[:, q0:q0 + 128],
                                                                           in1=maskb[:], op=ALU.mult),
                             reads=[pt, maskb], writes=[pt])
                    P.op("pe", lambda e, O=O, pt=pt, q0=q0, kt=kt, nkt=nkt, v=v: e.matmul(
                        O[:, q0:512], lhsT=v[:, kt, :], rhs=pt[:, q0:512], start=(kt == 0), stop=(kt == nkt - 1)),
                        reads=[v, pt], writes=[O], signal=False)
                    P.op("pe", lambda e, Rs=Rs, pt=pt, q0=q0, kt=kt, nkt=nkt: e.matmul(
                        Rs[:, q0:512], lhsT=ones[:], rhs=pt[:, q0:512], start=(kt == 0), stop=(kt == nkt - 1)),
                        reads=[ones, pt], writes=[Rs])
                P.op("dve", lambda e, Rs=Rs: e.reciprocal(out=rinv[:], in_=Rs[:]), reads=[Rs], writes=[rinv])
                P.op("dve", lambda e, O=O: e.tensor_tensor(out=o1[:], in0=O[:], in1=rinv[:], op=ALU.mult),
                     reads=[O, rinv], writes=[o1])
                st = stg.next()
                P.op("dve", lambda e, st=st, ga=ga: e.tensor_tensor(out=st[:], in0=o1[:], in1=ga[:], op=ALU.mult),
                     reads=[o1, ga], writes=[st])
                self.yc_store(16 + hl, 1, qi * 512, 512, lambda lo, hi, st=st: st[:, lo:hi].unsqueeze(1), st)
            if self.early_gather and "G" in self.phases:
                self.gather(16 + hl, 17 + hl, track=True)
        P.barrier()
        P.free_from(mark)


    def phase_B(self):
        P, L = self.P, self.L
        mark = len(P._ctx)
        psr = self.psr
        NCH = L // 128
        ident, ones = self.ident, self.ones

        def cload(name, src, shape, dt):
            b = P.sb(name, shape, dt)
            P.dma("sp", b[:], src[:], reads=[src], writes=[b])
            return b
        trile = cload("trile", self.c_trile, [128, 128], F32)
        trigt = cload("trigt", self.c_trigt, [128, 128], F32)
        onesf = cload("onesf", self.c_onesf, [128, 128], F32)
        maskb = cload("maskbB", self.c_maskb, [128, 128], BF16)
        Abc = cload("Abc", self.alog, [128, 32], F32)
        dsk = cload("dsk", self.dskip, [128, 32], F32)
        nws = cload("nws", self.nw_ssm, [128, 2048], F32)
        P.op("act", lambda e: e.activation(out=Abc[:], in_=Abc[:], func=AF.Exp), reads=[Abc], writes=[Abc])
        P.op("dve", lambda e: e.tensor_scalar(out=Abc[:], in0=Abc[:], scalar1=-1.0, scalar2=None, op0=ALU.mult),
             reads=[Abc], writes=[Abc])
        epsb = P.sb("epsbB", [128, 1], F32)
        P.op("pool", lambda e: e.memset(epsb[:], EPS), writes=[epsb])
        h = P.sb("h", [128, 2048], F32)
        hb = P.sb("hb", [128, 2048], BF16)
        P.op("pool", lambda e: e.memset(h[:], 0.0), writes=[h])
        P.op("pool", lambda e: e.memset(hb[:], 0.0), writes=[hb])
        xT4s = Rot([P.sb("xT4_%d" % i, [128, 20, 512], BF16) for i in range(2)])
        dt4s = Rot([P.sb("dt4_%d" % i, [128, 4, 32], F32) for i in range(2)])
        szs = Rot([P.sb("sz%d" % i, [128, 2048], BF16) for i in range(2)])
        xtm = P.sb("xtm", [128, 2048], BF16)
        btm = P.sb("btm", [128, 256], BF16)
        sm = {n: P.sb("sm_" + n, [128, 32], F32) for n in ("a", "acs", "ea", "cd", "dsd", "ds", "w2")}
        R = P.sb("R", [128, 32, 128], F32)
        E = P.sb("E", [128, 32, 128], BF16)
        cbm = P.sb("cbm", [128, 2, 128], BF16)
        S = P.sb("S", [128, 32, 128], BF16)
        xdt = P.sb("xdt", [128, 2048], BF16)
        xdd = P.sb("xdd", [128, 2048], BF16)
        xD = P.sb("xD", [128, 2048], BF16)
        y = P.sb("y", [128, 2048], F32)
        junk = P.sb("junk", [128, 1024], BF16)
        ssq = P.sb("ssq", [128, 2], F32)
        rtn = P.sb("rtn", [128, 2], F32)
        rsn = P.sb("rsn", [128, 2], F32)
        ytm = P.sb("ytm", [128, 2048], BF16)
        yT4 = P.sb("yT4", [128, 16, 512], BF16)

        def pbf(bank):
            return bank.ap.bitcast(BF16)

        xT4 = dt4 = None
        for c in range(NCH):
            cc = c % 4
            csl = slice(cc * 128, (cc + 1) * 128)
            if cc == 0:
                xT4, dt4 = xT4s.next(), dt4s.next()
                tok = slice(c * 128, c * 128 + 512)
                P.dma("sp", xT4[:], self.XC[:, tok].rearrange("(j p) t -> p j t", p=128), reads=[self.XC], writes=[xT4])
                P.dma("sp", dt4[:], self.DT[tok, :].rearrange("(a p) h -> p a h", p=128), reads=[self.DT], writes=[dt4])
            sz = szs.next()
            P.dma("sp", sz[:], self.SZ[c * 128:(c + 1) * 128, :], reads=[self.SZ], writes=[sz])
            dtc = dt4[:, cc, :]
            for i in range(2):
                bank = psr.next()
                pb = pbf(bank)
                for jj in range(8):
                    j = i * 8 + jj
                    P.op("pe", lambda e, pb=pb, jj=jj, j=j, xT4=xT4, csl=csl: e.transpose(
                        pb[:, jj * 128:(jj + 1) * 128], xT4[:, j, csl], ident[:]),
                        reads=[xT4, ident], writes=[bank], signal=(jj == 7))
                P.op("act", lambda e, pb=pb, i=i: e.activation(out=xtm[:, i * 1024:(i + 1) * 1024], in_=pb[:, 0:1024], func=AF.Copy),
                     reads=[bank], writes=[xtm])
            bank = psr.next()
            pb = pbf(bank)
            for g2 in range(2):
                P.op("pe", lambda e, pb=pb, g2=g2, xT4=xT4, csl=csl: e.transpose(
                    pb[:, g2 * 128:(g2 + 1) * 128], xT4[:, 16 + g2, csl], ident[:]),
                    reads=[xT4, ident], writes=[bank], signal=(g2 == 1))
            P.op("act", lambda e, pb=pb: e.activation(out=btm[:], in_=pb[:, 0:256], func=AF.Copy), reads=[bank], writes=[btm])
            a, acs, ea, cd, dsd, ds_, w2 = (sm[n] for n in ("a", "acs", "ea", "cd", "dsd", "ds", "w2"))
            P.op("dve", lambda e, dtc=dtc: e.tensor_tensor(out=a[:], in0=dtc, in1=Abc[:], op=ALU.mult),
                 reads=[dt4, Abc], writes=[a])
            bank = psr.next()
            P.op("pe", lambda e, bank=bank: e.matmul(bank[:, 0:32], lhsT=trile[:], rhs=a[:], start=True, stop=True),
                 reads=[trile, a], writes=[bank], signal=False)
            P.op("pe", lambda e, bank=bank: e.matmul(bank[:, 32:64], lhsT=onesf[:], rhs=a[:], start=True, stop=True),
                 reads=[onesf, a], writes=[bank])
            P.op("act", lambda e, bank=bank: e.activation(out=acs[:], in_=bank[:, 0:32], func=AF.Copy), reads=[bank], writes=[acs])
            P.op("act", lambda e, bank=bank: e.activation(out=ea[:], in_=bank[:, 0:32], func=AF.Exp), reads=[bank], writes=[ea])
            P.op("act", lambda e, bank=bank: e.activation(out=cd[:], in_=bank[:, 32:64], func=AF.Exp), reads=[bank], writes=[cd])
            P.op("dve", lambda e, bank=bank: e.tensor_tensor(out=dsd[:], in0=bank[:, 32:64], in1=acs[:], op=ALU.subtract),
                 reads=[bank, acs], writes=[dsd])
            P.op("act", lambda e: e.activation(out=ds_[:], in_=dsd[:], func=AF.Exp), reads=[dsd], writes=[ds_])
            P.op("dve", lambda e, dtc=dtc: e.tensor_tensor(out=w2[:], in0=dtc, in1=ds_[:], op=ALU.mult),
                 reads=[dt4, ds_], writes=[w2])
            P.op("pool", lambda e: e.tensor_tensor(out=R[:], in0=a[:].unsqueeze(2).to_broadcast([128, 32, 128]),
                                                  in1=trile[:].unsqueeze(1).to_broadcast([128, 32, 128]), op=ALU.mult),
                 reads=[a, trile], writes=[R])
            for j in range(8):
                bank = psr.next()
                P.op("pe", lambda e, bank=bank, j=j: e.matmul(
                    bank[:], lhsT=trigt[:], rhs=R[:, 4 * j:4 * j + 4, :].rearrange("p a b -> p (a b)"), start=True, stop=True),
                    reads=[trigt, R], writes=[bank])
                P.op("act", lambda e, bank=bank, j=j: e.activation(
                    out=E[:, 4 * j:4 * j + 4, :].rearrange("p a b -> p (a b)"), in_=bank[:], func=AF.Exp),
                    reads=[bank], writes=[E])
            bank = psr.next()
            for g2 in range(2):
                P.op("pe", lambda e, bank=bank, g2=g2, xT4=xT4, csl=csl: e.matmul(
                    bank[:, g2 * 128:(g2 + 1) * 128], lhsT=xT4[:, 16 + g2, csl], rhs=xT4[:, 18 + g2, csl], start=True, stop=True),
                    reads=[xT4], writes=[bank], signal=(g2 == 1))
            P.op("dve", lambda e, bank=bank: e.tensor_tensor(
                out=cbm[:], in0=bank[:, 0:256].rearrange("p (g l) -> p g l", g=2),
                in1=maskb[:].unsqueeze(1).to_broadcast([128, 2, 128]), op=ALU.mult),
                reads=[bank, maskb], writes=[cbm])
            P.op("dve", lambda e: e.tensor_tensor(
                out=S[:].rearrange("p (g r) l -> p g r l", g=2), in0=E[:].rearrange("p (g r) l -> p g r l", g=2),
                in1=cbm[:].unsqueeze(2).to_broadcast([128, 2, 16, 128]), op=ALU.mult),
                reads=[E, cbm], writes=[S])
            x3 = xtm[:].rearrange("p (h d) -> p h d", d=64)
            P.op("dve", lambda e, dtc=dtc, x3=x3: e.tensor_tensor(
                out=xdt[:].rearrange("p (h d) -> p h d", d=64), in0=x3,
                in1=dtc.unsqueeze(2).to_broadcast([128, 32, 64]), op=ALU.mult), reads=[xtm, dt4], writes=[xdt])
            P.op("pool", lambda e, x3=x3: e.tensor_tensor(
                out=xdd[:].rearrange("p (h d) -> p h d", d=64), in0=x3,
                in1=w2[:].unsqueeze(2).to_broadcast([128, 32, 64]), op=ALU.mult), reads=[xtm, w2], writes=[xdd])
            P.op("pool", lambda e, x3=x3: e.tensor_tensor(
                out=xD[:].rearrange("p (h d) -> p h d", d=64), in0=x3,
                in1=dsk[:].unsqueeze(2).to_broadcast([128, 32, 64]), op=ALU.mult), reads=[xtm, dsk], writes=[xD])
            for q in range(4):
                g2 = q // 2
                yo = psr.next()
                P.op("pe", lambda e, yo=yo, q=q, g2=g2, xT4=xT4, csl=csl: e.matmul(
                    yo[:], lhsT=xT4[:, 18 + g2, csl], rhs=hb[:, q * 512:(q + 1) * 512], start=True, stop=True),
                    reads=[xT4, hb], writes=[yo])
                yd = psr.next()
                P.op("pe", lambda e, yd=yd, q=q: e.matmul(yd[:], lhsT=ident[:], rhs=xD[:, q * 512:(q + 1) * 512],
                                                         start=True, stop=False),
                     reads=[ident, xD], writes=[yd], signal=False)
                for hh in range(8):
                    hd = q * 8 + hh
                    P.op("pe", lambda e, yd=yd, hh=hh, hd=hd: e.matmul(
                        yd[:, hh * 64:(hh + 1) * 64], lhsT=S[:, hd, :], rhs=xdt[:, hd * 64:(hd + 1) * 64],
                        start=False, stop=(hh == 7)), reads=[S, xdt], writes=[yd], signal=(hh == 7))
                ysl = y[:, q * 512:(q + 1) * 512]
                P.op("dve", lambda e, yo=yo, q=q, ysl=ysl: e.tensor_tensor(
                    out=ysl.rearrange("p (h d) -> p h d", d=64), in0=yo[:].rearrange("p (h d) -> p h d", d=64),
                    in1=ea[:, q * 8:(q + 1) * 8].unsqueeze(2).to_broadcast([128, 8, 64]), op=ALU.mult),
                    reads=[yo, ea], writes=[y])
                P.op("dve", lambda e, yd=yd, ysl=ysl: e.tensor_tensor(out=ysl, in0=yd[:], in1=ysl, op=ALU.add),
                     reads=[yd, y], writes=[y])
            P.op("dve", lambda e, sz=sz: e.tensor_tensor(out=y[:], in0=y[:], in1=sz[:], op=ALU.mult), reads=[y, sz], writes=[y])
            for g2 in range(2):
                P.op("act", lambda e, g2=g2: e.activation(out=junk[:], in_=y[:, g2 * 1024:(g2 + 1) * 1024], func=AF.Square,
                                                          accum_out=ssq[:, g2:g2 + 1]), reads=[y], writes=[junk, ssq])
            P.op("act", lambda e: e.activation(out=rtn[:], in_=ssq[:], func=AF.Sqrt, scale=1.0 / 1024, bias=epsb[:, 0:1]),
                 reads=[ssq, epsb], writes=[rtn])
            P.op("dve", lambda e: e.reciprocal(out=rsn[:], in_=rtn[:]), reads=[rtn], writes=[rsn])
            for g2 in range(2):
                gs = slice(g2 * 1024, (g2 + 1) * 1024)
                P.op("dve", lambda e, g2=g2, gs=gs: e.scalar_tensor_tensor(
                    out=ytm[:, gs], in0=y[:, gs], scalar=rsn[:, g2:g2 + 1], in1=nws[:, gs], op0=ALU.mult, op1=ALU.mult),
                    reads=[y, rsn, nws], writes=[ytm])
            for i in range(2):
                bank = psr.next()
                pb = pbf(bank)
                for jj in range(8):
                    j = i * 8 + jj
                    P.op("pe", lambda e, pb=pb, jj=jj, j=j: e.transpose(
                        pb[:, jj * 128:(jj + 1) * 128], ytm[:, j * 128:(j + 1) * 128], ident[:]),
                        reads=[ytm, ident], writes=[bank], signal=(jj == 7))
                P.op("act", lambda e, pb=pb, i=i, csl=csl: e.activation(
                    out=yT4[:, i * 8:(i + 1) * 8, csl], in_=pb[:, 0:1024].rearrange("p (j t) -> p j t", j=8), func=AF.Copy),
                    reads=[bank], writes=[yT4])
            if cc == 3:
                self.yc_store(0, 16, (c - 3) * 128, 512, lambda lo, hi: yT4[:, :, lo:hi], yT4)
            P.op("dve", lambda e: e.tensor_tensor(
                out=h[:].rearrange("p (h d) -> p h d", d=64), in0=h[:].rearrange("p (h d) -> p h d", d=64),
                in1=cd[:].unsqueeze(2).to_broadcast([128, 32, 64]), op=ALU.mult), reads=[h, cd], writes=[h])
            for q in range(4):
                g2 = q // 2
                st = psr.next()
                P.op("pe", lambda e, st=st, q=q, g2=g2: e.matmul(
                    st[:], lhsT=btm[:, g2 * 128:(g2 + 1) * 128], rhs=xdd[:, q * 512:(q + 1) * 512], start=True, stop=True),
                    reads=[btm, xdd], writes=[st])
                P.op("dve", lambda e, st=st, q=q: e.tensor_tensor(
                    out=h[:, q * 512:(q + 1) * 512], in0=st[:], in1=h[:, q * 512:(q + 1) * 512], op=ALU.add),
                    reads=[st, h], writes=[h])
            P.op("act", lambda e: e.activation(out=hb[:], in_=h[:], func=AF.Copy), reads=[h], writes=[hb])
        P.barrier()
        P.free_from(mark)

    def gather(self, jt0, jt1, track=False):
        P = self.P
        if "cc" not in P.sems:
            P._mksem("cc")
        sem = P.sems["cc"]
        YC, YG = self.YC, self.YG
        waits = []
        if track:
            need = P._deps("pool", [YC], [])
            waits = [(P.sems[k], v) for k, v in need.items()]

        def thunk(e):
            for (s_, v) in waits:
                e.wait_ge(s_, v)
            for jt in range(jt0, jt1):
                for tq in range(4):
                    e.collective_compute("AllGather", ALU.bypass, replica_groups=[[0, 1, 2, 3], [4, 5, 6, 7]],
                                         ins=[YC.ap[jt, tq]], outs=[YG.ap[jt, tq].rearrange("r p t -> (r p) t")]).then_inc(sem)
        P.streams["pool"].append(thunk)
        P.semval["cc"] = P.semval.get("cc", 0) + 4 * (jt1 - jt0)

    def phase_G(self):
        P = self.P
        markG = len(P._ctx)
        P.barrier()
        early = self.early_gather
        if not early:
            self.gather(0, 16)
            self.gather(16, 24)
        sem = P.sems["cc"]
        YC, YG = self.YC, self.YG

        def thunk(e):
            n = 96
            for jt in range(0):
                for tq in range(4):
                    e.collective_compute("AllGather", ALU.bypass, replica_groups=[[0, 1, 2, 3], [4, 5, 6, 7]],
                                         ins=[YC.ap[jt, tq]], outs=[YG.ap[jt, tq].rearrange("r p t -> (r p) t")]).then_inc(sem)
                    n += 1
            e.wait_ge(sem, n)
        P.streams["pool"].append(thunk)
        P.barrier()
        for nm, srcb in (("YCd", YC), ("YGd", YG)):
            if nm in self.debug:
                flat = srcb.ap[:].flatten_outer_dims()
                nrow = flat.shape[0]
                d = P.dram(nm, [nrow, self.LS], BF16, kind="ExternalOutput")
                self.out_bufs.append(d)
                tmp = P.sb("dbg" + nm, [128, self.LS], BF16)
                for r in range(nrow // 128):
                    P.dma("sp", tmp[:], flat[r * 128:(r + 1) * 128, :], reads=[srcb], writes=[tmp])
                    P.dma("sp", d.ap[r * 128:(r + 1) * 128, :], tmp[:], reads=[tmp], writes=[d])
                P.barrier()
        P.free_from(markG)

    def phase_D(self):
        P, L, LS = self.P, self.L, self.LS
        mark = len(P._ctx)
        psr = self.psr
        TB = min(256, LS)
        NT = TB // 128
        uT = P.sb("uTD", [128, 32, TB], BF16)
        xs = [P.sb("xsD%d" % i, [128, 32, 128], F32) for i in range(1)]
        sq = P.sb("sqD", [128, 32, 128], BF16)
        rt = P.sb("rtD", [128, 128], F32)
        rs = P.sb("rsD", [128, 128], F32)
        win = P.sb("winD", [128, 32], F32)
        self.epsb = P.sb("epsbD", [128, 1], F32)
        P.op("pool", lambda e, t=self.epsb: e.memset(t[:], EPS), writes=[self.epsb])
        P.dma("sp", win[:], self.nw_in[:], reads=[self.nw_in], writes=[win])
        wts = Rot([P.sb("wtD%d" % i, [128, 32, 512], BF16) for i in range(2)])
        sg = [P.sb("sg%d" % i, [128, 32, TB], BF16) for i in range(2)]
        yT = P.sb("yTD", [128, 64, TB], BF16)
        tmpf = Rot([P.sb("tmpf%d" % i, [128, TB], F32) for i in range(2)])
        hrow = P.sb("hrow", [128, 4096], F32)
        xr = Rot([P.sb("xr%d" % i, [128, 512], F32) for i in range(2)])
        nwf = P.sb("nwf", [128, 4096], F32)
        P.dma("sp", nwf[:], self.nw_fin[:], reads=[self.nw_fin], writes=[nwf])
        ssq = P.sb("ssqD", [128, 1], F32)
        rtn = P.sb("rtnD", [128, 1], F32)
        rsn = P.sb("rsnD", [128, 1], F32)
        YG = self.YG
        tokv = {}
        YQ = self.YQ
        if YQ.dsem is None:
            YQ.dsem = ("d", P.nsem + 1)
            P._mksem(YQ.dsem)
        qsem = P.sems[YQ.dsem]
        for j0 in range(0, 24, 2):
            YQ.dcount += 16
            P.semval[YQ.dsem] = YQ.dcount
            P._rec((YQ.dsem, YQ.dcount), [YG], [YQ])
            P.ninstr += 1

            def qthunk(e, j0=j0):
                if "v" not in tokv:
                    tokv["v"] = e.snap(e.partition_id() % 4, min_val=0, max_val=3)
                src = YG.ap[j0:j0 + 2, bass.ds(tokv["v"], 1), :, :, :].rearrange("j o r p t -> j (o r p t)")
                dst = YQ.ap[j0:j0 + 2].rearrange("j r p t -> j (r p t)")
                e.dma_start(out=dst, in_=src).then_inc(qsem, 16)
            P.streams["sp"].append(qthunk)

        def yg_load(dst_ap, row0, nj, bk):
            def thunk_fn(e):
                gidx = e.partition_id() % 4
                src = YG.ap[row0:row0 + nj * 128, bass.ds(gidx * LS + bk * TB, TB)].rearrange("(j p) t -> p j t", p=128)
                return src
            return thunk_fn

        for bk in range(LS // TB):
            self.make_uT(lambda s, bk=bk: (self.xTs[:, :, bk * TB + s * 128:bk * TB + (s + 1) * 128], self.xTs),
                         NT, uT, xs, sq, rt, rs, win, psr)
            for m in range(2):
                for ct in range(8):
                    wt = wts.next()
                    P.dma("sp", wt[:], self.WD.ap[m * 8 + ct].rearrange("p (k c) -> p k c", k=32), reads=[self.WD], writes=[wt])
                    for cs in range(4):
                        ps = psr.next()
                        for kc in range(32):
                            P.op("pe", lambda e, kc=kc, ps=ps, cs=cs, wt=wt: e.matmul(
                                ps[:, 0:TB], lhsT=wt[:, kc, cs * 128:(cs + 1) * 128], rhs=uT[:, kc, :],
                                start=(kc == 0), stop=(kc == 31)), reads=[uT, wt], writes=[ps], signal=(kc == 31))
                        P.op("act", lambda e, ps=ps, m=m, ct=ct, cs=cs: e.activation(
                            out=sg[m][:, ct * 4 + cs, :], in_=ps[:, 0:TB], func=AF.Sigmoid), reads=[ps], writes=[sg[m]])
            for m in range(2):
                nkh = 2 if m == 0 else 1
                for r in range(4):
                    row0 = 0 if m == 0 else 16
                    nj = 16 if m == 0 else 8
                    P.dma("sp", yT[:, r * nj:(r + 1) * nj, :],
                          self.YQ.ap[row0:row0 + nj, r, :, bk * TB:(bk + 1) * TB].rearrange("j p t -> p j t"),
                          reads=[self.YQ], writes=[yT])
                for ct in range(8):
                    w_list = []
                    for kh in range(nkh):
                        wt = wts.next()
                        srcw = self.WD.ap[(16 + kh * 8 + ct) if m == 0 else (32 + ct)].rearrange("p (k c) -> p k c", k=32)
                        sbuf_ = self.WD
                        w_list.append((wt, srcw, sbuf_))
                    pss = [psr.next() for _ in range(4)]
                    for kh, (wt, srcw, sbuf_) in enumerate(w_list):
                        P.dma("sp", wt[:], srcw, reads=[sbuf_], writes=[wt])
                        for cs in range(4):
                            ps = pss[cs]
                            for kc in range(32):
                                first = (kh == 0 and kc == 0)
                                last = (kh == nkh - 1 and kc == 31)
                                P.op("pe", lambda e, kc=kc, ps=ps, cs=cs, wt=wt, kh=kh, first=first, last=last: e.matmul(
                                    ps[:, 0:TB], lhsT=wt[:, kc, cs * 128:(cs + 1) * 128], rhs=yT[:, kh * 32 + kc, :],
                                    start=first, stop=last), reads=[yT, wt], writes=[ps], signal=(kc == 31))
                    for cs in range(4):
                        ps = pss[cs]
                        ci = ct * 4 + cs
                        if m == 0:
                            P.op("dve", lambda e, ps=ps, ci=ci: e.tensor_tensor(out=sg[0][:, ci, :], in0=ps[:, 0:TB],
                                                                               in1=sg[0][:, ci, :], op=ALU.mult),
                                 reads=[ps, sg[0]], writes=[sg[0]])
                        else:
                            tf = tmpf.next()
                            P.op("dve", lambda e, ps=ps, ci=ci, tf=tf: e.tensor_tensor(out=tf[:], in0=ps[:, 0:TB],
                                                                                     in1=sg[1][:, ci, :], op=ALU.mult),
                                 reads=[ps, sg[1]], writes=[tf])
                            P.op("pool", lambda e, ci=ci, tf=tf: e.tensor_tensor(out=sg[1][:, ci, :], in0=tf[:],
                                                                                in1=sg[0][:, ci, :], op=ALU.add),
                                 reads=[tf, sg[0], sg[1]], writes=[sg[1]])
            mT = sg[1]
            for tt in range(NT):
                r0 = bk * TB + tt * 128
                for ct in range(8):
                    xrow = xr.next()
                    P.dma("sp", xrow[:], self.xres[r0:r0 + 128, ct * 512:(ct + 1) * 512], reads=[self.xres], writes=[xrow])
                    wt = wts.next()
                    P.dma("sp", wt[:], self.WD.ap[40 + ct].rearrange("p (k c) -> p k c", k=32), reads=[self.WD], writes=[wt])
                    ps = psr.next()
                    for kc in range(32):
                        P.op("pe", lambda e, kc=kc, ps=ps, tt=tt, wt=wt: e.matmul(
                            ps[:], lhsT=mT[:, kc, tt * 128:(tt + 1) * 128], rhs=wt[:, kc, :],
                            start=(kc == 0), stop=(kc == 31)), reads=[mT, wt], writes=[ps], signal=(kc == 31))
                    P.op("dve", lambda e, ps=ps, ct=ct, xrow=xrow: e.tensor_tensor(
                        out=hrow[:, ct * 512:(ct + 1) * 512], in0=ps[:], in1=xrow[:], op=ALU.add),
                        reads=[ps, xrow], writes=[hrow])
                P.op("act", lambda e: e.activation(out=xs[0][:].rearrange("p a b -> p (a b)"), in_=hrow[:], func=AF.Square,
                                                   accum_out=ssq[:]), reads=[hrow], writes=[xs[0], ssq])
                P.op("act", lambda e, epsb=self.epsb: e.activation(out=rtn[:], in_=ssq[:], func=AF.Sqrt, scale=1.0 / 4096,
                                                                   bias=epsb[:, 0:1]),
                     reads=[ssq, self.epsb], writes=[rtn])
                P.op("dve", lambda e: e.reciprocal(out=rsn[:], in_=rtn[:]), reads=[rtn], writes=[rsn])
                P.op("dve", lambda e: e.scalar_tensor_tensor(out=hrow[:], in0=hrow[:], scalar=rsn[:, 0:1], in1=nwf[:],
                                                             op0=ALU.mult, op1=ALU.mult), reads=[hrow, rsn, nwf], writes=[hrow])
                P.dma("sp", self.OUT[r0:r0 + 128, :], hrow[:], reads=[hrow], writes=[self.OUT])
        P.barrier()
        P.free_from(mark)

    def finish(self):
        P = self.P
        P.barrier(include_bg=True)
        P.emit()
        P.close()
        return self.nc


def build_program(L, phases="A", debug=()):
    k = K(L, debug, phases)
    k.declare()
    k.load_consts()
    if "A" in phases:
        k.phase_A()
    if "M" in phases:
        k.phase_A2()
    if "B" in phases:
        k.phase_B()
    if "C" in phases:
        k.phase_C()
    if "G" in phases:
        k.phase_G()
    if "D" in phases:
        k.phase_D()
    return k


def _tile_w(w, cols=None):
    if cols is not None:
        wz = np.zeros((w.shape[0], len(cols)), np.float32)
        m = cols >= 0
        wz[:, m] = w[:, cols[m]]
        w = wz
    Kd, N = w.shape
    return np.ascontiguousarray(w.reshape(Kd // 128, 128, N // 512, 512).transpose(2, 1, 0, 3))


def _in_cols(g):
    c = []
    c += list(range(g * 2048, (g + 1) * 2048))
    c += list(range(8192 + g * 2048, 8192 + (g + 1) * 2048))
    c += list(range(16384 + g * 256, 16384 + (g + 1) * 256))
    c += list(range(17408 + g * 256, 17408 + (g + 1) * 256))
    c += list(range(18560, 18560 + 1024))
    c += list(range(19584, 19584 + 512))
    c += list(range(20160 + g * 1024, 20160 + (g + 1) * 1024))
    rope = list(range(20096, 20160))
    c += rope + rope[32:] + rope[:32]
    c += list(range(18432 + g * 32, 18432 + (g + 1) * 32))
    c += [-1] * (15 * 512 - len(c))
    return np.array(c, np.int64)


def prepare_inputs(inp, L):
    f = lambda a: np.asarray(a, np.float32)
    x = f(inp["x"])[:, :L]
    NB, LS = L // 1024, L // 4
    w_in = f(inp["w_in"])[0]
    conv_w, conv_b = f(inp["conv_w"])[0], f(inp["conv_b"])[0]
    w_uq, w_ukv = f(inp["w_uq"])[0], f(inp["w_ukv"])[0]
    shared = {}
    shared["w_gates"] = _tile_w(w_in[:, 24256:24256 + 8192])
    wbs = f(inp["w_branch_ssm"])[0]
    shared["w_bs"] = np.concatenate([_tile_w(wbs[:4096]), _tile_w(wbs[4096:])], 0)
    shared["w_ba"] = _tile_w(f(inp["w_branch_attn"])[0])
    shared["w_out"] = _tile_w(f(inp["w_out"])[0])
    shared["nw_in"] = np.ascontiguousarray(f(inp["norm_in_w"])[0].reshape(32, 128).T)
    shared["nw_q"] = np.ascontiguousarray(f(inp["q_norm_w"])[0].reshape(8, 128).T)
    shared["nw_kv"] = np.ascontiguousarray(f(inp["kv_norm_w"])[0].reshape(4, 128).T)
    shared["nw_fin"] = np.ascontiguousarray(np.broadcast_to(f(inp["norm_final_w"])[None, :], (128, 4096)))
    half = 32
    inv_freq = (np.float32(10000.0) ** (-(np.arange(0, half, dtype=np.float32) / np.float32(half)))).astype(np.float32)
    ang = (np.arange(L, dtype=np.float32)[None, :] * inv_freq[:, None]).astype(np.float32)
    cos, sin = np.cos(ang).astype(np.float32), np.sin(ang).astype(np.float32)
    shared["ropec"] = np.concatenate([cos, cos], 0)
    shared["ropes"] = np.concatenate([-sin, sin], 0)
    shared["c_ident"] = np.eye(128, dtype=np.float32).astype(ml_dtypes.bfloat16)
    shared["c_ones"] = np.ones((128, 128), ml_dtypes.bfloat16)
    shared["c_onesf"] = np.ones((128, 128), np.float32)
    k = np.arange(128)
    shared["c_trile"] = (k[:, None] <= k[None, :]).astype(np.float32)
    shared["c_trigt"] = (k[:, None] > k[None, :]).astype(np.float32)
    shared["c_maskb"] = (k[:, None] <= k[None, :]).astype(np.float32).astype(ml_dtypes.bfloat16)
    per_g = []
    for g in range(4):
        d = {}
        d["w_in"] = _tile_w(w_in, _in_cols(g))
        cols = []
        for hl in range(8):
            H = 8 * g + hl
            base = H * 192
            rope = list(range(base + 128, base + 192))
            cols += list(range(base, base + 128)) + rope + rope[32:] + rope[:32]
        d["w_uq"] = np.ascontiguousarray(w_uq[:, cols].reshape(8, 128, 2048).transpose(1, 0, 2))
        cols = []
        for hl in range(8):
            H = 8 * g + hl
            cols += list(range(H * 256, H * 256 + 128))
        for hl in range(8):
            H = 8 * g + hl
            cols += list(range(H * 256 + 128, H * 256 + 256))
        d["w_ukv"] = np.ascontiguousarray(w_ukv[:, cols].reshape(4, 128, 2048).transpose(1, 0, 2))
        ch = np.concatenate([np.arange(g * 2048, (g + 1) * 2048),
                             8192 + g * 256 + np.arange(256), 8192 + 1024 + g * 256 + np.arange(256)])
        cpar = np.concatenate([conv_w[:, ch], conv_b[None, ch]], 0)
        d["convp"] = np.ascontiguousarray(cpar.reshape(5, 20, 128).transpose(2, 1, 0))
        hs = slice(g * 32, (g + 1) * 32)
        bc = lambda v: np.ascontiguousarray(np.broadcast_to(v[None, :], (128, v.shape[0])))
        d["dtb"] = bc(f(inp["dt_bias"])[0, hs])
        d["alog"] = bc(f(inp["a_log"])[0, hs])
        d["dskip"] = bc(f(inp["d_skip"])[0, hs])
        d["nw_ssm"] = bc(f(inp["ssm_norm_w"])[0, g * 2048:(g + 1) * 2048])
        per_g.append(d)
    maps = []
    for c in range(8):
        b, g = c // 4, c % 4
        m = dict(shared)
        m.update(per_g[g])
        xb = x[b]
        m["xT"] = np.ascontiguousarray(xb.reshape(NB, 1024, 32, 128).transpose(0, 3, 2, 1))
        xsl = xb[g * LS:(g + 1) * LS]
        m["xTs"] = np.ascontiguousarray(xsl.reshape(LS, 32, 128).transpose(2, 1, 0))
        m["xres"] = np.ascontiguousarray(xsl)
        maps.append(m)
    return maps


_CACHE = {}


def run(inputs, L=SEQ, phases="AMBCGD", debug=()):
    import time
    t0 = time.time()
    key = (L, phases, tuple(debug))
    if key not in _CACHE:
        k = build_program(L, phases, debug)
        nc = k.finish()
        _CACHE[key] = (nc, set(k.inp.keys()), k.P.ninstr)
    nc, names, ninstr = _CACHE[key]
    t1 = time.time()
    maps = prepare_inputs(inputs, L)
    maps = [{n: m[n] for n in names} for m in maps]
    t2 = time.time()
    res = run_bass_kernel_spmd(nc, maps, core_ids=list(range(8)))
    if os.environ.get("KDBG"):
        print("ninstr %d build %.1fs prep %.1fs run %.1fs" % (ninstr, t1 - t0, t2 - t1, time.time() - t2), flush=True)
    return res.results


def kernel(**inputs):
    res = run(inputs)
    LS = SEQ // 4
    out = np.empty((2, SEQ, D_MODEL), np.float32)
    for c in range(8):
        b, g = c // 4, c % 4
        out[b, g * LS:(g + 1) * LS] = res[c]["out"]
    return out
```

```python
import os
import math
import numpy as np
import ml_dtypes
import concourse.bass as bass
import concourse.mybir as mybir
from concourse.bass_utils import run_bass_kernel_spmd

F32 = mybir.dt.float32
BF16 = mybir.dt.bfloat16
AF = mybir.ActivationFunctionType
ALU = mybir.AluOpType
EPS = 1e-6

D_MODEL = 4096
D_SSM = 8192
D_CONV = 10240
SEQ = 8192


class Buf:
    def __init__(self, name, ap, accum=False):
        self.name = name
        self.ap = ap
        self.ws = {}
        self.rs = {}
        self.accum = accum
        self.dsem = None
        self.dcount = 0

    def __getitem__(self, idx):
        return self.ap[idx]


class Prog:
    ENGS = ("pe", "act", "dve", "pool", "sp")

    def __init__(self, nc):
        self.nc = nc
        self.streams = {e: [] for e in self.ENGS}
        self.cnt = {e: 0 for e in self.ENGS}
        self.sems = {}
        self.semval = {}
        self._ctx = []
        self._semctx = []
        self.pending = {e: [] for e in self.ENGS}
        self.waited = {e: {} for e in self.ENGS}
        self.bg_keys = set()
        self.nsem = 0
        self.ninstr = 0
        for e in ("pe", "act", "dve", "pool"):
            self._mksem(e)

    def _mksem(self, key):
        self.nsem += 1
        g = self.nc.semaphore("sem%d" % self.nsem)
        h = g.__enter__()
        self._semctx.append(g)
        self.sems[key] = h
        self.semval[key] = 0
        return h

    def sb(self, name, shape, dt):
        g = self.nc.sbuf_tensor("sb_" + name, list(shape), dt)
        t = g.__enter__()
        self._ctx.append(g)
        return Buf(name, t)

    def ps(self, name, shape, dt=F32):
        g = self.nc.psum_tensor(name, list(shape), dt)
        t = g.__enter__()
        self._ctx.append(g)
        return Buf(name, t)

    def dram(self, name, shape, dt, kind="Internal"):
        t = self.nc.dram_tensor(name, list(shape), dt, kind=kind)
        return Buf(name, t, accum=True)

    def _deps(self, eng, reads, writes):
        need = {}

        def add(k, v):
            if k == "pe" and eng == "pe":
                return
            if need.get(k, 0) < v:
                need[k] = v
        for b in reads:
            for k, v in b.ws.items():
                add(k, v)
        for b in writes:
            if b.accum:
                continue
            for k, v in b.ws.items():
                add(k, v)
            for k, v in b.rs.items():
                add(k, v)
        w = self.waited[eng]
        need = {k: v for k, v in need.items() if w.get(k, 0) < v}
        for k, v in need.items():
            w[k] = v
        return need

    @staticmethod
    def _rec(tok, reads, writes):
        k, v = tok
        for b in reads:
            b.rs[k] = v
        for b in writes:
            if b.accum:
                b.ws[k] = v
            else:
                b.ws = {k: v}
                b.rs = {}

    def op(self, eng, fn, reads=(), writes=(), signal=True):
        need = self._deps(eng, reads, writes)
        waits = [(self.sems[k], v) for k, v in need.items()]
        self.ninstr += 1
        if signal:
            self.cnt[eng] += 1
            self.semval[eng] = self.cnt[eng]
            tok = (eng, self.cnt[eng])
            pr = [b for (b, kind) in self.pending[eng] if kind == "r"]
            pw = [b for (b, kind) in self.pending[eng] if kind == "w"]
            self.pending[eng] = []
            self._rec(tok, list(reads) + pr, list(writes) + pw)
        else:
            for b in reads:
                self.pending[eng].append((b, "r"))
            for b in writes:
                self.pending[eng].append((b, "w"))
        sem = self.sems[eng]

        def thunk(e, waits=waits, fn=fn, signal=signal, sem=sem):
            for (s, v) in waits:
                e.wait_ge(s, v)
            ins = fn(e)
            if signal:
                ins.then_inc(sem, 1)
        self.streams[eng].append(thunk)

    def dma(self, q, out_ap, in_ap, reads=(), writes=(), sembuf=None):
        need = self._deps(q, reads, writes)
        waits = [(self.sems[k], v) for k, v in need.items()]
        self.ninstr += 1
        sbf = sembuf
        if sbf is None:
            cands = [b for b in list(writes) + list(reads) if not b.accum]
            sbf = cands[0] if cands else (list(writes) + list(reads))[0]
        if sbf.dsem is None:
            sbf.dsem = ("d", self.nsem + 1)
            self._mksem(sbf.dsem)
        sbf.dcount += 16
        self.semval[sbf.dsem] = sbf.dcount
        tok = (sbf.dsem, sbf.dcount)
        self._rec(tok, reads, writes)
        sem = self.sems[sbf.dsem]

        def thunk(e, waits=waits, sem=sem, out_ap=out_ap, in_ap=in_ap):
            for (s, v) in waits:
                e.wait_ge(s, v)
            e.dma_start(out=out_ap, in_=in_ap).then_inc(sem, 16)
        self.streams[q].append(thunk)

    def barrier(self, include_bg=False):
        for e in self.ENGS:
            assert not self.pending[e]
        for eng in self.ENGS:
            w = self.waited[eng]
            items = [(k, v) for k, v in self.semval.items() if v > 0 and (include_bg or k not in self.bg_keys)]
            waits = [(self.sems[k], v) for k, v in items if w.get(k, 0) < v]
            for k, v in items:
                w[k] = max(w.get(k, 0), v)

            def thunk(e, waits=waits):
                for (s, v) in waits:
                    e.wait_ge(s, v)
            self.streams[eng].append(thunk)

    def emit(self):
        nc = self.nc
        for e in self.ENGS:
            assert not self.pending[e], "unsignalled pending ops on %s" % e
        with nc.Block() as block:
            @block.tensor
            def _(e):
                for t in self.streams["pe"]:
                    t(e)

            @block.scalar
            def _(e):
                for t in self.streams["act"]:
                    t(e)

            @block.vector
            def _(e):
                for t in self.streams["dve"]:
                    t(e)

            @block.gpsimd
            def _(e):
                for t in self.streams["pool"]:
                    t(e)

            @block.sync
            def _(e):
                for t in self.streams["sp"]:
                    t(e)

    def close(self):
        for g in reversed(self._ctx):
            g.__exit__(None, None, None)
        for g in reversed(self._semctx):
            g.__exit__(None, None, None)

    def free_from(self, mark):
        while len(self._ctx) > mark:
            g = self._ctx.pop()
            g.__exit__(None, None, None)


class Rot:
    def __init__(self, items):
        self.items = items
        self.i = 0

    def next(self):
        b = self.items[self.i % len(self.items)]
        self.i += 1
        return b


class K:
    def __init__(self, L, debug=(), phases="AMBCGD"):
        self.L = L
        self.phases = phases
        self.debug = set(debug)
        self.nc = nc = bass.Bass("TRN2", target_bir_lowering=False)
        nc.cache_partition_id()
        self.P = Prog(nc)
        self.inp = {}
        self.out_bufs = []
        self.NB = L // 1024
        self.LS = L // 4
        self.early_gather = True
        self.tokv = {}

    def ein(self, name, shape, dt=F32):
        need = {"w_gates": "D", "w_bs": "D", "w_ba": "D", "w_out": "D", "xres": "D", "xTs": "D", "nw_fin": "D",
                "tokoff": "D", "w_in": "A", "xT": "A"}
        if name in need and need[name] not in self.phases:
            return None
        t = self.nc.dram_tensor(name, list(shape), dt, kind="ExternalInput")
        b = Buf(name, t)
        self.inp[name] = b
        return b

    def scratch(self, name, shape, dt):
        kind = "ExternalOutput" if name in self.debug else "Internal"
        b = self.P.dram(name, shape, dt, kind=kind)
        if kind == "ExternalOutput":
            self.out_bufs.append(b)
        return b

    def declare(self):
        L, NB, LS = self.L, self.NB, self.LS
        e = self.ein
        self.xT = e("xT", [NB, 128, 32, 1024])
        self.xTs = e("xTs", [128, 32, LS])
        self.xres = e("xres", [LS, 4096])
        self.w_in = e("w_in", [15, 128, 32, 512])
        self.w_gates = e("w_gates", [16, 128, 32, 512])
        self.w_bs = e("w_bs", [16, 128, 32, 512])
        self.w_ba = e("w_ba", [8, 128, 32, 512])
        self.w_out = e("w_out", [8, 128, 32, 512])
        self.w_uq = e("w_uq", [128, 8, 2048])
        self.w_ukv = e("w_ukv", [128, 4, 2048])
        self.convp = e("convp", [128, 20, 5])
        self.nw_in = e("nw_in", [128, 32])
        self.nw_q = e("nw_q", [128, 8])
        self.nw_kv = e("nw_kv", [128, 4])
        self.dtb = e("dtb", [128, 32])
        self.alog = e("alog", [128, 32])
        self.dskip = e("dskip", [128, 32])
        self.nw_ssm = e("nw_ssm", [128, 2048])
        self.nw_fin = e("nw_fin", [128, 4096])
        self.ropec = e("ropec", [64, L])
        self.ropes = e("ropes", [64, L])
        self.c_ident = e("c_ident", [128, 128], BF16)
        self.c_ones = e("c_ones", [128, 128], BF16)
        self.c_onesf = e("c_onesf", [128, 128])
        self.c_trile = e("c_trile", [128, 128])
        self.c_trigt = e("c_trigt", [128, 128])
        self.c_maskb = e("c_maskb", [128, 128], BF16)
        s = self.scratch
        self.SZ = s("SZ", [L, 2048], BF16)
        self.XC = s("XC", [2560, L], BF16)
        self.DT = s("DT", [L, 32], F32)
        self.CQ = s("CQ", [1024, L], BF16)
        self.CKV = s("CKV", [512, L], BF16)
        self.RR = s("RR", [128, L], F32)
        self.GA = s("GA", [1024, L], BF16)
        self.QN = s("QN", [8, 128, L], BF16)
        self.QR = s("QR", [8, 64, L], BF16)
        self.KN = s("KN", [8, 128, L], BF16)
        self.KR = s("KR", [64, L], BF16)
        self.V = s("V", [L, 1024], BF16)
        self.YC = s("YC", [24, 4, 128, LS], BF16)
        self.YG = s("YG", [24, 4, 4, 128, LS], BF16)
        self.YQ = s("YQ", [24, 4, 128, LS], BF16)
        self.WD = s("WD", [48, 128, 32 * 512], BF16)
        self.OUT = self.P.dram("out", [LS, 4096], F32, kind="ExternalOutput")
        self.out_bufs.append(self.OUT)

    def yc_store(self, jt0, nj, t0, ntok, src_fn, srcbuf):
        LS = self.LS
        t = t0
        while t < t0 + ntok:
            tq, tl = t // LS, t % LS
            n = min(LS - tl, t0 + ntok - t)
            dst = self.YC.ap[jt0:jt0 + nj, tq, :, tl:tl + n].rearrange("j p t -> p j t")
            self.P.dma("sp", dst, src_fn(t - t0, t - t0 + n), reads=[srcbuf], writes=[self.YC])
            t += n

    def load_consts(self):
        P = self.P
        self.ident = P.sb("ident", [128, 128], BF16)
        self.ones = P.sb("ones", [128, 128], BF16)
        for sbuf, src in ((self.ident, self.c_ident), (self.ones, self.c_ones)):
            P.dma("sp", sbuf[:], src[:], reads=[src], writes=[sbuf])
        self.psb = [P.ps("psb%d" % i, [128, 512]) for i in range(8)]
        self.psr = Rot(self.psb)

    def make_uT(self, src_fn, nsub, uT, xs, sq, rt, rs, win, ps_rot):
        P = self.P
        ones = self.ones
        epsb = self.epsb
        for s in range(nsub):
            x = xs[s % len(xs)]
            P.dma("sp", x[:], src_fn(s)[0], reads=[src_fn(s)[1]], writes=[x])
            P.op("act", lambda e, x=x: e.activation(out=sq[:], in_=x[:], func=AF.Square),
                 reads=[x], writes=[sq])
            ps = ps_rot.next()
            for kc in range(32):
                P.op("pe", lambda e, kc=kc, ps=ps: e.matmul(ps[:, 0:128], lhsT=ones[:], rhs=sq[:, kc, :],
                                                          start=(kc == 0), stop=(kc == 31)),
                     reads=[ones, sq], writes=[ps], signal=(kc == 31))
            P.op("act", lambda e, ps=ps: e.activation(out=rt[:], in_=ps[:, 0:128], func=AF.Sqrt,
                                                      scale=1.0 / D_MODEL, bias=epsb[:, 0:1]),
                 reads=[ps, epsb], writes=[rt])
            P.op("dve", lambda e: e.reciprocal(out=rs[:], in_=rt[:]), reads=[rt], writes=[rs])
            P.op("dve", lambda e, x=x: e.tensor_tensor(out=x[:], in0=x[:],
                                                       in1=win[:].unsqueeze(2).to_broadcast([128, 32, 128]),
                                                       op=ALU.mult),
                 reads=[x, win], writes=[x])
            P.op("dve", lambda e, x=x, s=s: e.tensor_tensor(out=uT[:, :, s * 128:(s + 1) * 128], in0=x[:],
                                                            in1=rs[:].unsqueeze(1).to_broadcast([128, 32, 128]),
                                                            op=ALU.mult),
                 reads=[x, rs], writes=[uT])

    def phase_A(self):
        P, L, NB = self.P, self.L, self.NB
        mark = len(P._ctx)
        uT = P.sb("uT", [128, 32, 1024], BF16)
        xs = [P.sb("xs%d" % i, [128, 32, 128], F32) for i in range(2)]
        sq = P.sb("sq", [128, 32, 128], BF16)
        rt = P.sb("rt", [128, 128], F32)
        rs = P.sb("rs", [128, 128], F32)
        win = P.sb("win", [128, 32], F32)
        self.epsb = P.sb("epsb", [128, 1], F32)
        P.op("pool", lambda e, t=self.epsb: e.memset(t[:], EPS), writes=[self.epsb])
        P.dma("sp", win[:], self.nw_in[:], reads=[self.nw_in], writes=[win])
        wts = Rot([P.sb("wt%d" % i, [128, 32, 512], BF16) for i in range(2)])
        stg = Rot([P.sb("stg%d" % i, [128, 512], BF16) for i in range(4)])
        stf = Rot([P.sb("stf%d" % i, [128, 512], F32) for i in range(2)])
        raws = Rot([P.sb("raw%d" % i, [128, 515], F32) for i in range(2)])
        accs = Rot([P.sb("acc%d" % i, [128, 512], F32) for i in range(2)])
        halo = P.sb("halo", [128, 20, 3], F32)
        cp = P.sb("cp", [128, 20, 5], F32)
        dtb = P.sb("dtb", [128, 32], F32)
        dtx = P.sb("dtx", [128, 8, 32], F32)
        dte = P.sb("dte", [128, 8, 32], F32)
        dtt = P.sb("dtt", [128, 8, 32], F32)
        P.dma("sp", cp[:], self.convp[:], reads=[self.convp], writes=[cp])
        P.dma("sp", dtb[:], self.dtb[:], reads=[self.dtb], writes=[dtb])
        psr = self.psr

        for tb in range(NB):
            t0 = tb * 1024
            self.make_uT(lambda s, tb=tb: (self.xT[tb, :, :, s * 128:(s + 1) * 128], self.xT),
                         8, uT, xs, sq, rt, rs, win, psr)
            for t in range(15):
                wt = wts.next()
                P.dma("pool", wt[:], self.w_in[t], reads=[self.w_in], writes=[wt])
                if t < 4:
                    for tt in range(8):
                        ps = psr.next()
                        for kc in range(32):
                            P.op("pe", lambda e, kc=kc, ps=ps, tt=tt, wt=wt: e.matmul(
                                ps[:], lhsT=uT[:, kc, tt * 128:(tt + 1) * 128], rhs=wt[:, kc, :],
                                start=(kc == 0), stop=(kc == 31)),
                                reads=[uT, wt], writes=[ps], signal=(kc == 31))
                        st = stg.next()
                        P.op("act", lambda e, ps=ps, st=st: e.activation(out=st[:], in_=ps[:], func=AF.Silu),
                             reads=[ps], writes=[st])
                        P.dma("sp", self.SZ[t0 + tt * 128:t0 + (tt + 1) * 128, t * 512:(t + 1) * 512], st[:],
                              reads=[st], writes=[self.SZ])
                    continue
                ncs = 1 if t == 14 else 4
                for cs in range(ncs):
                    for th in range(2):
                        ps = psr.next()
                        for kc in range(32):
                            P.op("pe", lambda e, kc=kc, ps=ps, cs=cs, th=th, wt=wt: e.matmul(
                                ps[:], lhsT=wt[:, kc, cs * 128:(cs + 1) * 128], rhs=uT[:, kc, th * 512:(th + 1) * 512],
                                start=(kc == 0), stop=(kc == 31)),
                                reads=[uT, wt], writes=[ps], signal=(kc == 31))
                        tok = slice(t0 + th * 512, t0 + (th + 1) * 512)
                        if 4 <= t <= 8:
                            idx = (t - 4) * 4 + cs
                            raw = raws.next()
                            acc = accs.next()
                            P.op("dve", lambda e, raw=raw, ps=ps: e.tensor_copy(out=raw[:, 3:515], in_=ps[:]),
                                 reads=[ps], writes=[raw])
                            if tb == 0 and th == 0:
                                P.op("pool", lambda e, raw=raw: e.memset(raw[:, 0:3], 0.0), writes=[raw])
                            else:
                                P.op("pool", lambda e, raw=raw, idx=idx: e.tensor_copy(out=raw[:, 0:3], in_=halo[:, idx, :]),
                                     reads=[halo], writes=[raw])
                            P.op("pool", lambda e, raw=raw, idx=idx: e.tensor_copy(out=halo[:, idx, :], in_=raw[:, 512:515]),
                                 reads=[raw], writes=[halo])
                            P.op("dve", lambda e, raw=raw, acc=acc, idx=idx: e.tensor_scalar(
                                out=acc[:], in0=raw[:, 3:515], scalar1=cp[:, idx, 3:4], scalar2=cp[:, idx, 4:5],
                                op0=ALU.mult, op1=ALU.add), reads=[raw, cp], writes=[acc])
                            for k in range(3):
                                P.op("dve", lambda e, raw=raw, acc=acc, idx=idx, k=k: e.scalar_tensor_tensor(
                                    out=acc[:], in0=raw[:, k:k + 512], scalar=cp[:, idx, k:k + 1], in1=acc[:],
                                    op0=ALU.mult, op1=ALU.add), reads=[raw, cp, acc], writes=[acc])
                            st = stg.next()
                            P.op("act", lambda e, acc=acc, st=st: e.activation(out=st[:], in_=acc[:], func=AF.Silu),
                                 reads=[acc], writes=[st])
                            P.dma("sp", self.XC[idx * 128:(idx + 1) * 128, tok], st[:], reads=[st], writes=[self.XC])
                        elif t in (9, 10, 11, 12, 13):
                            st = stg.next()
                            fn = AF.Silu if t >= 12 else AF.Copy
                            P.op("act", lambda e, ps=ps, st=st, fn=fn: e.activation(out=st[:], in_=ps[:], func=fn),
                                 reads=[ps], writes=[st])
                            if t <= 10:
                                r0 = ((t - 9) * 4 + cs) * 128
                                dst, dbuf = self.CQ[r0:r0 + 128, tok], self.CQ
                            elif t == 11:
                                dst, dbuf = self.CKV[cs * 128:(cs + 1) * 128, tok], self.CKV
                            else:
                                r0 = ((t - 12) * 4 + cs) * 128
                                dst, dbuf = self.GA[r0:r0 + 128, tok], self.GA
                            P.dma("sp", dst, st[:], reads=[st], writes=[dbuf])
                        else:
                            st = stf.next()
                            P.op("act", lambda e, ps=ps, st=st: e.activation(out=st[:], in_=ps[:], func=AF.Copy),
                                 reads=[ps], writes=[st])
                            P.dma("sp", self.RR[:, tok], st[:], reads=[st], writes=[self.RR])
                if t == 14:
                    ps = psr.next()
                    for tt in range(8):
                        for kc in range(32):
                            P.op("pe", lambda e, kc=kc, ps=ps, tt=tt, wt=wt: e.matmul(
                                ps[:, tt * 32:(tt + 1) * 32], lhsT=uT[:, kc, tt * 128:(tt + 1) * 128],
                                rhs=wt[:, kc, 128:160], start=(kc == 0), stop=(kc == 31)),
                                reads=[uT, wt], writes=[ps], signal=(kc == 31 and tt == 7))
                    P.op("dve", lambda e, ps=ps: e.tensor_tensor(
                        out=dtx[:], in0=ps[:, 0:256].rearrange("p (a b) -> p a b", a=8),
                        in1=dtb[:].unsqueeze(1).to_broadcast([128, 8, 32]), op=ALU.add),
                        reads=[ps, dtb], writes=[dtx])
                    P.op("act", lambda e: e.activation(out=dte[:], in_=dtx[:], func=AF.Exp), reads=[dtx], writes=[dte])
                    P.op("act", lambda e: e.activation(out=dtt[:], in_=dte[:], func=AF.Ln, bias=1.0, scale=1.0),
                         reads=[dte], writes=[dtt])
                    P.dma("sp", self.DT[t0:t0 + 1024, :].rearrange("(a p) h -> p a h", p=128), dtt[:],
                          reads=[dtt], writes=[self.DT])
        P.barrier()
        P.free_from(mark)


    def convert_weights(self):
        P = self.P
        for (src, n, base) in ((self.w_gates, 16, 0), (self.w_bs, 16, 16), (self.w_ba, 8, 32), (self.w_out, 8, 40)):
            for t in range(n):
                P.dma("pool", self.WD.ap[base + t], src.ap[t].rearrange("p k c -> p (k c)"), reads=[src], writes=[self.WD])
                P.bg_keys.add(src.dsem)
                yield

    def yq_copy(self, j0s, cc_count):
        P = self.P
        YQ, YG = self.YQ, self.YG
        if YQ.dsem is None:
            YQ.dsem = ("d", P.nsem + 1)
            P._mksem(YQ.dsem)
        qsem = P.sems[YQ.dsem]
        ccsem = P.sems["cc"]
        tokv = self.tokv
        for j0 in j0s:
            YQ.dcount += 16
            P.semval[YQ.dsem] = YQ.dcount
            P._rec((YQ.dsem, YQ.dcount), [YG], [YQ])
            P.ninstr += 1

            def qthunk(e, j0=j0):
                e.wait_ge(ccsem, cc_count)
                if "v" not in tokv:
                    tokv["v"] = e.snap(e.partition_id() % 4, min_val=0, max_val=3)
                src = YG.ap[j0:j0 + 2, bass.ds(tokv["v"], 1), :, :, :].rearrange("j o r p t -> j (o r p t)")
                dst = YQ.ap[j0:j0 + 2].rearrange("j r p t -> j (r p t)")
                e.dma_start(out=dst, in_=src).then_inc(qsem, 16)
            P.streams["sp"].append(qthunk)

    def phase_A2(self):
        P, L = self.P, self.L
        mark = len(P._ctx)
        psr = self.psr
        wq = P.sb("wq", [128, 8, 2048], BF16)
        wkv = P.sb("wkv", [128, 4, 2048], BF16)
        P.dma("pool", wq[:], self.w_uq[:], reads=[self.w_uq], writes=[wq])
        P.dma("pool", wkv[:], self.w_ukv[:], reads=[self.w_ukv], writes=[wkv])
        nwq = P.sb("nwq", [128, 8], F32)
        nwkv = P.sb("nwkv", [128, 4], F32)
        P.dma("sp", nwq[:], self.nw_q[:], reads=[self.nw_q], writes=[nwq])
        P.dma("sp", nwkv[:], self.nw_kv[:], reads=[self.nw_kv], writes=[nwkv])
        epsb = P.sb("epsb2", [128, 1], F32)
        P.op("pool", lambda e: e.memset(epsb[:], EPS), writes=[epsb])
        cqs = Rot([P.sb("cq%d" % i, [128, 8, 512], BF16) for i in range(2)])
        ckvs = Rot([P.sb("ckv%d" % i, [128, 4, 512], BF16) for i in range(2)])
        rras = Rot([P.sb("rra%d" % i, [64, 512], F32) for i in range(2)])
        rrbs = Rot([P.sb("rrb%d" % i, [64, 512], F32) for i in range(2)])
        coss = Rot([P.sb("cos%d" % i, [64, 512], F32) for i in range(2)])
        sins = Rot([P.sb("sin%d" % i, [64, 512], F32) for i in range(2)])
        sqb = P.sb("sqb", [128, 8, 512], BF16)
        rt = P.sb("rt2", [128, 512], F32)
        rq = P.sb("rq", [128, 512], F32)
        rkv = P.sb("rkv", [128, 512], F32)
        cqw = P.sb("cqw", [128, 8, 512], BF16)
        ckvw = P.sb("ckvw", [128, 4, 512], BF16)
        stg = Rot([P.sb("stg2_%d" % i, [128, 512], BF16) for i in range(4)])
        tA = Rot([P.sb("tA%d" % i, [64, 512], F32) for i in range(2)])
        tB = Rot([P.sb("tB%d" % i, [64, 512], F32) for i in range(2)])
        ones = self.ones

        def rope(srcA, srcB, bufsA, bufsB, cos, sin, dst, dbuf):
            a, b = tA.next(), tB.next()
            P.op("dve", lambda e: e.tensor_tensor(out=a[:], in0=srcA, in1=cos[:], op=ALU.mult),
                 reads=bufsA + [cos], writes=[a])
            P.op("dve", lambda e: e.tensor_tensor(out=b[:], in0=srcB, in1=sin[:], op=ALU.mult),
                 reads=bufsB + [sin], writes=[b])
            st = stg.next()
            P.op("pool", lambda e: e.tensor_tensor(out=st[0:64, :], in0=a[:], in1=b[:], op=ALU.add),
                 reads=[a, b], writes=[st])
            P.dma("sp", dst, st[0:64, :], reads=[st], writes=[dbuf])

        for tk in range(L // 512):
            tok = slice(tk * 512, (tk + 1) * 512)
            cq, ckv, rra, rrb, cos, sin = cqs.next(), ckvs.next(), rras.next(), rrbs.next(), coss.next(), sins.next()
            P.dma("sp", cq[:], self.CQ[:, tok].rearrange("(j p) t -> p j t", p=128), reads=[self.CQ], writes=[cq])
            P.dma("sp", ckv[:], self.CKV[:, tok].rearrange("(j p) t -> p j t", p=128), reads=[self.CKV], writes=[ckv])
            P.dma("sp", rra[:], self.RR[0:64, tok], reads=[self.RR], writes=[rra])
            P.dma("sp", rrb[:], self.RR[64:128, tok], reads=[self.RR], writes=[rrb])
            P.dma("sp", cos[:], self.ropec[:, tok], reads=[self.ropec], writes=[cos])
            P.dma("sp", sin[:], self.ropes[:, tok], reads=[self.ropes], writes=[sin])
            for (src, nk, nw, rr_, dst, dim) in ((cq, 8, nwq, rq, cqw, 1024), (ckv, 4, nwkv, rkv, ckvw, 512)):
                P.op("act", lambda e, src=src, nk=nk: e.activation(out=sqb[:, 0:nk, :], in_=src[:], func=AF.Square),
                     reads=[src], writes=[sqb])
                ps = psr.next()
                for kc in range(nk):
                    P.op("pe", lambda e, kc=kc, ps=ps, nk=nk: e.matmul(ps[:], lhsT=ones[:], rhs=sqb[:, kc, :],
                                                                    start=(kc == 0), stop=(kc == nk - 1)),
                         reads=[ones, sqb], writes=[ps], signal=(kc == nk - 1))
                P.op("act", lambda e, ps=ps, dim=dim: e.activation(out=rt[:], in_=ps[:], func=AF.Sqrt,
                                                                   scale=1.0 / dim, bias=epsb[:, 0:1]),
                     reads=[ps, epsb], writes=[rt])
                P.op("dve", lambda e, rr_=rr_: e.reciprocal(out=rr_[:], in_=rt[:]), reads=[rt], writes=[rr_])
                for kc in range(nk):
                    P.op("dve", lambda e, kc=kc, src=src, nw=nw, rr_=rr_, dst=dst: e.scalar_tensor_tensor(
                        out=dst[:, kc, :], in0=src[:, kc, :], scalar=nw[:, kc:kc + 1], in1=rr_[:],
                        op0=ALU.mult, op1=ALU.mult), reads=[src, nw, rr_], writes=[dst])
            for hl in range(8):
                ps = psr.next()
                for kc in range(8):
                    P.op("pe", lambda e, kc=kc, ps=ps, hl=hl: e.matmul(
                        ps[:], lhsT=wq[:, kc, hl * 256:hl * 256 + 128], rhs=cqw[:, kc, :],
                        start=(kc == 0), stop=(kc == 7)), reads=[wq, cqw], writes=[ps], signal=(kc == 7))
                st = stg.next()
                P.op("act", lambda e, ps=ps, st=st: e.activation(out=st[:], in_=ps[:], func=AF.Copy), reads=[ps], writes=[st])
                P.dma("sp", self.QN[hl, :, tok], st[:], reads=[st], writes=[self.QN])
                psA, psB = psr.next(), psr.next()
                for (pp, off) in ((psA, 128), (psB, 192)):
                    for kc in range(8):
                        P.op("pe", lambda e, kc=kc, pp=pp, hl=hl, off=off: e.matmul(
                            pp[0:64, :], lhsT=wq[:, kc, hl * 256 + off:hl * 256 + off + 64], rhs=cqw[:, kc, :],
                            start=(kc == 0), stop=(kc == 7)), reads=[wq, cqw], writes=[pp], signal=(kc == 7))
                rope(psA[0:64, :], psB[0:64, :], [psA], [psB], cos, sin, self.QR[hl, :, tok], self.QR)
                ps = psr.next()
                for kc in range(4):
                    P.op("pe", lambda e, kc=kc, ps=ps, hl=hl: e.matmul(
                        ps[:], lhsT=wkv[:, kc, hl * 128:(hl + 1) * 128], rhs=ckvw[:, kc, :],
                        start=(kc == 0), stop=(kc == 3)), reads=[wkv, ckvw], writes=[ps], signal=(kc == 3))
                st = stg.next()
                P.op("act", lambda e, ps=ps, st=st: e.activation(out=st[:], in_=ps[:], func=AF.Copy), reads=[ps], writes=[st])
                P.dma("sp", self.KN[hl, :, tok], st[:], reads=[st], writes=[self.KN])
            for tt in range(4):
                for hf in range(2):
                    ps = psr.next()
                    for kc in range(4):
                        P.op("pe", lambda e, kc=kc, ps=ps, tt=tt, hf=hf: e.matmul(
                            ps[:], lhsT=ckvw[:, kc, tt * 128:(tt + 1) * 128],
                            rhs=wkv[:, kc, 1024 + hf * 512:1024 + (hf + 1) * 512],
                            start=(kc == 0), stop=(kc == 3)), reads=[wkv, ckvw], writes=[ps], signal=(kc == 3))
                    st = stg.next()
                    P.op("act", lambda e, ps=ps, st=st: e.activation(out=st[:], in_=ps[:], func=AF.Copy), reads=[ps], writes=[st])
                    r0 = tk * 512 + tt * 128
                    P.dma("sp", self.V[r0:r0 + 128, hf * 512:(hf + 1) * 512], st[:], reads=[st], writes=[self.V])
            rope(rra[:], rrb[:], [rra], [rrb], cos, sin, self.KR[:, tok], self.KR)
        P.barrier()
        P.free_from(mark)

    def phase_C(self):
        P, L = self.P, self.L
        mark = len(P._ctx)
        if self.early_gather and "G" in self.phases:
            self.gather(0, 16)
        conv = self.convert_weights() if "D" in self.phases else iter(())
        slot = 0
        psr = self.psr
        NKT = L // 128
        scale = 1.0 / math.sqrt(192.0)
        ones = self.ones
        maskb = P.sb("maskb", [128, 128], BF16)
        P.dma("sp", maskb[:], self.c_maskb[:], reads=[self.c_maskb], writes=[maskb])
        kr = P.sb("kr", [64, L], BF16)
        P.dma("sp", kr[:], self.KR[:, :], reads=[self.KR], writes=[kr])
        qns = Rot([P.sb("qn%d" % i, [128, L], BF16) for i in range(2)])
        qrs = Rot([P.sb("qr%d" % i, [64, L], BF16) for i in range(2)])
        kns = Rot([P.sb("kn%d" % i, [128, L], BF16) for i in range(2)])
        vs = Rot([P.sb("v%d" % i, [128, NKT, 128], BF16) for i in range(2)])
        pts = Rot([P.sb("pt%d" % i, [128, 512], BF16) for i in range(3)])
        gas = Rot([P.sb("ga%d" % i, [128, 512], BF16) for i in range(2)])
        rinv = P.sb("rinv", [128, 512], F32)
        o1 = P.sb("o1", [128, 512], F32)
        stg = Rot([P.sb("stg3_%d" % i, [128, 512], BF16) for i in range(2)])
        acc_r = Rot(self.psb[0:4])
        s_r = Rot(self.psb[4:8])
        for hl in range(8):
            qn, qr, kn, v = qns.next(), qrs.next(), kns.next(), vs.next()
            P.dma("sp", qn[:], self.QN[hl], reads=[self.QN], writes=[qn])
            P.dma("sp", qr[:], self.QR[hl], reads=[self.QR], writes=[qr])
            P.dma("sp", kn[:], self.KN[hl], reads=[self.KN], writes=[kn])
            P.dma("sp", v[:], self.V[:, hl * 128:(hl + 1) * 128].rearrange("(a p) d -> p a d", p=128),
                  reads=[self.V], writes=[v])
            for qi in range(L // 512):
                slot += 1
                if slot % 2 == 0:
                    next(conv, None)
                ga = gas.next()
                P.dma("sp", ga[:], self.GA[hl * 128:(hl + 1) * 128, qi * 512:(qi + 1) * 512], reads=[self.GA], writes=[ga])
                O, Rs = acc_r.next(), acc_r.next()
                nkt = 4 * qi + 4
                def emit_qk(kt, qi=qi, kn=kn, qn=qn, qr=qr):
                    d = kt - 4 * qi
                    q0 = max(d, 0) * 128
                    Sb = s_r.next()
                    qs = slice(qi * 512 + q0, (qi + 1) * 512)
                    ks = slice(kt * 128, (kt + 1) * 128)
                    P.op("pe", lambda e, Sb=Sb, q0=q0, qs=qs, ks=ks, kn=kn, qn=qn: e.matmul(
                        Sb[:, q0:512], lhsT=kn[:, ks], rhs=qn[:, qs], start=True, stop=False),
                        reads=[kn, qn], writes=[Sb], signal=False)
                    P.op("pe", lambda e, Sb=Sb, q0=q0, qs=qs, ks=ks, qr=qr: e.matmul(
                        Sb[:, q0:512], lhsT=kr[:, ks], rhs=qr[:, qs], start=False, stop=True),
                        reads=[kr, qr], writes=[Sb])
                    return Sb, d, q0
                nxt = emit_qk(0)
                for kt in range(nkt):
                    Sb, d, q0 = nxt
                    if kt + 1 < nkt:
                        nxt = emit_qk(kt + 1)
                    pt = pts.next()
                    P.op("act", lambda e, Sb=Sb, pt=pt, q0=q0: e.activation(out=pt[:, q0:512], in_=Sb[:, q0:512],
                                                                         func=AF.Exp, scale=scale),
                         reads=[Sb], writes=[pt])
                    if d >= 0:
                        P.op("dve", lambda e, pt=pt, q0=q0: e.tensor_tensor(out=pt[:, q0:q0 + 128], in0=pt[:, q0:q0 + 128],
                                                                           in1=maskb[:], op=ALU.mult),
                             reads=[pt, maskb], writes=[pt])
                    P.op("pe", lambda e, O=O, pt=pt, q0=q0, kt=kt, nkt=nkt, v=v: e.matmul(
                        O[:, q0:512], lhsT=v[:, kt, :], rhs=pt[:, q0:512], start=(kt == 0), stop=(kt == nkt - 1)),
                        reads=[v, pt], writes=[O], signal=False)
                    P.op("pe", lambda e, Rs=Rs, pt=pt, q0=q0, kt=kt, nkt=nkt: e.matmul(
                        Rs[:, q0:512], lhsT=ones[:], rhs=pt[:, q0:512], start=(kt == 0), stop=(kt == nkt - 1)),
                        reads=[ones, pt], writes=[Rs])
                P.op("dve", lambda e, Rs=Rs: e.reciprocal(out=rinv[:], in_=Rs[:]), reads=[Rs], writes=[rinv])
                P.op("dve", lambda e, O=O: e.tensor_tensor(out=o1[:], in0=O[:], in1=rinv[:], op=ALU.mult),
                     reads=[O, rinv], writes=[o1])
                st = stg.next()
                P.op("dve", lambda e, st=st, ga=ga: e.tensor_tensor(out=st[:], in0=o1[:], in1=ga[:], op=ALU.mult),
                     reads=[o1, ga], writes=[st])
                self.yc_store(16 + hl, 1, qi * 512, 512, lambda lo, hi, st=st: st[:, lo:hi].unsqueeze(1), st)
            if self.early_gather and "G" in self.phases:
                self.gather(16 + hl, 17 + hl, track=True)
                if hl == 4 and "D" in self.phases:
                    self.yq_copy(range(0, 16, 2), 64)
        for _ in conv:
            pass
        P.barrier()
        P.free_from(mark)


    def phase_B(self):
        P, L = self.P, self.L
        mark = len(P._ctx)
        psr = self.psr
        NCH = L // 128
        ident, ones = self.ident, self.ones

        def cload(name, src, shape, dt):
            b = P.sb(name, shape, dt)
            P.dma("sp", b[:], src[:], reads=[src], writes=[b])
            return b
        trile = cload("trile", self.c_trile, [128, 128], F32)
        trigt = cload("trigt", self.c_trigt, [128, 128], F32)
        onesf = cload("onesf", self.c_onesf, [128, 128], F32)
        maskb = cload("maskbB", self.c_maskb, [128, 128], BF16)
        Abc = cload("Abc", self.alog, [128, 32], F32)
        dsk = cload("dsk", self.dskip, [128, 32], F32)
        nws = cload("nws", self.nw_ssm, [128, 2048], F32)
        P.op("act", lambda e: e.activation(out=Abc[:], in_=Abc[:], func=AF.Exp), reads=[Abc], writes=[Abc])
        P.op("dve", lambda e: e.tensor_scalar(out=Abc[:], in0=Abc[:], scalar1=-1.0, scalar2=None, op0=ALU.mult),
             reads=[Abc], writes=[Abc])
        epsb = P.sb("epsbB", [128, 1], F32)
        P.op("pool", lambda e: e.memset(epsb[:], EPS), writes=[epsb])
        h = P.sb("h", [128, 2048], F32)
        hb = P.sb("hb", [128, 2048], BF16)
        P.op("pool", lambda e: e.memset(h[:], 0.0), writes=[h])
        P.op("pool", lambda e: e.memset(hb[:], 0.0), writes=[hb])
        xT4s = Rot([P.sb("xT4_%d" % i, [128, 20, 512], BF16) for i in range(2)])
        dt4s = Rot([P.sb("dt4_%d" % i, [128, 4, 32], F32) for i in range(2)])
        szs = Rot([P.sb("sz%d" % i, [128, 2048], BF16) for i in range(2)])
        xtm = P.sb("xtm", [128, 2048], BF16)
        btm = P.sb("btm", [128, 256], BF16)
        sm = {n: P.sb("sm_" + n, [128, 32], F32) for n in ("a", "acs", "ea", "cd", "dsd", "ds", "w2")}
        R = P.sb("R", [128, 32, 128], F32)
        E = P.sb("E", [128, 32, 128], BF16)
        cbm = P.sb("cbm", [128, 2, 128], BF16)
        S = P.sb("S", [128, 32, 128], BF16)
        xdt = P.sb("xdt", [128, 2048], BF16)
        xdd = P.sb("xdd", [128, 2048], BF16)
        xD = P.sb("xD", [128, 2048], BF16)
        y = P.sb("y", [128, 2048], F32)
        junk = P.sb("junk", [128, 1024], BF16)
        ssq = P.sb("ssq", [128, 2], F32)
        rtn = P.sb("rtn", [128, 2], F32)
        rsn = P.sb("rsn", [128, 2], F32)
        ytm = P.sb("ytm", [128, 2048], BF16)
        yT4 = P.sb("yT4", [128, 16, 512], BF16)

        def pbf(bank):
            return bank.ap.bitcast(BF16)

        xT4 = dt4 = None
        for c in range(NCH):
            cc = c % 4
            csl = slice(cc * 128, (cc + 1) * 128)
            if cc == 0:
                xT4, dt4 = xT4s.next(), dt4s.next()
                tok = slice(c * 128, c * 128 + 512)
                P.dma("sp", xT4[:], self.XC[:, tok].rearrange("(j p) t -> p j t", p=128), reads=[self.XC], writes=[xT4])
                P.dma("sp", dt4[:], self.DT[tok, :].rearrange("(a p) h -> p a h", p=128), reads=[self.DT], writes=[dt4])
            sz = szs.next()
            P.dma("sp", sz[:], self.SZ[c * 128:(c + 1) * 128, :], reads=[self.SZ], writes=[sz])
            dtc = dt4[:, cc, :]
            for i in range(2):
                bank = psr.next()
                pb = pbf(bank)
                for jj in range(8):
                    j = i * 8 + jj
                    P.op("pe", lambda e, pb=pb, jj=jj, j=j, xT4=xT4, csl=csl: e.transpose(
                        pb[:, jj * 128:(jj + 1) * 128], xT4[:, j, csl], ident[:]),
                        reads=[xT4, ident], writes=[bank], signal=(jj == 7))
                P.op("act", lambda e, pb=pb, i=i: e.activation(out=xtm[:, i * 1024:(i + 1) * 1024], in_=pb[:, 0:1024], func=AF.Copy),
                     reads=[bank], writes=[xtm])
            bank = psr.next()
            pb = pbf(bank)
            for g2 in range(2):
                P.op("pe", lambda e, pb=pb, g2=g2, xT4=xT4, csl=csl: e.transpose(
                    pb[:, g2 * 128:(g2 + 1) * 128], xT4[:, 16 + g2, csl], ident[:]),
                    reads=[xT4, ident], writes=[bank], signal=(g2 == 1))
            P.op("act", lambda e, pb=pb: e.activation(out=btm[:], in_=pb[:, 0:256], func=AF.Copy), reads=[bank], writes=[btm])
            a, acs, ea, cd, dsd, ds_, w2 = (sm[n] for n in ("a", "acs", "ea", "cd", "dsd", "ds", "w2"))
            P.op("dve", lambda e, dtc=dtc: e.tensor_tensor(out=a[:], in0=dtc, in1=Abc[:], op=ALU.mult),
                 reads=[dt4, Abc], writes=[a])
            bank = psr.next()
            P.op("pe", lambda e, bank=bank: e.matmul(bank[:, 0:32], lhsT=trile[:], rhs=a[:], start=True, stop=True),
                 reads=[trile, a], writes=[bank], signal=False)
            P.op("pe", lambda e, bank=bank: e.matmul(bank[:, 32:64], lhsT=onesf[:], rhs=a[:], start=True, stop=True),
                 reads=[onesf, a], writes=[bank])
            P.op("act", lambda e, bank=bank: e.activation(out=acs[:], in_=bank[:, 0:32], func=AF.Copy), reads=[bank], writes=[acs])
            P.op("act", lambda e, bank=bank: e.activation(out=ea[:], in_=bank[:, 0:32], func=AF.Exp), reads=[bank], writes=[ea])
            P.op("act", lambda e, bank=bank: e.activation(out=cd[:], in_=bank[:, 32:64], func=AF.Exp), reads=[bank], writes=[cd])
            P.op("dve", lambda e, bank=bank: e.tensor_tensor(out=dsd[:], in0=bank[:, 32:64], in1=acs[:], op=ALU.subtract),
                 reads=[bank, acs], writes=[dsd])
            P.op("act", lambda e: e.activation(out=ds_[:], in_=dsd[:], func=AF.Exp), reads=[dsd], writes=[ds_])
            P.op("dve", lambda e, dtc=dtc: e.tensor_tensor(out=w2[:], in0=dtc, in1=ds_[:], op=ALU.mult),
                 reads=[dt4, ds_], writes=[w2])
            P.op("pool", lambda e: e.tensor_tensor(out=R[:], in0=a[:].unsqueeze(2).to_broadcast([128, 32, 128]),
                                                  in1=trile[:].unsqueeze(1).to_broadcast([128, 32, 128]), op=ALU.mult),
                 reads=[a, trile], writes=[R])
            for j in range(8):
                bank = psr.next()
                P.op("pe", lambda e, bank=bank, j=j: e.matmul(
                    bank[:], lhsT=trigt[:], rhs=R[:, 4 * j:4 * j + 4, :].rearrange("p a b -> p (a b)"), start=True, stop=True),
                    reads=[trigt, R], writes=[bank])
                P.op("act", lambda e, bank=bank, j=j: e.activation(
                    out=E[:, 4 * j:4 * j + 4, :].rearrange("p a b -> p (a b)"), in_=bank[:], func=AF.Exp),
                    reads=[bank], writes=[E])
            bank = psr.next()
            for g2 in range(2):
                P.op("pe", lambda e, bank=bank, g2=g2, xT4=xT4, csl=csl: e.matmul(
                    bank[:, g2 * 128:(g2 + 1) * 128], lhsT=xT4[:, 16 + g2, csl], rhs=xT4[:, 18 + g2, csl], start=True, stop=True),
                    reads=[xT4], writes=[bank], signal=(g2 == 1))
            P.op("dve", lambda e, bank=bank: e.tensor_tensor(
                out=cbm[:], in0=bank[:, 0:256].rearrange("p (g l) -> p g l", g=2),
                in1=maskb[:].unsqueeze(1).to_broadcast([128, 2, 128]), op=ALU.mult),
                reads=[bank, maskb], writes=[cbm])
            P.op("dve", lambda e: e.tensor_tensor(
                out=S[:].rearrange("p (g r) l -> p g r l", g=2), in0=E[:].rearrange("p (g r) l -> p g r l", g=2),
                in1=cbm[:].unsqueeze(2).to_broadcast([128, 2, 16, 128]), op=ALU.mult),
                reads=[E, cbm], writes=[S])
            x3 = xtm[:].rearrange("p (h d) -> p h d", d=64)
            P.op("dve", lambda e, dtc=dtc, x3=x3: e.tensor_tensor(
                out=xdt[:].rearrange("p (h d) -> p h d", d=64), in0=x3,
                in1=dtc.unsqueeze(2).to_broadcast([128, 32, 64]), op=ALU.mult), reads=[xtm, dt4], writes=[xdt])
            P.op("pool", lambda e, x3=x3: e.tensor_tensor(
                out=xdd[:].rearrange("p (h d) -> p h d", d=64), in0=x3,
                in1=w2[:].unsqueeze(2).to_broadcast([128, 32, 64]), op=ALU.mult), reads=[xtm, w2], writes=[xdd])
            P.op("pool", lambda e, x3=x3: e.tensor_tensor(
                out=xD[:].rearrange("p (h d) -> p h d", d=64), in0=x3,
                in1=dsk[:].unsqueeze(2).to_broadcast([128, 32, 64]), op=ALU.mult), reads=[xtm, dsk], writes=[xD])
            for q in range(4):
                g2 = q // 2
                yo = psr.next()
                P.op("pe", lambda e, yo=yo, q=q, g2=g2, xT4=xT4, csl=csl: e.matmul(
                    yo[:], lhsT=xT4[:, 18 + g2, csl], rhs=hb[:, q * 512:(q + 1) * 512], start=True, stop=True),
                    reads=[xT4, hb], writes=[yo])
                yd = psr.next()
                P.op("pe", lambda e, yd=yd, q=q: e.matmul(yd[:], lhsT=ident[:], rhs=xD[:, q * 512:(q + 1) * 512],
                                                         start=True, stop=False),
                     reads=[ident, xD], writes=[yd], signal=False)
                for hh in range(8):
                    hd = q * 8 + hh
                    P.op("pe", lambda e, yd=yd, hh=hh, hd=hd: e.matmul(
                        yd[:, hh * 64:(hh + 1) * 64], lhsT=S[:, hd, :], rhs=xdt[:, hd * 64:(hd + 1) * 64],
                        start=False, stop=(hh == 7)), reads=[S, xdt], writes=[yd], signal=(hh == 7))
                ysl = y[:, q * 512:(q + 1) * 512]
                P.op("dve", lambda e, yo=yo, q=q, ysl=ysl: e.tensor_tensor(
                    out=ysl.rearrange("p (h d) -> p h d", d=64), in0=yo[:].rearrange("p (h d) -> p h d", d=64),
                    in1=ea[:, q * 8:(q + 1) * 8].unsqueeze(2).to_broadcast([128, 8, 64]), op=ALU.mult),
                    reads=[yo, ea], writes=[y])
                P.op("dve", lambda e, yd=yd, ysl=ysl: e.tensor_tensor(out=ysl, in0=yd[:], in1=ysl, op=ALU.add),
                     reads=[yd, y], writes=[y])
            P.op("dve", lambda e, sz=sz: e.tensor_tensor(out=y[:], in0=y[:], in1=sz[:], op=ALU.mult), reads=[y, sz], writes=[y])
            for g2 in range(2):
                P.op("act", lambda e, g2=g2: e.activation(out=junk[:], in_=y[:, g2 * 1024:(g2 + 1) * 1024], func=AF.Square,
                                                          accum_out=ssq[:, g2:g2 + 1]), reads=[y], writes=[junk, ssq])
            P.op("act", lambda e: e.activation(out=rtn[:], in_=ssq[:], func=AF.Sqrt, scale=1.0 / 1024, bias=epsb[:, 0:1]),
                 reads=[ssq, epsb], writes=[rtn])
            P.op("dve", lambda e: e.reciprocal(out=rsn[:], in_=rtn[:]), reads=[rtn], writes=[rsn])
            for g2 in range(2):
                gs = slice(g2 * 1024, (g2 + 1) * 1024)
                P.op("dve", lambda e, g2=g2, gs=gs: e.scalar_tensor_tensor(
                    out=ytm[:, gs], in0=y[:, gs], scalar=rsn[:, g2:g2 + 1], in1=nws[:, gs], op0=ALU.mult, op1=ALU.mult),
                    reads=[y, rsn, nws], writes=[ytm])
            for i in range(2):
                bank = psr.next()
                pb = pbf(bank)
                for jj in range(8):
                    j = i * 8 + jj
                    P.op("pe", lambda e, pb=pb, jj=jj, j=j: e.transpose(
                        pb[:, jj * 128:(jj + 1) * 128], ytm[:, j * 128:(j + 1) * 128], ident[:]),
                        reads=[ytm, ident], writes=[bank], signal=(jj == 7))
                P.op("act", lambda e, pb=pb, i=i, csl=csl: e.activation(
                    out=yT4[:, i * 8:(i + 1) * 8, csl], in_=pb[:, 0:1024].rearrange("p (j t) -> p j t", j=8), func=AF.Copy),
                    reads=[bank], writes=[yT4])
            if cc == 3:
                self.yc_store(0, 16, (c - 3) * 128, 512, lambda lo, hi: yT4[:, :, lo:hi], yT4)
            P.op("dve", lambda e: e.tensor_tensor(
                out=h[:].rearrange("p (h d) -> p h d", d=64), in0=h[:].rearrange("p (h d) -> p h d", d=64),
                in1=cd[:].unsqueeze(2).to_broadcast([128, 32, 64]), op=ALU.mult), reads=[h, cd], writes=[h])
            for q in range(4):
                g2 = q // 2
                st = psr.next()
                P.op("pe", lambda e, st=st, q=q, g2=g2: e.matmul(
                    st[:], lhsT=btm[:, g2 * 128:(g2 + 1) * 128], rhs=xdd[:, q * 512:(q + 1) * 512], start=True, stop=True),
                    reads=[btm, xdd], writes=[st])
                P.op("dve", lambda e, st=st, q=q: e.tensor_tensor(
                    out=h[:, q * 512:(q + 1) * 512], in0=st[:], in1=h[:, q * 512:(q + 1) * 512], op=ALU.add),
                    reads=[st, h], writes=[h])
            P.op("act", lambda e: e.activation(out=hb[:], in_=h[:], func=AF.Copy), reads=[h], writes=[hb])
        P.barrier()
        P.free_from(mark)

    def gather(self, jt0, jt1, track=False):
        P = self.P
        if "cc" not in P.sems:
            P._mksem("cc")
        sem = P.sems["cc"]
        YC, YG = self.YC, self.YG
        waits = []
        if track:
            need = P._deps("pool", [YC], [])
            waits = [(P.sems[k], v) for k, v in need.items()]

        def thunk(e):
            for (s_, v) in waits:
                e.wait_ge(s_, v)
            for jt in range(jt0, jt1):
                for tq in range(4):
                    e.collective_compute("AllGather", ALU.bypass, replica_groups=[[0, 1, 2, 3], [4, 5, 6, 7]],
                                         ins=[YC.ap[jt, tq]], outs=[YG.ap[jt, tq].rearrange("r p t -> (r p) t")]).then_inc(sem)
        P.streams["pool"].append(thunk)
        P.semval["cc"] = P.semval.get("cc", 0) + 4 * (jt1 - jt0)

    def phase_G(self):
        P = self.P
        markG = len(P._ctx)
        P.barrier()
        early = self.early_gather
        if not early:
            self.gather(0, 16)
            self.gather(16, 24)
        sem = P.sems["cc"]
        YC, YG = self.YC, self.YG

        def thunk(e):
            n = 96
            for jt in range(0):
                for tq in range(4):
                    e.collective_compute("AllGather", ALU.bypass, replica_groups=[[0, 1, 2, 3], [4, 5, 6, 7]],
                                         ins=[YC.ap[jt, tq]], outs=[YG.ap[jt, tq].rearrange("r p t -> (r p) t")]).then_inc(sem)
                    n += 1
            e.wait_ge(sem, n)
        P.streams["pool"].append(thunk)
        P.barrier()
        for nm, srcb in (("YCd", YC), ("YGd", YG)):
            if nm in self.debug:
                flat = srcb.ap[:].flatten_outer_dims()
                nrow = flat.shape[0]
                d = P.dram(nm, [nrow, self.LS], BF16, kind="ExternalOutput")
                self.out_bufs.append(d)
                tmp = P.sb("dbg" + nm, [128, self.LS], BF16)
                for r in range(nrow // 128):
                    P.dma("sp", tmp[:], flat[r * 128:(r + 1) * 128, :], reads=[srcb], writes=[tmp])
                    P.dma("sp", d.ap[r * 128:(r + 1) * 128, :], tmp[:], reads=[tmp], writes=[d])
                P.barrier()
        P.free_from(markG)

    def phase_D(self):
        P, L, LS = self.P, self.L, self.LS
        mark = len(P._ctx)
        psr = self.psr
        TB = min(256, LS)
        NT = TB // 128
        uT = P.sb("uTD", [128, 32, TB], BF16)
        xs = [P.sb("xsD%d" % i, [128, 32, 128], F32) for i in range(1)]
        sq = P.sb("sqD", [128, 32, 128], BF16)
        rt = P.sb("rtD", [128, 128], F32)
        rs = P.sb("rsD", [128, 128], F32)
        win = P.sb("winD", [128, 32], F32)
        self.epsb = P.sb("epsbD", [128, 1], F32)
        P.op("pool", lambda e, t=self.epsb: e.memset(t[:], EPS), writes=[self.epsb])
        P.dma("sp", win[:], self.nw_in[:], reads=[self.nw_in], writes=[win])
        wts = Rot([P.sb("wtD%d" % i, [128, 32, 512], BF16) for i in range(2)])
        sg = [P.sb("sg%d" % i, [128, 32, TB], BF16) for i in range(2)]
        yT = P.sb("yTD", [128, 64, TB], BF16)
        tmpf = Rot([P.sb("tmpf%d" % i, [128, TB], F32) for i in range(2)])
        hrow = P.sb("hrow", [128, 4096], F32)
        xr = Rot([P.sb("xr%d" % i, [128, 512], F32) for i in range(2)])
        nwf = P.sb("nwf", [128, 4096], F32)
        P.dma("sp", nwf[:], self.nw_fin[:], reads=[self.nw_fin], writes=[nwf])
        ssq = P.sb("ssqD", [128, 1], F32)
        rtn = P.sb("rtnD", [128, 1], F32)
        rsn = P.sb("rsnD", [128, 1], F32)
        YG = self.YG
        tokv = {}
        if self.early_gather:
            self.yq_copy(range(16, 24, 2), 96)
        else:
            self.yq_copy(range(0, 24, 2), 96)

        def yg_load(dst_ap, row0, nj, bk):
            def thunk_fn(e):
                gidx = e.partition_id() % 4
                src = YG.ap[row0:row0 + nj * 128, bass.ds(gidx * LS + bk * TB, TB)].rearrange("(j p) t -> p j t", p=128)
                return src
            return thunk_fn

        for bk in range(LS // TB):
            self.make_uT(lambda s, bk=bk: (self.xTs[:, :, bk * TB + s * 128:bk * TB + (s + 1) * 128], self.xTs),
                         NT, uT, xs, sq, rt, rs, win, psr)
            for m in range(2):
                for ct in range(8):
                    wt = wts.next()
                    P.dma("sp", wt[:], self.WD.ap[m * 8 + ct].rearrange("p (k c) -> p k c", k=32), reads=[self.WD], writes=[wt])
                    for cs in range(4):
                        ps = psr.next()
                        for kc in range(32):
                            P.op("pe", lambda e, kc=kc, ps=ps, cs=cs, wt=wt: e.matmul(
                                ps[:, 0:TB], lhsT=wt[:, kc, cs * 128:(cs + 1) * 128], rhs=uT[:, kc, :],
                                start=(kc == 0), stop=(kc == 31)), reads=[uT, wt], writes=[ps], signal=(kc == 31))
                        P.op("act", lambda e, ps=ps, m=m, ct=ct, cs=cs: e.activation(
                            out=sg[m][:, ct * 4 + cs, :], in_=ps[:, 0:TB], func=AF.Sigmoid), reads=[ps], writes=[sg[m]])
            for m in range(2):
                nkh = 2 if m == 0 else 1
                for r in range(4):
                    row0 = 0 if m == 0 else 16
                    nj = 16 if m == 0 else 8
                    P.dma("sp", yT[:, r * nj:(r + 1) * nj, :],
                          self.YQ.ap[row0:row0 + nj, r, :, bk * TB:(bk + 1) * TB].rearrange("j p t -> p j t"),
                          reads=[self.YQ], writes=[yT])
                for ct in range(8):
                    w_list = []
                    for kh in range(nkh):
                        wt = wts.next()
                        srcw = self.WD.ap[(16 + kh * 8 + ct) if m == 0 else (32 + ct)].rearrange("p (k c) -> p k c", k=32)
                        sbuf_ = self.WD
                        w_list.append((wt, srcw, sbuf_))
                    pss = [psr.next() for _ in range(4)]
                    for kh, (wt, srcw, sbuf_) in enumerate(w_list):
                        P.dma("sp", wt[:], srcw, reads=[sbuf_], writes=[wt])
                        for cs in range(4):
                            ps = pss[cs]
                            for kc in range(32):
                                first = (kh == 0 and kc == 0)
                                last = (kh == nkh - 1 and kc == 31)
                                P.op("pe", lambda e, kc=kc, ps=ps, cs=cs, wt=wt, kh=kh, first=first, last=last: e.matmul(
                                    ps[:, 0:TB], lhsT=wt[:, kc, cs * 128:(cs + 1) * 128], rhs=yT[:, kh * 32 + kc, :],
                                    start=first, stop=last), reads=[yT, wt], writes=[ps], signal=(kc == 31))
                    for cs in range(4):
                        ps = pss[cs]
                        ci = ct * 4 + cs
                        if m == 0:
                            P.op("dve", lambda e, ps=ps, ci=ci: e.tensor_tensor(out=sg[0][:, ci, :], in0=ps[:, 0:TB],
                                                                               in1=sg[0][:, ci, :], op=ALU.mult),
                                 reads=[ps, sg[0]], writes=[sg[0]])
                        else:
                            tf = tmpf.next()
                            P.op("dve", lambda e, ps=ps, ci=ci, tf=tf: e.tensor_tensor(out=tf[:], in0=ps[:, 0:TB],
                                                                                     in1=sg[1][:, ci, :], op=ALU.mult),
                                 reads=[ps, sg[1]], writes=[tf])
                            P.op("pool", lambda e, ci=ci, tf=tf: e.tensor_tensor(out=sg[1][:, ci, :], in0=tf[:],
                                                                                in1=sg[0][:, ci, :], op=ALU.add),
                                 reads=[tf, sg[0], sg[1]], writes=[sg[1]])
            mT = sg[1]
            for tt in range(NT):
                r0 = bk * TB + tt * 128
                for ct in range(8):
                    xrow = xr.next()
                    P.dma("sp", xrow[:], self.xres[r0:r0 + 128, ct * 512:(ct + 1) * 512], reads=[self.xres], writes=[xrow])
                    wt = wts.next()
                    P.dma("sp", wt[:], self.WD.ap[40 + ct].rearrange("p (k c) -> p k c", k=32), reads=[self.WD], writes=[wt])
                    ps = psr.next()
                    for kc in range(32):
                        P.op("pe", lambda e, kc=kc, ps=ps, tt=tt, wt=wt: e.matmul(
                            ps[:], lhsT=mT[:, kc, tt * 128:(tt + 1) * 128], rhs=wt[:, kc, :],
                            start=(kc == 0), stop=(kc == 31)), reads=[mT, wt], writes=[ps], signal=(kc == 31))
                    P.op("dve", lambda e, ps=ps, ct=ct, xrow=xrow: e.tensor_tensor(
                        out=hrow[:, ct * 512:(ct + 1) * 512], in0=ps[:], in1=xrow[:], op=ALU.add),
                        reads=[ps, xrow], writes=[hrow])
                P.op("act", lambda e: e.activation(out=xs[0][:].rearrange("p a b -> p (a b)"), in_=hrow[:], func=AF.Square,
                                                   accum_out=ssq[:]), reads=[hrow], writes=[xs[0], ssq])
                P.op("act", lambda e, epsb=self.epsb: e.activation(out=rtn[:], in_=ssq[:], func=AF.Sqrt, scale=1.0 / 4096,
                                                                   bias=epsb[:, 0:1]),
                     reads=[ssq, self.epsb], writes=[rtn])
                P.op("dve", lambda e: e.reciprocal(out=rsn[:], in_=rtn[:]), reads=[rtn], writes=[rsn])
                P.op("dve", lambda e: e.scalar_tensor_tensor(out=hrow[:], in0=hrow[:], scalar=rsn[:, 0:1], in1=nwf[:],
                                                             op0=ALU.mult, op1=ALU.mult), reads=[hrow, rsn, nwf], writes=[hrow])
                P.dma("sp", self.OUT[r0:r0 + 128, :], hrow[:], reads=[hrow], writes=[self.OUT])
        P.barrier()
        P.free_from(mark)

    def finish(self):
        P = self.P
        P.barrier(include_bg=True)
        P.emit()
        P.close()
        return self.nc


def build_program(L, phases="A", debug=()):
    k = K(L, debug, phases)
    k.declare()
    k.load_consts()
    if "A" in phases:
        k.phase_A()
    if "M" in phases:
        k.phase_A2()
    if "B" in phases:
        k.phase_B()
    if "C" in phases:
        k.phase_C()
    if "G" in phases:
        k.phase_G()
    if "D" in phases:
        k.phase_D()
    return k


def _tile_w(w, cols=None):
    if cols is not None:
        wz = np.zeros((w.shape[0], len(cols)), np.float32)
        m = cols >= 0
        wz[:, m] = w[:, cols[m]]
        w = wz
    Kd, N = w.shape
    return np.ascontiguousarray(w.reshape(Kd // 128, 128, N // 512, 512).transpose(2, 1, 0, 3))


def _in_cols(g):
    c = []
    c += list(range(g * 2048, (g + 1) * 2048))
    c += list(range(8192 + g * 2048, 8192 + (g + 1) * 2048))
    c += list(range(16384 + g * 256, 16384 + (g + 1) * 256))
    c += list(range(17408 + g * 256, 17408 + (g + 1) * 256))
    c += list(range(18560, 18560 + 1024))
    c += list(range(19584, 19584 + 512))
    c += list(range(20160 + g * 1024, 20160 + (g + 1) * 1024))
    rope = list(range(20096, 20160))
    c += rope + rope[32:] + rope[:32]
    c += list(range(18432 + g * 32, 18432 + (g + 1) * 32))
    c += [-1] * (15 * 512 - len(c))
    return np.array(c, np.int64)


def prepare_inputs(inp, L):
    f = lambda a: np.asarray(a, np.float32)
    x = f(inp["x"])[:, :L]
    NB, LS = L // 1024, L // 4
    w_in = f(inp["w_in"])[0]
    conv_w, conv_b = f(inp["conv_w"])[0], f(inp["conv_b"])[0]
    w_uq, w_ukv = f(inp["w_uq"])[0], f(inp["w_ukv"])[0]
    shared = {}
    shared["w_gates"] = _tile_w(w_in[:, 24256:24256 + 8192])
    wbs = f(inp["w_branch_ssm"])[0]
    shared["w_bs"] = np.concatenate([_tile_w(wbs[:4096]), _tile_w(wbs[4096:])], 0)
    shared["w_ba"] = _tile_w(f(inp["w_branch_attn"])[0])
    shared["w_out"] = _tile_w(f(inp["w_out"])[0])
    shared["nw_in"] = np.ascontiguousarray(f(inp["norm_in_w"])[0].reshape(32, 128).T)
    shared["nw_q"] = np.ascontiguousarray(f(inp["q_norm_w"])[0].reshape(8, 128).T)
    shared["nw_kv"] = np.ascontiguousarray(f(inp["kv_norm_w"])[0].reshape(4, 128).T)
    shared["nw_fin"] = np.ascontiguousarray(np.broadcast_to(f(inp["norm_final_w"])[None, :], (128, 4096)))
    half = 32
    inv_freq = (np.float32(10000.0) ** (-(np.arange(0, half, dtype=np.float32) / np.float32(half)))).astype(np.float32)
    ang = (np.arange(L, dtype=np.float32)[None, :] * inv_freq[:, None]).astype(np.float32)
    cos, sin = np.cos(ang).astype(np.float32), np.sin(ang).astype(np.float32)
    shared["ropec"] = np.concatenate([cos, cos], 0)
    shared["ropes"] = np.concatenate([-sin, sin], 0)
    shared["c_ident"] = np.eye(128, dtype=np.float32).astype(ml_dtypes.bfloat16)
    shared["c_ones"] = np.ones((128, 128), ml_dtypes.bfloat16)
    shared["c_onesf"] = np.ones((128, 128), np.float32)
    k = np.arange(128)
    shared["c_trile"] = (k[:, None] <= k[None, :]).astype(np.float32)
    shared["c_trigt"] = (k[:, None] > k[None, :]).astype(np.float32)
    shared["c_maskb"] = (k[:, None] <= k[None, :]).astype(np.float32).astype(ml_dtypes.bfloat16)
    per_g = []
    for g in range(4):
        d = {}
        d["w_in"] = _tile_w(w_in, _in_cols(g))
        cols = []
        for hl in range(8):
            H = 8 * g + hl
            base = H * 192
            rope = list(range(base + 128, base + 192))
            cols += list(range(base, base + 128)) + rope + rope[32:] + rope[:32]
        d["w_uq"] = np.ascontiguousarray(w_uq[:, cols].reshape(8, 128, 2048).transpose(1, 0, 2))
        cols = []
        for hl in range(8):
            H = 8 * g + hl
            cols += list(range(H * 256, H * 256 + 128))
        for hl in range(8):
            H = 8 * g + hl
            cols += list(range(H * 256 + 128, H * 256 + 256))
        d["w_ukv"] = np.ascontiguousarray(w_ukv[:, cols].reshape(4, 128, 2048).transpose(1, 0, 2))
        ch = np.concatenate([np.arange(g * 2048, (g + 1) * 2048),
                             8192 + g * 256 + np.arange(256), 8192 + 1024 + g * 256 + np.arange(256)])
        cpar = np.concatenate([conv_w[:, ch], conv_b[None, ch]], 0)
        d["convp"] = np.ascontiguousarray(cpar.reshape(5, 20, 128).transpose(2, 1, 0))
        hs = slice(g * 32, (g + 1) * 32)
        bc = lambda v: np.ascontiguousarray(np.broadcast_to(v[None, :], (128, v.shape[0])))
        d["dtb"] = bc(f(inp["dt_bias"])[0, hs])
        d["alog"] = bc(f(inp["a_log"])[0, hs])
        d["dskip"] = bc(f(inp["d_skip"])[0, hs])
        d["nw_ssm"] = bc(f(inp["ssm_norm_w"])[0, g * 2048:(g + 1) * 2048])
        per_g.append(d)
    maps = []
    for c in range(8):
        b, g = c // 4, c % 4
        m = dict(shared)
        m.update(per_g[g])
        xb = x[b]
        m["xT"] = np.ascontiguousarray(xb.reshape(NB, 1024, 32, 128).transpose(0, 3, 2, 1))
        xsl = xb[g * LS:(g + 1) * LS]
        m["xTs"] = np.ascontiguousarray(xsl.reshape(LS, 32, 128).transpose(2, 1, 0))
        m["xres"] = np.ascontiguousarray(xsl)
        maps.append(m)
    return maps


_CACHE = {}


def run(inputs, L=SEQ, phases="AMBCGD", debug=()):
    import time
    t0 = time.time()
    key = (L, phases, tuple(debug))
    if key not in _CACHE:
        k = build_program(L, phases, debug)
        nc = k.finish()
        _CACHE[key] = (nc, set(k.inp.keys()), k.P.ninstr)
    nc, names, ninstr = _CACHE[key]
    t1 = time.time()
    maps = prepare_inputs(inputs, L)
    maps = [{n: m[n] for n in names} for m in maps]
    t2 = time.time()
    res = run_bass_kernel_spmd(nc, maps, core_ids=list(range(8)))
    if os.environ.get("KDBG"):
        print("ninstr %d build %.1fs prep %.1fs run %.1fs" % (ninstr, t1 - t0, t2 - t1, time.time() - t2), flush=True)
    return res.results


def kernel(**inputs):
    res = run(inputs)
    LS = SEQ // 4
    out = np.empty((2, SEQ, D_MODEL), np.float32)
    for c in range(8):
        b, g = c // 4, c % 4
        out[b, g * LS:(g + 1) * LS] = res[c]["out"]
    return out
```

```python
import os
import math
import numpy as np
import ml_dtypes
import concourse.bass as bass
import concourse.mybir as mybir
from concourse.bass_utils import run_bass_kernel_spmd

F32 = mybir.dt.float32
BF16 = mybir.dt.bfloat16
AF = mybir.ActivationFunctionType
ALU = mybir.AluOpType
EPS = 1e-6

D_MODEL = 4096
D_SSM = 8192
D_CONV = 10240
SEQ = 8192


class Buf:
    def __init__(self, name, ap, accum=False):
        self.name = name
        self.ap = ap
        self.ws = {}
        self.rs = {}
        self.accum = accum
        self.dsem = None
        self.dcount = 0

    def __getitem__(self, idx):
        return self.ap[idx]


class Prog:
    ENGS = ("pe", "act", "dve", "pool", "sp")

    def __init__(self, nc):
        self.nc = nc
        self.streams = {e: [] for e in self.ENGS}
        self.cnt = {e: 0 for e in self.ENGS}
        self.sems = {}
        self.semval = {}
        self._ctx = []
        self._semctx = []
        self.pending = {e: [] for e in self.ENGS}
        self.waited = {e: {} for e in self.ENGS}
        self.bg_keys = set()
        self.nsem = 0
        self.ninstr = 0
        for e in ("pe", "act", "dve", "pool"):
            self._mksem(e)

    def _mksem(self, key):
        self.nsem += 1
        g = self.nc.semaphore("sem%d" % self.nsem)
        h = g.__enter__()
        self._semctx.append(g)
        self.sems[key] = h
        self.semval[key] = 0
        return h

    def sb(self, name, shape, dt):
        g = self.nc.sbuf_tensor("sb_" + name, list(shape), dt)
        t = g.__enter__()
        self._ctx.append(g)
        return Buf(name, t)

    def ps(self, name, shape, dt=F32):
        g = self.nc.psum_tensor(name, list(shape), dt)
        t = g.__enter__()
        self._ctx.append(g)
        return Buf(name, t)

    def dram(self, name, shape, dt, kind="Internal"):
        t = self.nc.dram_tensor(name, list(shape), dt, kind=kind)
        return Buf(name, t, accum=True)

    def _deps(self, eng, reads, writes):
        need = {}

        def add(k, v):
            if k == "pe" and eng == "pe":
                return
            if need.get(k, 0) < v:
                need[k] = v
        for b in reads:
            for k, v in b.ws.items():
                add(k, v)
        for b in writes:
            if b.accum:
                continue
            for k, v in b.ws.items():
                add(k, v)
            for k, v in b.rs.items():
                add(k, v)
        w = self.waited[eng]
        need = {k: v for k, v in need.items() if w.get(k, 0) < v}
        for k, v in need.items():
            w[k] = v
        return need

    @staticmethod
    def _rec(tok, reads, writes):
        k, v = tok
        for b in reads:
            b.rs[k] = v
        for b in writes:
            if b.accum:
                b.ws[k] = v
            else:
                b.ws = {k: v}
                b.rs = {}

    def op(self, eng, fn, reads=(), writes=(), signal=True):
        need = self._deps(eng, reads, writes)
        waits = [(self.sems[k], v) for k, v in need.items()]
        self.ninstr += 1
        if signal:
            self.cnt[eng] += 1
            self.semval[eng] = self.cnt[eng]
            tok = (eng, self.cnt[eng])
            pr = [b for (b, kind) in self.pending[eng] if kind == "r"]
            pw = [b for (b, kind) in self.pending[eng] if kind == "w"]
            self.pending[eng] = []
            self._rec(tok, list(reads) + pr, list(writes) + pw)
        else:
            for b in reads:
                self.pending[eng].append((b, "r"))
            for b in writes:
                self.pending[eng].append((b, "w"))
        sem = self.sems[eng]

        def thunk(e, waits=waits, fn=fn, signal=signal, sem=sem):
            for (s, v) in waits:
                e.wait_ge(s, v)
            ins = fn(e)
            if signal:
                ins.then_inc(sem, 1)
        self.streams[eng].append(thunk)

    def dma(self, q, out_ap, in_ap, reads=(), writes=(), sembuf=None):
        need = self._deps(q, reads, writes)
        waits = [(self.sems[k], v) for k, v in need.items()]
        self.ninstr += 1
        sbf = sembuf
        if sbf is None:
            cands = [b for b in list(writes) + list(reads) if not b.accum]
            sbf = cands[0] if cands else (list(writes) + list(reads))[0]
        if sbf.dsem is None:
            sbf.dsem = ("d", self.nsem + 1)
            self._mksem(sbf.dsem)
        sbf.dcount += 16
        self.semval[sbf.dsem] = sbf.dcount
        tok = (sbf.dsem, sbf.dcount)
        self._rec(tok, reads, writes)
        sem = self.sems[sbf.dsem]

        def thunk(e, waits=waits, sem=sem, out_ap=out_ap, in_ap=in_ap):
            for (s, v) in waits:
                e.wait_ge(s, v)
            e.dma_start(out=out_ap, in_=in_ap).then_inc(sem, 16)
        self.streams[q].append(thunk)

    def barrier(self, include_bg=False):
        for e in self.ENGS:
            assert not self.pending[e]
        for eng in self.ENGS:
            w = self.waited[eng]
            items = [(k, v) for k, v in self.semval.items() if v > 0 and (include_bg or k not in self.bg_keys)]
            waits = [(self.sems[k], v) for k, v in items if w.get(k, 0) < v]
            for k, v in items:
                w[k] = max(w.get(k, 0), v)

            def thunk(e, waits=waits):
                for (s, v) in waits:
                    e.wait_ge(s, v)
            self.streams[eng].append(thunk)

    def emit(self):
        nc = self.nc
        for e in self.ENGS:
            assert not self.pending[e], "unsignalled pending ops on %s" % e
        with nc.Block() as block:
            @block.tensor
            def _(e):
                for t in self.streams["pe"]:
                    t(e)

            @block.scalar
            def _(e):
                for t in self.streams["act"]:
                    t(e)

            @block.vector
            def _(e):
                for t in self.streams["dve"]:
                    t(e)

            @block.gpsimd
            def _(e):
                for t in self.streams["pool"]:
                    t(e)

            @block.sync
            def _(e):
                for t in self.streams["sp"]:
                    t(e)

    def close(self):
        for g in reversed(self._ctx):
            g.__exit__(None, None, None)
        for g in reversed(self._semctx):
            g.__exit__(None, None, None)

    def free_from(self, mark):
        while len(self._ctx) > mark:
            g = self._ctx.pop()
            g.__exit__(None, None, None)


class Rot:
    def __init__(self, items):
        self.items = items
        self.i = 0

    def next(self):
        b = self.items[self.i % len(self.items)]
        self.i += 1
        return b


class K:
    def __init__(self, L, debug=(), phases="AMBCGD"):
        self.L = L
        self.phases = phases
        self.debug = set(debug)
        self.nc = nc = bass.Bass("TRN2", target_bir_lowering=False)
        nc.cache_partition_id()
        self.P = Prog(nc)
        self.inp = {}
        self.out_bufs = []
        self.NB = L // 1024
        self.LS = L // 4
        self.early_gather = True
        self.tokv = {}
        self.gathered_q = 0

    def ein(self, name, shape, dt=F32):
        need = {"w_gates": "D", "w_bs": "D", "w_ba": "D", "w_out": "D", "xres": "D", "xTs": "D", "nw_fin": "D",
                "tokoff": "D", "w_in": "A", "xT": "A"}
        if name in need and need[name] not in self.phases:
            return None
        t = self.nc.dram_tensor(name, list(shape), dt, kind="ExternalInput")
        b = Buf(name, t)
        self.inp[name] = b
        return b

    def scratch(self, name, shape, dt):
        kind = "ExternalOutput" if name in self.debug else "Internal"
        b = self.P.dram(name, shape, dt, kind=kind)
        if kind == "ExternalOutput":
            self.out_bufs.append(b)
        return b

    def declare(self):
        L, NB, LS = self.L, self.NB, self.LS
        e = self.ein
        self.xT = e("xT", [NB, 128, 32, 1024])
        self.xTs = e("xTs", [128, 32, LS])
        self.xres = e("xres", [LS, 4096])
        self.w_in = e("w_in", [15, 128, 32, 512])
        self.w_gates = e("w_gates", [16, 128, 32, 512])
        self.w_bs = e("w_bs", [16, 128, 32, 512])
        self.w_ba = e("w_ba", [8, 128, 32, 512])
        self.w_out = e("w_out", [8, 128, 32, 512])
        self.w_uq = e("w_uq", [128, 8, 2048])
        self.w_ukv = e("w_ukv", [128, 4, 2048])
        self.convp = e("convp", [128, 20, 5])
        self.nw_in = e("nw_in", [128, 32])
        self.nw_q = e("nw_q", [128, 8])
        self.nw_kv = e("nw_kv", [128, 4])
        self.dtb = e("dtb", [128, 32])
        self.alog = e("alog", [128, 32])
        self.dskip = e("dskip", [128, 32])
        self.nw_ssm = e("nw_ssm", [128, 2048])
        self.nw_fin = e("nw_fin", [128, 4096])
        self.ropec = e("ropec", [64, L])
        self.ropes = e("ropes", [64, L])
        self.c_ident = e("c_ident", [128, 128], BF16)
        self.c_ones = e("c_ones", [128, 128], BF16)
        self.c_onesf = e("c_onesf", [128, 128])
        self.c_trile = e("c_trile", [128, 128])
        self.c_trigt = e("c_trigt", [128, 128])
        self.c_maskb = e("c_maskb", [128, 128], BF16)
        s = self.scratch
        self.SZ = s("SZ", [L, 2048], BF16)
        self.XC = s("XC", [2560, L], BF16)
        self.DT = s("DT", [L, 32], F32)
        self.CQ = s("CQ", [1024, L], BF16)
        self.CKV = s("CKV", [512, L], BF16)
        self.RR = s("RR", [128, L], F32)
        self.GA = s("GA", [1024, L], BF16)
        self.QN = s("QN", [8, 128, L], BF16)
        self.QR = s("QR", [8, 64, L], BF16)
        self.KN = s("KN", [8, 128, L], BF16)
        self.KR = s("KR", [64, L], BF16)
        self.V = s("V", [L, 1024], BF16)
        self.YC = s("YC", [24, 4, 128, LS], BF16)
        self.YG = s("YG", [24, 4, 4, 128, LS], BF16)
        self.YQ = s("YQ", [24, 4, 128, LS], BF16)
        self.WD = s("WD", [48, 128, 32 * 512], BF16)
        self.OUT = self.P.dram("out", [LS, 4096], F32, kind="ExternalOutput")
        self.out_bufs.append(self.OUT)

    def yc_store(self, jt0, nj, t0, ntok, src_fn, srcbuf):
        LS = self.LS
        t = t0
        while t < t0 + ntok:
            tq, tl = t // LS, t % LS
            n = min(LS - tl, t0 + ntok - t)
            dst = self.YC.ap[jt0:jt0 + nj, tq, :, tl:tl + n].rearrange("j p t -> p j t")
            self.P.dma("sp", dst, src_fn(t - t0, t - t0 + n), reads=[srcbuf], writes=[self.YC])
            t += n

    def load_consts(self):
        P = self.P
        self.ident = P.sb("ident", [128, 128], BF16)
        self.ones = P.sb("ones", [128, 128], BF16)
        for sbuf, src in ((self.ident, self.c_ident), (self.ones, self.c_ones)):
            P.dma("sp", sbuf[:], src[:], reads=[src], writes=[sbuf])
        self.psb = [P.ps("psb%d" % i, [128, 512]) for i in range(8)]
        self.psr = Rot(self.psb)

    def make_uT(self, src_fn, nsub, uT, xs, sq, rt, rs, win, ps_rot):
        P = self.P
        ones = self.ones
        epsb = self.epsb
        for s in range(nsub):
            x = xs[s % len(xs)]
            P.dma("sp", x[:], src_fn(s)[0], reads=[src_fn(s)[1]], writes=[x])
            P.op("act", lambda e, x=x: e.activation(out=sq[:], in_=x[:], func=AF.Square),
                 reads=[x], writes=[sq])
            ps = ps_rot.next()
            for kc in range(32):
                P.op("pe", lambda e, kc=kc, ps=ps: e.matmul(ps[:, 0:128], lhsT=ones[:], rhs=sq[:, kc, :],
                                                          start=(kc == 0), stop=(kc == 31)),
                     reads=[ones, sq], writes=[ps], signal=(kc == 31))
            P.op("act", lambda e, ps=ps: e.activation(out=rt[:], in_=ps[:, 0:128], func=AF.Sqrt,
                                                      scale=1.0 / D_MODEL, bias=epsb[:, 0:1]),
                 reads=[ps, epsb], writes=[rt])
            P.op("dve", lambda e: e.reciprocal(out=rs[:], in_=rt[:]), reads=[rt], writes=[rs])
            P.op("dve", lambda e, x=x: e.tensor_tensor(out=x[:], in0=x[:],
                                                       in1=win[:].unsqueeze(2).to_broadcast([128, 32, 128]),
                                                       op=ALU.mult),
                 reads=[x, win], writes=[x])
            P.op("dve", lambda e, x=x, s=s: e.tensor_tensor(out=uT[:, :, s * 128:(s + 1) * 128], in0=x[:],
                                                            in1=rs[:].unsqueeze(1).to_broadcast([128, 32, 128]),
                                                            op=ALU.mult),
                 reads=[x, rs], writes=[uT])

    def phase_A(self):
        P, L, NB = self.P, self.L, self.NB
        mark = len(P._ctx)
        uT = P.sb("uT", [128, 32, 1024], BF16)
        xs = [P.sb("xs%d" % i, [128, 32, 128], F32) for i in range(2)]
        sq = P.sb("sq", [128, 32, 128], BF16)
        rt = P.sb("rt", [128, 128], F32)
        rs = P.sb("rs", [128, 128], F32)
        win = P.sb("win", [128, 32], F32)
        self.epsb = P.sb("epsb", [128, 1], F32)
        P.op("pool", lambda e, t=self.epsb: e.memset(t[:], EPS), writes=[self.epsb])
        P.dma("sp", win[:], self.nw_in[:], reads=[self.nw_in], writes=[win])
        wts = Rot([P.sb("wt%d" % i, [128, 32, 512], BF16) for i in range(2)])
        stg = Rot([P.sb("stg%d" % i, [128, 512], BF16) for i in range(4)])
        stf = Rot([P.sb("stf%d" % i, [128, 512], F32) for i in range(2)])
        raws = Rot([P.sb("raw%d" % i, [128, 515], F32) for i in range(2)])
        accs = Rot([P.sb("acc%d" % i, [128, 512], F32) for i in range(2)])
        halo = P.sb("halo", [128, 20, 3], F32)
        cp = P.sb("cp", [128, 20, 5], F32)
        dtb = P.sb("dtb", [128, 32], F32)
        dtx = P.sb("dtx", [128, 8, 32], F32)
        dte = P.sb("dte", [128, 8, 32], F32)
        dtt = P.sb("dtt", [128, 8, 32], F32)
        P.dma("sp", cp[:], self.convp[:], reads=[self.convp], writes=[cp])
        P.dma("sp", dtb[:], self.dtb[:], reads=[self.dtb], writes=[dtb])
        psr = self.psr

        for tb in range(NB):
            t0 = tb * 1024
            self.make_uT(lambda s, tb=tb: (self.xT[tb, :, :, s * 128:(s + 1) * 128], self.xT),
                         8, uT, xs, sq, rt, rs, win, psr)
            for t in range(15):
                wt = wts.next()
                P.dma("pool", wt[:], self.w_in[t], reads=[self.w_in], writes=[wt])
                if t < 4:
                    for tt in range(8):
                        ps = psr.next()
                        for kc in range(32):
                            P.op("pe", lambda e, kc=kc, ps=ps, tt=tt, wt=wt: e.matmul(
                                ps[:], lhsT=uT[:, kc, tt * 128:(tt + 1) * 128], rhs=wt[:, kc, :],
                                start=(kc == 0), stop=(kc == 31)),
                                reads=[uT, wt], writes=[ps], signal=(kc == 31))
                        st = stg.next()
                        P.op("act", lambda e, ps=ps, st=st: e.activation(out=st[:], in_=ps[:], func=AF.Silu),
                             reads=[ps], writes=[st])
                        P.dma("sp", self.SZ[t0 + tt * 128:t0 + (tt + 1) * 128, t * 512:(t + 1) * 512], st[:],
                              reads=[st], writes=[self.SZ])
                    continue
                ncs = 1 if t == 14 else 4
                for cs in range(ncs):
                    for th in range(2):
                        ps = psr.next()
                        for kc in range(32):
                            P.op("pe", lambda e, kc=kc, ps=ps, cs=cs, th=th, wt=wt: e.matmul(
                                ps[:], lhsT=wt[:, kc, cs * 128:(cs + 1) * 128], rhs=uT[:, kc, th * 512:(th + 1) * 512],
                                start=(kc == 0), stop=(kc == 31)),
                                reads=[uT, wt], writes=[ps], signal=(kc == 31))
                        tok = slice(t0 + th * 512, t0 + (th + 1) * 512)
                        if 4 <= t <= 8:
                            idx = (t - 4) * 4 + cs
                            raw = raws.next()
                            acc = accs.next()
                            P.op("dve", lambda e, raw=raw, ps=ps: e.tensor_copy(out=raw[:, 3:515], in_=ps[:]),
                                 reads=[ps], writes=[raw])
                            if tb == 0 and th == 0:
                                P.op("pool", lambda e, raw=raw: e.memset(raw[:, 0:3], 0.0), writes=[raw])
                            else:
                                P.op("pool", lambda e, raw=raw, idx=idx: e.tensor_copy(out=raw[:, 0:3], in_=halo[:, idx, :]),
                                     reads=[halo], writes=[raw])
                            P.op("pool", lambda e, raw=raw, idx=idx: e.tensor_copy(out=halo[:, idx, :], in_=raw[:, 512:515]),
                                 reads=[raw], writes=[halo])
                            P.op("dve", lambda e, raw=raw, acc=acc, idx=idx: e.tensor_scalar(
                                out=acc[:], in0=raw[:, 3:515], scalar1=cp[:, idx, 3:4], scalar2=cp[:, idx, 4:5],
                                op0=ALU.mult, op1=ALU.add), reads=[raw, cp], writes=[acc])
                            for k in range(3):
                                P.op("dve", lambda e, raw=raw, acc=acc, idx=idx, k=k: e.scalar_tensor_tensor(
                                    out=acc[:], in0=raw[:, k:k + 512], scalar=cp[:, idx, k:k + 1], in1=acc[:],
                                    op0=ALU.mult, op1=ALU.add), reads=[raw, cp, acc], writes=[acc])
                            st = stg.next()
                            P.op("act", lambda e, acc=acc, st=st: e.activation(out=st[:], in_=acc[:], func=AF.Silu),
                                 reads=[acc], writes=[st])
                            P.dma("sp", self.XC[idx * 128:(idx + 1) * 128, tok], st[:], reads=[st], writes=[self.XC])
                        elif t in (9, 10, 11, 12, 13):
                            st = stg.next()
                            fn = AF.Silu if t >= 12 else AF.Copy
                            P.op("act", lambda e, ps=ps, st=st, fn=fn: e.activation(out=st[:], in_=ps[:], func=fn),
                                 reads=[ps], writes=[st])
                            if t <= 10:
                                r0 = ((t - 9) * 4 + cs) * 128
                                dst, dbuf = self.CQ[r0:r0 + 128, tok], self.CQ
                            elif t == 11:
                                dst, dbuf = self.CKV[cs * 128:(cs + 1) * 128, tok], self.CKV
                            else:
                                r0 = ((t - 12) * 4 + cs) * 128
                                dst, dbuf = self.GA[r0:r0 + 128, tok], self.GA
                            P.dma("sp", dst, st[:], reads=[st], writes=[dbuf])
                        else:
                            st = stf.next()
                            P.op("act", lambda e, ps=ps, st=st: e.activation(out=st[:], in_=ps[:], func=AF.Copy),
                                 reads=[ps], writes=[st])
                            P.dma("sp", self.RR[:, tok], st[:], reads=[st], writes=[self.RR])
                if t == 14:
                    ps = psr.next()
                    for tt in range(8):
                        for kc in range(32):
                            P.op("pe", lambda e, kc=kc, ps=ps, tt=tt, wt=wt: e.matmul(
                                ps[:, tt * 32:(tt + 1) * 32], lhsT=uT[:, kc, tt * 128:(tt + 1) * 128],
                                rhs=wt[:, kc, 128:160], start=(kc == 0), stop=(kc == 31)),
                                reads=[uT, wt], writes=[ps], signal=(kc == 31 and tt == 7))
                    P.op("dve", lambda e, ps=ps: e.tensor_tensor(
                        out=dtx[:], in0=ps[:, 0:256].rearrange("p (a b) -> p a b", a=8),
                        in1=dtb[:].unsqueeze(1).to_broadcast([128, 8, 32]), op=ALU.add),
                        reads=[ps, dtb], writes=[dtx])
                    P.op("act", lambda e: e.activation(out=dte[:], in_=dtx[:], func=AF.Exp), reads=[dtx], writes=[dte])
                    P.op("act", lambda e: e.activation(out=dtt[:], in_=dte[:], func=AF.Ln, bias=1.0, scale=1.0),
                         reads=[dte], writes=[dtt])
                    P.dma("sp", self.DT[t0:t0 + 1024, :].rearrange("(a p) h -> p a h", p=128), dtt[:],
                          reads=[dtt], writes=[self.DT])
        P.barrier()
        P.free_from(mark)


    def convert_weights(self):
        P = self.P
        for (src, n, base) in ((self.w_gates, 16, 0), (self.w_bs, 16, 16), (self.w_ba, 8, 32), (self.w_out, 8, 40)):
            for t in range(n):
                P.dma("pool", self.WD.ap[base + t], src.ap[t].rearrange("p k c -> p (k c)"), reads=[src], writes=[self.WD])
                P.bg_keys.add(src.dsem)
                yield

    def yq_copy(self, j0s, cc_count):
        P = self.P
        YQ, YG = self.YQ, self.YG
        if YQ.dsem is None:
            YQ.dsem = ("d", P.nsem + 1)
            P._mksem(YQ.dsem)
        qsem = P.sems[YQ.dsem]
        ccsem = P.sems["cc"]
        tokv = self.tokv
        for j0 in j0s:
            YQ.dcount += 16
            P.semval[YQ.dsem] = YQ.dcount
            P._rec((YQ.dsem, YQ.dcount), [YG], [YQ])
            P.ninstr += 1

            def qthunk(e, j0=j0):
                e.wait_ge(ccsem, cc_count)
                if "v" not in tokv:
                    tokv["v"] = e.snap(e.partition_id() % 4, min_val=0, max_val=3)
                src = YG.ap[j0:j0 + 2, bass.ds(tokv["v"], 1), :, :, :].rearrange("j o r p t -> j (o r p t)")
                dst = YQ.ap[j0:j0 + 2].rearrange("j r p t -> j (r p t)")
                e.dma_start(out=dst, in_=src).then_inc(qsem, 16)
            P.streams["pool"].append(qthunk)

    def phase_A2(self):
        P, L = self.P, self.L
        mark = len(P._ctx)
        psr = self.psr
        wq = P.sb("wq", [128, 8, 2048], BF16)
        wkv = P.sb("wkv", [128, 4, 2048], BF16)
        P.dma("pool", wq[:], self.w_uq[:], reads=[self.w_uq], writes=[wq])
        P.dma("pool", wkv[:], self.w_ukv[:], reads=[self.w_ukv], writes=[wkv])
        nwq = P.sb("nwq", [128, 8], F32)
        nwkv = P.sb("nwkv", [128, 4], F32)
        P.dma("sp", nwq[:], self.nw_q[:], reads=[self.nw_q], writes=[nwq])
        P.dma("sp", nwkv[:], self.nw_kv[:], reads=[self.nw_kv], writes=[nwkv])
        epsb = P.sb("epsb2", [128, 1], F32)
        P.op("pool", lambda e: e.memset(epsb[:], EPS), writes=[epsb])
        cqs = Rot([P.sb("cq%d" % i, [128, 8, 512], BF16) for i in range(2)])
        ckvs = Rot([P.sb("ckv%d" % i, [128, 4, 512], BF16) for i in range(2)])
        rras = Rot([P.sb("rra%d" % i, [64, 512], F32) for i in range(2)])
        rrbs = Rot([P.sb("rrb%d" % i, [64, 512], F32) for i in range(2)])
        coss = Rot([P.sb("cos%d" % i, [64, 512], F32) for i in range(2)])
        sins = Rot([P.sb("sin%d" % i, [64, 512], F32) for i in range(2)])
        sqb = P.sb("sqb", [128, 8, 512], BF16)
        rt = P.sb("rt2", [128, 512], F32)
        rq = P.sb("rq", [128, 512], F32)
        rkv = P.sb("rkv", [128, 512], F32)
        cqw = P.sb("cqw", [128, 8, 512], BF16)
        ckvw = P.sb("ckvw", [128, 4, 512], BF16)
        stg = Rot([P.sb("stg2_%d" % i, [128, 512], BF16) for i in range(4)])
        tA = Rot([P.sb("tA%d" % i, [64, 512], F32) for i in range(2)])
        tB = Rot([P.sb("tB%d" % i, [64, 512], F32) for i in range(2)])
        ones = self.ones

        def rope(srcA, srcB, bufsA, bufsB, cos, sin, dst, dbuf):
            a, b = tA.next(), tB.next()
            P.op("dve", lambda e: e.tensor_tensor(out=a[:], in0=srcA, in1=cos[:], op=ALU.mult),
                 reads=bufsA + [cos], writes=[a])
            P.op("dve", lambda e: e.tensor_tensor(out=b[:], in0=srcB, in1=sin[:], op=ALU.mult),
                 reads=bufsB + [sin], writes=[b])
            st = stg.next()
            P.op("pool", lambda e: e.tensor_tensor(out=st[0:64, :], in0=a[:], in1=b[:], op=ALU.add),
                 reads=[a, b], writes=[st])
            P.dma("sp", dst, st[0:64, :], reads=[st], writes=[dbuf])

        for tk in range(L // 512):
            tok = slice(tk * 512, (tk + 1) * 512)
            cq, ckv, rra, rrb, cos, sin = cqs.next(), ckvs.next(), rras.next(), rrbs.next(), coss.next(), sins.next()
            P.dma("sp", cq[:], self.CQ[:, tok].rearrange("(j p) t -> p j t", p=128), reads=[self.CQ], writes=[cq])
            P.dma("sp", ckv[:], self.CKV[:, tok].rearrange("(j p) t -> p j t", p=128), reads=[self.CKV], writes=[ckv])
            P.dma("sp", rra[:], self.RR[0:64, tok], reads=[self.RR], writes=[rra])
            P.dma("sp", rrb[:], self.RR[64:128, tok], reads=[self.RR], writes=[rrb])
            P.dma("sp", cos[:], self.ropec[:, tok], reads=[self.ropec], writes=[cos])
            P.dma("sp", sin[:], self.ropes[:, tok], reads=[self.ropes], writes=[sin])
            for (src, nk, nw, rr_, dst, dim) in ((cq, 8, nwq, rq, cqw, 1024), (ckv, 4, nwkv, rkv, ckvw, 512)):
                P.op("act", lambda e, src=src, nk=nk: e.activation(out=sqb[:, 0:nk, :], in_=src[:], func=AF.Square),
                     reads=[src], writes=[sqb])
                ps = psr.next()
                for kc in range(nk):
                    P.op("pe", lambda e, kc=kc, ps=ps, nk=nk: e.matmul(ps[:], lhsT=ones[:], rhs=sqb[:, kc, :],
                                                                    start=(kc == 0), stop=(kc == nk - 1)),
                         reads=[ones, sqb], writes=[ps], signal=(kc == nk - 1))
                P.op("act", lambda e, ps=ps, dim=dim: e.activation(out=rt[:], in_=ps[:], func=AF.Sqrt,
                                                                   scale=1.0 / dim, bias=epsb[:, 0:1]),
                     reads=[ps, epsb], writes=[rt])
                P.op("dve", lambda e, rr_=rr_: e.reciprocal(out=rr_[:], in_=rt[:]), reads=[rt], writes=[rr_])
                for kc in range(nk):
                    P.op("dve", lambda e, kc=kc, src=src, nw=nw, rr_=rr_, dst=dst: e.scalar_tensor_tensor(
                        out=dst[:, kc, :], in0=src[:, kc, :], scalar=nw[:, kc:kc + 1], in1=rr_[:],
                        op0=ALU.mult, op1=ALU.mult), reads=[src, nw, rr_], writes=[dst])
            for hl in range(8):
                ps = psr.next()
                for kc in range(8):
                    P.op("pe", lambda e, kc=kc, ps=ps, hl=hl: e.matmul(
                        ps[:], lhsT=wq[:, kc, hl * 256:hl * 256 + 128], rhs=cqw[:, kc, :],
                        start=(kc == 0), stop=(kc == 7)), reads=[wq, cqw], writes=[ps], signal=(kc == 7))
                st = stg.next()
                P.op("act", lambda e, ps=ps, st=st: e.activation(out=st[:], in_=ps[:], func=AF.Copy), reads=[ps], writes=[st])
                P.dma("sp", self.QN[hl, :, tok], st[:], reads=[st], writes=[self.QN])
                psA, psB = psr.next(), psr.next()
                for (pp, off) in ((psA, 128), (psB, 192)):
                    for kc in range(8):
                        P.op("pe", lambda e, kc=kc, pp=pp, hl=hl, off=off: e.matmul(
                            pp[0:64, :], lhsT=wq[:, kc, hl * 256 + off:hl * 256 + off + 64], rhs=cqw[:, kc, :],
                            start=(kc == 0), stop=(kc == 7)), reads=[wq, cqw], writes=[pp], signal=(kc == 7))
                rope(psA[0:64, :], psB[0:64, :], [psA], [psB], cos, sin, self.QR[hl, :, tok], self.QR)
                ps = psr.next()
                for kc in range(4):
                    P.op("pe", lambda e, kc=kc, ps=ps, hl=hl: e.matmul(
                        ps[:], lhsT=wkv[:, kc, hl * 128:(hl + 1) * 128], rhs=ckvw[:, kc, :],
                        start=(kc == 0), stop=(kc == 3)), reads=[wkv, ckvw], writes=[ps], signal=(kc == 3))
                st = stg.next()
                P.op("act", lambda e, ps=ps, st=st: e.activation(out=st[:], in_=ps[:], func=AF.Copy), reads=[ps], writes=[st])
                P.dma("sp", self.KN[hl, :, tok], st[:], reads=[st], writes=[self.KN])
            for tt in range(4):
                for hf in range(2):
                    ps = psr.next()
                    for kc in range(4):
                        P.op("pe", lambda e, kc=kc, ps=ps, tt=tt, hf=hf: e.matmul(
                            ps[:], lhsT=ckvw[:, kc, tt * 128:(tt + 1) * 128],
                            rhs=wkv[:, kc, 1024 + hf * 512:1024 + (hf + 1) * 512],
                            start=(kc == 0), stop=(kc == 3)), reads=[wkv, ckvw], writes=[ps], signal=(kc == 3))
                    st = stg.next()
                    P.op("act", lambda e, ps=ps, st=st: e.activation(out=st[:], in_=ps[:], func=AF.Copy), reads=[ps], writes=[st])
                    r0 = tk * 512 + tt * 128
                    P.dma("sp", self.V[r0:r0 + 128, hf * 512:(hf + 1) * 512], st[:], reads=[st], writes=[self.V])
            rope(rra[:], rrb[:], [rra], [rrb], cos, sin, self.KR[:, tok], self.KR)
        P.barrier()
        P.free_from(mark)

    def phase_C(self):
        P, L = self.P, self.L
        mark = len(P._ctx)
        if self.early_gather and "G" in self.phases and self.gathered_q < 4:
            self.gather(0, 16, tqs=tuple(range(self.gathered_q, 4)))
        conv = self.convert_weights() if "D" in self.phases else iter(())
        slot = 0
        psr = self.psr
        NKT = L // 128
        scale = 1.0 / math.sqrt(192.0)
        ones = self.ones
        maskb = P.sb("maskb", [128, 128], BF16)
        P.dma("sp", maskb[:], self.c_maskb[:], reads=[self.c_maskb], writes=[maskb])
        kr = P.sb("kr", [64, L], BF16)
        P.dma("sp", kr[:], self.KR[:, :], reads=[self.KR], writes=[kr])
        qns = Rot([P.sb("qn%d" % i, [128, L], BF16) for i in range(2)])
        qrs = Rot([P.sb("qr%d" % i, [64, L], BF16) for i in range(2)])
        kns = Rot([P.sb("kn%d" % i, [128, L], BF16) for i in range(2)])
        vs = Rot([P.sb("v%d" % i, [128, NKT, 128], BF16) for i in range(2)])
        pts = Rot([P.sb("pt%d" % i, [128, 512], BF16) for i in range(3)])
        gas = Rot([P.sb("ga%d" % i, [128, 512], BF16) for i in range(2)])
        rinv = P.sb("rinv", [128, 512], F32)
        o1 = P.sb("o1", [128, 512], F32)
        stg = Rot([P.sb("stg3_%d" % i, [128, 512], BF16) for i in range(2)])
        acc_r = Rot(self.psb[0:4])
        s_r = Rot(self.psb[4:8])
        def load_head(hl):
            qn, qr, kn, v = qns.next(), qrs.next(), kns.next(), vs.next()
            P.dma("sp", qn[:], self.QN[hl], reads=[self.QN], writes=[qn])
            P.dma("sp", qr[:], self.QR[hl], reads=[self.QR], writes=[qr])
            P.dma("sp", kn[:], self.KN[hl], reads=[self.KN], writes=[kn])
            P.dma("sp", v[:], self.V[:, hl * 128:(hl + 1) * 128].rearrange("(a p) d -> p a d", p=128),
                  reads=[self.V], writes=[v])
            return qn, qr, kn, v
        nxt_head = load_head(0)
        for hl in range(8):
            qn, qr, kn, v = nxt_head
            for qi in range(L // 512):
                if qi == min(2, L // 512 - 1) and hl + 1 < 8:
                    nxt_head = load_head(hl + 1)
                slot += 1
                if slot % 2 == 0:
                    next(conv, None)
                ga = gas.next()
                P.dma("sp", ga[:], self.GA[hl * 128:(hl + 1) * 128, qi * 512:(qi + 1) * 512], reads=[self.GA], writes=[ga])
                O, Rs = acc_r.next(), acc_r.next()
                nkt = 4 * qi + 4
                def emit_qk(kt, qi=qi, kn=kn, qn=qn, qr=qr):
                    d = kt - 4 * qi
                    q0 = max(d, 0) * 128
                    Sb = s_r.next()
                    qs = slice(qi * 512 + q0, (qi + 1) * 512)
                    ks = slice(kt * 128, (kt + 1) * 128)
                    P.op("pe", lambda e, Sb=Sb, q0=q0, qs=qs, ks=ks, kn=kn, qn=qn: e.matmul(
                        Sb[:, q0:512], lhsT=kn[:, ks], rhs=qn[:, qs], start=True, stop=False),
                        reads=[kn, qn], writes=[Sb], signal=False)
                    P.op("pe", lambda e, Sb=Sb, q0=q0, qs=qs, ks=ks, qr=qr: e.matmul(
                        Sb[:, q0:512], lhsT=kr[:, ks], rhs=qr[:, qs], start=False, stop=True),
                        reads=[kr, qr], writes=[Sb])
                    return Sb, d, q0
                nxt = emit_qk(0)
                for kt in range(nkt):
                    Sb, d, q0 = nxt
                    if kt + 1 < nkt:
                        nxt = emit_qk(kt + 1)
                    pt = pts.next()
                    P.op("act", lambda e, Sb=Sb, pt=pt, q0=q0: e.activation(out=pt[:, q0:512], in_=Sb[:, q0:512],
                                                                         func=AF.Exp, scale=scale),
                         reads=[Sb], writes=[pt])
                    if d >= 0:
                        P.op("dve", lambda e, pt=pt, q0=q0: e.tensor_tensor(out=pt[:, q0:q0 + 128], in0=pt[:, q0:q0 + 128],
                                                                           in1=maskb[:], op=ALU.mult),
                             reads=[pt, maskb], writes=[pt])
                    P.op("pe", lambda e, O=O, pt=pt, q0=q0, kt=kt, nkt=nkt, v=v: e.matmul(
                        O[:, q0:512], lhsT=v[:, kt, :], rhs=pt[:, q0:512], start=(kt == 0), stop=(kt == nkt - 1)),
                        reads=[v, pt], writes=[O], signal=False)
                    P.op("pe", lambda e, Rs=Rs, pt=pt, q0=q0, kt=kt, nkt=nkt: e.matmul(
                        Rs[:, q0:512], lhsT=ones[:], rhs=pt[:, q0:512], start=(kt == 0), stop=(kt == nkt - 1)),
                        reads=[ones, pt], writes=[Rs])
                P.op("dve", lambda e, Rs=Rs: e.reciprocal(out=rinv[:], in_=Rs[:]), reads=[Rs], writes=[rinv])
                P.op("dve", lambda e, O=O: e.tensor_tensor(out=o1[:], in0=O[:], in1=rinv[:], op=ALU.mult),
                     reads=[O, rinv], writes=[o1])
                st = stg.next()
                P.op("dve", lambda e, st=st, ga=ga: e.tensor_tensor(out=st[:], in0=o1[:], in1=ga[:], op=ALU.mult),
                     reads=[o1, ga], writes=[st])
                self.yc_store(16 + hl, 1, qi * 512, 512, lambda lo, hi, st=st: st[:, lo:hi].unsqueeze(1), st)
            if self.early_gather and "G" in self.phases:
                self.gather(16 + hl, 17 + hl, track=True)
                if hl == 4 and "D" in self.phases:
                    self.yq_copy(range(0, 16, 2), 64)
        for _ in conv:
            pass
        P.barrier()
        P.free_from(mark)


    def phase_B(self):
        P, L = self.P, self.L
        mark = len(P._ctx)
        psr = self.psr
        NCH = L // 128
        ident, ones = self.ident, self.ones

        def cload(name, src, shape, dt):
            b = P.sb(name, shape, dt)
            P.dma("sp", b[:], src[:], reads=[src], writes=[b])
            return b
        trile = cload("trile", self.c_trile, [128, 128], F32)
        trigt = cload("trigt", self.c_trigt, [128, 128], F32)
        onesf = cload("onesf", self.c_onesf, [128, 128], F32)
        maskb = cload("maskbB", self.c_maskb, [128, 128], BF16)
        Abc = cload("Abc", self.alog, [128, 32], F32)
        dsk = cload("dsk", self.dskip, [128, 32], F32)
        nws = cload("nws", self.nw_ssm, [128, 2048], F32)
        P.op("act", lambda e: e.activation(out=Abc[:], in_=Abc[:], func=AF.Exp), reads=[Abc], writes=[Abc])
        P.op("dve", lambda e: e.tensor_scalar(out=Abc[:], in0=Abc[:], scalar1=-1.0, scalar2=None, op0=ALU.mult),
             reads=[Abc], writes=[Abc])
        epsb = P.sb("epsbB", [128, 1], F32)
        P.op("pool", lambda e: e.memset(epsb[:], EPS), writes=[epsb])
        h = P.sb("h", [128, 2048], F32)
        hb = P.sb("hb", [128, 2048], BF16)
        P.op("pool", lambda e: e.memset(h[:], 0.0), writes=[h])
        P.op("pool", lambda e: e.memset(hb[:], 0.0), writes=[hb])
        xT4s = Rot([P.sb("xT4_%d" % i, [128, 20, 512], BF16) for i in range(2)])
        dt4s = Rot([P.sb("dt4_%d" % i, [128, 4, 32], F32) for i in range(2)])
        szs = Rot([P.sb("sz%d" % i, [128, 2048], BF16) for i in range(2)])
        xtm = P.sb("xtm", [128, 2048], BF16)
        btm = P.sb("btm", [128, 256], BF16)
        sm = {n: P.sb("sm_" + n, [128, 32], F32) for n in ("a", "acs", "ea", "cd", "dsd", "ds", "w2")}
        R = P.sb("R", [128, 32, 128], F32)
        E = P.sb("E", [128, 32, 128], BF16)
        cbm = P.sb("cbm", [128, 2, 128], BF16)
        S = P.sb("S", [128, 32, 128], BF16)
        xdt = P.sb("xdt", [128, 2048], BF16)
        xdd = P.sb("xdd", [128, 2048], BF16)
        xD = P.sb("xD", [128, 2048], BF16)
        y = P.sb("y", [128, 2048], F32)
        junk = P.sb("junk", [128, 1024], BF16)
        ssq = P.sb("ssq", [128, 2], F32)
        rtn = P.sb("rtn", [128, 2], F32)
        rsn = P.sb("rsn", [128, 2], F32)
        ytm = P.sb("ytm", [128, 2048], BF16)
        yT4 = P.sb("yT4", [128, 16, 512], BF16)

        def pbf(bank):
            return bank.ap.bitcast(BF16)

        xT4 = dt4 = None
        for c in range(NCH):
            cc = c % 4
            csl = slice(cc * 128, (cc + 1) * 128)
            if cc == 0:
                xT4, dt4 = xT4s.next(), dt4s.next()
                tok = slice(c * 128, c * 128 + 512)
                P.dma("sp", xT4[:], self.XC[:, tok].rearrange("(j p) t -> p j t", p=128), reads=[self.XC], writes=[xT4])
                P.dma("sp", dt4[:], self.DT[tok, :].rearrange("(a p) h -> p a h", p=128), reads=[self.DT], writes=[dt4])
            sz = szs.next()
            P.dma("sp", sz[:], self.SZ[c * 128:(c + 1) * 128, :], reads=[self.SZ], writes=[sz])
            dtc = dt4[:, cc, :]
            for i in range(2):
                bank = psr.next()
                pb = pbf(bank)
                for jj in range(8):
                    j = i * 8 + jj
                    P.op("pe", lambda e, pb=pb, jj=jj, j=j, xT4=xT4, csl=csl: e.transpose(
                        pb[:, jj * 128:(jj + 1) * 128], xT4[:, j, csl], ident[:]),
                        reads=[xT4, ident], writes=[bank], signal=(jj == 7))
                P.op("act", lambda e, pb=pb, i=i: e.activation(out=xtm[:, i * 1024:(i + 1) * 1024], in_=pb[:, 0:1024], func=AF.Copy),
                     reads=[bank], writes=[xtm])
            bank = psr.next()
            pb = pbf(bank)
            for g2 in range(2):
                P.op("pe", lambda e, pb=pb, g2=g2, xT4=xT4, csl=csl: e.transpose(
                    pb[:, g2 * 128:(g2 + 1) * 128], xT4[:, 16 + g2, csl], ident[:]),
                    reads=[xT4, ident], writes=[bank], signal=(g2 == 1))
            P.op("act", lambda e, pb=pb: e.activation(out=btm[:], in_=pb[:, 0:256], func=AF.Copy), reads=[bank], writes=[btm])
            a, acs, ea, cd, dsd, ds_, w2 = (sm[n] for n in ("a", "acs", "ea", "cd", "dsd", "ds", "w2"))
            P.op("dve", lambda e, dtc=dtc: e.tensor_tensor(out=a[:], in0=dtc, in1=Abc[:], op=ALU.mult),
                 reads=[dt4, Abc], writes=[a])
            bank = psr.next()
            P.op("pe", lambda e, bank=bank: e.matmul(bank[:, 0:32], lhsT=trile[:], rhs=a[:], start=True, stop=True),
                 reads=[trile, a], writes=[bank], signal=False)
            P.op("pe", lambda e, bank=bank: e.matmul(bank[:, 32:64], lhsT=onesf[:], rhs=a[:], start=True, stop=True),
                 reads=[onesf, a], writes=[bank])
            P.op("act", lambda e, bank=bank: e.activation(out=acs[:], in_=bank[:, 0:32], func=AF.Copy), reads=[bank], writes=[acs])
            P.op("act", lambda e, bank=bank: e.activation(out=ea[:], in_=bank[:, 0:32], func=AF.Exp), reads=[bank], writes=[ea])
            P.op("act", lambda e, bank=bank: e.activation(out=cd[:], in_=bank[:, 32:64], func=AF.Exp), reads=[bank], writes=[cd])
            P.op("dve", lambda e, bank=bank: e.tensor_tensor(out=dsd[:], in0=bank[:, 32:64], in1=acs[:], op=ALU.subtract),
                 reads=[bank, acs], writes=[dsd])
            P.op("act", lambda e: e.activation(out=ds_[:], in_=dsd[:], func=AF.Exp), reads=[dsd], writes=[ds_])
            P.op("dve", lambda e, dtc=dtc: e.tensor_tensor(out=w2[:], in0=dtc, in1=ds_[:], op=ALU.mult),
                 reads=[dt4, ds_], writes=[w2])
            P.op("pool", lambda e: e.tensor_tensor(out=R[:], in0=a[:].unsqueeze(2).to_broadcast([128, 32, 128]),
                                                  in1=trile[:].unsqueeze(1).to_broadcast([128, 32, 128]), op=ALU.mult),
                 reads=[a, trile], writes=[R])
            for j in range(8):
                bank = psr.next()
                P.op("pe", lambda e, bank=bank, j=j: e.matmul(
                    bank[:], lhsT=trigt[:], rhs=R[:, 4 * j:4 * j + 4, :].rearrange("p a b -> p (a b)"), start=True, stop=True),
                    reads=[trigt, R], writes=[bank])
                P.op("act", lambda e, bank=bank, j=j: e.activation(
                    out=E[:, 4 * j:4 * j + 4, :].rearrange("p a b -> p (a b)"), in_=bank[:], func=AF.Exp),
                    reads=[bank], writes=[E])
            bank = psr.next()
            for g2 in range(2):
                P.op("pe", lambda e, bank=bank, g2=g2, xT4=xT4, csl=csl: e.matmul(
                    bank[:, g2 * 128:(g2 + 1) * 128], lhsT=xT4[:, 16 + g2, csl], rhs=xT4[:, 18 + g2, csl], start=True, stop=True),
                    reads=[xT4], writes=[bank], signal=(g2 == 1))
            P.op("dve", lambda e, bank=bank: e.tensor_tensor(
                out=cbm[:], in0=bank[:, 0:256].rearrange("p (g l) -> p g l", g=2),
                in1=maskb[:].unsqueeze(1).to_broadcast([128, 2, 128]), op=ALU.mult),
                reads=[bank, maskb], writes=[cbm])
            P.op("dve", lambda e: e.tensor_tensor(
                out=S[:].rearrange("p (g r) l -> p g r l", g=2), in0=E[:].rearrange("p (g r) l -> p g r l", g=2),
                in1=cbm[:].unsqueeze(2).to_broadcast([128, 2, 16, 128]), op=ALU.mult),
                reads=[E, cbm], writes=[S])
            x3 = xtm[:].rearrange("p (h d) -> p h d", d=64)
            P.op("dve", lambda e, dtc=dtc, x3=x3: e.tensor_tensor(
                out=xdt[:].rearrange("p (h d) -> p h d", d=64), in0=x3,
                in1=dtc.unsqueeze(2).to_broadcast([128, 32, 64]), op=ALU.mult), reads=[xtm, dt4], writes=[xdt])
            P.op("pool", lambda e, x3=x3: e.tensor_tensor(
                out=xdd[:].rearrange("p (h d) -> p h d", d=64), in0=x3,
                in1=w2[:].unsqueeze(2).to_broadcast([128, 32, 64]), op=ALU.mult), reads=[xtm, w2], writes=[xdd])
            P.op("pool", lambda e, x3=x3: e.tensor_tensor(
                out=xD[:].rearrange("p (h d) -> p h d", d=64), in0=x3,
                in1=dsk[:].unsqueeze(2).to_broadcast([128, 32, 64]), op=ALU.mult), reads=[xtm, dsk], writes=[xD])
            for q in range(4):
                g2 = q // 2
                yo = psr.next()
                P.op("pe", lambda e, yo=yo, q=q, g2=g2, xT4=xT4, csl=csl: e.matmul(
                    yo[:], lhsT=xT4[:, 18 + g2, csl], rhs=hb[:, q * 512:(q + 1) * 512], start=True, stop=True),
                    reads=[xT4, hb], writes=[yo])
                yd = psr.next()
                P.op("pe", lambda e, yd=yd, q=q: e.matmul(yd[:], lhsT=ident[:], rhs=xD[:, q * 512:(q + 1) * 512],
                                                         start=True, stop=False),
                     reads=[ident, xD], writes=[yd], signal=False)
                for hh in range(8):
                    hd = q * 8 + hh
                    P.op("pe", lambda e, yd=yd, hh=hh, hd=hd: e.matmul(
                        yd[:, hh * 64:(hh + 1) * 64], lhsT=S[:, hd, :], rhs=xdt[:, hd * 64:(hd + 1) * 64],
                        start=False, stop=(hh == 7)), reads=[S, xdt], writes=[yd], signal=(hh == 7))
                ysl = y[:, q * 512:(q + 1) * 512]
                P.op("dve", lambda e, yo=yo, q=q, ysl=ysl: e.tensor_tensor(
                    out=ysl.rearrange("p (h d) -> p h d", d=64), in0=yo[:].rearrange("p (h d) -> p h d", d=64),
                    in1=ea[:, q * 8:(q + 1) * 8].unsqueeze(2).to_broadcast([128, 8, 64]), op=ALU.mult),
                    reads=[yo, ea], writes=[y])
                P.op("dve", lambda e, yd=yd, ysl=ysl: e.tensor_tensor(out=ysl, in0=yd[:], in1=ysl, op=ALU.add),
                     reads=[yd, y], writes=[y])
            P.op("dve", lambda e, sz=sz: e.tensor_tensor(out=y[:], in0=y[:], in1=sz[:], op=ALU.mult), reads=[y, sz], writes=[y])
            for g2 in range(2):
                P.op("act", lambda e, g2=g2: e.activation(out=junk[:], in_=y[:, g2 * 1024:(g2 + 1) * 1024], func=AF.Square,
                                                          accum_out=ssq[:, g2:g2 + 1]), reads=[y], writes=[junk, ssq])
            P.op("act", lambda e: e.activation(out=rtn[:], in_=ssq[:], func=AF.Sqrt, scale=1.0 / 1024, bias=epsb[:, 0:1]),
                 reads=[ssq, epsb], writes=[rtn])
            P.op("dve", lambda e: e.reciprocal(out=rsn[:], in_=rtn[:]), reads=[rtn], writes=[rsn])
            for g2 in range(2):
                gs = slice(g2 * 1024, (g2 + 1) * 1024)
                P.op("dve", lambda e, g2=g2, gs=gs: e.scalar_tensor_tensor(
                    out=ytm[:, gs], in0=y[:, gs], scalar=rsn[:, g2:g2 + 1], in1=nws[:, gs], op0=ALU.mult, op1=ALU.mult),
                    reads=[y, rsn, nws], writes=[ytm])
            for i in range(2):
                bank = psr.next()
                pb = pbf(bank)
                for jj in range(8):
                    j = i * 8 + jj
                    P.op("pe", lambda e, pb=pb, jj=jj, j=j: e.transpose(
                        pb[:, jj * 128:(jj + 1) * 128], ytm[:, j * 128:(j + 1) * 128], ident[:]),
                        reads=[ytm, ident], writes=[bank], signal=(jj == 7))
                P.op("act", lambda e, pb=pb, i=i, csl=csl: e.activation(
                    out=yT4[:, i * 8:(i + 1) * 8, csl], in_=pb[:, 0:1024].rearrange("p (j t) -> p j t", j=8), func=AF.Copy),
                    reads=[bank], writes=[yT4])
            if cc == 3:
                self.yc_store(0, 16, (c - 3) * 128, 512, lambda lo, hi: yT4[:, :, lo:hi], yT4)
                if self.early_gather and "G" in self.phases:
                    done_q = ((c + 1) * 128) // self.LS
                    newq = tuple(range(self.gathered_q, done_q))
                    if newq:
                        self.gather(0, 16, track=True, tqs=newq)
                        self.gathered_q = done_q
            P.op("dve", lambda e: e.tensor_tensor(
                out=h[:].rearrange("p (h d) -> p h d", d=64), in0=h[:].rearrange("p (h d) -> p h d", d=64),
                in1=cd[:].unsqueeze(2).to_broadcast([128, 32, 64]), op=ALU.mult), reads=[h, cd], writes=[h])
            for q in range(4):
                g2 = q // 2
                st = psr.next()
                P.op("pe", lambda e, st=st, q=q, g2=g2: e.matmul(
                    st[:], lhsT=btm[:, g2 * 128:(g2 + 1) * 128], rhs=xdd[:, q * 512:(q + 1) * 512], start=True, stop=True),
                    reads=[btm, xdd], writes=[st])
                P.op("dve", lambda e, st=st, q=q: e.tensor_tensor(
                    out=h[:, q * 512:(q + 1) * 512], in0=st[:], in1=h[:, q * 512:(q + 1) * 512], op=ALU.add),
                    reads=[st, h], writes=[h])
            P.op("act", lambda e: e.activation(out=hb[:], in_=h[:], func=AF.Copy), reads=[h], writes=[hb])
        P.barrier()
        P.free_from(mark)

    def gather(self, jt0, jt1, track=False, tqs=(0, 1, 2, 3)):
        P = self.P
        if "cc" not in P.sems:
            P._mksem("cc")
        sem = P.sems["cc"]
        YC, YG = self.YC, self.YG
        waits = []
        if track:
            need = P._deps("pool", [YC], [])
            waits = [(P.sems[k], v) for k, v in need.items()]

        def thunk(e):
            for (s_, v) in waits:
                e.wait_ge(s_, v)
            for jt in range(jt0, jt1):
                for tq in tqs:
                    e.collective_compute("AllGather", ALU.bypass, replica_groups=[[0, 1, 2, 3], [4, 5, 6, 7]],
                                         ins=[YC.ap[jt, tq]], outs=[YG.ap[jt, tq].rearrange("r p t -> (r p) t")]).then_inc(sem)
        P.streams["pool"].append(thunk)
        P.semval["cc"] = P.semval.get("cc", 0) + len(tqs) * (jt1 - jt0)

    def phase_G(self):
        P = self.P
        markG = len(P._ctx)
        P.barrier()
        early = self.early_gather
        if not early:
            self.gather(0, 16)
            self.gather(16, 24)
        sem = P.sems["cc"]
        YC, YG = self.YC, self.YG

        def thunk(e):
            n = 96
            for jt in range(0):
                for tq in range(4):
                    e.collective_compute("AllGather", ALU.bypass, replica_groups=[[0, 1, 2, 3], [4, 5, 6, 7]],
                                         ins=[YC.ap[jt, tq]], outs=[YG.ap[jt, tq].rearrange("r p t -> (r p) t")]).then_inc(sem)
                    n += 1
            e.wait_ge(sem, n)
        P.streams["pool"].append(thunk)
        P.barrier()
        for nm, srcb in (("YCd", YC), ("YGd", YG)):
            if nm in self.debug:
                flat = srcb.ap[:].flatten_outer_dims()
                nrow = flat.shape[0]
                d = P.dram(nm, [nrow, self.LS], BF16, kind="ExternalOutput")
                self.out_bufs.append(d)
                tmp = P.sb("dbg" + nm, [128, self.LS], BF16)
                for r in range(nrow // 128):
                    P.dma("sp", tmp[:], flat[r * 128:(r + 1) * 128, :], reads=[srcb], writes=[tmp])
                    P.dma("sp", d.ap[r * 128:(r + 1) * 128, :], tmp[:], reads=[tmp], writes=[d])
                P.barrier()
        P.free_from(markG)

    def phase_D(self):
        P, L, LS = self.P, self.L, self.LS
        mark = len(P._ctx)
        psr = self.psr
        TB = min(256, LS)
        NT = TB // 128
        uT = P.sb("uTD", [128, 32, TB], BF16)
        xs = [P.sb("xsD%d" % i, [128, 32, 128], F32) for i in range(1)]
        sq = P.sb("sqD", [128, 32, 128], BF16)
        rt = P.sb("rtD", [128, 128], F32)
        rs = P.sb("rsD", [128, 128], F32)
        win = P.sb("winD", [128, 32], F32)
        self.epsb = P.sb("epsbD", [128, 1], F32)
        P.op("pool", lambda e, t=self.epsb: e.memset(t[:], EPS), writes=[self.epsb])
        P.dma("sp", win[:], self.nw_in[:], reads=[self.nw_in], writes=[win])
        wts = Rot([P.sb("wtD%d" % i, [128, 32, 512], BF16) for i in range(2)])
        sg = [P.sb("sg%d" % i, [128, 32, TB], BF16) for i in range(2)]
        yT = P.sb("yTD", [128, 64, TB], BF16)
        tmpf = Rot([P.sb("tmpf%d" % i, [128, TB], F32) for i in range(2)])
        hrow = P.sb("hrow", [128, 4096], F32)
        xr = Rot([P.sb("xr%d" % i, [128, 512], F32) for i in range(2)])
        nwf = P.sb("nwf", [128, 4096], F32)
        P.dma("sp", nwf[:], self.nw_fin[:], reads=[self.nw_fin], writes=[nwf])
        ssq = P.sb("ssqD", [128, 1], F32)
        rtn = P.sb("rtnD", [128, 1], F32)
        rsn = P.sb("rsnD", [128, 1], F32)
        YG = self.YG
        tokv = {}
        if self.early_gather:
            self.yq_copy(range(16, 24, 2), 96)
        else:
            self.yq_copy(range(0, 24, 2), 96)

        def yg_load(dst_ap, row0, nj, bk):
            def thunk_fn(e):
                gidx = e.partition_id() % 4
                src = YG.ap[row0:row0 + nj * 128, bass.ds(gidx * LS + bk * TB, TB)].rearrange("(j p) t -> p j t", p=128)
                return src
            return thunk_fn

        for bk in range(LS // TB):
            self.make_uT(lambda s, bk=bk: (self.xTs[:, :, bk * TB + s * 128:bk * TB + (s + 1) * 128], self.xTs),
                         NT, uT, xs, sq, rt, rs, win, psr)
            for m in range(2):
                for ct in range(8):
                    wt = wts.next()
                    P.dma("sp", wt[:], self.WD.ap[m * 8 + ct].rearrange("p (k c) -> p k c", k=32), reads=[self.WD], writes=[wt])
                    for cs in range(4):
                        ps = psr.next()
                        for kc in range(32):
                            P.op("pe", lambda e, kc=kc, ps=ps, cs=cs, wt=wt: e.matmul(
                                ps[:, 0:TB], lhsT=wt[:, kc, cs * 128:(cs + 1) * 128], rhs=uT[:, kc, :],
                                start=(kc == 0), stop=(kc == 31)), reads=[uT, wt], writes=[ps], signal=(kc == 31))
                        P.op("act", lambda e, ps=ps, m=m, ct=ct, cs=cs: e.activation(
                            out=sg[m][:, ct * 4 + cs, :], in_=ps[:, 0:TB], func=AF.Sigmoid), reads=[ps], writes=[sg[m]])
            for m in range(2):
                nkh = 2 if m == 0 else 1
                for r in range(4):
                    row0 = 0 if m == 0 else 16
                    nj = 16 if m == 0 else 8
                    P.dma("sp", yT[:, r * nj:(r + 1) * nj, :],
                          self.YQ.ap[row0:row0 + nj, r, :, bk * TB:(bk + 1) * TB].rearrange("j p t -> p j t"),
                          reads=[self.YQ], writes=[yT])
                for ct in range(8):
                    w_list = []
                    for kh in range(nkh):
                        wt = wts.next()
                        srcw = self.WD.ap[(16 + kh * 8 + ct) if m == 0 else (32 + ct)].rearrange("p (k c) -> p k c", k=32)
                        sbuf_ = self.WD
                        w_list.append((wt, srcw, sbuf_))
                    pss = [psr.next() for _ in range(4)]
                    for kh, (wt, srcw, sbuf_) in enumerate(w_list):
                        P.dma("sp", wt[:], srcw, reads=[sbuf_], writes=[wt])
                        for cs in range(4):
                            ps = pss[cs]
                            for kc in range(32):
                                first = (kh == 0 and kc == 0)
                                last = (kh == nkh - 1 and kc == 31)
                                P.op("pe", lambda e, kc=kc, ps=ps, cs=cs, wt=wt, kh=kh, first=first, last=last: e.matmul(
                                    ps[:, 0:TB], lhsT=wt[:, kc, cs * 128:(cs + 1) * 128], rhs=yT[:, kh * 32 + kc, :],
                                    start=first, stop=last), reads=[yT, wt], writes=[ps], signal=(kc == 31))
                    for cs in range(4):
                        ps = pss[cs]
                        ci = ct * 4 + cs
                        if m == 0:
                            P.op("dve", lambda e, ps=ps, ci=ci: e.tensor_tensor(out=sg[0][:, ci, :], in0=ps[:, 0:TB],
                                                                               in1=sg[0][:, ci, :], op=ALU.mult),
                                 reads=[ps, sg[0]], writes=[sg[0]])
                        else:
                            tf = tmpf.next()
                            P.op("dve", lambda e, ps=ps, ci=ci, tf=tf: e.tensor_tensor(out=tf[:], in0=ps[:, 0:TB],
                                                                                     in1=sg[1][:, ci, :], op=ALU.mult),
                                 reads=[ps, sg[1]], writes=[tf])
                            P.op("pool", lambda e, ci=ci, tf=tf: e.tensor_tensor(out=sg[1][:, ci, :], in0=tf[:],
                                                                                in1=sg[0][:, ci, :], op=ALU.add),
                                 reads=[tf, sg[0], sg[1]], writes=[sg[1]])
            mT = sg[1]
            for tt in range(NT):
                r0 = bk * TB + tt * 128
                for ct in range(8):
                    xrow = xr.next()
                    P.dma("sp", xrow[:], self.xres[r0:r0 + 128, ct * 512:(ct + 1) * 512], reads=[self.xres], writes=[xrow])
                    wt = wts.next()
                    P.dma("sp", wt[:], self.WD.ap[40 + ct].rearrange("p (k c) -> p k c", k=32), reads=[self.WD], writes=[wt])
                    ps = psr.next()
                    for kc in range(32):
                        P.op("pe", lambda e, kc=kc, ps=ps, tt=tt, wt=wt: e.matmul(
                            ps[:], lhsT=mT[:, kc, tt * 128:(tt + 1) * 128], rhs=wt[:, kc, :],
                            start=(kc == 0), stop=(kc == 31)), reads=[mT, wt], writes=[ps], signal=(kc == 31))
                    P.op("dve", lambda e, ps=ps, ct=ct, xrow=xrow: e.tensor_tensor(
                        out=hrow[:, ct * 512:(ct + 1) * 512], in0=ps[:], in1=xrow[:], op=ALU.add),
                        reads=[ps, xrow], writes=[hrow])
                P.op("act", lambda e: e.activation(out=xs[0][:].rearrange("p a b -> p (a b)"), in_=hrow[:], func=AF.Square,
                                                   accum_out=ssq[:]), reads=[hrow], writes=[xs[0], ssq])
                P.op("act", lambda e, epsb=self.epsb: e.activation(out=rtn[:], in_=ssq[:], func=AF.Sqrt, scale=1.0 / 4096,
                                                                   bias=epsb[:, 0:1]),
                     reads=[ssq, self.epsb], writes=[rtn])
                P.op("dve", lambda e: e.reciprocal(out=rsn[:], in_=rtn[:]), reads=[rtn], writes=[rsn])
                P.op("dve", lambda e: e.scalar_tensor_tensor(out=hrow[:], in0=hrow[:], scalar=rsn[:, 0:1], in1=nwf[:],
                                                             op0=ALU.mult, op1=ALU.mult), reads=[hrow, rsn, nwf], writes=[hrow])
                P.dma("sp", self.OUT[r0:r0 + 128, :], hrow[:], reads=[hrow], writes=[self.OUT])
        P.barrier()
        P.free_from(mark)

    def finish(self):
        P = self.P
        P.barrier(include_bg=True)
        P.emit()
        P.close()
        return self.nc


def build_program(L, phases="A", debug=()):
    k = K(L, debug, phases)
    k.declare()
    k.load_consts()
    if "A" in phases:
        k.phase_A()
    if "M" in phases:
        k.phase_A2()
    if "B" in phases:
        k.phase_B()
    if "C" in phases:
        k.phase_C()
    if "G" in phases:
        k.phase_G()
    if "D" in phases:
        k.phase_D()
    return k


def _tile_w(w, cols=None):
    if cols is not None:
        wz = np.zeros((w.shape[0], len(cols)), np.float32)
        m = cols >= 0
        wz[:, m] = w[:, cols[m]]
        w = wz
    Kd, N = w.shape
    return np.ascontiguousarray(w.reshape(Kd // 128, 128, N // 512, 512).transpose(2, 1, 0, 3))


def _in_cols(g):
    c = []
    c += list(range(g * 2048, (g + 1) * 2048))
    c += list(range(8192 + g * 2048, 8192 + (g + 1) * 2048))
    c += list(range(16384 + g * 256, 16384 + (g + 1) * 256))
    c += list(range(17408 + g * 256, 17408 + (g + 1) * 256))
    c += list(range(18560, 18560 + 1024))
    c += list(range(19584, 19584 + 512))
    c += list(range(20160 + g * 1024, 20160 + (g + 1) * 1024))
    rope = list(range(20096, 20160))
    c += rope + rope[32:] + rope[:32]
    c += list(range(18432 + g * 32, 18432 + (g + 1) * 32))
    c += [-1] * (15 * 512 - len(c))
    return np.array(c, np.int64)


def prepare_inputs(inp, L):
    f = lambda a: np.asarray(a, np.float32)
    x = f(inp["x"])[:, :L]
    NB, LS = L // 1024, L // 4
    w_in = f(inp["w_in"])[0]
    conv_w, conv_b = f(inp["conv_w"])[0], f(inp["conv_b"])[0]
    w_uq, w_ukv = f(inp["w_uq"])[0], f(inp["w_ukv"])[0]
    shared = {}
    shared["w_gates"] = _tile_w(w_in[:, 24256:24256 + 8192])
    wbs = f(inp["w_branch_ssm"])[0]
    shared["w_bs"] = np.concatenate([_tile_w(wbs[:4096]), _tile_w(wbs[4096:])], 0)
    shared["w_ba"] = _tile_w(f(inp["w_branch_attn"])[0])
    shared["w_out"] = _tile_w(f(inp["w_out"])[0])
    shared["nw_in"] = np.ascontiguousarray(f(inp["norm_in_w"])[0].reshape(32, 128).T)
    shared["nw_q"] = np.ascontiguousarray(f(inp["q_norm_w"])[0].reshape(8, 128).T)
    shared["nw_kv"] = np.ascontiguousarray(f(inp["kv_norm_w"])[0].reshape(4, 128).T)
    shared["nw_fin"] = np.ascontiguousarray(np.broadcast_to(f(inp["norm_final_w"])[None, :], (128, 4096)))
    half = 32
    inv_freq = (np.float32(10000.0) ** (-(np.arange(0, half, dtype=np.float32) / np.float32(half)))).astype(np.float32)
    ang = (np.arange(L, dtype=np.float32)[None, :] * inv_freq[:, None]).astype(np.float32)
    cos, sin = np.cos(ang).astype(np.float32), np.sin(ang).astype(np.float32)
    shared["ropec"] = np.concatenate([cos, cos], 0)
    shared["ropes"] = np.concatenate([-sin, sin], 0)
    shared["c_ident"] = np.eye(128, dtype=np.float32).astype(ml_dtypes.bfloat16)
    shared["c_ones"] = np.ones((128, 128), ml_dtypes.bfloat16)
    shared["c_onesf"] = np.ones((128, 128), np.float32)
    k = np.arange(128)
    shared["c_trile"] = (k[:, None] <= k[None, :]).astype(np.float32)
    shared["c_trigt"] = (k[:, None] > k[None, :]).astype(np.float32)
    shared["c_maskb"] = (k[:, None] <= k[None, :]).astype(np.float32).astype(ml_dtypes.bfloat16)
    per_g = []
    for g in range(4):
        d = {}
        d["w_in"] = _tile_w(w_in, _in_cols(g))
        cols = []
        for hl in range(8):
            H = 8 * g + hl
            base = H * 192
            rope = list(range(base + 128, base + 192))
            cols += list(range(base, base + 128)) + rope + rope[32:] + rope[:32]
        d["w_uq"] = np.ascontiguousarray(w_uq[:, cols].reshape(8, 128, 2048).transpose(1, 0, 2))
        cols = []
        for hl in range(8):
            H = 8 * g + hl
            cols += list(range(H * 256, H * 256 + 128))
        for hl in range(8):
            H = 8 * g + hl
            cols += list(range(H * 256 + 128, H * 256 + 256))
        d["w_ukv"] = np.ascontiguousarray(w_ukv[:, cols].reshape(4, 128, 2048).transpose(1, 0, 2))
        ch = np.concatenate([np.arange(g * 2048, (g + 1) * 2048),
                             8192 + g * 256 + np.arange(256), 8192 + 1024 + g * 256 + np.arange(256)])
        cpar = np.concatenate([conv_w[:, ch], conv_b[None, ch]], 0)
        d["convp"] = np.ascontiguousarray(cpar.reshape(5, 20, 128).transpose(2, 1, 0))
        hs = slice(g * 32, (g + 1) * 32)
        bc = lambda v: np.ascontiguousarray(np.broadcast_to(v[None, :], (128, v.shape[0])))
        d["dtb"] = bc(f(inp["dt_bias"])[0, hs])
        d["alog"] = bc(f(inp["a_log"])[0, hs])
        d["dskip"] = bc(f(inp["d_skip"])[0, hs])
        d["nw_ssm"] = bc(f(inp["ssm_norm_w"])[0, g * 2048:(g + 1) * 2048])
        per_g.append(d)
    maps = []
    for c in range(8):
        b, g = c // 4, c % 4
        m = dict(shared)
        m.update(per_g[g])
        xb = x[b]
        m["xT"] = np.ascontiguousarray(xb.reshape(NB, 1024, 32, 128).transpose(0, 3, 2, 1))
        xsl = xb[g * LS:(g + 1) * LS]
        m["xTs"] = np.ascontiguousarray(xsl.reshape(LS, 32, 128).transpose(2, 1, 0))
        m["xres"] = np.ascontiguousarray(xsl)
        maps.append(m)
    return maps


_CACHE = {}


def run(inputs, L=SEQ, phases="AMBCGD", debug=()):
    import time
    t0 = time.time()
    key = (L, phases, tuple(debug))
    if key not in _CACHE:
        k = build_program(L, phases, debug)
        nc = k.finish()
        _CACHE[key] = (nc, set(k.inp.keys()), k.P.ninstr)
    nc, names, ninstr = _CACHE[key]
    t1 = time.time()
    maps = prepare_inputs(inputs, L)
    maps = [{n: m[n] for n in names} for m in maps]
    t2 = time.time()
    res = run_bass_kernel_spmd(nc, maps, core_ids=list(range(8)))
    if os.environ.get("KDBG"):
        print("ninstr %d build %.1fs prep %.1fs run %.1fs" % (ninstr, t1 - t0, t2 - t1, time.time() - t2), flush=True)
    return res.results


def kernel(**inputs):
    res = run(inputs)
    LS = SEQ // 4
    out = np.empty((2, SEQ, D_MODEL), np.float32)
    for c in range(8):
        b, g = c // 4, c % 4
        out[b, g * LS:(g + 1) * LS] = res[c]["out"]
    return out
```

```python
import os
import math
import numpy as np
import ml_dtypes
import concourse.bass as bass
import concourse.mybir as mybir
from concourse.bass_utils import run_bass_kernel_spmd

F32 = mybir.dt.float32
BF16 = mybir.dt.bfloat16
AF = mybir.ActivationFunctionType
ALU = mybir.AluOpType
EPS = 1e-6

D_MODEL = 4096
D_SSM = 8192
D_CONV = 10240
SEQ = 8192


class Buf:
    def __init__(self, name, ap, accum=False):
        self.name = name
        self.ap = ap
        self.ws = {}
        self.rs = {}
        self.accum = accum
        self.dsem = None
        self.dcount = 0

    def __getitem__(self, idx):
        return self.ap[idx]


class Prog:
    ENGS = ("pe", "act", "dve", "pool", "sp")

    def __init__(self, nc):
        self.nc = nc
        self.streams = {e: [] for e in self.ENGS}
        self.cnt = {e: 0 for e in self.ENGS}
        self.sems = {}
        self.semval = {}
        self._ctx = []
        self._semctx = []
        self.pending = {e: [] for e in self.ENGS}
        self.waited = {e: {} for e in self.ENGS}
        self.bg_keys = set()
        self.nsem = 0
        self.ninstr = 0
        for e in ("pe", "act", "dve", "pool"):
            self._mksem(e)

    def _mksem(self, key):
        self.nsem += 1
        g = self.nc.semaphore("sem%d" % self.nsem)
        h = g.__enter__()
        self._semctx.append(g)
        self.sems[key] = h
        self.semval[key] = 0
        return h

    def sb(self, name, shape, dt):
        g = self.nc.sbuf_tensor("sb_" + name, list(shape), dt)
        t = g.__enter__()
        self._ctx.append(g)
        return Buf(name, t)

    def ps(self, name, shape, dt=F32):
        g = self.nc.psum_tensor(name, list(shape), dt)
        t = g.__enter__()
        self._ctx.append(g)
        return Buf(name, t)

    def dram(self, name, shape, dt, kind="Internal"):
        t = self.nc.dram_tensor(name, list(shape), dt, kind=kind)
        return Buf(name, t, accum=True)

    def _deps(self, eng, reads, writes):
        need = {}

        def add(k, v):
            if k == "pe" and eng == "pe":
                return
            if need.get(k, 0) < v:
                need[k] = v
        for b in reads:
            for k, v in b.ws.items():
                add(k, v)
        for b in writes:
            if b.accum:
                continue
            for k, v in b.ws.items():
                add(k, v)
            for k, v in b.rs.items():
                add(k, v)
        w = self.waited[eng]
        need = {k: v for k, v in need.items() if w.get(k, 0) < v}
        for k, v in need.items():
            w[k] = v
        return need

    @staticmethod
    def _rec(tok, reads, writes):
        k, v = tok
        for b in reads:
            b.rs[k] = v
        for b in writes:
            if b.accum:
                b.ws[k] = v
            else:
                b.ws = {k: v}
                b.rs = {}

    def op(self, eng, fn, reads=(), writes=(), signal=True):
        need = self._deps(eng, reads, writes)
        waits = [(self.sems[k], v) for k, v in need.items()]
        self.ninstr += 1
        if signal:
            self.cnt[eng] += 1
            self.semval[eng] = self.cnt[eng]
            tok = (eng, self.cnt[eng])
            pr = [b for (b, kind) in self.pending[eng] if kind == "r"]
            pw = [b for (b, kind) in self.pending[eng] if kind == "w"]
            self.pending[eng] = []
            self._rec(tok, list(reads) + pr, list(writes) + pw)
        else:
            for b in reads:
                self.pending[eng].append((b, "r"))
            for b in writes:
                self.pending[eng].append((b, "w"))
        sem = self.sems[eng]

        def thunk(e, waits=waits, fn=fn, signal=signal, sem=sem):
            for (s, v) in waits:
                e.wait_ge(s, v)
            ins = fn(e)
            if signal:
                ins.then_inc(sem, 1)
        self.streams[eng].append(thunk)

    def dma(self, q, out_ap, in_ap, reads=(), writes=(), sembuf=None):
        need = self._deps(q, reads, writes)
        waits = [(self.sems[k], v) for k, v in need.items()]
        self.ninstr += 1
        sbf = sembuf
        if sbf is None:
            cands = [b for b in list(writes) + list(reads) if not b.accum]
            sbf = cands[0] if cands else (list(writes) + list(reads))[0]
        if sbf.dsem is None:
            sbf.dsem = ("d", self.nsem + 1)
            self._mksem(sbf.dsem)
        sbf.dcount += 16
        self.semval[sbf.dsem] = sbf.dcount
        tok = (sbf.dsem, sbf.dcount)
        self._rec(tok, reads, writes)
        sem = self.sems[sbf.dsem]

        def thunk(e, waits=waits, sem=sem, out_ap=out_ap, in_ap=in_ap):
            for (s, v) in waits:
                e.wait_ge(s, v)
            e.dma_start(out=out_ap, in_=in_ap).then_inc(sem, 16)
        self.streams[q].append(thunk)

    def barrier(self, include_bg=False):
        for e in self.ENGS:
            assert not self.pending[e]
        for eng in self.ENGS:
            w = self.waited[eng]
            items = [(k, v) for k, v in self.semval.items() if v > 0 and (include_bg or k not in self.bg_keys)]
            waits = [(self.sems[k], v) for k, v in items if w.get(k, 0) < v]
            for k, v in items:
                w[k] = max(w.get(k, 0), v)

            def thunk(e, waits=waits):
                for (s, v) in waits:
                    e.wait_ge(s, v)
            self.streams[eng].append(thunk)

    def emit(self):
        nc = self.nc
        for e in self.ENGS:
            assert not self.pending[e], "unsignalled pending ops on %s" % e
        with nc.Block() as block:
            @block.tensor
            def _(e):
                for t in self.streams["pe"]:
                    t(e)

            @block.scalar
            def _(e):
                for t in self.streams["act"]:
                    t(e)

            @block.vector
            def _(e):
                for t in self.streams["dve"]:
                    t(e)

            @block.gpsimd
            def _(e):
                for t in self.streams["pool"]:
                    t(e)

            @block.sync
            def _(e):
                for t in self.streams["sp"]:
                    t(e)

    def close(self):
        for g in reversed(self._ctx):
            g.__exit__(None, None, None)
        for g in reversed(self._semctx):
            g.__exit__(None, None, None)

    def free_from(self, mark):
        while len(self._ctx) > mark:
            g = self._ctx.pop()
            g.__exit__(None, None, None)


class Rot:
    def __init__(self, items):
        self.items = items
        self.i = 0

    def next(self):
        b = self.items[self.i % len(self.items)]
        self.i += 1
        return b


class K:
    def __init__(self, L, debug=(), phases="AMBCGD"):
        self.L = L
        self.phases = phases
        self.debug = set(debug)
        self.nc = nc = bass.Bass("TRN2", target_bir_lowering=False)
        nc.cache_partition_id()
        self.P = Prog(nc)
        self.inp = {}
        self.out_bufs = []
        self.NB = L // 1024
        self.LS = L // 4
        self.early_gather = True
        self.tokv = {}
        self.gathered_q = 0

    def ein(self, name, shape, dt=F32):
        need = {"w_gates": "D", "w_bs": "D", "w_ba": "D", "w_out": "D", "xres": "D", "xTs": "D", "nw_fin": "D",
                "tokoff": "D", "w_in": "A", "xT": "A"}
        if name in need and need[name] not in self.phases:
            return None
        t = self.nc.dram_tensor(name, list(shape), dt, kind="ExternalInput")
        b = Buf(name, t)
        self.inp[name] = b
        return b

    def scratch(self, name, shape, dt):
        kind = "ExternalOutput" if name in self.debug else "Internal"
        b = self.P.dram(name, shape, dt, kind=kind)
        if kind == "ExternalOutput":
            self.out_bufs.append(b)
        return b

    def declare(self):
        L, NB, LS = self.L, self.NB, self.LS
        e = self.ein
        self.xT = e("xT", [NB, 128, 32, 1024])
        self.xTs = e("xTs", [128, 32, LS])
        self.xres = e("xres", [LS, 4096])
        self.w_in = e("w_in", [15, 128, 32, 512])
        self.w_gates = e("w_gates", [16, 128, 32, 512])
        self.w_bs = e("w_bs", [16, 128, 32, 512])
        self.w_ba = e("w_ba", [8, 128, 32, 512])
        self.w_out = e("w_out", [8, 128, 32, 512])
        self.w_uq = e("w_uq", [128, 8, 2048])
        self.w_ukv = e("w_ukv", [128, 4, 2048])
        self.convp = e("convp", [128, 20, 5])
        self.nw_in = e("nw_in", [128, 32])
        self.nw_q = e("nw_q", [128, 8])
        self.nw_kv = e("nw_kv", [128, 4])
        self.dtb = e("dtb", [128, 32])
        self.alog = e("alog", [128, 32])
        self.dskip = e("dskip", [128, 32])
        self.nw_ssm = e("nw_ssm", [128, 2048])
        self.nw_fin = e("nw_fin", [128, 4096])
        self.ropec = e("ropec", [64, L])
        self.ropes = e("ropes", [64, L])
        self.c_ident = e("c_ident", [128, 128], BF16)
        self.c_ones = e("c_ones", [128, 128], BF16)
        self.c_onesf = e("c_onesf", [128, 128])
        self.c_trile = e("c_trile", [128, 128])
        self.c_trigt = e("c_trigt", [128, 128])
        self.c_maskb = e("c_maskb", [128, 128], BF16)
        s = self.scratch
        self.SZ = s("SZ", [L, 2048], BF16)
        self.XC = s("XC", [2560, L], BF16)
        self.DT = s("DT", [L, 32], F32)
        self.CQ = s("CQ", [1024, L], BF16)
        self.CKV = s("CKV", [512, L], BF16)
        self.RR = s("RR", [128, L], F32)
        self.GA = s("GA", [1024, L], BF16)
        self.QN = s("QN", [8, 128, L], BF16)
        self.QR = s("QR", [8, 64, L], BF16)
        self.KN = s("KN", [8, 128, L], BF16)
        self.KR = s("KR", [64, L], BF16)
        self.V = s("V", [L, 1024], BF16)
        self.YC = s("YC", [24, 4, 128, LS], BF16)
        self.YG = s("YG", [24, 4, 4, 128, LS], BF16)
        self.YQ = s("YQ", [24, 4, 128, LS], BF16)
        self.WD = s("WD", [48, 128, 32 * 512], BF16)
        self.OUT = self.P.dram("out", [LS, 4096], F32, kind="ExternalOutput")
        self.out_bufs.append(self.OUT)

    def yc_store(self, jt0, nj, t0, ntok, src_fn, srcbuf):
        LS = self.LS
        t = t0
        while t < t0 + ntok:
            tq, tl = t // LS, t % LS
            n = min(LS - tl, t0 + ntok - t)
            dst = self.YC.ap[jt0:jt0 + nj, tq, :, tl:tl + n].rearrange("j p t -> p j t")
            self.P.dma("sp", dst, src_fn(t - t0, t - t0 + n), reads=[srcbuf], writes=[self.YC])
            t += n

    def load_consts(self):
        P = self.P
        self.ident = P.sb("ident", [128, 128], BF16)
        self.ones = P.sb("ones", [128, 128], BF16)
        for sbuf, src in ((self.ident, self.c_ident), (self.ones, self.c_ones)):
            P.dma("sp", sbuf[:], src[:], reads=[src], writes=[sbuf])
        self.psb = [P.ps("psb%d" % i, [128, 512]) for i in range(8)]
        self.psr = Rot(self.psb)

    def make_uT(self, src_fn, nsub, uT, xs, sq, rt, rs, win, ps_rot):
        P = self.P
        ones = self.ones
        epsb = self.epsb
        for s in range(nsub):
            x = xs[s % len(xs)]
            P.dma("sp", x[:], src_fn(s)[0], reads=[src_fn(s)[1]], writes=[x])
            P.op("act", lambda e, x=x: e.activation(out=sq[:], in_=x[:], func=AF.Square),
                 reads=[x], writes=[sq])
            ps = ps_rot.next()
            for kc in range(32):
                P.op("pe", lambda e, kc=kc, ps=ps: e.matmul(ps[:, 0:128], lhsT=ones[:], rhs=sq[:, kc, :],
                                                          start=(kc == 0), stop=(kc == 31)),
                     reads=[ones, sq], writes=[ps], signal=(kc == 31))
            P.op("act", lambda e, ps=ps: e.activation(out=rt[:], in_=ps[:, 0:128], func=AF.Sqrt,
                                                      scale=1.0 / D_MODEL, bias=epsb[:, 0:1]),
                 reads=[ps, epsb], writes=[rt])
            P.op("dve", lambda e: e.reciprocal(out=rs[:], in_=rt[:]), reads=[rt], writes=[rs])
            P.op("dve", lambda e, x=x: e.tensor_tensor(out=x[:], in0=x[:],
                                                       in1=win[:].unsqueeze(2).to_broadcast([128, 32, 128]),
                                                       op=ALU.mult),
                 reads=[x, win], writes=[x])
            P.op("dve", lambda e, x=x, s=s: e.tensor_tensor(out=uT[:, :, s * 128:(s + 1) * 128], in0=x[:],
                                                            in1=rs[:].unsqueeze(1).to_broadcast([128, 32, 128]),
                                                            op=ALU.mult),
                 reads=[x, rs], writes=[uT])

    def phase_A(self):
        P, L, NB = self.P, self.L, self.NB
        mark = len(P._ctx)
        uT = P.sb("uT", [128, 32, 1024], BF16)
        xs = [P.sb("xs%d" % i, [128, 32, 128], F32) for i in range(2)]
        sq = P.sb("sq", [128, 32, 128], BF16)
        rt = P.sb("rt", [128, 128], F32)
        rs = P.sb("rs", [128, 128], F32)
        win = P.sb("win", [128, 32], F32)
        self.epsb = P.sb("epsb", [128, 1], F32)
        P.op("pool", lambda e, t=self.epsb: e.memset(t[:], EPS), writes=[self.epsb])
        P.dma("sp", win[:], self.nw_in[:], reads=[self.nw_in], writes=[win])
        wts = Rot([P.sb("wt%d" % i, [128, 32, 512], BF16) for i in range(2)])
        stg = Rot([P.sb("stg%d" % i, [128, 512], BF16) for i in range(4)])
        stf = Rot([P.sb("stf%d" % i, [128, 512], F32) for i in range(2)])
        raws = Rot([P.sb("raw%d" % i, [128, 515], F32) for i in range(2)])
        accs = Rot([P.sb("acc%d" % i, [128, 512], F32) for i in range(2)])
        halo = P.sb("halo", [128, 20, 3], F32)
        cp = P.sb("cp", [128, 20, 5], F32)
        dtb = P.sb("dtb", [128, 32], F32)
        dtx = P.sb("dtx", [128, 8, 32], F32)
        dte = P.sb("dte", [128, 8, 32], F32)
        dtt = P.sb("dtt", [128, 8, 32], F32)
        P.dma("sp", cp[:], self.convp[:], reads=[self.convp], writes=[cp])
        P.dma("sp", dtb[:], self.dtb[:], reads=[self.dtb], writes=[dtb])
        psr = self.psr

        for tb in range(NB):
            t0 = tb * 1024
            self.make_uT(lambda s, tb=tb: (self.xT[tb, :, :, s * 128:(s + 1) * 128], self.xT),
                         8, uT, xs, sq, rt, rs, win, psr)
            for t in range(15):
                wt = wts.next()
                P.dma("pool", wt[:], self.w_in[t], reads=[self.w_in], writes=[wt])
                if t < 4:
                    for tt in range(8):
                        ps = psr.next()
                        for kc in range(32):
                            P.op("pe", lambda e, kc=kc, ps=ps, tt=tt, wt=wt: e.matmul(
                                ps[:], lhsT=uT[:, kc, tt * 128:(tt + 1) * 128], rhs=wt[:, kc, :],
                                start=(kc == 0), stop=(kc == 31)),
                                reads=[uT, wt], writes=[ps], signal=(kc == 31))
                        st = stg.next()
                        P.op("act", lambda e, ps=ps, st=st: e.activation(out=st[:], in_=ps[:], func=AF.Silu),
                             reads=[ps], writes=[st])
                        P.dma("sp", self.SZ[t0 + tt * 128:t0 + (tt + 1) * 128, t * 512:(t + 1) * 512], st[:],
                              reads=[st], writes=[self.SZ])
                    continue
                ncs = 1 if t == 14 else 4
                for cs in range(ncs):
                    for th in range(2):
                        ps = psr.next()
                        for kc in range(32):
                            P.op("pe", lambda e, kc=kc, ps=ps, cs=cs, th=th, wt=wt: e.matmul(
                                ps[:], lhsT=wt[:, kc, cs * 128:(cs + 1) * 128], rhs=uT[:, kc, th * 512:(th + 1) * 512],
                                start=(kc == 0), stop=(kc == 31)),
                                reads=[uT, wt], writes=[ps], signal=(kc == 31))
                        tok = slice(t0 + th * 512, t0 + (th + 1) * 512)
                        if 4 <= t <= 8:
                            idx = (t - 4) * 4 + cs
                            raw = raws.next()
                            acc = accs.next()
                            P.op("dve", lambda e, raw=raw, ps=ps: e.tensor_copy(out=raw[:, 3:515], in_=ps[:]),
                                 reads=[ps], writes=[raw])
                            if tb == 0 and th == 0:
                                P.op("pool", lambda e, raw=raw: e.memset(raw[:, 0:3], 0.0), writes=[raw])
                            else:
                                P.op("pool", lambda e, raw=raw, idx=idx: e.tensor_copy(out=raw[:, 0:3], in_=halo[:, idx, :]),
                                     reads=[halo], writes=[raw])
                            P.op("pool", lambda e, raw=raw, idx=idx: e.tensor_copy(out=halo[:, idx, :], in_=raw[:, 512:515]),
                                 reads=[raw], writes=[halo])
                            P.op("dve", lambda e, raw=raw, acc=acc, idx=idx: e.tensor_scalar(
                                out=acc[:], in0=raw[:, 3:515], scalar1=cp[:, idx, 3:4], scalar2=cp[:, idx, 4:5],
                                op0=ALU.mult, op1=ALU.add), reads=[raw, cp], writes=[acc])
                            for k in range(3):
                                P.op("dve", lambda e, raw=raw, acc=acc, idx=idx, k=k: e.scalar_tensor_tensor(
                                    out=acc[:], in0=raw[:, k:k + 512], scalar=cp[:, idx, k:k + 1], in1=acc[:],
                                    op0=ALU.mult, op1=ALU.add), reads=[raw, cp, acc], writes=[acc])
                            st = stg.next()
                            P.op("act", lambda e, acc=acc, st=st: e.activation(out=st[:], in_=acc[:], func=AF.Silu),
                                 reads=[acc], writes=[st])
                            P.dma("sp", self.XC[idx * 128:(idx + 1) * 128, tok], st[:], reads=[st], writes=[self.XC])
                        elif t in (9, 10, 11, 12, 13):
                            st = stg.next()
                            fn = AF.Silu if t >= 12 else AF.Copy
                            P.op("act", lambda e, ps=ps, st=st, fn=fn: e.activation(out=st[:], in_=ps[:], func=fn),
                                 reads=[ps], writes=[st])
                            if t <= 10:
                                r0 = ((t - 9) * 4 + cs) * 128
                                dst, dbuf = self.CQ[r0:r0 + 128, tok], self.CQ
                            elif t == 11:
                                dst, dbuf = self.CKV[cs * 128:(cs + 1) * 128, tok], self.CKV
                            else:
                                r0 = ((t - 12) * 4 + cs) * 128
                                dst, dbuf = self.GA[r0:r0 + 128, tok], self.GA
                            P.dma("sp", dst, st[:], reads=[st], writes=[dbuf])
                        else:
                            st = stf.next()
                            P.op("act", lambda e, ps=ps, st=st: e.activation(out=st[:], in_=ps[:], func=AF.Copy),
                                 reads=[ps], writes=[st])
                            P.dma("sp", self.RR[:, tok], st[:], reads=[st], writes=[self.RR])
                if t == 14:
                    ps = psr.next()
                    for tt in range(8):
                        for kc in range(32):
                            P.op("pe", lambda e, kc=kc, ps=ps, tt=tt, wt=wt: e.matmul(
                                ps[:, tt * 32:(tt + 1) * 32], lhsT=uT[:, kc, tt * 128:(tt + 1) * 128],
                                rhs=wt[:, kc, 128:160], start=(kc == 0), stop=(kc == 31)),
                                reads=[uT, wt], writes=[ps], signal=(kc == 31 and tt == 7))
                    P.op("dve", lambda e, ps=ps: e.tensor_tensor(
                        out=dtx[:], in0=ps[:, 0:256].rearrange("p (a b) -> p a b", a=8),
                        in1=dtb[:].unsqueeze(1).to_broadcast([128, 8, 32]), op=ALU.add),
                        reads=[ps, dtb], writes=[dtx])
                    P.op("act", lambda e: e.activation(out=dte[:], in_=dtx[:], func=AF.Exp), reads=[dtx], writes=[dte])
                    P.op("act", lambda e: e.activation(out=dtt[:], in_=dte[:], func=AF.Ln, bias=1.0, scale=1.0),
                         reads=[dte], writes=[dtt])
                    P.dma("sp", self.DT[t0:t0 + 1024, :].rearrange("(a p) h -> p a h", p=128), dtt[:],
                          reads=[dtt], writes=[self.DT])
        P.barrier()
        P.free_from(mark)


    def convert_weights(self):
        P = self.P
        for (src, n, base) in ((self.w_gates, 16, 0), (self.w_bs, 16, 16), (self.w_ba, 8, 32), (self.w_out, 8, 40)):
            for t in range(n):
                P.dma("pool", self.WD.ap[base + t], src.ap[t].rearrange("p k c -> p (k c)"), reads=[src], writes=[self.WD])
                P.bg_keys.add(src.dsem)
                yield

    def yq_copy(self, j0s, cc_count):
        P = self.P
        YQ, YG = self.YQ, self.YG
        if YQ.dsem is None:
            YQ.dsem = ("d", P.nsem + 1)
            P._mksem(YQ.dsem)
        qsem = P.sems[YQ.dsem]
        ccsem = P.sems["cc"]
        tokv = self.tokv
        for j0 in j0s:
            YQ.dcount += 16
            P.semval[YQ.dsem] = YQ.dcount
            P._rec((YQ.dsem, YQ.dcount), [YG], [YQ])
            P.ninstr += 1

            def qthunk(e, j0=j0):
                e.wait_ge(ccsem, cc_count)
                if "v" not in tokv:
                    tokv["v"] = e.snap(e.partition_id() % 4, min_val=0, max_val=3)
                src = YG.ap[j0:j0 + 2, bass.ds(tokv["v"], 1), :, :, :].rearrange("j o r p t -> j (o r p t)")
                dst = YQ.ap[j0:j0 + 2].rearrange("j r p t -> j (r p t)")
                e.dma_start(out=dst, in_=src).then_inc(qsem, 16)
            P.streams["pool"].append(qthunk)

    def phase_A2(self):
        P, L = self.P, self.L
        mark = len(P._ctx)
        psr = self.psr
        wq = P.sb("wq", [128, 8, 2048], BF16)
        wkv = P.sb("wkv", [128, 4, 2048], BF16)
        P.dma("pool", wq[:], self.w_uq[:], reads=[self.w_uq], writes=[wq])
        P.dma("pool", wkv[:], self.w_ukv[:], reads=[self.w_ukv], writes=[wkv])
        nwq = P.sb("nwq", [128, 8], F32)
        nwkv = P.sb("nwkv", [128, 4], F32)
        P.dma("sp", nwq[:], self.nw_q[:], reads=[self.nw_q], writes=[nwq])
        P.dma("sp", nwkv[:], self.nw_kv[:], reads=[self.nw_kv], writes=[nwkv])
        epsb = P.sb("epsb2", [128, 1], F32)
        P.op("pool", lambda e: e.memset(epsb[:], EPS), writes=[epsb])
        cqs = Rot([P.sb("cq%d" % i, [128, 8, 512], BF16) for i in range(2)])
        ckvs = Rot([P.sb("ckv%d" % i, [128, 4, 512], BF16) for i in range(2)])
        rras = Rot([P.sb("rra%d" % i, [64, 512], F32) for i in range(2)])
        rrbs = Rot([P.sb("rrb%d" % i, [64, 512], F32) for i in range(2)])
        coss = Rot([P.sb("cos%d" % i, [64, 512], F32) for i in range(2)])
        sins = Rot([P.sb("sin%d" % i, [64, 512], F32) for i in range(2)])
        sqb = P.sb("sqb", [128, 8, 512], BF16)
        rt = P.sb("rt2", [128, 512], F32)
        rq = P.sb("rq", [128, 512], F32)
        rkv = P.sb("rkv", [128, 512], F32)
        cqw = P.sb("cqw", [128, 8, 512], BF16)
        ckvw = P.sb("ckvw", [128, 4, 512], BF16)
        stg = Rot([P.sb("stg2_%d" % i, [128, 512], BF16) for i in range(4)])
        tA = Rot([P.sb("tA%d" % i, [64, 512], F32) for i in range(2)])
        tB = Rot([P.sb("tB%d" % i, [64, 512], F32) for i in range(2)])
        ones = self.ones

        def rope(srcA, srcB, bufsA, bufsB, cos, sin, dst, dbuf):
            a, b = tA.next(), tB.next()
            P.op("dve", lambda e: e.tensor_tensor(out=a[:], in0=srcA, in1=cos[:], op=ALU.mult),
                 reads=bufsA + [cos], writes=[a])
            P.op("dve", lambda e: e.tensor_tensor(out=b[:], in0=srcB, in1=sin[:], op=ALU.mult),
                 reads=bufsB + [sin], writes=[b])
            st = stg.next()
            P.op("pool", lambda e: e.tensor_tensor(out=st[0:64, :], in0=a[:], in1=b[:], op=ALU.add),
                 reads=[a, b], writes=[st])
            P.dma("sp", dst, st[0:64, :], reads=[st], writes=[dbuf])

        for tk in range(L // 512):
            tok = slice(tk * 512, (tk + 1) * 512)
            cq, ckv, rra, rrb, cos, sin = cqs.next(), ckvs.next(), rras.next(), rrbs.next(), coss.next(), sins.next()
            P.dma("sp", cq[:], self.CQ[:, tok].rearrange("(j p) t -> p j t", p=128), reads=[self.CQ], writes=[cq])
            P.dma("sp", ckv[:], self.CKV[:, tok].rearrange("(j p) t -> p j t", p=128), reads=[self.CKV], writes=[ckv])
            P.dma("sp", rra[:], self.RR[0:64, tok], reads=[self.RR], writes=[rra])
            P.dma("sp", rrb[:], self.RR[64:128, tok], reads=[self.RR], writes=[rrb])
            P.dma("sp", cos[:], self.ropec[:, tok], reads=[self.ropec], writes=[cos])
            P.dma("sp", sin[:], self.ropes[:, tok], reads=[self.ropes], writes=[sin])
            for (src, nk, nw, rr_, dst, dim) in ((cq, 8, nwq, rq, cqw, 1024), (ckv, 4, nwkv, rkv, ckvw, 512)):
                P.op("act", lambda e, src=src, nk=nk: e.activation(out=sqb[:, 0:nk, :], in_=src[:], func=AF.Square),
                     reads=[src], writes=[sqb])
                ps = psr.next()
                for kc in range(nk):
                    P.op("pe", lambda e, kc=kc, ps=ps, nk=nk: e.matmul(ps[:], lhsT=ones[:], rhs=sqb[:, kc, :],
                                                                    start=(kc == 0), stop=(kc == nk - 1)),
                         reads=[ones, sqb], writes=[ps], signal=(kc == nk - 1))
                P.op("act", lambda e, ps=ps, dim=dim: e.activation(out=rt[:], in_=ps[:], func=AF.Sqrt,
                                                                   scale=1.0 / dim, bias=epsb[:, 0:1]),
                     reads=[ps, epsb], writes=[rt])
                P.op("dve", lambda e, rr_=rr_: e.reciprocal(out=rr_[:], in_=rt[:]), reads=[rt], writes=[rr_])
                for kc in range(nk):
                    P.op("dve", lambda e, kc=kc, src=src, nw=nw, rr_=rr_, dst=dst: e.scalar_tensor_tensor(
                        out=dst[:, kc, :], in0=src[:, kc, :], scalar=nw[:, kc:kc + 1], in1=rr_[:],
                        op0=ALU.mult, op1=ALU.mult), reads=[src, nw, rr_], writes=[dst])
            for hl in range(8):
                ps = psr.next()
                for kc in range(8):
                    P.op("pe", lambda e, kc=kc, ps=ps, hl=hl: e.matmul(
                        ps[:], lhsT=wq[:, kc, hl * 256:hl * 256 + 128], rhs=cqw[:, kc, :],
                        start=(kc == 0), stop=(kc == 7)), reads=[wq, cqw], writes=[ps], signal=(kc == 7))
                st = stg.next()
                P.op("act", lambda e, ps=ps, st=st: e.activation(out=st[:], in_=ps[:], func=AF.Copy), reads=[ps], writes=[st])
                P.dma("sp", self.QN[hl, :, tok], st[:], reads=[st], writes=[self.QN])
                psA, psB = psr.next(), psr.next()
                for (pp, off) in ((psA, 128), (psB, 192)):
                    for kc in range(8):
                        P.op("pe", lambda e, kc=kc, pp=pp, hl=hl, off=off: e.matmul(
                            pp[0:64, :], lhsT=wq[:, kc, hl * 256 + off:hl * 256 + off + 64], rhs=cqw[:, kc, :],
                            start=(kc == 0), stop=(kc == 7)), reads=[wq, cqw], writes=[pp], signal=(kc == 7))
                rope(psA[0:64, :], psB[0:64, :], [psA], [psB], cos, sin, self.QR[hl, :, tok], self.QR)
                ps = psr.next()
                for kc in range(4):
                    P.op("pe", lambda e, kc=kc, ps=ps, hl=hl: e.matmul(
                        ps[:], lhsT=wkv[:, kc, hl * 128:(hl + 1) * 128], rhs=ckvw[:, kc, :],
                        start=(kc == 0), stop=(kc == 3)), reads=[wkv, ckvw], writes=[ps], signal=(kc == 3))
                st = stg.next()
                P.op("act", lambda e, ps=ps, st=st: e.activation(out=st[:], in_=ps[:], func=AF.Copy), reads=[ps], writes=[st])
                P.dma("sp", self.KN[hl, :, tok], st[:], reads=[st], writes=[self.KN])
            for tt in range(4):
                for hf in range(2):
                    ps = psr.next()
                    for kc in range(4):
                        P.op("pe", lambda e, kc=kc, ps=ps, tt=tt, hf=hf: e.matmul(
                            ps[:], lhsT=ckvw[:, kc, tt * 128:(tt + 1) * 128],
                            rhs=wkv[:, kc, 1024 + hf * 512:1024 + (hf + 1) * 512],
                            start=(kc == 0), stop=(kc == 3)), reads=[wkv, ckvw], writes=[ps], signal=(kc == 3))
                    st = stg.next()
                    P.op("act", lambda e, ps=ps, st=st: e.activation(out=st[:], in_=ps[:], func=AF.Copy), reads=[ps], writes=[st])
                    r0 = tk * 512 + tt * 128
                    P.dma("sp", self.V[r0:r0 + 128, hf * 512:(hf + 1) * 512], st[:], reads=[st], writes=[self.V])
            rope(rra[:], rrb[:], [rra], [rrb], cos, sin, self.KR[:, tok], self.KR)
        P.barrier()
        P.free_from(mark)

    def phase_C(self):
        P, L = self.P, self.L
        mark = len(P._ctx)
        if self.early_gather and "G" in self.phases and self.gathered_q < 4:
            self.gather(0, 16, tqs=tuple(range(self.gathered_q, 4)))
        conv = self.convert_weights() if "D" in self.phases else iter(())
        slot = 0
        psr = self.psr
        NKT = L // 128
        scale = 1.0 / math.sqrt(192.0)
        ones = self.ones
        maskb = P.sb("maskb", [128, 128], BF16)
        P.dma("sp", maskb[:], self.c_maskb[:], reads=[self.c_maskb], writes=[maskb])
        kr = P.sb("kr", [64, L], BF16)
        P.dma("sp", kr[:], self.KR[:, :], reads=[self.KR], writes=[kr])
        qns = Rot([P.sb("qn%d" % i, [128, L], BF16) for i in range(2)])
        qrs = Rot([P.sb("qr%d" % i, [64, L], BF16) for i in range(2)])
        kns = Rot([P.sb("kn%d" % i, [128, L], BF16) for i in range(2)])
        vs = Rot([P.sb("v%d" % i, [128, NKT, 128], BF16) for i in range(2)])
        pts = Rot([P.sb("pt%d" % i, [128, 512], BF16) for i in range(3)])
        gas = Rot([P.sb("ga%d" % i, [128, 512], BF16) for i in range(2)])
        rinv = P.sb("rinv", [128, 512], F32)
        o1 = P.sb("o1", [128, 512], F32)
        stg = Rot([P.sb("stg3_%d" % i, [128, 512], BF16) for i in range(2)])
        acc_r = Rot(self.psb[0:4])
        s_r = Rot(self.psb[4:8])
        def load_head(hl, q="pool"):
            qn, qr, kn, v = qns.next(), qrs.next(), kns.next(), vs.next()
            P.dma(q, qn[:], self.QN[hl], reads=[self.QN], writes=[qn])
            P.dma(q, qr[:], self.QR[hl], reads=[self.QR], writes=[qr])
            P.dma(q, kn[:], self.KN[hl], reads=[self.KN], writes=[kn])
            P.dma(q, v[:], self.V[:, hl * 128:(hl + 1) * 128].rearrange("(a p) d -> p a d", p=128),
                  reads=[self.V], writes=[v])
            return qn, qr, kn, v
        nxt_head = load_head(0, "sp")
        for hl in range(8):
            qn, qr, kn, v = nxt_head
            for qi in range(L // 512):
                if qi == min(2, L // 512 - 1) and hl + 1 < 8:
                    nxt_head = load_head(hl + 1)
                slot += 1
                if slot % 2 == 0:
                    next(conv, None)
                ga = gas.next()
                P.dma("sp", ga[:], self.GA[hl * 128:(hl + 1) * 128, qi * 512:(qi + 1) * 512], reads=[self.GA], writes=[ga])
                O, Rs = acc_r.next(), acc_r.next()
                nkt = 4 * qi + 4
                def emit_qk(kt, qi=qi, kn=kn, qn=qn, qr=qr):
                    d = kt - 4 * qi
                    q0 = max(d, 0) * 128
                    Sb = s_r.next()
                    qs = slice(qi * 512 + q0, (qi + 1) * 512)
                    ks = slice(kt * 128, (kt + 1) * 128)
                    P.op("pe", lambda e, Sb=Sb, q0=q0, qs=qs, ks=ks, kn=kn, qn=qn: e.matmul(
                        Sb[:, q0:512], lhsT=kn[:, ks], rhs=qn[:, qs], start=True, stop=False),
                        reads=[kn, qn], writes=[Sb], signal=False)
                    P.op("pe", lambda e, Sb=Sb, q0=q0, qs=qs, ks=ks, qr=qr: e.matmul(
                        Sb[:, q0:512], lhsT=kr[:, ks], rhs=qr[:, qs], start=False, stop=True),
                        reads=[kr, qr], writes=[Sb])
                    return Sb, d, q0
                nxt = emit_qk(0)
                for kt in range(nkt):
                    Sb, d, q0 = nxt
                    if kt + 1 < nkt:
                        nxt = emit_qk(kt + 1)
                    pt = pts.next()
                    P.op("act", lambda e, Sb=Sb, pt=pt, q0=q0: e.activation(out=pt[:, q0:512], in_=Sb[:, q0:512],
                                                                         func=AF.Exp, scale=scale),
                         reads=[Sb], writes=[pt])
                    if d >= 0:
                        P.op("dve", lambda e, pt=pt, q0=q0: e.tensor_tensor(out=pt[:, q0:q0 + 128], in0=pt[:, q0:q0 + 128],
                                                                           in1=maskb[:], op=ALU.mult),
                             reads=[pt, maskb], writes=[pt])
                    P.op("pe", lambda e, O=O, pt=pt, q0=q0, kt=kt, nkt=nkt, v=v: e.matmul(
                        O[:, q0:512], lhsT=v[:, kt, :], rhs=pt[:, q0:512], start=(kt == 0), stop=(kt == nkt - 1)),
                        reads=[v, pt], writes=[O], signal=False)
                    P.op("pe", lambda e, Rs=Rs, pt=pt, q0=q0, kt=kt, nkt=nkt: e.matmul(
                        Rs[:, q0:512], lhsT=ones[:], rhs=pt[:, q0:512], start=(kt == 0), stop=(kt == nkt - 1)),
                        reads=[ones, pt], writes=[Rs])
                P.op("dve", lambda e, Rs=Rs: e.reciprocal(out=rinv[:], in_=Rs[:]), reads=[Rs], writes=[rinv])
                P.op("dve", lambda e, O=O: e.tensor_tensor(out=o1[:], in0=O[:], in1=rinv[:], op=ALU.mult),
                     reads=[O, rinv], writes=[o1])
                st = stg.next()
                P.op("dve", lambda e, st=st, ga=ga: e.tensor_tensor(out=st[:], in0=o1[:], in1=ga[:], op=ALU.mult),
                     reads=[o1, ga], writes=[st])
                self.yc_store(16 + hl, 1, qi * 512, 512, lambda lo, hi, st=st: st[:, lo:hi].unsqueeze(1), st)
            if self.early_gather and "G" in self.phases:
                self.gather(16 + hl, 17 + hl, track=True)
                if hl == 4 and "D" in self.phases:
                    self.yq_copy(range(0, 16, 2), 64)
        for _ in conv:
            pass
        P.barrier()
        P.free_from(mark)


    def phase_B(self):
        P, L = self.P, self.L
        mark = len(P._ctx)
        psr = self.psr
        NCH = L // 128
        ident, ones = self.ident, self.ones

        def cload(name, src, shape, dt):
            b = P.sb(name, shape, dt)
            P.dma("sp", b[:], src[:], reads=[src], writes=[b])
            return b
        trile = cload("trile", self.c_trile, [128, 128], F32)
        trigt = cload("trigt", self.c_trigt, [128, 128], F32)
        onesf = cload("onesf", self.c_onesf, [128, 128], F32)
        maskb = cload("maskbB", self.c_maskb, [128, 128], BF16)
        Abc = cload("Abc", self.alog, [128, 32], F32)
        dsk = cload("dsk", self.dskip, [128, 32], F32)
        nws = cload("nws", self.nw_ssm, [128, 2048], F32)
        P.op("act", lambda e: e.activation(out=Abc[:], in_=Abc[:], func=AF.Exp), reads=[Abc], writes=[Abc])
        P.op("dve", lambda e: e.tensor_scalar(out=Abc[:], in0=Abc[:], scalar1=-1.0, scalar2=None, op0=ALU.mult),
             reads=[Abc], writes=[Abc])
        epsb = P.sb("epsbB", [128, 1], F32)
        P.op("pool", lambda e: e.memset(epsb[:], EPS), writes=[epsb])
        h = P.sb("h", [128, 2048], F32)
        hb = P.sb("hb", [128, 2048], BF16)
        P.op("pool", lambda e: e.memset(h[:], 0.0), writes=[h])
        P.op("pool", lambda e: e.memset(hb[:], 0.0), writes=[hb])
        xT4s = Rot([P.sb("xT4_%d" % i, [128, 20, 512], BF16) for i in range(2)])
        dt4s = Rot([P.sb("dt4_%d" % i, [128, 4, 32], F32) for i in range(2)])
        szs = Rot([P.sb("sz%d" % i, [128, 2048], BF16) for i in range(2)])
        xtm = P.sb("xtm", [128, 2048], BF16)
        btm = P.sb("btm", [128, 256], BF16)
        sm = {n: P.sb("sm_" + n, [128, 32], F32) for n in ("a", "acs", "ea", "cd", "dsd", "ds", "w2")}
        R = P.sb("R", [128, 32, 128], F32)
        E = P.sb("E", [128, 32, 128], BF16)
        cbm = P.sb("cbm", [128, 2, 128], BF16)
        S = P.sb("S", [128, 32, 128], BF16)
        xdt = P.sb("xdt", [128, 2048], BF16)
        xdd = P.sb("xdd", [128, 2048], BF16)
        xD = P.sb("xD", [128, 2048], BF16)
        y = P.sb("y", [128, 2048], F32)
        junk = P.sb("junk", [128, 1024], BF16)
        ssq = P.sb("ssq", [128, 2], F32)
        rtn = P.sb("rtn", [128, 2], F32)
        rsn = P.sb("rsn", [128, 2], F32)
        ytm = P.sb("ytm", [128, 2048], BF16)
        yT4 = P.sb("yT4", [128, 16, 512], BF16)

        def pbf(bank):
            return bank.ap.bitcast(BF16)

        xT4 = dt4 = None
        for c in range(NCH):
            cc = c % 4
            csl = slice(cc * 128, (cc + 1) * 128)
            if cc == 0:
                xT4, dt4 = xT4s.next(), dt4s.next()
                tok = slice(c * 128, c * 128 + 512)
                P.dma("sp", xT4[:], self.XC[:, tok].rearrange("(j p) t -> p j t", p=128), reads=[self.XC], writes=[xT4])
                P.dma("sp", dt4[:], self.DT[tok, :].rearrange("(a p) h -> p a h", p=128), reads=[self.DT], writes=[dt4])
            sz = szs.next()
            P.dma("sp", sz[:], self.SZ[c * 128:(c + 1) * 128, :], reads=[self.SZ], writes=[sz])
            dtc = dt4[:, cc, :]
            for i in range(2):
                bank = psr.next()
                pb = pbf(bank)
                for jj in range(8):
                    j = i * 8 + jj
                    P.op("pe", lambda e, pb=pb, jj=jj, j=j, xT4=xT4, csl=csl: e.transpose(
                        pb[:, jj * 128:(jj + 1) * 128], xT4[:, j, csl], ident[:]),
                        reads=[xT4, ident], writes=[bank], signal=(jj == 7))
                P.op("act", lambda e, pb=pb, i=i: e.activation(out=xtm[:, i * 1024:(i + 1) * 1024], in_=pb[:, 0:1024], func=AF.Copy),
                     reads=[bank], writes=[xtm])
            bank = psr.next()
            pb = pbf(bank)
            for g2 in range(2):
                P.op("pe", lambda e, pb=pb, g2=g2, xT4=xT4, csl=csl: e.transpose(
                    pb[:, g2 * 128:(g2 + 1) * 128], xT4[:, 16 + g2, csl], ident[:]),
                    reads=[xT4, ident], writes=[bank], signal=(g2 == 1))
            P.op("act", lambda e, pb=pb: e.activation(out=btm[:], in_=pb[:, 0:256], func=AF.Copy), reads=[bank], writes=[btm])
            a, acs, ea, cd, dsd, ds_, w2 = (sm[n] for n in ("a", "acs", "ea", "cd", "dsd", "ds", "w2"))
            P.op("dve", lambda e, dtc=dtc: e.tensor_tensor(out=a[:], in0=dtc, in1=Abc[:], op=ALU.mult),
                 reads=[dt4, Abc], writes=[a])
            bank = psr.next()
            P.op("pe", lambda e, bank=bank: e.matmul(bank[:, 0:32], lhsT=trile[:], rhs=a[:], start=True, stop=True),
                 reads=[trile, a], writes=[bank], signal=False)
            P.op("pe", lambda e, bank=bank: e.matmul(bank[:, 32:64], lhsT=onesf[:], rhs=a[:], start=True, stop=True),
                 reads=[onesf, a], writes=[bank])
            P.op("act", lambda e, bank=bank: e.activation(out=acs[:], in_=bank[:, 0:32], func=AF.Copy), reads=[bank], writes=[acs])
            P.op("act", lambda e, bank=bank: e.activation(out=ea[:], in_=bank[:, 0:32], func=AF.Exp), reads=[bank], writes=[ea])
            P.op("act", lambda e, bank=bank: e.activation(out=cd[:], in_=bank[:, 32:64], func=AF.Exp), reads=[bank], writes=[cd])
            P.op("dve", lambda e, bank=bank: e.tensor_tensor(out=dsd[:], in0=bank[:, 32:64], in1=acs[:], op=ALU.subtract),
                 reads=[bank, acs], writes=[dsd])
            P.op("act", lambda e: e.activation(out=ds_[:], in_=dsd[:], func=AF.Exp), reads=[dsd], writes=[ds_])
            P.op("dve", lambda e, dtc=dtc: e.tensor_tensor(out=w2[:], in0=dtc, in1=ds_[:], op=ALU.mult),
                 reads=[dt4, ds_], writes=[w2])
            P.op("pool", lambda e: e.tensor_tensor(out=R[:], in0=a[:].unsqueeze(2).to_broadcast([128, 32, 128]),
                                                  in1=trile[:].unsqueeze(1).to_broadcast([128, 32, 128]), op=ALU.mult),
                 reads=[a, trile], writes=[R])
            for j in range(8):
                bank = psr.next()
                P.op("pe", lambda e, bank=bank, j=j: e.matmul(
                    bank[:], lhsT=trigt[:], rhs=R[:, 4 * j:4 * j + 4, :].rearrange("p a b -> p (a b)"), start=True, stop=True),
                    reads=[trigt, R], writes=[bank])
                P.op("act", lambda e, bank=bank, j=j: e.activation(
                    out=E[:, 4 * j:4 * j + 4, :].rearrange("p a b -> p (a b)"), in_=bank[:], func=AF.Exp),
                    reads=[bank], writes=[E])
            bank = psr.next()
            for g2 in range(2):
                P.op("pe", lambda e, bank=bank, g2=g2, xT4=xT4, csl=csl: e.matmul(
                    bank[:, g2 * 128:(g2 + 1) * 128], lhsT=xT4[:, 16 + g2, csl], rhs=xT4[:, 18 + g2, csl], start=True, stop=True),
                    reads=[xT4], writes=[bank], signal=(g2 == 1))
            P.op("dve", lambda e, bank=bank: e.tensor_tensor(
                out=cbm[:], in0=bank[:, 0:256].rearrange("p (g l) -> p g l", g=2),
                in1=maskb[:].unsqueeze(1).to_broadcast([128, 2, 128]), op=ALU.mult),
                reads=[bank, maskb], writes=[cbm])
            P.op("dve", lambda e: e.tensor_tensor(
                out=S[:].rearrange("p (g r) l -> p g r l", g=2), in0=E[:].rearrange("p (g r) l -> p g r l", g=2),
                in1=cbm[:].unsqueeze(2).to_broadcast([128, 2, 16, 128]), op=ALU.mult),
                reads=[E, cbm], writes=[S])
            x3 = xtm[:].rearrange("p (h d) -> p h d", d=64)
            P.op("dve", lambda e, dtc=dtc, x3=x3: e.tensor_tensor(
                out=xdt[:].rearrange("p (h d) -> p h d", d=64), in0=x3,
                in1=dtc.unsqueeze(2).to_broadcast([128, 32, 64]), op=ALU.mult), reads=[xtm, dt4], writes=[xdt])
            P.op("pool", lambda e, x3=x3: e.tensor_tensor(
                out=xdd[:].rearrange("p (h d) -> p h d", d=64), in0=x3,
                in1=w2[:].unsqueeze(2).to_broadcast([128, 32, 64]), op=ALU.mult), reads=[xtm, w2], writes=[xdd])
            P.op("pool", lambda e, x3=x3: e.tensor_tensor(
                out=xD[:].rearrange("p (h d) -> p h d", d=64), in0=x3,
                in1=dsk[:].unsqueeze(2).to_broadcast([128, 32, 64]), op=ALU.mult), reads=[xtm, dsk], writes=[xD])
            for q in range(4):
                g2 = q // 2
                yo = psr.next()
                P.op("pe", lambda e, yo=yo, q=q, g2=g2, xT4=xT4, csl=csl: e.matmul(
                    yo[:], lhsT=xT4[:, 18 + g2, csl], rhs=hb[:, q * 512:(q + 1) * 512], start=True, stop=True),
                    reads=[xT4, hb], writes=[yo])
                yd = psr.next()
                P.op("pe", lambda e, yd=yd, q=q: e.matmul(yd[:], lhsT=ident[:], rhs=xD[:, q * 512:(q + 1) * 512],
                                                         start=True, stop=False),
                     reads=[ident, xD], writes=[yd], signal=False)
                for hh in range(8):
                    hd = q * 8 + hh
                    P.op("pe", lambda e, yd=yd, hh=hh, hd=hd: e.matmul(
                        yd[:, hh * 64:(hh + 1) * 64], lhsT=S[:, hd, :], rhs=xdt[:, hd * 64:(hd + 1) * 64],
                        start=False, stop=(hh == 7)), reads=[S, xdt], writes=[yd], signal=(hh == 7))
                ysl = y[:, q * 512:(q + 1) * 512]
                P.op("dve", lambda e, yo=yo, q=q, ysl=ysl: e.tensor_tensor(
                    out=ysl.rearrange("p (h d) -> p h d", d=64), in0=yo[:].rearrange("p (h d) -> p h d", d=64),
                    in1=ea[:, q * 8:(q + 1) * 8].unsqueeze(2).to_broadcast([128, 8, 64]), op=ALU.mult),
                    reads=[yo, ea], writes=[y])
                P.op("dve", lambda e, yd=yd, ysl=ysl: e.tensor_tensor(out=ysl, in0=yd[:], in1=ysl, op=ALU.add),
                     reads=[yd, y], writes=[y])
            P.op("dve", lambda e, sz=sz: e.tensor_tensor(out=y[:], in0=y[:], in1=sz[:], op=ALU.mult), reads=[y, sz], writes=[y])
            for g2 in range(2):
                P.op("act", lambda e, g2=g2: e.activation(out=junk[:], in_=y[:, g2 * 1024:(g2 + 1) * 1024], func=AF.Square,
                                                          accum_out=ssq[:, g2:g2 + 1]), reads=[y], writes=[junk, ssq])
            P.op("act", lambda e: e.activation(out=rtn[:], in_=ssq[:], func=AF.Sqrt, scale=1.0 / 1024, bias=epsb[:, 0:1]),
                 reads=[ssq, epsb], writes=[rtn])
            P.op("dve", lambda e: e.reciprocal(out=rsn[:], in_=rtn[:]), reads=[rtn], writes=[rsn])
            for g2 in range(2):
                gs = slice(g2 * 1024, (g2 + 1) * 1024)
                P.op("dve", lambda e, g2=g2, gs=gs: e.scalar_tensor_tensor(
                    out=ytm[:, gs], in0=y[:, gs], scalar=rsn[:, g2:g2 + 1], in1=nws[:, gs], op0=ALU.mult, op1=ALU.mult),
                    reads=[y, rsn, nws], writes=[ytm])
            for i in range(2):
                bank = psr.next()
                pb = pbf(bank)
                for jj in range(8):
                    j = i * 8 + jj
                    P.op("pe", lambda e, pb=pb, jj=jj, j=j: e.transpose(
                        pb[:, jj * 128:(jj + 1) * 128], ytm[:, j * 128:(j + 1) * 128], ident[:]),
                        reads=[ytm, ident], writes=[bank], signal=(jj == 7))
                P.op("act", lambda e, pb=pb, i=i, csl=csl: e.activation(
                    out=yT4[:, i * 8:(i + 1) * 8, csl], in_=pb[:, 0:1024].rearrange("p (j t) -> p j t", j=8), func=AF.Copy),
                    reads=[bank], writes=[yT4])
            if cc == 3:
                self.yc_store(0, 16, (c - 3) * 128, 512, lambda lo, hi: yT4[:, :, lo:hi], yT4)
                if self.early_gather and "G" in self.phases:
                    done_q = ((c + 1) * 128) // self.LS
                    newq = tuple(range(self.gathered_q, done_q))
                    if newq:
                        self.gather(0, 16, track=True, tqs=newq)
                        self.gathered_q = done_q
            P.op("dve", lambda e: e.tensor_tensor(
                out=h[:].rearrange("p (h d) -> p h d", d=64), in0=h[:].rearrange("p (h d) -> p h d", d=64),
                in1=cd[:].unsqueeze(2).to_broadcast([128, 32, 64]), op=ALU.mult), reads=[h, cd], writes=[h])
            for q in range(4):
                g2 = q // 2
                st = psr.next()
                P.op("pe", lambda e, st=st, q=q, g2=g2: e.matmul(
                    st[:], lhsT=btm[:, g2 * 128:(g2 + 1) * 128], rhs=xdd[:, q * 512:(q + 1) * 512], start=True, stop=True),
                    reads=[btm, xdd], writes=[st])
                P.op("dve", lambda e, st=st, q=q: e.tensor_tensor(
                    out=h[:, q * 512:(q + 1) * 512], in0=st[:], in1=h[:, q * 512:(q + 1) * 512], op=ALU.add),
                    reads=[st, h], writes=[h])
            P.op("act", lambda e: e.activation(out=hb[:], in_=h[:], func=AF.Copy), reads=[h], writes=[hb])
        P.barrier()
        P.free_from(mark)

    def gather(self, jt0, jt1, track=False, tqs=(0, 1, 2, 3)):
        P = self.P
        if "cc" not in P.sems:
            P._mksem("cc")
        sem = P.sems["cc"]
        YC, YG = self.YC, self.YG
        waits = []
        if track:
            need = P._deps("pool", [YC], [])
            waits = [(P.sems[k], v) for k, v in need.items()]

        def thunk(e):
            for (s_, v) in waits:
                e.wait_ge(s_, v)
            for jt in range(jt0, jt1):
                for tq in tqs:
                    e.collective_compute("AllGather", ALU.bypass, replica_groups=[[0, 1, 2, 3], [4, 5, 6, 7]],
                                         ins=[YC.ap[jt, tq]], outs=[YG.ap[jt, tq].rearrange("r p t -> (r p) t")]).then_inc(sem)
        P.streams["pool"].append(thunk)
        P.semval["cc"] = P.semval.get("cc", 0) + len(tqs) * (jt1 - jt0)

    def phase_G(self):
        P = self.P
        markG = len(P._ctx)
        P.barrier()
        early = self.early_gather
        if not early:
            self.gather(0, 16)
            self.gather(16, 24)
        sem = P.sems["cc"]
        YC, YG = self.YC, self.YG

        def thunk(e):
            n = 96
            for jt in range(0):
                for tq in range(4):
                    e.collective_compute("AllGather", ALU.bypass, replica_groups=[[0, 1, 2, 3], [4, 5, 6, 7]],
                                         ins=[YC.ap[jt, tq]], outs=[YG.ap[jt, tq].rearrange("r p t -> (r p) t")]).then_inc(sem)
                    n += 1
            e.wait_ge(sem, n)
        P.streams["pool"].append(thunk)
        P.barrier()
        for nm, srcb in (("YCd", YC), ("YGd", YG)):
            if nm in self.debug:
                flat = srcb.ap[:].flatten_outer_dims()
                nrow = flat.shape[0]
                d = P.dram(nm, [nrow, self.LS], BF16, kind="ExternalOutput")
                self.out_bufs.append(d)
                tmp = P.sb("dbg" + nm, [128, self.LS], BF16)
                for r in range(nrow // 128):
                    P.dma("sp", tmp[:], flat[r * 128:(r + 1) * 128, :], reads=[srcb], writes=[tmp])
                    P.dma("sp", d.ap[r * 128:(r + 1) * 128, :], tmp[:], reads=[tmp], writes=[d])
                P.barrier()
        P.free_from(markG)

    def phase_D(self):
        P, L, LS = self.P, self.L, self.LS
        mark = len(P._ctx)
        psr = self.psr
        TB = min(256, LS)
        NT = TB // 128
        uT = P.sb("uTD", [128, 32, TB], BF16)
        xs = [P.sb("xsD%d" % i, [128, 32, 128], F32) for i in range(1)]
        sq = P.sb("sqD", [128, 32, 128], BF16)
        rt = P.sb("rtD", [128, 128], F32)
        rs = P.sb("rsD", [128, 128], F32)
        win = P.sb("winD", [128, 32], F32)
        self.epsb = P.sb("epsbD", [128, 1], F32)
        P.op("pool", lambda e, t=self.epsb: e.memset(t[:], EPS), writes=[self.epsb])
        P.dma("sp", win[:], self.nw_in[:], reads=[self.nw_in], writes=[win])
        wts = Rot([P.sb("wtD%d" % i, [128, 32, 512], BF16) for i in range(2)])
        sg = [P.sb("sg%d" % i, [128, 32, TB], BF16) for i in range(2)]
        yT = P.sb("yTD", [128, 64, TB], BF16)
        tmpf = Rot([P.sb("tmpf%d" % i, [128, TB], F32) for i in range(2)])
        hrow = P.sb("hrow", [128, 4096], F32)
        xr = Rot([P.sb("xr%d" % i, [128, 512], F32) for i in range(2)])
        nwf = P.sb("nwf", [128, 4096], F32)
        P.dma("sp", nwf[:], self.nw_fin[:], reads=[self.nw_fin], writes=[nwf])
        ssq = P.sb("ssqD", [128, 1], F32)
        rtn = P.sb("rtnD", [128, 1], F32)
        rsn = P.sb("rsnD", [128, 1], F32)
        YG = self.YG
        tokv = {}
        if self.early_gather:
            self.yq_copy(range(16, 24, 2), 96)
        else:
            self.yq_copy(range(0, 24, 2), 96)

        def yg_load(dst_ap, row0, nj, bk):
            def thunk_fn(e):
                gidx = e.partition_id() % 4
                src = YG.ap[row0:row0 + nj * 128, bass.ds(gidx * LS + bk * TB, TB)].rearrange("(j p) t -> p j t", p=128)
                return src
            return thunk_fn

        for bk in range(LS // TB):
            self.make_uT(lambda s, bk=bk: (self.xTs[:, :, bk * TB + s * 128:bk * TB + (s + 1) * 128], self.xTs),
                         NT, uT, xs, sq, rt, rs, win, psr)
            for m in range(2):
                for ct in range(8):
                    wt = wts.next()
                    P.dma("sp", wt[:], self.WD.ap[m * 8 + ct].rearrange("p (k c) -> p k c", k=32), reads=[self.WD], writes=[wt])
                    for cs in range(4):
                        ps = psr.next()
                        for kc in range(32):
                            P.op("pe", lambda e, kc=kc, ps=ps, cs=cs, wt=wt: e.matmul(
                                ps[:, 0:TB], lhsT=wt[:, kc, cs * 128:(cs + 1) * 128], rhs=uT[:, kc, :],
                                start=(kc == 0), stop=(kc == 31)), reads=[uT, wt], writes=[ps], signal=(kc == 31))
                        P.op("act", lambda e, ps=ps, m=m, ct=ct, cs=cs: e.activation(
                            out=sg[m][:, ct * 4 + cs, :], in_=ps[:, 0:TB], func=AF.Sigmoid), reads=[ps], writes=[sg[m]])
            for m in range(2):
                nkh = 2 if m == 0 else 1
                for r in range(4):
                    row0 = 0 if m == 0 else 16
                    nj = 16 if m == 0 else 8
                    P.dma("sp", yT[:, r * nj:(r + 1) * nj, :],
                          self.YQ.ap[row0:row0 + nj, r, :, bk * TB:(bk + 1) * TB].rearrange("j p t -> p j t"),
                          reads=[self.YQ], writes=[yT])
                for ct in range(8):
                    w_list = []
                    for kh in range(nkh):
                        wt = wts.next()
                        srcw = self.WD.ap[(16 + kh * 8 + ct) if m == 0 else (32 + ct)].rearrange("p (k c) -> p k c", k=32)
                        sbuf_ = self.WD
                        w_list.append((wt, srcw, sbuf_))
                    pss = [psr.next() for _ in range(4)]
                    for kh, (wt, srcw, sbuf_) in enumerate(w_list):
                        P.dma("sp", wt[:], srcw, reads=[sbuf_], writes=[wt])
                        for cs in range(4):
                            ps = pss[cs]
                            for kc in range(32):
                                first = (kh == 0 and kc == 0)
                                last = (kh == nkh - 1 and kc == 31)
                                P.op("pe", lambda e, kc=kc, ps=ps, cs=cs, wt=wt, kh=kh, first=first, last=last: e.matmul(
                                    ps[:, 0:TB], lhsT=wt[:, kc, cs * 128:(cs + 1) * 128], rhs=yT[:, kh * 32 + kc, :],
                                    start=first, stop=last), reads=[yT, wt], writes=[ps], signal=(kc == 31))
                    for cs in range(4):
                        ps = pss[cs]
                        ci = ct * 4 + cs
                        if m == 0:
                            P.op("dve", lambda e, ps=ps, ci=ci: e.tensor_tensor(out=sg[0][:, ci, :], in0=ps[:, 0:TB],
                                                                               in1=sg[0][:, ci, :], op=ALU.mult),
                                 reads=[ps, sg[0]], writes=[sg[0]])
                        else:
                            tf = tmpf.next()
                            P.op("dve", lambda e, ps=ps, ci=ci, tf=tf: e.tensor_tensor(out=tf[:], in0=ps[:, 0:TB],
                                                                                     in1=sg[1][:, ci, :], op=ALU.mult),
                                 reads=[ps, sg[1]], writes=[tf])
                            P.op("pool", lambda e, ci=ci, tf=tf: e.tensor_tensor(out=sg[1][:, ci, :], in0=tf[:],
                                                                                in1=sg[0][:, ci, :], op=ALU.add),
                                 reads=[tf, sg[0], sg[1]], writes=[sg[1]])
            mT = sg[1]
            for tt in range(NT):
                r0 = bk * TB + tt * 128
                for ct in range(8):
                    xrow = xr.next()
                    P.dma("sp", xrow[:], self.xres[r0:r0 + 128, ct * 512:(ct + 1) * 512], reads=[self.xres], writes=[xrow])
                    wt = wts.next()
                    P.dma("sp", wt[:], self.WD.ap[40 + ct].rearrange("p (k c) -> p k c", k=32), reads=[self.WD], writes=[wt])
                    ps = psr.next()
                    for kc in range(32):
                        P.op("pe", lambda e, kc=kc, ps=ps, tt=tt, wt=wt: e.matmul(
                            ps[:], lhsT=mT[:, kc, tt * 128:(tt + 1) * 128], rhs=wt[:, kc, :],
                            start=(kc == 0), stop=(kc == 31)), reads=[mT, wt], writes=[ps], signal=(kc == 31))
                    P.op("dve", lambda e, ps=ps, ct=ct, xrow=xrow: e.tensor_tensor(
                        out=hrow[:, ct * 512:(ct + 1) * 512], in0=ps[:], in1=xrow[:], op=ALU.add),
                        reads=[ps, xrow], writes=[hrow])
                P.op("act", lambda e: e.activation(out=xs[0][:].rearrange("p a b -> p (a b)"), in_=hrow[:], func=AF.Square,
                                                   accum_out=ssq[:]), reads=[hrow], writes=[xs[0], ssq])
                P.op("act", lambda e, epsb=self.epsb: e.activation(out=rtn[:], in_=ssq[:], func=AF.Sqrt, scale=1.0 / 4096,
                                                                   bias=epsb[:, 0:1]),
                     reads=[ssq, self.epsb], writes=[rtn])
                P.op("dve", lambda e: e.reciprocal(out=rsn[:], in_=rtn[:]), reads=[rtn], writes=[rsn])
                P.op("dve", lambda e: e.scalar_tensor_tensor(out=hrow[:], in0=hrow[:], scalar=rsn[:, 0:1], in1=nwf[:],
                                                             op0=ALU.mult, op1=ALU.mult), reads=[hrow, rsn, nwf], writes=[hrow])
                P.dma("sp", self.OUT[r0:r0 + 128, :], hrow[:], reads=[hrow], writes=[self.OUT])
        P.barrier()
        P.free_from(mark)

    def finish(self):
        P = self.P
        P.barrier(include_bg=True)
        P.emit()
        P.close()
        return self.nc


def build_program(L, phases="A", debug=()):
    k = K(L, debug, phases)
    k.declare()
    k.load_consts()
    if "A" in phases:
        k.phase_A()
    if "M" in phases:
        k.phase_A2()
    if "B" in phases:
        k.phase_B()
    if "C" in phases:
        k.phase_C()
    if "G" in phases:
        k.phase_G()
    if "D" in phases:
        k.phase_D()
    return k


def _tile_w(w, cols=None):
    if cols is not None:
        wz = np.zeros((w.shape[0], len(cols)), np.float32)
        m = cols >= 0
        wz[:, m] = w[:, cols[m]]
        w = wz
    Kd, N = w.shape
    return np.ascontiguousarray(w.reshape(Kd // 128, 128, N // 512, 512).transpose(2, 1, 0, 3))


def _in_cols(g):
    c = []
    c += list(range(g * 2048, (g + 1) * 2048))
    c += list(range(8192 + g * 2048, 8192 + (g + 1) * 2048))
    c += list(range(16384 + g * 256, 16384 + (g + 1) * 256))
    c += list(range(17408 + g * 256, 17408 + (g + 1) * 256))
    c += list(range(18560, 18560 + 1024))
    c += list(range(19584, 19584 + 512))
    c += list(range(20160 + g * 1024, 20160 + (g + 1) * 1024))
    rope = list(range(20096, 20160))
    c += rope + rope[32:] + rope[:32]
    c += list(range(18432 + g * 32, 18432 + (g + 1) * 32))
    c += [-1] * (15 * 512 - len(c))
    return np.array(c, np.int64)


def prepare_inputs(inp, L):
    f = lambda a: np.asarray(a, np.float32)
    x = f(inp["x"])[:, :L]
    NB, LS = L // 1024, L // 4
    w_in = f(inp["w_in"])[0]
    conv_w, conv_b = f(inp["conv_w"])[0], f(inp["conv_b"])[0]
    w_uq, w_ukv = f(inp["w_uq"])[0], f(inp["w_ukv"])[0]
    shared = {}
    shared["w_gates"] = _tile_w(w_in[:, 24256:24256 + 8192])
    wbs = f(inp["w_branch_ssm"])[0]
    shared["w_bs"] = np.concatenate([_tile_w(wbs[:4096]), _tile_w(wbs[4096:])], 0)
    shared["w_ba"] = _tile_w(f(inp["w_branch_attn"])[0])
    shared["w_out"] = _tile_w(f(inp["w_out"])[0])
    shared["nw_in"] = np.ascontiguousarray(f(inp["norm_in_w"])[0].reshape(32, 128).T)
    shared["nw_q"] = np.ascontiguousarray(f(inp["q_norm_w"])[0].reshape(8, 128).T)
    shared["nw_kv"] = np.ascontiguousarray(f(inp["kv_norm_w"])[0].reshape(4, 128).T)
    shared["nw_fin"] = np.ascontiguousarray(np.broadcast_to(f(inp["norm_final_w"])[None, :], (128, 4096)))
    half = 32
    inv_freq = (np.float32(10000.0) ** (-(np.arange(0, half, dtype=np.float32) / np.float32(half)))).astype(np.float32)
    ang = (np.arange(L, dtype=np.float32)[None, :] * inv_freq[:, None]).astype(np.float32)
    cos, sin = np.cos(ang).astype(np.float32), np.sin(ang).astype(np.float32)
    shared["ropec"] = np.concatenate([cos, cos], 0)
    shared["ropes"] = np.concatenate([-sin, sin], 0)
    shared["c_ident"] = np.eye(128, dtype=np.float32).astype(ml_dtypes.bfloat16)
    shared["c_ones"] = np.ones((128, 128), ml_dtypes.bfloat16)
    shared["c_onesf"] = np.ones((128, 128), np.float32)
    k = np.arange(128)
    shared["c_trile"] = (k[:, None] <= k[None, :]).astype(np.float32)
    shared["c_trigt"] = (k[:, None] > k[None, :]).astype(np.float32)
    shared["c_maskb"] = (k[:, None] <= k[None, :]).astype(np.float32).astype(ml_dtypes.bfloat16)
    per_g = []
    for g in range(4):
        d = {}
        d["w_in"] = _tile_w(w_in, _in_cols(g))
        cols = []
        for hl in range(8):
            H = 8 * g + hl
            base = H * 192
            rope = list(range(base + 128, base + 192))
            cols += list(range(base, base + 128)) + rope + rope[32:] + rope[:32]
        d["w_uq"] = np.ascontiguousarray(w_uq[:, cols].reshape(8, 128, 2048).transpose(1, 0, 2))
        cols = []
        for hl in range(8):
            H = 8 * g + hl
            cols += list(range(H * 256, H * 256 + 128))
        for hl in range(8):
            H = 8 * g + hl
            cols += list(range(H * 256 + 128, H * 256 + 256))
        d["w_ukv"] = np.ascontiguousarray(w_ukv[:, cols].reshape(4, 128, 2048).transpose(1, 0, 2))
        ch = np.concatenate([np.arange(g * 2048, (g + 1) * 2048),
                             8192 + g * 256 + np.arange(256), 8192 + 1024 + g * 256 + np.arange(256)])
        cpar = np.concatenate([conv_w[:, ch], conv_b[None, ch]], 0)
        d["convp"] = np.ascontiguousarray(cpar.reshape(5, 20, 128).transpose(2, 1, 0))
        hs = slice(g * 32, (g + 1) * 32)
        bc = lambda v: np.ascontiguousarray(np.broadcast_to(v[None, :], (128, v.shape[0])))
        d["dtb"] = bc(f(inp["dt_bias"])[0, hs])
        d["alog"] = bc(f(inp["a_log"])[0, hs])
        d["dskip"] = bc(f(inp["d_skip"])[0, hs])
        d["nw_ssm"] = bc(f(inp["ssm_norm_w"])[0, g * 2048:(g + 1) * 2048])
        per_g.append(d)
    maps = []
    for c in range(8):
        b, g = c // 4, c % 4
        m = dict(shared)
        m.update(per_g[g])
        xb = x[b]
        m["xT"] = np.ascontiguousarray(xb.reshape(NB, 1024, 32, 128).transpose(0, 3, 2, 1))
        xsl = xb[g * LS:(g + 1) * LS]
        m["xTs"] = np.ascontiguousarray(xsl.reshape(LS, 32, 128).transpose(2, 1, 0))
        m["xres"] = np.ascontiguousarray(xsl)
        maps.append(m)
    return maps


_CACHE = {}


def run(inputs, L=SEQ, phases="AMBCGD", debug=()):
    import time
    t0 = time.time()
    key = (L, phases, tuple(debug))
    if key not in _CACHE:
        k = build_program(L, phases, debug)
        nc = k.finish()
        _CACHE[key] = (nc, set(k.inp.keys()), k.P.ninstr)
    nc, names, ninstr = _CACHE[key]
    t1 = time.time()
    maps = prepare_inputs(inputs, L)
    maps = [{n: m[n] for n in names} for m in maps]
    t2 = time.time()
    res = run_bass_kernel_spmd(nc, maps, core_ids=list(range(8)))
    if os.environ.get("KDBG"):
        print("ninstr %d build %.1fs prep %.1fs run %.1fs" % (ninstr, t1 - t0, t2 - t1, time.time() - t2), flush=True)
    return res.results


def kernel(**inputs):
    res = run(inputs)
    LS = SEQ // 4
    out = np.empty((2, SEQ, D_MODEL), np.float32)
    for c in range(8):
        b, g = c // 4, c % 4
        out[b, g * LS:(g + 1) * LS] = res[c]["out"]
    return out
```
